# Optimizing a Trainium2 kernel written in Bass

```python
import math
import jax, jax.numpy as jnp
from jax import lax
import numpy as np

D_MODEL = 1024
BATCH = 2
SEQ = 8192
DEPTH = 2
DEC_BATCH = 32
DEC_SEQ = 8
PAST_LEN = 8192
PAGE_SIZE = 128

N_A_LAYERS = DEPTH // 2
N_B_LAYERS = DEPTH - N_A_LAYERS
D_RNN = D_MODEL
N_RG_BLOCKS = 8
RG_BLOCK = D_RNN // N_RG_BLOCKS
CONV_W = 4
RG_C = 8.0
N_HEADS = 16
HEAD_DIM = D_MODEL // N_HEADS
N_KV = 4
HPG = N_HEADS // N_KV
CMP_BLOCK = 64
N_SEL = 16
WINDOW = 512
D_PHI = 128
Q_BLOCK = 128
DN_ALPHA = (2.0 * DEPTH) ** 0.25
DN_BETA = (8.0 * DEPTH) ** -0.25
LN_EPS = 1e-5
NEG = -1e30
FORCED = 1e6

kernel_name = 'hawk_nsa_yoco_decoder_step'


def alibi_slopes():
    return jnp.asarray(2.0 ** (-8.0 * np.arange(1, N_HEADS + 1) / N_HEADS), jnp.float32)


def layer_norm(x, g, b):
    xf = x.astype(jnp.float32)
    mu = xf.mean(-1, keepdims=True)
    var = jnp.square(xf - mu).mean(-1, keepdims=True)
    return ((xf - mu) * lax.rsqrt(var + LN_EPS) * g.astype(jnp.float32) + b.astype(jnp.float32)).astype(x.dtype)


def ada_mod(c, w, b):
    m = jax.nn.silu(c) @ w + b
    shift, scale, gate = jnp.split(m[:, None, :], 3, axis=-1)
    return shift, scale, gate


def masked_softmax(s, mask):
    s = jnp.where(mask, s.astype(jnp.float32), NEG)
    p = jax.nn.softmax(s, axis=-1)
    return jnp.where(mask, p, 0.0)


def lru_combine(left, right):
    a1, b1 = left
    a2, b2 = right
    return a1 * a2, a2 * b1 + b2


def rglru_mixer(h, h0, conv0, w_in, conv_w, conv_b, w_r, b_r, w_i, b_i, lam, w_out):
    B, T, _ = h.shape
    xb, zg = jnp.split(h @ w_in, 2, axis=-1)
    xp = jnp.concatenate([conv0.astype(xb.dtype), xb], axis=1)
    xc = conv_b + xp[:, 0:T] * conv_w[0]
    for k in range(1, CONV_W):
        xc = xc + xp[:, k:k + T] * conv_w[k]
    xg = xc.reshape(B, T, N_RG_BLOCKS, RG_BLOCK)
    r = jax.nn.sigmoid((jnp.einsum('btnc,ncd->btnd', xg, w_r).reshape(B, T, D_RNN) + b_r).astype(jnp.float32))
    i = jax.nn.sigmoid((jnp.einsum('btnc,ncd->btnd', xg, w_i).reshape(B, T, D_RNN) + b_i).astype(jnp.float32))
    log_a = -RG_C * jax.nn.softplus(-lam.astype(jnp.float32)) * r
    a = jnp.exp(log_a)
    gain = jnp.sqrt(jnp.maximum(-jnp.expm1(2.0 * log_a), 0.0))
    b = gain * i * xc.astype(jnp.float32)
    b = b.at[:, 0].add(a[:, 0] * h0.astype(jnp.float32))
    _, hs = lax.associative_scan(lru_combine, (a, b), axis=1)
    y = (hs.astype(h.dtype) * jax.nn.silu(zg)) @ w_out
    return y, hs[:, -1], xp[:, -(CONV_W - 1):]


def compress_blocks(kv, phi_pe, w_phi1, b_phi1, w_phi2, b_phi2):
    B, L = kv.shape[:2]
    nc = L // CMP_BLOCK
    blk = kv.reshape(B, nc, CMP_BLOCK, N_KV, 2, HEAD_DIM) + phi_pe[None, None, :, None]
    hid = jax.nn.silu(jnp.einsum('bnlgcd,cldp->bngcp', blk, w_phi1) + b_phi1)
    return jnp.einsum('bngcp,cpd->bngcd', hid, w_phi2) + b_phi2


def nsa_query_side(h, w_in, b_gate):
    B, T, _ = h.shape
    hd = N_HEADS * HEAD_DIM
    u = h @ w_in
    q = u[..., :hd].reshape(B, T, N_HEADS, HEAD_DIM)
    z = u[..., hd:2 * hd]
    gl = (u[..., 2 * hd:] + b_gate).reshape(B, T, N_HEADS, 3)
    return q, z, gl


def nsa_attend(q, gate_logits, t, kc, vc, c_end, fetch_sel, n_sel_blocks, wk, wv, w_pos):
    B, Tq = q.shape[:2]
    dt = q.dtype
    slopes = alibi_slopes().reshape(N_KV, HPG)
    qg = q.reshape(B, Tq, N_KV, HPG, HEAD_DIM) * (HEAD_DIM ** -0.5)
    tf = t.astype(jnp.float32)
    s = jnp.einsum('bqghd,bngd->bqghn', qg, kc).astype(jnp.float32)
    dist = tf[:, None] - c_end.astype(jnp.float32)[None, :]
    s = s - slopes[None, None, :, :, None] * dist[None, :, None, None, :]
    mask = (c_end[None, :] <= t[:, None])[None, :, None, None, :]
    p_cmp = masked_softmax(s, mask)
    o_cmp = jnp.einsum('bqghn,bngd->bqghd', p_cmp.astype(dt), vc)
    nc = kc.shape[1]
    imp = jnp.pad(p_cmp.sum(axis=3), ((0, 0), (0, 0), (0, 0), (0, n_sel_blocks - nc)))
    j = jnp.arange(n_sel_blocks)
    cb = t // CMP_BLOCK
    forced = (j[None, :] == 0) | (j[None, :] == cb[:, None]) | (j[None, :] == cb[:, None] - 1)
    causal_blk = j[None, :] <= cb[:, None]
    score = jnp.where(forced[:, None, :], FORCED, jnp.where(causal_blk[:, None, :], imp, -1.0))
    _, idx = lax.top_k(score, min(N_SEL, n_sel_blocks))
    kv_sel, pos = fetch_sel(idx)
    kl = pos.shape[-2] * pos.shape[-1]
    s = jnp.einsum('bqghd,bqgkld->bqghkl', qg, kv_sel[..., 0, :]).astype(jnp.float32)
    dist = (t[None, :, None, None, None] - pos).astype(jnp.float32)
    s = s - slopes[None, None, :, :, None, None] * dist[:, :, :, None]
    mask = (pos <= t[None, :, None, None, None])[:, :, :, None]
    p_sel = masked_softmax(s.reshape(B, Tq, N_KV, HPG, kl), mask.reshape(B, Tq, N_KV, 1, kl))
    o_sel = jnp.einsum('bqghs,bqgsd->bqghd', p_sel.astype(dt),
                       kv_sel[..., 1, :].reshape(B, Tq, N_KV, kl, HEAD_DIM))
    s = jnp.einsum('bqghd,bsgd->bqghs', qg, wk).astype(jnp.float32)
    dist = tf[:, None] - w_pos.astype(jnp.float32)[None, :]
    s = s - slopes[None, None, :, :, None] * dist[None, :, None, None, :]
    dpos = t[:, None] - w_pos[None, :]
    mask = ((dpos >= 0) & (dpos <= WINDOW) & (w_pos >= 0)[None, :])[None, :, None, None, :]
    p_win = masked_softmax(s, mask)
    o_win = jnp.einsum('bqghs,bsgd->bqghd', p_win.astype(dt), wv)
    g = jax.nn.sigmoid(gate_logits.astype(jnp.float32)).astype(dt).reshape(B, Tq, N_KV, HPG, 3)
    o = g[..., 0:1] * o_cmp + g[..., 1:2] * o_sel + g[..., 2:3] * o_win
    return o.reshape(B, Tq, N_HEADS * HEAD_DIM)


def nsa_prompt(h, kv_sel, kv_win, kc, vc, c_end, w_in, b_gate, w_out):
    B, T, _ = h.shape
    q, z, gl = nsa_query_side(h, w_in, b_gate)
    nqb = T // Q_BLOCK
    win_pad = jnp.pad(kv_win, ((0, 0), (WINDOW, 0), (0, 0), (0, 0), (0, 0)))
    bidx = jnp.arange(B)[:, None, None, None, None]
    gidx = jnp.arange(N_KV)[None, None, :, None, None]
    offs = jnp.arange(CMP_BLOCK)

    def fetch(idx):
        pos = idx[..., None] * CMP_BLOCK + offs
        return kv_sel[bidx, pos, gidx], pos

    def one_block(args):
        qb, gb, start = args
        t = start + jnp.arange(Q_BLOCK)
        wkv = lax.dynamic_slice_in_dim(win_pad, start, WINDOW + Q_BLOCK, axis=1)
        w_pos = start - WINDOW + jnp.arange(WINDOW + Q_BLOCK)
        return nsa_attend(qb, gb, t, kc, vc, c_end, fetch, T // CMP_BLOCK,
                          wkv[..., 0, :], wkv[..., 1, :], w_pos)

    qb = q.reshape(B, nqb, Q_BLOCK, N_HEADS, HEAD_DIM).swapaxes(0, 1)
    gb = gl.reshape(B, nqb, Q_BLOCK, N_HEADS, 3).swapaxes(0, 1)
    starts = jnp.arange(nqb, dtype=jnp.int32) * Q_BLOCK
    o = lax.map(one_block, (qb, gb, starts))
    o = o.swapaxes(0, 1).reshape(B, T, N_HEADS * HEAD_DIM)
    return (o * jax.nn.silu(z)) @ w_out


def nsa_sample(h, new_sel, win_keys, past_len, kc, vc, c_end, cache_sel, page_table, w_in, b_gate, w_out):
    B, S, _ = h.shape
    q, z, gl = nsa_query_side(h, w_in, b_gate)
    t = past_len + jnp.arange(S)
    bidx = jnp.arange(B)[:, None, None, None, None]
    gidx = jnp.arange(N_KV)[None, None, :, None, None]
    offs = jnp.arange(CMP_BLOCK)

    def fetch(idx):
        pos = idx[..., None] * CMP_BLOCK + offs
        pp = jnp.minimum(pos, past_len - 1)
        phys = page_table[bidx, pp // PAGE_SIZE]
        past_rows = cache_sel[phys, pp % PAGE_SIZE, gidx]
        new_rows = new_sel[bidx, jnp.clip(pos - past_len, 0, S - 1), gidx]
        return jnp.where((pos < past_len)[..., None, None], past_rows, new_rows), pos

    wb = win_keys.shape[1] - S
    w_pos = past_len - wb + jnp.arange(wb + S)
    n_sb = -(-(past_len + S) // CMP_BLOCK)
    o = nsa_attend(q, gl, t, kc, vc, c_end, fetch, n_sb, win_keys[..., 0, :], win_keys[..., 1, :], w_pos)
    return (o * jax.nn.silu(z)) @ w_out


def setup_inputs(seed: int = 0) -> dict:
    key = jax.random.key(seed)
    ks = iter(jax.random.split(key, 48))

    def nrm(shape, s):
        return jax.random.normal(next(ks), shape, jnp.float32) * s

    n_pages = PAST_LEN // PAGE_SIZE
    n_phys = (5 * DEC_BATCH * n_pages) // 4
    wb = min(WINDOW, PAST_LEN)
    perm = jax.random.permutation(next(ks), n_phys)[:DEC_BATCH * n_pages]
    page_table = perm.reshape(DEC_BATCH, n_pages).astype(jnp.int32)
    a0 = jax.random.uniform(next(ks), (N_A_LAYERS, D_RNN), jnp.float32, 0.9, 0.999)
    lam_a = jnp.log(a0) - jnp.log1p(-a0)
    hd = N_HEADS * HEAD_DIM
    return {
        'x_prompt': nrm((BATCH, SEQ, D_MODEL), 1.0),
        'x_sample': nrm((DEC_BATCH, DEC_SEQ, D_MODEL), 1.0),
        'c_prompt': nrm((BATCH, D_MODEL), 1.0),
        'c_sample': nrm((DEC_BATCH, D_MODEL), 1.0),
        'state_h': nrm((N_A_LAYERS, DEC_BATCH, D_RNN), 0.5),
        'state_conv': nrm((N_A_LAYERS, DEC_BATCH, CONV_W - 1, D_RNN), 1.0),
        'cache_cmp': nrm((n_phys, PAGE_SIZE, N_KV, 2, HEAD_DIM), 1.0),
        'cache_sel': nrm((n_phys, PAGE_SIZE, N_KV, 2, HEAD_DIM), 1.0),
        'state_win': nrm((DEC_BATCH, wb, N_KV, 2, HEAD_DIM), 1.0),
        'page_table': page_table,
        'w_ada': nrm((DEPTH, D_MODEL, 3 * D_MODEL), 0.2 * D_MODEL ** -0.5),
        'b_ada': nrm((DEPTH, 3 * D_MODEL), 0.01),
        'ln_g': 1.0 + nrm((DEPTH, D_MODEL), 0.02),
        'ln_b': nrm((DEPTH, D_MODEL), 0.02),
        'w_in_a': nrm((N_A_LAYERS, D_MODEL, 2 * D_RNN), D_MODEL ** -0.5),
        'conv_w_a': nrm((N_A_LAYERS, CONV_W, D_RNN), CONV_W ** -0.5),
        'conv_b_a': nrm((N_A_LAYERS, D_RNN), 0.02),
        'w_r_a': nrm((N_A_LAYERS, N_RG_BLOCKS, RG_BLOCK, RG_BLOCK), RG_BLOCK ** -0.5),
        'b_r_a': nrm((N_A_LAYERS, D_RNN), 0.02),
        'w_i_a': nrm((N_A_LAYERS, N_RG_BLOCKS, RG_BLOCK, RG_BLOCK), RG_BLOCK ** -0.5),
        'b_i_a': nrm((N_A_LAYERS, D_RNN), 0.02),
        'lam_a': lam_a,
        'w_out_a': nrm((N_A_LAYERS, D_RNN, D_MODEL), DN_BETA * D_RNN ** -0.5),
        'w_kv': nrm((D_MODEL, 3 * N_KV * 2 * HEAD_DIM), D_MODEL ** -0.5),
        'phi_pe': nrm((CMP_BLOCK, 2, HEAD_DIM), 0.1),
        'w_phi1': nrm((2, CMP_BLOCK, HEAD_DIM, D_PHI), (CMP_BLOCK * HEAD_DIM) ** -0.5),
        'b_phi1': nrm((2, D_PHI), 0.02),
        'w_phi2': nrm((2, D_PHI, HEAD_DIM), D_PHI ** -0.5),
        'b_phi2': nrm((2, HEAD_DIM), 0.02),
        'w_in_b': nrm((N_B_LAYERS, D_MODEL, 2 * hd + 3 * N_HEADS), D_MODEL ** -0.5),
        'b_gate_b': nrm((N_B_LAYERS, 3 * N_HEADS), 0.1),
        'w_out_b': nrm((N_B_LAYERS, hd, D_MODEL), DN_BETA * hd ** -0.5),
    }


def reference(x_prompt, x_sample, c_prompt, c_sample, state_h, state_conv, cache_cmp, cache_sel, state_win,
              page_table, w_ada, b_ada, ln_g, ln_b, w_in_a, conv_w_a, conv_b_a, w_r_a, b_r_a, w_i_a, b_i_a,
              lam_a, w_out_a, w_kv, phi_pe, w_phi1, b_phi1, w_phi2, b_phi2, w_in_b, b_gate_b, w_out_b):
    Bp, T, _ = x_prompt.shape
    Bs, S, _ = x_sample.shape
    past_len = page_table.shape[1] * PAGE_SIZE
    xp, xs = x_prompt, x_sample
    hp_list, cp_list, hs_list, cs_list = [], [], [], []
    for layer in range(DEPTH):
        shp, scp, gp = ada_mod(c_prompt, w_ada[layer], b_ada[layer])
        shs, scs, gs = ada_mod(c_sample, w_ada[layer], b_ada[layer])
        mp = xp * (1.0 + scp) + shp
        ms = xs * (1.0 + scs) + shs
        if layer < N_A_LAYERS:
            a = layer
            fp, hp, cp = rglru_mixer(mp, jnp.zeros((Bp, D_RNN), xp.dtype),
                                     jnp.zeros((Bp, CONV_W - 1, D_RNN), xp.dtype),
                                     w_in_a[a], conv_w_a[a], conv_b_a[a], w_r_a[a], b_r_a[a],
                                     w_i_a[a], b_i_a[a], lam_a[a], w_out_a[a])
            fs, hs, cs = rglru_mixer(ms, state_h[a], state_conv[a],
                                     w_in_a[a], conv_w_a[a], conv_b_a[a], w_r_a[a], b_r_a[a],
                                     w_i_a[a], b_i_a[a], lam_a[a], w_out_a[a])
            hp_list.append(hp)
            cp_list.append(cp)
            hs_list.append(hs)
            cs_list.append(cs)
        else:
            if layer == N_A_LAYERS:
                kvp = (xp @ w_kv).reshape(Bp, T, 3, N_KV, 2, HEAD_DIM)
                kvs = (xs @ w_kv).reshape(Bs, S, 3, N_KV, 2, HEAD_DIM)
                new_cmp_p, new_sel_p, win_p = kvp[:, :, 0], kvp[:, :, 1], kvp[:, :, 2]
                new_cmp_s, new_sel_s, win_s = kvs[:, :, 0], kvs[:, :, 1], kvs[:, :, 2]
                nc_p = T // CMP_BLOCK
                comp_p = compress_blocks(new_cmp_p[:, :nc_p * CMP_BLOCK], phi_pe, w_phi1, b_phi1, w_phi2, b_phi2)
                c_end_p = (jnp.arange(nc_p) + 1) * CMP_BLOCK - 1
                past_cmp = cache_cmp[page_table].reshape(Bs, past_len, N_KV, 2, HEAD_DIM)
                full_cmp = jnp.concatenate([past_cmp, new_cmp_s.astype(past_cmp.dtype)], axis=1)
                nc_s = (past_len + S) // CMP_BLOCK
                comp_s = compress_blocks(full_cmp[:, :nc_s * CMP_BLOCK], phi_pe, w_phi1, b_phi1, w_phi2, b_phi2)
                c_end_s = (jnp.arange(nc_s) + 1) * CMP_BLOCK - 1
                win_keys_s = jnp.concatenate([state_win.astype(win_s.dtype), win_s], axis=1)
                new_win_p = win_p[:, -min(WINDOW, T):]
                new_win_s = win_keys_s[:, -state_win.shape[1]:]
            bl = layer - N_A_LAYERS
            fp = nsa_prompt(mp, new_sel_p, win_p, comp_p[..., 0, :], comp_p[..., 1, :], c_end_p,
                            w_in_b[bl], b_gate_b[bl], w_out_b[bl])
            fs = nsa_sample(ms, new_sel_s, win_keys_s, past_len, comp_s[..., 0, :], comp_s[..., 1, :], c_end_s,
                            cache_sel, page_table, w_in_b[bl], b_gate_b[bl], w_out_b[bl])
        xp = layer_norm(DN_ALPHA * xp + (1.0 + gp) * fp, ln_g[layer], ln_b[layer])
        xs = layer_norm(DN_ALPHA * xs + (1.0 + gs) * fs, ln_g[layer], ln_b[layer])
    new_h_p = jnp.stack(hp_list)
    new_conv_p = jnp.stack(cp_list)
    new_h_s = jnp.stack(hs_list)
    new_conv_s = jnp.stack(cs_list)
    return (xp, xs, new_cmp_p, new_sel_p, new_win_p, new_h_p, new_conv_p,
            new_cmp_s, new_sel_s, new_win_s, new_h_s, new_conv_s)
```

```python
import contextlib
import numpy as np
import ml_dtypes
import concourse.bass as bass
import concourse.mybir as mybir
from concourse.bass_utils import run_bass_kernel_spmd

F32 = mybir.dt.float32
BF16 = mybir.dt.bfloat16
I32 = mybir.dt.int32
AF = mybir.ActivationFunctionType
ALU = mybir.AluOpType
AX = mybir.AxisListType

D = 1024
NCH = 8
SEQ = 8192
TT = 256
NSUB = TT // 128
DEC_B = 4
DEC_S = 8
NS_TOK = DEC_B * DEC_S
ALPHA = 4.0 ** 0.25
LN_EPS = 1e-5
RG_C = 8.0
NEGM = -30000.0
FORCEDV = 1.0e6


class Prog:
    ENG = ("pe", "act", "dve", "pool", "sp")

    def __init__(self):
        self.nc = bass.Bass("TRN2", target_bir_lowering=False)
        self.es = contextlib.ExitStack()
        nc = self.nc
        self.eng = {"pe": nc.tensor, "act": nc.scalar, "dve": nc.vector, "pool": nc.gpsimd, "sp": nc.sync}
        self.sem = {e: self.es.enter_context(nc.semaphore("s_" + e)) for e in self.ENG}
        self.cnt = {e: 0 for e in self.ENG}
        self.seen = {e: {} for e in self.ENG}
        self.dsem = {}
        self.dcnt = {}
        self.bufs = {}
        self.n_ins = 0
        self.stack = [self.es]

    def sb(self, name, shape, dt):
        return self.stack[-1].enter_context(self.nc.sbuf_tensor(name, list(shape), dt))

    def barrier(self):
        deps = {}
        for e2 in self.ENG:
            if self.cnt[e2]:
                deps[("eng", e2)] = self.cnt[e2]
        for k in self.dcnt:
            deps[("dma", k)] = self.dcnt[k]
        for e in self.ENG:
            self._wait(e, dict(deps))

    @contextlib.contextmanager
    def scope(self):
        st = contextlib.ExitStack()
        self.stack.append(st)
        try:
            yield
        finally:
            self.barrier()
            self.stack.pop()
            st.close()

    def ps(self, name, shape, dt):
        return self.es.enter_context(self.nc.psum_tensor(name, list(shape), dt))

    def dram(self, name, shape, dt, kind="Internal"):
        return self.nc.dram_tensor(name, list(shape), dt, kind=kind).ap()

    def _state(self, k):
        st = self.bufs.get(k)
        if st is None:
            st = self.bufs[k] = {"w": {}, "r": {}}
        return st

    def _deps(self, r, w):
        deps = {}
        for k in r:
            for s, v in self._state(k)["w"].items():
                deps[s] = max(deps.get(s, 0), v)
        for k in w:
            st = self._state(k)
            for s, v in st["w"].items():
                deps[s] = max(deps.get(s, 0), v)
            for s, v in st["r"].items():
                deps[s] = max(deps.get(s, 0), v)
        return deps

    def _wait(self, e, deps):
        eng = self.eng[e]
        seen = self.seen[e]
        for s, v in deps.items():
            if s[0] == "dma":
                v = max(v, self.dcnt[s[1]])
                if seen.get(s, 0) >= v:
                    continue
                eng.wait_ge(self.dsem[s[1]], v)
            else:
                if s[1] == e and False:
                    continue
                if seen.get(s, 0) >= v:
                    continue
                eng.wait_ge(self.sem[s[1]], v)
            seen[s] = v

    def _commit(self, me_src, me_val, r, w):
        for k in w:
            st = self._state(k)
            st["w"] = {me_src: me_val}
            st["r"] = {}
        for k in r:
            if k in w:
                continue
            st = self._state(k)
            st["r"][me_src] = max(st["r"].get(me_src, 0), me_val)

    def op(self, e, fn, r=(), w=()):
        self._wait(e, self._deps(r, w))
        ins = fn(self.eng[e])
        self.cnt[e] += 1
        ins.then_inc(self.sem[e], 1)
        self._commit(("eng", e), self.cnt[e], r, w)
        self.n_ins += 1
        return ins

    def dma(self, q, out, in_, r=(), w=(), semkey=None, **kw):
        self._wait(q, self._deps(r, w))
        if semkey is None:
            semkey = (tuple(w) + tuple(r))[0]
        if semkey not in self.dsem:
            self.dsem[semkey] = self.es.enter_context(self.nc.semaphore("d%d" % len(self.dsem)))
            self.dcnt[semkey] = 0
        ins = self.eng[q].dma_start(out=out, in_=in_, **kw)
        self.dcnt[semkey] += 16
        ins.then_inc(self.dsem[semkey], 16)
        self._commit(("dma", semkey), self.dcnt[semkey], r, w)
        self.n_ins += 1
        return ins

    def gather(self, out, in_, idx_ap, r=(), w=(), semkey=None):
        q = "pool"
        self._wait(q, self._deps(r, w))
        if semkey is None:
            semkey = tuple(w)[0]
        if semkey not in self.dsem:
            self.dsem[semkey] = self.es.enter_context(self.nc.semaphore("d%d" % len(self.dsem)))
            self.dcnt[semkey] = 0
        ins = self.nc.gpsimd.indirect_dma_start(
            out=out, out_offset=None, in_=in_, in_offset=bass.IndirectOffsetOnAxis(ap=idx_ap, axis=0))
        self.dcnt[semkey] += 16
        ins.then_inc(self.dsem[semkey], 16)
        self._commit(("dma", semkey), self.dcnt[semkey], r, w)
        self.n_ins += 1
        return ins

    def finish(self):
        for e in ("sp",):
            deps = {}
            for k, st in self.bufs.items():
                for s, v in list(st["w"].items()) + list(st["r"].items()):
                    deps[s] = max(deps.get(s, 0), v)
            for k in self.dcnt:
                deps[("dma", k)] = self.dcnt[k]
            for e2 in self.ENG:
                if self.cnt[e2]:
                    deps[("eng", e2)] = self.cnt[e2]
            self._wait(e, deps)


def build(n_phys, stage=9):
    P = Prog()
    nc = P.nc
    es = P.es
    ctx_nc = nc.allow_non_contiguous_dma(reason="small strided parameter / state loads")
    es.enter_context(ctx_nc)

    def din(name, shape, dt=F32):
        return nc.dram_tensor(name, list(shape), dt, kind="ExternalInput").ap()

    def dout(name, shape, dt=F32):
        return nc.dram_tensor(name, list(shape), dt, kind="ExternalOutput").ap()

    xf = din("xf", [SEQ, D])
    xs = din("xs", [NS_TOK, D])
    cvec = din("cvec", [5, D])
    sh0 = din("sh0", [DEC_B, D])
    sc0 = din("sc0", [DEC_B * 3, D])
    swin = din("swin", [DEC_B * 512, 512])
    ptab = din("ptab", [DEC_B, 64], I32)
    ccmp = din("ccmp", [n_phys * 128, 512])
    csel = din("csel", [n_phys * 128, 512])
    w_ada = din("w_ada", [2, D, 3 * D])
    b_ada = din("b_ada", [2, 3 * D])
    ln_g = din("ln_g", [2, D])
    ln_b = din("ln_b", [2, D])
    w_in_a = din("w_in_a", [D, 2 * D])
    conv_w = din("conv_w", [4, D])
    conv_b = din("conv_b", [1, D])
    w_r = din("w_r", [8, 128, 128])
    b_r = din("b_r", [1, D])
    w_i = din("w_i", [8, 128, 128])
    b_i = din("b_i", [1, D])
    lam = din("lam", [1, D])
    w_out_a = din("w_out_a", [D, D])
    w_kv = din("w_kv", [D, 1536])
    phi_pe = din("phi_pe", [64, 128])
    w_phi1 = din("w_phi1", [2, 64, 64, 128])
    b_phi1 = din("b_phi1", [2, 128])
    w_phi2 = din("w_phi2", [2, 128, 64])
    b_phi2 = din("b_phi2", [2, 64])
    w_in_b = din("w_in_b", [D, 2096])
    b_gate = din("b_gate", [1, 48])
    w_out_b = din("w_out_b", [D, D])

    def dtab(name, shape, dt):
        return nc.dram_tensor(name, list(shape), dt, kind="ExternalInput").ap()
    t_idx_own = dtab("t_idx_own", [128, 16], I32)
    t_idx_win = dtab("t_idx_win", [128, 20], I32)
    t_idx_slot = dtab("t_idx_slot", [128, 1], I32)
    t_kaug_sel = dtab("t_kaug_sel", [8, 8192], BF16)
    t_kaug_win = dtab("t_kaug_win", [8, 2560], BF16)
    t_kaug_cmp = dtab("t_kaug_cmp", [8, 128], BF16)
    t_fbn = dtab("t_fbn", [16, 128, 128], F32)
    t_caus = dtab("t_caus", [16, 128, 128], F32)
    t_tm = dtab("t_tm", [16, 128, 128], F32)
    t_qaug_p = dtab("t_qaug_p", [8, 16 * 2048], BF16)
    t_tri_p = dtab("t_tri_p", [128, 512], BF16)
    t_tri2_p = dtab("t_tri2_p", [128, 512], BF16)
    t_tri_s = dtab("t_tri_s", [128, 32], BF16)
    t_tri2_s = dtab("t_tri2_s", [128, 32], BF16)
    t_qaug_s = dtab("t_qaug_s", [8, 128], BF16)
    t_kaug_sel_s = dtab("t_kaug_sel_s", [8, 8192], BF16)
    t_kaug_win_s = dtab("t_kaug_win_s", [8, 512], BF16)
    t_kaug_new_s = dtab("t_kaug_new_s", [8, 128], BF16)
    t_kaug_cmp_s = dtab("t_kaug_cmp_s", [8, 128], BF16)
    t_fbn_s = dtab("t_fbn_s", [8, 128], F32)
    t_caus_s = dtab("t_caus_s", [8, 128], F32)
    t_tm_s = dtab("t_tm_s", [8, 128], F32)

    y_p = dout("y_p", [2048, D])
    y_s = dout("y_s", [NS_TOK, D])
    o_cmp_p = dout("o_cmp_p", [SEQ, 512])
    o_sel_p = dout("o_sel_p", [SEQ, 512])
    o_win_p = dout("o_win_p", [512, 512])
    o_h_p = dout("o_h_p", [1, D])
    o_conv_p = dout("o_conv_p", [3, D])
    o_cmp_s = dout("o_cmp_s", [NS_TOK, 512])
    o_sel_s = dout("o_sel_s", [NS_TOK, 512])
    o_win_s = dout("o_win_s", [DEC_B * 512, 512])
    o_h_s = dout("o_h_s", [DEC_B, D])
    o_conv_s = dout("o_conv_s", [DEC_B * 3, D])

    modscr = P.dram("modscr", [2, 5, 3 * D], F32)
    x1scr = P.dram("x1scr", [SEQ, D], F32)
    x1s_scr = P.dram("x1s_scr", [NS_TOK, D], F32)
    winscr = P.dram("winscr", [SEQ, 512], F32)

    ident_b = P.sb("ident_b", [128, 128], BF16)
    ident_f = P.sb("ident_f", [128, 128], F32)
    for t, k in ((ident_b, "ident_b"), (ident_f, "ident_f")):
        P.op("pool", lambda g, t=t: g.memset(t[:], 0.0), w=[k])
        P.op("pool", lambda g, t=t: g.affine_select(out=t[:], in_=t[:], pattern=[[-1, 128]],
                                                    compare_op=ALU.not_equal, fill=1.0, base=0,
                                                    channel_multiplier=1), r=[k], w=[k])

    psb = [P.ps("ps%d" % i, [128, 512], F32) for i in range(8)]

    def pskey(i):
        return "ps%d" % i

    mod_fm = P.sb("mod_fm", [128, 2, 24, 8], F32)
    scA = P.scope()
    scA.__enter__()
    W_in = P.sb("W_in", [128, NCH, 2 * D], BF16)
    W_out = P.sb("W_out", [128, NCH, D], BF16)
    W_kv = P.sb("W_kv", [128, NCH, 1536], BF16)
    W_r = P.sb("W_r", [128, 8, 128], BF16)
    W_i = P.sb("W_i", [128, 8, 128], BF16)
    for k in range(NCH):
        P.dma("pool", W_in[:, k, :], w_in_a[k * 128:(k + 1) * 128, :], w=["W_in"])
    for k in range(NCH):
        P.dma("pool", W_out[:, k, :], w_out_a[k * 128:(k + 1) * 128, :], w=["W_out"])
    for k in range(NCH):
        P.dma("pool", W_kv[:, k, :], w_kv[k * 128:(k + 1) * 128, :], w=["W_kv"])
    P.dma("pool", W_r[:], w_r.rearrange("n c d -> c n d"), w=["W_r"])
    P.dma("pool", W_i[:], w_i.rearrange("n c d -> c n d"), w=["W_i"])

    pf = P.sb("pf", [128, 10, NCH], F32)
    for k in range(4):
        P.dma("sp", pf[:, k, :], conv_w[k:k + 1, :].rearrange("o (c p) -> p (o c)", p=128), w=["pf"])
    for j, src in ((4, conv_b), (5, b_r), (6, b_i), (7, lam)):
        P.dma("sp", pf[:, j, :], src[0:1, :].rearrange("o (c p) -> p (o c)", p=128), w=["pf"])
    P.op("act", lambda a: a.activation(out=pf[:, 9, :], in_=pf[:, 7, :], func=AF.Exp, scale=-1.0), r=["pf"], w=["pf"])
    P.op("act", lambda a: a.activation(out=pf[:, 9, :], in_=pf[:, 9, :], func=AF.Ln, bias=1.0), r=["pf"], w=["pf"])
    P.op("dve", lambda v: v.tensor_scalar_mul(out=pf[:, 7, :], in0=pf[:, 9, :], scalar1=-RG_C), r=["pf"], w=["pf"])
    P.op("dve", lambda v: v.tensor_scalar_mul(out=pf[:, 8, :], in0=pf[:, 9, :], scalar1=-2.0 * RG_C), r=["pf"], w=["pf"])

    lnG = P.sb("lnG", [128, 1, D], F32)
    lnB = P.sb("lnB", [128, 1, D], F32)
    for l in range(1):
        P.dma("sp", lnG[:, l, :], ln_g[l:l + 1, :].partition_broadcast(128), w=["lnG"])
        P.dma("sp", lnB[:, l, :], ln_b[l:l + 1, :].partition_broadcast(128), w=["lnB"])

    vt = [P.sb("vt%d" % i, [128, D], F32) for i in range(2)]
    c5, c5s = vt[0], vt[1]
    csT = P.sb("csT", [128, NCH, 8], BF16)
    P.dma("sp", c5[0:5, :], cvec[:, :], w=["vt0"])
    P.op("act", lambda a: a.activation(out=c5s[0:5, :], in_=c5[0:5, :], func=AF.Silu), r=["vt0"], w=["vt1"])
    for k in range(NCH):
        P.op("pe", lambda t, k=k: t.transpose(psb[0][:, k * 8:k * 8 + 5], c5s[0:5, k * 128:(k + 1) * 128],
                                              ident_f[0:5, 0:5]), r=["vt1", "ident_f"], w=[pskey(0)])
    P.op("dve", lambda v: v.tensor_copy(out=csT[:, :, 0:5],
                                        in_=psb[0][:, 0:64].rearrange("p (k e) -> p k e", e=8)[:, :, 0:5]),
         r=[pskey(0)], w=["csT"])
    AW = 256
    NA = 3 * D // AW
    wada_buf = [P.sb("wada%d" % i, [128, NCH, AW], BF16) for i in range(2)]
    modc = [P.sb("modc%d" % i, [5, AW], F32) for i in range(2)]
    badac = [P.sb("badac%d" % i, [5, AW], F32) for i in range(2)]
    it = 0
    for l in range(2):
        for n6 in range(NA):
            wb = wada_buf[it % 2]
            wk = "wada%d" % (it % 2)
            mk = "modc%d" % (it % 2)
            bk_ = "badac%d" % (it % 2)
            mc = modc[it % 2]
            bc = badac[it % 2]
            P.dma("pool", wb[:], w_ada[l, :, n6 * AW:(n6 + 1) * AW].rearrange("(k p) n -> p k n", p=128), w=[wk])
            P.dma("sp", bc[:], b_ada[l:l + 1, n6 * AW:(n6 + 1) * AW].partition_broadcast(5), w=[bk_])
            pb = 2 + (it % 2)
            for k in range(NCH):
                P.op("pe", lambda t, k=k, wb=wb, pb=pb: t.matmul(psb[pb][0:5, 0:AW], lhsT=csT[:, k, 0:5], rhs=wb[:, k, :],
                                                                 start=(k == 0), stop=(k == NCH - 1)),
                     r=["csT", wk], w=[pskey(pb)])
            P.op("dve", lambda v, mc=mc, bc=bc, pb=pb: v.tensor_tensor(
                out=mc[:], in0=psb[pb][0:5, 0:AW], in1=bc[:], op=ALU.add),
                r=[pskey(pb), bk_], w=[mk])
            P.dma("sp", modscr[l, :, n6 * AW:(n6 + 1) * AW], mc[:], r=[mk], w=[("modscr", l, n6)], semkey=mk)
            nq = AW // 128
            for q in range(nq):
                P.op("pe", lambda t, q=q, mc=mc: t.transpose(psb[1][:, q * 8:q * 8 + 5], mc[0:5, q * 128:(q + 1) * 128],
                                                             ident_f[0:5, 0:5]), r=[mk, "ident_f"], w=[pskey(1)])
            P.op("dve", lambda v, l=l, n6=n6, nq=nq: v.tensor_copy(
                out=mod_fm[:, l, n6 * nq:(n6 + 1) * nq, 0:5],
                in_=psb[1][:, 0:8 * nq].rearrange("p (k e) -> p k e", e=8)[:, :, 0:5]),
                r=[pskey(1)], w=["mod_fm"])
            it += 1
    P.op("dve", lambda v: v.tensor_scalar_add(out=mod_fm[:, :, 8:24, :], in0=mod_fm[:, :, 8:24, :], scalar1=1.0),
         r=["mod_fm"], w=["mod_fm"])
    Gp = P.sb("Gp", [128, 1, D], F32)
    Gs = P.sb("Gs", [NS_TOK, 1, D], F32)
    mod_keys = [("modscr", l, n6) for l in range(2) for n6 in range(NA)]
    for l in range(1):
        P.dma("sp", Gp[:, l, :], modscr[l, 0:1, 2 * D:3 * D].partition_broadcast(128), r=mod_keys, w=["Gp"])
        for b in range(DEC_B):
            P.dma("sp", Gs[b * 8:(b + 1) * 8, l, :], modscr[l, 1 + b:2 + b, 2 * D:3 * D].partition_broadcast(8),
                  r=mod_keys, w=["Gs"])
    P.op("pool", lambda g: g.tensor_scalar_add(out=Gp[:], in0=Gp[:], scalar1=1.0), r=["Gp"], w=["Gp"])
    P.op("pool", lambda g: g.tensor_scalar_add(out=Gs[:], in0=Gs[:], scalar1=1.0), r=["Gs"], w=["Gs"])

    xtok = [P.sb("xtok%d" % i, [128, NSUB, D], F32) for i in range(2)]
    xbf = P.sb("xbf", [128, NSUB, D], BF16)
    mT = P.sb("mT", [128, NCH, TT], BF16)
    xbe = P.sb("xbe", [128, NCH, 3 + TT], F32)
    xbe_s = P.sb("xbe_s", [128, NCH, DEC_B, 3 + DEC_S], F32)
    hprev = P.sb("hprev", [128, NCH], F32)
    h0s = P.sb("h0s", [128, NCH, DEC_B], F32)
    hlast_s = P.sb("hlast_s", [128, NCH, DEC_B], F32)
    NT = 2
    xc = [P.sb("xc%d" % i, [128, TT], F32) for i in range(NT)]
    xcb = [P.sb("xcb%d" % i, [128, TT], BF16) for i in range(NT)]
    zs = [P.sb("zs%d" % i, [128, TT], F32) for i in range(NT)]
    ra = [P.sb("ra%d" % i, [128, TT], F32) for i in range(NT)]
    ri = [P.sb("ri%d" % i, [128, TT], F32) for i in range(NT)]
    ga = [P.sb("ga%d" % i, [128, TT], F32) for i in range(NT)]
    bb = [P.sb("bb%d" % i, [128, TT], F32) for i in range(NT)]
    hs = [P.sb("hs%d" % i, [128, TT], F32) for i in range(NT)]
    yg = P.sb("yg", [128, NCH, TT], BF16)
    x1t = [P.sb("x1t%d" % i, [128, D], F32) for i in range(2)]
    x1b = [P.sb("x1b%d" % i, [128, D], BF16) for i in range(2)]
    x1T = P.sb("x1T", [128, NCH, TT], BF16)
    kvst = [P.sb("kvst%d" % i, [128, 1536], F32) for i in range(2)]
    stat = [P.sb("stat%d" % i, [128, 16], F32) for i in range(2)]

    P.op("pool", lambda g: g.memset(xbe[:, :, 0:3], 0.0), w=["xbe"])
    P.op("pool", lambda g: g.memset(hprev[:], 0.0), w=["hprev"])
    for n in range(NCH):
        for b in range(DEC_B):
            P.dma("sp", xbe_s[:, n, b, 0:3], sc0[b * 3:(b + 1) * 3, n * 128:(n + 1) * 128].rearrange("k p -> p k"),
                  w=["xbe_s"])
        P.dma("sp", h0s[:, n, :], sh0.rearrange("b (c p) -> c p b", p=128)[n], w=["h0s"])

    cnt = {"tile": 0, "ch": 0, "sub": 0}

    def layernorm_tm(vin, vkey, out, okey, np_, layer, st, skey, Gt_=None, gk_="lnG", Bt_=None, bk_="lnB"):
        Gt_ = lnG if Gt_ is None else Gt_
        Bt_ = lnB if Bt_ is None else Bt_
        P.op("dve", lambda v: v.bn_stats(out=st[0:np_, 0:6], in_=vin[0:np_, 0:512]), r=[vkey], w=[skey])
        P.op("dve", lambda v: v.bn_stats(out=st[0:np_, 6:12], in_=vin[0:np_, 512:1024]), r=[vkey], w=[skey])
        P.op("dve", lambda v: v.bn_aggr(out=st[0:np_, 12:14], in_=st[0:np_, 0:12]),
             r=[skey], w=[skey])
        P.op("dve", lambda v: v.tensor_scalar_add(out=st[0:np_, 14:15], in0=st[0:np_, 13:14], scalar1=LN_EPS),
             r=[skey], w=[skey])
        P.op("act", lambda a: a.activation(out=st[0:np_, 14:15], in_=st[0:np_, 14:15], func=AF.Sqrt), r=[skey], w=[skey])
        P.op("dve", lambda v: v.reciprocal(out=st[0:np_, 14:15], in_=st[0:np_, 14:15]), r=[skey], w=[skey])
        P.op("dve", lambda v: v.scalar_tensor_tensor(out=st[0:np_, 15:16], in0=st[0:np_, 12:13], scalar=-1.0,
                                                     in1=st[0:np_, 14:15], op0=ALU.mult, op1=ALU.mult),
             r=[skey], w=[skey])
        P.op("act", lambda a: a.activation(out=out[0:np_, :], in_=vin[0:np_, :], func=AF.Identity,
                                           scale=st[0:np_, 14:15], bias=st[0:np_, 15:16]),
             r=[vkey, skey], w=[okey])
        P.op("pool", lambda g: g.tensor_tensor(out=out[0:np_, :], in0=out[0:np_, :], in1=Gt_[0:np_, layer, :], op=ALU.mult),
             r=[okey, gk_], w=[okey])
        P.op("pool", lambda g: g.tensor_tensor(out=out[0:np_, :], in0=out[0:np_, :], in1=Bt_[0:np_, layer, :], op=ALU.add),
             r=[okey, bk_], w=[okey])

    def l0_tile(ti, sample):
        if sample:
            ncols, nsub, np_ = NS_TOK, 1, NS_TOK
            segs = [(b * DEC_S, DEC_S, 1 + b) for b in range(DEC_B)]
        else:
            ncols, nsub, np_ = TT, NSUB, 128
            segs = [(0, TT, 0)]
        t0 = ti * TT
        xt = xtok[cnt["tile"] % 2]
        xk = "xtok%d" % (cnt["tile"] % 2)
        cnt["tile"] += 1
        if sample:
            P.dma("sp", xt[0:np_, 0, :], xs[:, :], w=[xk])
        else:
            for s in range(nsub):
                P.dma("sp", xt[:, s, :], xf[t0 + s * 128:t0 + (s + 1) * 128, :], w=[xk])
        for s in range(nsub):
            P.op("pool", lambda g, s=s: g.tensor_copy(out=xbf[0:np_, s, :], in_=xt[0:np_, s, :]), r=[xk], w=["xbf"])
        for half in range(2):
            pb = half
            for kk in range(4):
                k = half * 4 + kk
                for s in range(nsub):
                    P.op("pe", lambda t, k=k, kk=kk, s=s, pb=pb: t.transpose(
                        psb[pb][:, :].bitcast(BF16)[:, kk * TT + s * 128:kk * TT + s * 128 + np_],
                        xbf[0:np_, s, k * 128:(k + 1) * 128], ident_b[0:np_, 0:np_]),
                        r=["xbf", "ident_b"], w=[pskey(pb)])
            for kk in range(4):
                k = half * 4 + kk
                for (c0, cn, mj) in segs:
                    P.op("act", lambda a, k=k, kk=kk, pb=pb, c0=c0, cn=cn, mj=mj: a.activation(
                        out=mT[:, k, c0:c0 + cn], in_=psb[pb][:, :].bitcast(BF16)[:, kk * TT + c0:kk * TT + c0 + cn],
                        func=AF.Identity, scale=mod_fm[:, 0, 8 + k, mj:mj + 1], bias=mod_fm[:, 0, k, mj:mj + 1]),
                        r=[pskey(pb), "mod_fm"], w=["mT"])
        for n in range(NCH):
            ci = cnt["ch"] % NT
            cnt["ch"] += 1
            pb = 2 + (n % 2)
            pk = pskey(pb)
            for k in range(NCH):
                P.op("pe", lambda t, k=k, n=n, pb=pb: t.matmul(psb[pb][:, 0:ncols], lhsT=W_in[:, k, n * 128:(n + 1) * 128],
                                                               rhs=mT[:, k, 0:ncols], start=(k == 0), stop=(k == NCH - 1)),
                     r=["W_in", "mT"], w=[pk])
            for k in range(NCH):
                P.op("pe", lambda t, k=k, n=n, pb=pb: t.matmul(psb[pb][:, 256:256 + ncols],
                                                               lhsT=W_in[:, k, D + n * 128:D + (n + 1) * 128],
                                                               rhs=mT[:, k, 0:ncols], start=(k == 0), stop=(k == NCH - 1)),
                     r=["W_in", "mT"], w=[pk])
            if sample:
                xe = xbe_s[:, n, :, :]
                xek = "xbe_s"
                P.op("dve", lambda v, pb=pb, xe=xe: v.tensor_copy(
                    out=xe[:, :, 3:3 + DEC_S], in_=psb[pb][:, 0:ncols].rearrange("p (b t) -> p b t", t=DEC_S)),
                    r=[pk], w=[xek])
                sh = lambda k: xe[:, :, k:k + DEC_S]
                v3 = lambda ap: ap[:, 0:ncols].rearrange("p (b t) -> p b t", t=DEC_S)
            else:
                xe = xbe[:, n, :]
                xek = ("xbe", n)
                if ti > 0:
                    P.op("dve", lambda v, xe=xe: v.tensor_copy(out=xe[:, 0:3], in_=xe[:, TT:TT + 3]), r=[xek], w=[xek])
                P.op("dve", lambda v, pb=pb, xe=xe: v.tensor_copy(out=xe[:, 3:3 + TT], in_=psb[pb][:, 0:TT]), r=[pk], w=[xek])
                sh = lambda k: xe[:, k:k + TT]
                v3 = lambda ap: ap[:, 0:ncols]
            zk = "zs%d" % ci
            P.op("act", lambda a, pb=pb, ci=ci: a.activation(out=zs[ci][:, 0:ncols], in_=psb[pb][:, 256:256 + ncols],
                                                             func=AF.Silu), r=[pk], w=[zk])
            ck = "xc%d" % ci
            P.op("dve", lambda v, ci=ci, n=n: v.tensor_scalar(out=v3(xc[ci]), in0=sh(0), scalar1=pf[:, 0, n:n + 1],
                                                              scalar2=pf[:, 4, n:n + 1], op0=ALU.mult, op1=ALU.add),
                 r=[xek, "pf"], w=[ck])
            for k in range(1, 4):
                P.op("dve", lambda v, ci=ci, n=n, k=k: v.scalar_tensor_tensor(
                    out=v3(xc[ci]), in0=sh(k), scalar=pf[:, k, n:n + 1], in1=v3(xc[ci]), op0=ALU.mult, op1=ALU.add),
                    r=[xek, "pf", ck], w=[ck])
            cbk = "xcb%d" % ci
            P.op("pool", lambda g, ci=ci: g.tensor_copy(out=xcb[ci][:, 0:ncols], in_=xc[ci][:, 0:ncols]), r=[ck], w=[cbk])
            pg = 4 + (n % 2)
            pgk = pskey(pg)
            P.op("pe", lambda t, n=n, ci=ci, pg=pg: t.matmul(psb[pg][:, 0:ncols], lhsT=W_r[:, n, :], rhs=xcb[ci][:, 0:ncols],
                                                             start=True, stop=True), r=["W_r", cbk], w=[pgk])
            P.op("pe", lambda t, n=n, ci=ci, pg=pg: t.matmul(psb[pg][:, 256:256 + ncols], lhsT=W_i[:, n, :],
                                                             rhs=xcb[ci][:, 0:ncols], start=True, stop=True),
                 r=["W_i", cbk], w=[pgk])
            rk, ik, gk, bk, hk = "ra%d" % ci, "ri%d" % ci, "ga%d" % ci, "bb%d" % ci, "hs%d" % ci
            P.op("act", lambda a, n=n, ci=ci, pg=pg: a.activation(out=ra[ci][:, 0:ncols], in_=psb[pg][:, 0:ncols],
                                                                  func=AF.Sigmoid, bias=pf[:, 5, n:n + 1]),
                 r=[pgk, "pf"], w=[rk])
            P.op("act", lambda a, n=n, ci=ci, pg=pg: a.activation(out=ri[ci][:, 0:ncols], in_=psb[pg][:, 256:256 + ncols],
                                                                  func=AF.Sigmoid, bias=pf[:, 6, n:n + 1]),
                 r=[pgk, "pf"], w=[ik])
            P.op("act", lambda a, n=n, ci=ci: a.activation(out=ga[ci][:, 0:ncols], in_=ra[ci][:, 0:ncols], func=AF.Exp,
                                                           scale=pf[:, 8, n:n + 1]), r=[rk, "pf"], w=[gk])
            P.op("act", lambda a, n=n, ci=ci: a.activation(out=ra[ci][:, 0:ncols], in_=ra[ci][:, 0:ncols], func=AF.Exp,
                                                           scale=pf[:, 7, n:n + 1]), r=[rk, "pf"], w=[rk])
            P.op("dve", lambda v, ci=ci: v.tensor_scalar(out=ga[ci][:, 0:ncols], in0=ga[ci][:, 0:ncols], scalar1=-1.0,
                                                         scalar2=1.0, op0=ALU.mult, op1=ALU.add), r=[gk], w=[gk])
            P.op("dve", lambda v, ci=ci: v.tensor_scalar_max(out=ga[ci][:, 0:ncols], in0=ga[ci][:, 0:ncols], scalar1=0.0),
                 r=[gk], w=[gk])
            P.op("act", lambda a, ci=ci: a.activation(out=ga[ci][:, 0:ncols], in_=ga[ci][:, 0:ncols], func=AF.Sqrt),
                 r=[gk], w=[gk])
            P.op("pool", lambda g, ci=ci: g.tensor_tensor(out=bb[ci][:, 0:ncols], in0=ri[ci][:, 0:ncols],
                                                          in1=xc[ci][:, 0:ncols], op=ALU.mult), r=[ik, ck], w=[bk])
            P.op("dve", lambda v, ci=ci: v.tensor_tensor(out=bb[ci][:, 0:ncols], in0=bb[ci][:, 0:ncols],
                                                         in1=ga[ci][:, 0:ncols], op=ALU.mult), r=[bk, gk], w=[bk])
            if sample:
                for b in range(DEC_B):
                    P.op("dve", lambda v, ci=ci, n=n, b=b: v.tensor_tensor_scan(
                        out=hs[ci][:, b * DEC_S:(b + 1) * DEC_S], data0=ra[ci][:, b * DEC_S:(b + 1) * DEC_S],
                        data1=bb[ci][:, b * DEC_S:(b + 1) * DEC_S], initial=h0s[:, n, b:b + 1], op0=ALU.mult, op1=ALU.add),
                        r=[rk, bk, "h0s"], w=[hk])
                P.op("dve", lambda v, ci=ci, n=n: v.tensor_copy(
                    out=hlast_s[:, n, :], in_=hs[ci][:, 0:ncols].rearrange("p (b t) -> p b t", t=DEC_S)[:, :, DEC_S - 1]),
                    r=[hk], w=["hlast_s"])
            else:
                P.op("dve", lambda v, ci=ci, n=n: v.tensor_tensor_scan(
                    out=hs[ci][:, 0:TT], data0=ra[ci][:, 0:TT], data1=bb[ci][:, 0:TT], initial=hprev[:, n:n + 1],
                    op0=ALU.mult, op1=ALU.add), r=[rk, bk, ("hprev", n)], w=[hk])
                P.op("dve", lambda v, ci=ci, n=n: v.tensor_copy(out=hprev[:, n:n + 1], in_=hs[ci][:, TT - 1:TT]),
                     r=[hk], w=[("hprev", n)])
            P.op("pool", lambda g, ci=ci, n=n: g.tensor_tensor(out=yg[:, n, 0:ncols], in0=hs[ci][:, 0:ncols],
                                                               in1=zs[ci][:, 0:ncols], op=ALU.mult), r=[hk, zk], w=["yg"])
        for s in range(nsub):
            si = cnt["sub"] % 2
            cnt["sub"] += 1
            vk, x1k, x1bk, kvk, stk = "vt%d" % si, "x1t%d" % si, "x1b%d" % si, "kvst%d" % si, "stat%d" % si
            for h in range(2):
                pb = 4 + h
                for k in range(NCH):
                    P.op("pe", lambda t, k=k, h=h, s=s, pb=pb: t.matmul(
                        psb[pb][0:np_, :], lhsT=yg[:, k, s * 128:s * 128 + np_], rhs=W_out[:, k, h * 512:(h + 1) * 512],
                        start=(k == 0), stop=(k == NCH - 1)), r=["yg", "W_out"], w=[pskey(pb)])
                G = Gs if sample else Gp
                P.op("dve", lambda v, h=h, pb=pb, si=si, G=G: v.tensor_tensor(
                    out=vt[si][0:np_, h * 512:(h + 1) * 512], in0=psb[pb][0:np_, :], in1=G[0:np_, 0, h * 512:(h + 1) * 512],
                    op=ALU.mult), r=[pskey(pb), "Gs" if sample else "Gp"], w=[vk])
            P.op("dve", lambda v, si=si, s=s: v.scalar_tensor_tensor(
                out=vt[si][0:np_, :], in0=xt[0:np_, s, :], scalar=ALPHA, in1=vt[si][0:np_, :], op0=ALU.mult, op1=ALU.add),
                r=[xk, vk], w=[vk])
            layernorm_tm(vt[si], vk, x1t[si], x1k, np_, 0, stat[si], stk)
            if not sample:
                P.dma("sp", x1scr[t0 + s * 128:t0 + (s + 1) * 128, :], x1t[si][:, :], r=[x1k], w=[("x1scr", ti, s)], semkey=x1k)
            else:
                P.dma("sp", x1s_scr[:, :], x1t[si][0:np_, :], r=[x1k], w=["x1s_scr"], semkey=x1k)
            P.op("act", lambda a, si=si: a.activation(out=x1b[si][0:np_, :], in_=x1t[si][0:np_, :], func=AF.Identity),
                 r=[x1k], w=[x1bk])
            pb = 6 + (s % 2)
            for k in range(NCH):
                P.op("pe", lambda t, k=k, si=si, pb=pb: t.transpose(
                    psb[pb][:, :].bitcast(BF16)[:, k * 128:k * 128 + np_], x1b[si][0:np_, k * 128:(k + 1) * 128],
                    ident_b[0:np_, 0:np_]), r=[x1bk, "ident_b"], w=[pskey(pb)])
            P.op("dve", lambda v, pb=pb, s=s: v.tensor_copy(
                out=x1T[:, :, s * 128:s * 128 + np_],
                in_=psb[pb][:, :].bitcast(BF16)[:, 0:1024].rearrange("p (k t) -> p k t", t=128)[:, :, 0:np_]),
                r=[pskey(pb)], w=[("x1T", s)])
            for c3 in range(3):
                pb = c3 % 2
                for k in range(NCH):
                    P.op("pe", lambda t, k=k, c3=c3, s=s, pb=pb: t.matmul(
                        psb[pb][0:np_, :], lhsT=x1T[:, k, s * 128:s * 128 + np_], rhs=W_kv[:, k, c3 * 512:(c3 + 1) * 512],
                        start=(k == 0), stop=(k == NCH - 1)), r=[("x1T", s), "W_kv"], w=[pskey(pb)])
                P.op("act", lambda a, c3=c3, pb=pb, si=si: a.activation(
                    out=kvst[si][0:np_, c3 * 512:(c3 + 1) * 512], in_=psb[pb][0:np_, :], func=AF.Identity),
                    r=[pskey(pb)], w=[kvk])
            if sample:
                P.dma("sp", o_cmp_s[:, :], kvst[si][0:np_, 0:512], r=[kvk], w=["o_cmp_s"], semkey=kvk)
                P.dma("sp", o_sel_s[:, :], kvst[si][0:np_, 512:1024], r=[kvk], w=["o_sel_s"], semkey=kvk)
                for b in range(DEC_B):
                    P.dma("sp", o_win_s[b * 512 + 504:(b + 1) * 512, :], kvst[si][b * 8:(b + 1) * 8, 1024:1536],
                          r=[kvk], w=[("o_win_s", b, 1)], semkey=kvk)
            else:
                r0 = t0 + s * 128
                P.dma("sp", o_cmp_p[r0:r0 + 128, :], kvst[si][:, 0:512], r=[kvk], w=[("o_cmp_p", ti, s)], semkey=kvk)
                P.dma("sp", o_sel_p[r0:r0 + 128, :], kvst[si][:, 512:1024], r=[kvk], w=[("o_sel_p", ti, s)], semkey=kvk)
                P.dma("sp", winscr[r0:r0 + 128, :], kvst[si][:, 1024:1536], r=[kvk], w=[("winscr", ti, s)], semkey=kvk)
                if r0 >= SEQ - 512:
                    P.dma("sp", o_win_p[r0 - (SEQ - 512):r0 - (SEQ - 512) + 128, :], kvst[si][:, 1024:1536],
                          r=[kvk], w=[("o_win_p", ti, s)], semkey=kvk)
        if sample:
            for n in range(NCH):
                P.dma("sp", o_h_s.rearrange("b (c p) -> c p b", p=128)[n], hlast_s[:, n, :], r=["hlast_s"], w=[("o_h_s", n)],
                      semkey="hlast_s")
                for b in range(DEC_B):
                    P.dma("sp", o_conv_s[b * 3:(b + 1) * 3, n * 128:(n + 1) * 128].rearrange("k p -> p k"),
                          xbe_s[:, n, b, DEC_S:DEC_S + 3], r=["xbe_s"], w=[("o_conv_s", n, b)], semkey="xbe_s_o")
        elif ti == SEQ // TT - 1:
            P.dma("sp", o_h_p[0:1, :].rearrange("o (c p) -> p (o c)", p=128), hprev[:, :],
                  r=[("hprev", n) for n in range(NCH)], w=["o_h_p"], semkey="hprev_o")
            for n in range(NCH):
                P.dma("sp", o_conv_p.rearrange("k (c p) -> c p k", p=128)[n], xbe[:, n, TT:TT + 3], r=[("xbe", n)],
                      w=[("o_conv_p", n)], semkey="xbe_o")

    n_ptiles = SEQ // TT if stage >= 1 else 2
    l0_tile(0, True)
    for ti in range(n_ptiles):
        l0_tile(ti, False)

    scA.__exit__(None, None, None)
    cmpscr_p = P.dram("cmpscr_p", [128, 512], F32)
    cmpscr_s = P.dram("cmpscr_s", [DEC_B * 128, 512], F32)
    do_sample = stage >= 3
    with P.scope():
        W1r = P.sb("W1r", [128, 2, 64, 128], BF16)
        CB2 = P.sb("CB2", [128, 64, 512], BF16)
        Hh = P.sb("Hh", [128, 2, 2, 256], BF16)
        W2 = P.sb("W2", [128, 2, 64], BF16)
        PEsb = P.sb("PEsb", [64, 128], BF16)
        bias1 = P.sb("bias1", [128, 4], F32)
        b2bc = P.sb("b2bc", [64, 2, 4, 128], F32)
        CS = P.sb("CS", [64, 2, 512], F32)
        IDXf = P.sb("IDXf", [128, DEC_B * 64], F32)
        IDXi = P.sb("IDXi", [128, DEC_B * 64], I32)
        iop = P.sb("iop", [128, 2], I32)
        iopf = P.sb("iopf", [128, 2], F32)
        for c in range(2):
            for half in range(2):
                P.dma("pool", W1r[half * 64:(half + 1) * 64, c, :, :], w_phi1[c], w=["W1r"])
            P.dma("pool", W2[:, c, :], w_phi2[c], w=["W2"])
            P.dma("sp", bias1[:, c:c + 1], b_phi1[c:c + 1, :].rearrange("o p -> p o"), w=["bias1"])
        P.dma("pool", PEsb[:], phi_pe[:, :], w=["PEsb"])
        for nl in range(2):
            for g in range(4):
                P.dma("sp", b2bc[:, nl, g, :], b_phi2.rearrange("c d -> (c d)").rearrange("(o n) -> o n", o=1).partition_broadcast(64),
                      w=["b2bc"])
        for c in range(2):
            for d in range(64):
                P.op("pe", lambda t, c=c, d=d: t.matmul(psb[0][:, c:c + 1], lhsT=W1r[0:64, c, d, :],
                                                        rhs=PEsb[0:64, c * 64 + d:c * 64 + d + 1],
                                                        start=(d == 0), stop=(d == 63)), r=["W1r", "PEsb"], w=[pskey(0)])
        P.op("dve", lambda v: v.tensor_tensor(out=bias1[:, 0:2], in0=psb[0][:, 0:2], in1=bias1[:, 0:2], op=ALU.add),
             r=[pskey(0), "bias1"], w=["bias1"])
        P.op("pool", lambda g_: g_.iota(out=iop[:, 0:1], pattern=[[0, 1]], base=0, channel_multiplier=1), w=["iop"])
        P.op("dve", lambda v: v.tensor_copy(out=iopf[:, 0:1], in_=iop[:, 0:1]), r=["iop"], w=["iopf"])
        P.dma("sp", IDXi[:], ptab.rearrange("b n -> (b n)").rearrange("(o n) -> o n", o=1).partition_broadcast(128), w=["IDXi"])
        P.op("dve", lambda v: v.tensor_copy(out=IDXf[:], in_=IDXi[:]), r=["IDXi"], w=["IDXf"])
        P.op("dve", lambda v: v.tensor_scalar(out=IDXf[:], in0=IDXf[:], scalar1=128.0, scalar2=iopf[:, 0:1],
                                              op0=ALU.mult, op1=ALU.add), r=["IDXf", "iopf"], w=["IDXf"])
        P.op("dve", lambda v: v.tensor_copy(out=IDXi[:], in_=IDXf[:]), r=["IDXf"], w=["IDXi"])
        idxscr = P.dram("idxscr", [128, DEC_B * 64], I32)
        P.dma("sp", idxscr[:, :], IDXi[:], r=["IDXi"], w=["idxscr"], semkey="IDXi_o")

        CB2v = CB2[:].rearrange("p n (g c d) -> p n g c d", g=4, c=2)

        def compress(load_pages, out_rows, okey):
            load_pages()
            for c in range(2):
                for nl in range(2):
                    pb = c * 2 + nl
                    for d in range(64):
                        P.op("pe", lambda t, c=c, nl=nl, d=d, pb=pb: t.matmul(
                            psb[pb][:, 0:256], lhsT=W1r[nl * 64:(nl + 1) * 64, c, d, :],
                            rhs=CB2v[nl * 64:(nl + 1) * 64, :, :, c, d], start=(d == 0), stop=(d == 63)),
                            r=["W1r", "CB2"], w=[pskey(pb)])
                    P.op("act", lambda a, c=c, nl=nl, pb=pb: a.activation(out=Hh[:, c, nl, :], in_=psb[pb][:, 0:256], func=AF.Silu,
                                                                          bias=bias1[:, c:c + 1]), r=[pskey(pb), "bias1"], w=["Hh"])
            Hv = Hh[:].rearrange("p c n (pg g) -> p c n pg g", g=4)
            for nl in range(2):
                pb = 4 + nl
                for g in range(4):
                    for c in range(2):
                        col = (g * 2 + c) * 64
                        P.op("pe", lambda t, nl=nl, g=g, c=c, pb=pb, col=col: t.matmul(
                            psb[pb][0:64, col:col + 64], lhsT=Hv[:, c, nl, :, g], rhs=W2[:, c, :], start=True, stop=True),
                            r=["Hh", "W2"], w=[pskey(pb)])
                P.op("dve", lambda v, nl=nl, pb=pb: v.tensor_tensor(
                    out=CS[:, nl, :], in0=psb[pb][0:64, :], in1=b2bc[:, nl, :, :].rearrange("p g f -> p (g f)"), op=ALU.add),
                    r=[pskey(pb), "b2bc"], w=["CS"])
            P.dma("sp", out_rows.rearrange("(pg n) f -> pg n f", n=2), CS[:], r=["CS"], w=[okey], semkey="CS")

        def load_prompt_pages():
            for pg in range(64):
                P.dma("pool", CB2[:, pg, :], o_cmp_p[pg * 128:(pg + 1) * 128, :], r=[("o_cmp_p", pg // NSUB, pg % NSUB)], w=["CB2"])

        if stage >= 2:
            compress(load_prompt_pages, cmpscr_p[:, :], "cmpscr_p")
        if do_sample:
            for b in range(DEC_B):
                def load_sample_pages(b=b):
                    for pg in range(64):
                        P.gather(CB2[:, pg, :], ccmp[:, :], IDXi[:, b * 64 + pg:b * 64 + pg + 1], r=["IDXi"], w=["CB2"])
                compress(load_sample_pages, cmpscr_s[b * 128:(b + 1) * 128, :], ("cmpscr_s", b))

    NTOK = 2048 + NS_TOK
    QTscr = P.dram("QTscr", [4, 64, 4, NTOK], BF16)
    ZSscr = P.dram("ZSscr", [NTOK, D], BF16)
    GLscr = P.dram("GLscr", [NTOK, 48], F32)
    OGscr = P.dram("OGscr", [NTOK, D], BF16)
    qtiles = [(jl * 128, 128, jl, None) for jl in range(16)] if stage >= 2 else []
    if do_sample:
        qtiles += [(2048 + b * 8, 8, None, b) for b in range(DEC_B)]
    if stage == 2.5:
        qtiles = qtiles[:2]

    with P.scope():
        W_inb = P.sb("W_inb", [128, NCH, 2096], BF16)
        for k in range(NCH):
            P.dma("pool", W_inb[:, k, :], w_in_b[k * 128:(k + 1) * 128, :], w=["W_inb"])
        bgbc = P.sb("bgbc", [128, 48], F32)
        P.dma("sp", bgbc[:], b_gate[0:1, :].partition_broadcast(128), w=["bgbc"])
        idxo = P.sb("idxo", [128, 16], I32)
        P.dma("sp", idxo[:], t_idx_own[:, :], w=["idxo"])
        X1 = [P.sb("X1_%d" % i, [128, D], F32) for i in range(2)]
        X1b = P.sb("X1b", [128, D], BF16)
        m1T = P.sb("m1T", [128, NCH, 128], BF16)
        QTst = [P.sb("QTst%d" % i, [64, 4, 128], BF16) for i in range(2)]
        ZSt = [P.sb("ZSt%d" % i, [128, D], BF16) for i in range(2)]
        GLt = [P.sb("GLt%d" % i, [128, 48], F32) for i in range(2)]
        qi = 0
        for (tok0, nq, jl, sb_) in qtiles:
            i2 = qi % 2
            xk = "X1_%d" % i2
            if jl is not None:
                P.gather(X1[i2][:, :], x1scr[:, :], idxo[:, jl:jl + 1], r=["idxo"], w=[xk])
                mj = 0
            else:
                P.dma("sp", X1[i2][0:nq, :], x1s_scr[sb_ * 8:(sb_ + 1) * 8, :], w=[xk])
                mj = 1 + sb_
            P.op("pool", lambda g_, i2=i2, nq=nq: g_.tensor_copy(out=X1b[0:nq, :], in_=X1[i2][0:nq, :]), r=[xk], w=["X1b"])
            for k in range(NCH):
                P.op("pe", lambda t, k=k, nq=nq: t.transpose(psb[0][:, :].bitcast(BF16)[:, k * 128:k * 128 + nq],
                                                             X1b[0:nq, k * 128:(k + 1) * 128], ident_b[0:nq, 0:nq]),
                     r=["X1b", "ident_b"], w=[pskey(0)])
            for k in range(NCH):
                P.op("act", lambda a, k=k, nq=nq, mj=mj: a.activation(
                    out=m1T[:, k, 0:nq], in_=psb[0][:, :].bitcast(BF16)[:, k * 128:k * 128 + nq], func=AF.Identity,
                    scale=mod_fm[:, 1, 8 + k, mj:mj + 1], bias=mod_fm[:, 1, k, mj:mj + 1]), r=[pskey(0), "mod_fm"], w=["m1T"])
            for g in range(4):
                pb = 2 + (g % 2)
                qk = "QTst%d" % (g % 2)
                for hh in range(4):
                    for k in range(NCH):
                        P.op("pe", lambda t, k=k, hh=hh, g=g, pb=pb, nq=nq: t.matmul(
                            psb[pb][0:64, hh * 128:hh * 128 + nq], lhsT=W_inb[:, k, (4 * g + hh) * 64:(4 * g + hh + 1) * 64],
                            rhs=m1T[:, k, 0:nq], start=(k == 0), stop=(k == NCH - 1)), r=["W_inb", "m1T"], w=[pskey(pb)])
                P.op("dve", lambda v, g=g, pb=pb, nq=nq: v.tensor_scalar_mul(
                    out=QTst[g % 2][:, :, 0:nq], in0=psb[pb][0:64, :].rearrange("p (h q) -> p h q", h=4)[:, :, 0:nq],
                    scalar1=0.125), r=[pskey(pb)], w=[qk])
                P.dma("sp", QTscr[g, :, :, tok0:tok0 + nq], QTst[g % 2][:, :, 0:nq], r=[qk], w=[("QTscr", g, tok0)], semkey=qk)
            zk = "ZSt%d" % i2
            for half in range(2):
                pb = 4 + half
                for k in range(NCH):
                    P.op("pe", lambda t, k=k, half=half, pb=pb, nq=nq: t.matmul(
                        psb[pb][0:nq, :], lhsT=m1T[:, k, 0:nq], rhs=W_inb[:, k, D + half * 512:D + (half + 1) * 512],
                        start=(k == 0), stop=(k == NCH - 1)), r=["W_inb", "m1T"], w=[pskey(pb)])
                P.op("act", lambda a, half=half, pb=pb, nq=nq, i2=i2: a.activation(
                    out=ZSt[i2][0:nq, half * 512:(half + 1) * 512], in_=psb[pb][0:nq, :], func=AF.Silu), r=[pskey(pb)], w=[zk])
            P.dma("sp", ZSscr[tok0:tok0 + nq, :], ZSt[i2][0:nq, :], r=[zk], w=[("ZSscr", tok0)], semkey=zk)
            gk = "GLt%d" % i2
            for k in range(NCH):
                P.op("pe", lambda t, k=k, nq=nq: t.matmul(psb[6][0:nq, 0:48], lhsT=m1T[:, k, 0:nq], rhs=W_inb[:, k, 2048:2096],
                                                          start=(k == 0), stop=(k == NCH - 1)), r=["W_inb", "m1T"], w=[pskey(6)])
            P.op("dve", lambda v, nq=nq, i2=i2: v.tensor_tensor(out=GLt[i2][0:nq, :], in0=psb[6][0:nq, 0:48], in1=bgbc[0:nq, :],
                                                                op=ALU.add), r=[pskey(6), "bgbc"], w=[gk])
            P.op("act", lambda a, nq=nq, i2=i2: a.activation(out=GLt[i2][0:nq, :], in_=GLt[i2][0:nq, :], func=AF.Sigmoid),
                 r=[gk], w=[gk])
            P.dma("sp", GLscr[tok0:tok0 + nq, :], GLt[i2][0:nq, :], r=[gk], w=[("GLscr", tok0)], semkey=gk)
            qi += 1

    with P.scope():
        EE = P.sb("EE", [128, 64, 128], BF16)
        ones_b = P.sb("ones_b", [128, 1024], BF16)
        P.op("pool", lambda g_: g_.memset(ones_b[:], 1.0), w=["ones_b"])
        for T8 in range(8):
            P.op("pool", lambda g_, T8=T8: g_.affine_select(
                out=EE[:, T8 * 8:(T8 + 1) * 8, :].rearrange("p t (a b) -> p t a b", a=2),
                in_=ones_b[:, :].rearrange("p (t a b) -> p t a b", t=8, a=2),
                pattern=[[-2, 8], [-1, 2], [0, 64]], compare_op=ALU.is_equal, fill=0.0, base=-16 * T8, channel_multiplier=1),
                r=["ones_b"], w=["EE"])
        TRIp = P.sb("TRIp", [128, 512], BF16)
        TRI2p = P.sb("TRI2p", [128, 512], BF16)
        TRIs = P.sb("TRIs", [128, 32], BF16)
        TRI2s = P.sb("TRI2s", [128, 32], BF16)
        P.dma("sp", TRIp[:], t_tri_p[:, :], w=["TRIp"])
        P.dma("sp", TRI2p[:], t_tri2_p[:, :], w=["TRI2p"])
        P.dma("sp", TRIs[:], t_tri_s[:, :], w=["TRIs"])
        P.dma("sp", TRI2s[:], t_tri2_s[:, :], w=["TRI2s"])
        KsT = P.sb("KsT", [72, 65 * 128], BF16)
        Vs = P.sb("Vs", [128, 65, 65], BF16)
        KwT = P.sb("KwT", [72, 21 * 128], BF16)
        Vw = P.sb("Vw", [128, 21, 65], BF16)
        KcT = P.sb("KcT", [72, 128], BF16)
        Vc = P.sb("Vc", [128, 64], BF16)
        P.op("pool", lambda g_: g_.memset(Vs[:, :, 64:65], 1.0), w=["Vs"])
        P.op("pool", lambda g_: g_.memset(Vw[:, :, 64:65], 1.0), w=["Vw"])
        RB = [P.sb("RB%d" % i, [128, 8, 128], BF16) for i in range(2)]
        RBF = [P.sb("RBF%d" % i, [128, 512], BF16) for i in range(2)]
        idxo2 = P.sb("idxo2", [128, 16], I32)
        idxw = P.sb("idxw", [128, 20], I32)
        idxs = P.sb("idxs", [128, 1], I32)
        idxpg = P.sb("idxpg", [128, DEC_B * 64], I32)
        P.dma("sp", idxo2[:], t_idx_own[:, :], w=["idxo2"])
        P.dma("sp", idxw[:], t_idx_win[:, :], w=["idxw"])
        P.dma("sp", idxs[:], t_idx_slot[:, :], w=["idxs"])
        P.dma("sp", idxpg[:], idxscr[:, :], w=["idxpg"])
        QT = [P.sb("QT%d" % i, [72, 4, 128], BF16) for i in range(2)]
        FBNt = [P.sb("FBNt%d" % i, [128, 128], F32) for i in range(2)]
        CAUt = [P.sb("CAUt%d" % i, [128, 128], F32) for i in range(2)]
        TMt = [P.sb("TMt%d" % i, [128, 128], F32) for i in range(2)]
        GLg = [P.sb("GLg%d" % i, [128, 48], F32) for i in range(2)]
        ZSg = [P.sb("ZSg%d" % i, [128, 256], BF16) for i in range(2)]
        Ssb = P.sb("Ssb", [128, 512], F32)
        Esb = P.sb("Esb", [128, 512], F32)
        Pn = P.sb("Pn", [128, 512], F32)
        Pnb = P.sb("Pnb", [128, 512], BF16)
        imp = P.sb("imp", [128, 128], F32)
        scr = P.sb("scr", [128, 128], F32)
        scr2 = P.sb("scr2", [128, 128], F32)
        m8 = P.sb("m8", [128, 16], F32)
        sm = P.sb("sm", [128, 16], F32)
        MselT = P.sb("MselT", [128, 128], BF16)
        Msel4 = P.sb("Msel4", [128, 512], BF16)
        PTc = P.sb("PTc", [128, 512], BF16)
        PT = [P.sb("PT%d" % i, [128, 512], BF16) for i in range(3)]
        OaugSB = P.sb("OaugSB", [65, 512], F32)
        acc = P.sb("acc", [128, 256], F32)
        OGt = [P.sb("OGt%d" % i, [128, 256], BF16) for i in range(2)]
        cnt2 = {"rb": 0, "rbf": 0, "pt": 0, "ps": 0, "q": 0}

        def prep_from_rbf(rbf, rk, g, nk, ktdst, kkey, vdst, vkey):
            P.op("pe", lambda t: t.transpose(psb[7][:, :].bitcast(BF16)[0:64, 0:nk], rbf[0:nk, g * 128:g * 128 + 64],
                                             ident_b[0:nk, 0:nk]), r=[rk, "ident_b"], w=[pskey(7)])
            P.op("dve", lambda v: v.tensor_copy(out=ktdst, in_=psb[7][:, :].bitcast(BF16)[0:64, 0:nk]), r=[pskey(7)], w=[kkey])
            P.op("pool", lambda g_: g_.tensor_copy(out=vdst, in_=rbf[0:nk, g * 128 + 64:g * 128 + 128]), r=[rk], w=[vkey])

        def load_rows_gather(src, idx_ap, ikey):
            i = cnt2["rbf"] % 2
            cnt2["rbf"] += 1
            P.gather(RBF[i][:, :], src, idx_ap, r=[ikey], w=["RBF%d" % i])
            return RBF[i], "RBF%d" % i

        def load_rows_plain(src_rows, nk):
            i = cnt2["rbf"] % 2
            cnt2["rbf"] += 1
            P.dma("pool", RBF[i][0:nk, :], src_rows, w=["RBF%d" % i])
            return RBF[i], "RBF%d" % i

        def nsa_tile(g, tok0, nq, jl, sb_, kth):
            ncol = 4 * nq
            sample = jl is None
            i2 = cnt2["q"] % 2
            cnt2["q"] += 1
            qt, qk = QT[i2], "QT%d" % i2
            P.dma("sp", qt[0:64, :, 0:nq], QTscr[g, :, :, tok0:tok0 + nq], w=[qk])
            if sample:
                P.dma("sp", qt[64:72, :, 0:nq], t_qaug_s.rearrange("r (h q) -> r h q", h=16)[:, 4 * g:4 * g + 4, :], w=[qk])
                P.dma("sp", FBNt[i2][0:nq, :], t_fbn_s[:, :], w=["FBNt%d" % i2])
                P.dma("sp", CAUt[i2][0:nq, :], t_caus_s[:, :], w=["CAUt%d" % i2])
                P.dma("sp", TMt[i2][0:nq, :], t_tm_s[:, :], w=["TMt%d" % i2])
            else:
                P.dma("sp", qt[64:72, :, 0:nq], t_qaug_p.rearrange("r (h q) -> r h q", h=16)[:, 4 * g:4 * g + 4, tok0:tok0 + nq],
                      w=[qk])
                P.dma("sp", FBNt[i2][:, :], t_fbn[jl], w=["FBNt%d" % i2])
                P.dma("sp", CAUt[i2][:, :], t_caus[jl], w=["CAUt%d" % i2])
                P.dma("sp", TMt[i2][:, :], t_tm[jl], w=["TMt%d" % i2])
            fk, ck, tk, glk, zk = "FBNt%d" % i2, "CAUt%d" % i2, "TMt%d" % i2, "GLg%d" % i2, "ZSg%d" % i2
            P.dma("sp", GLg[i2][0:nq, :], GLscr[tok0:tok0 + nq, :], w=[glk])
            P.dma("sp", ZSg[i2][0:nq, :], ZSscr[tok0:tok0 + nq, g * 256:(g + 1) * 256], w=[zk])
            qrhs = qt[0:72, :, 0:nq]
            gl3 = GLg[i2][0:nq, :].rearrange("p (h b) -> p h b", b=3)
            for hh in range(4):
                P.op("pe", lambda t, hh=hh: t.matmul(psb[6][0:nq, hh * 128:(hh + 1) * 128], lhsT=qt[0:72, hh, 0:nq], rhs=KcT[0:72, :],
                                                     start=True, stop=True), r=[qk, "KcT"], w=[pskey(6)])
            P.op("dve", lambda v: v.tensor_tensor(
                out=Ssb[0:nq, :].rearrange("p (h s) -> p h s", h=4), in0=psb[6][0:nq, :].rearrange("p (h s) -> p h s", h=4),
                in1=TMt[i2][0:nq, :].unsqueeze(1).to_broadcast([nq, 4, 128]), op=ALU.add), r=[pskey(6), tk], w=["Ssb"])
            for hh in range(4):
                P.op("act", lambda a, hh=hh: a.activation(out=Esb[0:nq, hh * 128:(hh + 1) * 128], in_=Ssb[0:nq, hh * 128:(hh + 1) * 128],
                                                          func=AF.Exp, accum_out=sm[0:nq, hh:hh + 1]), r=["Ssb"], w=["Esb", "sm"])
            P.op("dve", lambda v: v.tensor_scalar_add(out=sm[0:nq, 4:8], in0=sm[0:nq, 0:4], scalar1=1e-30), r=["sm"], w=["sm"])
            P.op("dve", lambda v: v.reciprocal(out=sm[0:nq, 4:8], in_=sm[0:nq, 4:8]), r=["sm"], w=["sm"])
            for hh in range(4):
                P.op("dve", lambda v, hh=hh: v.tensor_scalar_mul(out=Pn[0:nq, hh * 128:(hh + 1) * 128],
                                                                 in0=Esb[0:nq, hh * 128:(hh + 1) * 128],
                                                                 scalar1=sm[0:nq, 4 + hh:5 + hh]), r=["Esb", "sm"], w=["Pn"])
            P.op("pool", lambda g_: g_.tensor_copy(out=Pnb[0:nq, :], in_=Pn[0:nq, :]), r=["Pn"], w=["Pnb"])
            P.op("dve", lambda v: v.tensor_tensor(out=imp[0:nq, :], in0=Pn[0:nq, 0:128], in1=Pn[0:nq, 128:256], op=ALU.add),
                 r=["Pn"], w=["imp"])
            P.op("dve", lambda v: v.tensor_tensor(out=imp[0:nq, :], in0=imp[0:nq, :], in1=Pn[0:nq, 256:384], op=ALU.add),
                 r=["Pn", "imp"], w=["imp"])
            P.op("dve", lambda v: v.tensor_tensor(out=imp[0:nq, :], in0=imp[0:nq, :], in1=Pn[0:nq, 384:512], op=ALU.add),
                 r=["Pn", "imp"], w=["imp"])
            P.op("dve", lambda v: v.tensor_tensor(out=scr[0:nq, :], in0=imp[0:nq, :], in1=CAUt[i2][0:nq, :], op=ALU.mult),
                 r=["imp", ck], w=["scr"])
            P.op("dve", lambda v: v.tensor_tensor(out=scr[0:nq, :], in0=scr[0:nq, :], in1=FBNt[i2][0:nq, :], op=ALU.add),
                 r=["scr", fk], w=["scr"])
            P.op("dve", lambda v: v.max(out=m8[0:nq, 0:8], in_=scr[0:nq, :]), r=["scr"], w=["m8"])
            P.op("dve", lambda v: v.match_replace(out=scr2[0:nq, :], in_to_replace=m8[0:nq, 0:8], in_values=scr[0:nq, :],
                                                  imm_value=-1.0e30), r=["scr", "m8"], w=["scr2"])
            P.op("dve", lambda v: v.max(out=m8[0:nq, 8:16], in_=scr2[0:nq, :]), r=["scr2"], w=["m8"])
            thr = m8[0:nq, kth - 1:kth]
            P.op("dve", lambda v: v.tensor_scalar(out=scr2[0:nq, :], in0=scr[0:nq, :], scalar1=thr, scalar2=None, op0=ALU.is_ge),
                 r=["scr", "m8"], w=["scr2"])
            P.op("dve", lambda v: v.tensor_tensor(out=scr2[0:nq, :], in0=scr2[0:nq, :], in1=CAUt[i2][0:nq, :], op=ALU.mult),
                 r=["scr2", ck], w=["scr2"])
            P.op("dve", lambda v: v.tensor_scalar(out=MselT[0:nq, :], in0=scr2[0:nq, :], scalar1=-1.0, scalar2=-NEGM,
                                                  op0=ALU.add, op1=ALU.mult), r=["scr2"], w=["MselT"])
            P.op("pe", lambda t: t.transpose(psb[7][:, :].bitcast(BF16)[:, 0:nq], MselT[0:nq, :], ident_b[0:nq, 0:nq]),
                 r=["MselT", "ident_b"], w=[pskey(7)])
            P.op("dve", lambda v: v.tensor_copy(
                out=Msel4[:, 0:ncol].rearrange("p (h q) -> p h q", h=4),
                in_=psb[7][:, :].bitcast(BF16)[:, 0:nq].unsqueeze(1).to_broadcast([128, 4, nq])), r=[pskey(7)], w=["Msel4"])
            for hh in range(4):
                P.op("pe", lambda t, hh=hh: t.transpose(psb[7][:, :].bitcast(BF16)[:, 512 + hh * nq:512 + (hh + 1) * nq],
                                                        Pnb[0:nq, hh * 128:(hh + 1) * 128], ident_b[0:nq, 0:nq]),
                     r=["Pnb", "ident_b"], w=[pskey(7)])
            P.op("dve", lambda v: v.tensor_copy(out=PTc[:, 0:ncol], in_=psb[7][:, :].bitcast(BF16)[:, 512:512 + ncol]),
                 r=[pskey(7)], w=["PTc"])
            for hh in range(4):
                P.op("pe", lambda t, hh=hh: t.matmul(psb[5][0:nq, hh * 64:(hh + 1) * 64], lhsT=PTc[:, hh * nq:(hh + 1) * nq], rhs=Vc[:, :],
                                                     start=True, stop=True), r=["PTc", "Vc"], w=[pskey(5)])
            for hh in range(4):
                P.op("dve", lambda v, hh=hh: v.tensor_scalar_mul(out=acc[0:nq, hh * 64:(hh + 1) * 64],
                                                                 in0=psb[5][0:nq, hh * 64:(hh + 1) * 64],
                                                                 scalar1=gl3[:, 4 * g + hh, 0:1]), r=[pskey(5), glk], w=["acc"])

            def attend(tiles, br, ob):
                nt = len(tiles)
                for i, T in enumerate(tiles):
                    sbk = cnt2["ps"] % 3
                    cnt2["ps"] += 1
                    nk = T["nk"]
                    mm = [(T["kt"], qrhs, T["kkey"], qk)] + T["masks"]
                    for j, (l, r_, lk, rk) in enumerate(mm):
                        P.op("pe", lambda t, l=l, r_=r_, j=j, nk=nk, sbk=sbk, n=len(mm): t.matmul(
                            psb[sbk][0:nk, 0:ncol], lhsT=l, rhs=r_, start=(j == 0), stop=(j == n - 1)),
                            r=[lk, rk], w=[pskey(sbk)])
                    pi = cnt2["pt"] % 3
                    cnt2["pt"] += 1
                    P.op("act", lambda a, nk=nk, sbk=sbk, pi=pi: a.activation(out=PT[pi][0:nk, 0:ncol], in_=psb[sbk][0:nk, 0:ncol],
                                                                              func=AF.Exp), r=[pskey(sbk)], w=["PT%d" % pi])
                    P.op("pe", lambda t, T=T, nk=nk, pi=pi, i=i: t.matmul(psb[ob][0:65, 0:ncol], lhsT=T["v"], rhs=PT[pi][0:nk, 0:ncol],
                                                                          start=(i == 0), stop=(i == nt - 1)),
                         r=[T["vkey"], "PT%d" % pi], w=[pskey(ob)])
                P.op("dve", lambda v: v.tensor_copy(out=OaugSB[0:65, 0:ncol], in_=psb[ob][0:65, 0:ncol]), r=[pskey(ob)], w=["OaugSB"])
                for hh in range(4):
                    P.op("pe", lambda t, hh=hh: t.transpose(psb[5][0:nq, hh * 65:(hh + 1) * 65], OaugSB[0:65, hh * nq:(hh + 1) * nq],
                                                            ident_f[0:65, 0:65]), r=["OaugSB", "ident_f"], w=[pskey(5)])
                o3 = psb[5][0:nq, 0:260].rearrange("p (h e) -> p h e", e=65)
                P.op("dve", lambda v: v.tensor_scalar_add(out=sm[0:nq, 8:12], in0=o3[:, :, 64], scalar1=1e-30), r=[pskey(5)], w=["sm"])
                P.op("dve", lambda v: v.reciprocal(out=sm[0:nq, 8:12], in_=sm[0:nq, 8:12]), r=["sm"], w=["sm"])
                P.op("dve", lambda v: v.tensor_tensor(out=sm[0:nq, 12:16], in0=sm[0:nq, 8:12], in1=gl3[:, 4 * g:4 * g + 4, br],
                                                      op=ALU.mult), r=["sm", glk], w=["sm"])
                for hh in range(4):
                    P.op("dve", lambda v, hh=hh: v.scalar_tensor_tensor(
                        out=acc[0:nq, hh * 64:(hh + 1) * 64], in0=o3[:, hh, 0:64], scalar=sm[0:nq, 12 + hh:13 + hh],
                        in1=acc[0:nq, hh * 64:(hh + 1) * 64], op0=ALU.mult, op1=ALU.add), r=[pskey(5), "sm", "acc"], w=["acc"])

            def ktile(ktbuf, kkey, vbuf, vkey, T, nk=128, masks=()):
                return {"kt": ktbuf[0:72, T * 128:T * 128 + nk], "kkey": kkey, "v": vbuf[0:nk, T, :], "vkey": vkey, "nk": nk,
                        "masks": list(masks)}

            msel = lambda T: (EE[:, T, :], Msel4[:, 0:ncol], "EE", "Msel4")
            if sample:
                tri = (ident_b[0:8, 0:8], TRIs[0:8, 0:ncol], "ident_b", "TRIs")
                tri2 = (ident_b[:, :], TRI2s[:, 0:ncol], "ident_b", "TRI2s")
                sel_tiles = [ktile(KsT, "KsT", Vs, "Vs", T, masks=[msel(T)]) for T in range(64)]
                sel_tiles.append(ktile(KsT, "KsT", Vs, "Vs", 64, nk=8, masks=[tri]))
                win_tiles = [ktile(KwT, "KwT", Vw, "Vw", 0, masks=[tri2])] + [ktile(KwT, "KwT", Vw, "Vw", w) for w in range(1, 4)]
                win_tiles.append(ktile(KwT, "KwT", Vw, "Vw", 4, nk=8, masks=[tri]))
            else:
                tri = (ident_b[:, :], TRIp[:, 0:ncol], "ident_b", "TRIp")
                tri2 = (ident_b[:, :], TRI2p[:, 0:ncol], "ident_b", "TRI2p")
                sel_tiles = [ktile(KsT, "KsT", Vs, "Vs", T, masks=[msel(T)]) for T in range(48)]
                for j2 in range(jl + 1):
                    ms = [msel(48 + j2)] + ([tri] if j2 == jl else [])
                    sel_tiles.append(ktile(KsT, "KsT", Vs, "Vs", 48 + j2, masks=ms))
                win_tiles = []
                for w in range(jl, jl + 5):
                    ms = [tri2] if w == jl else ([tri] if w == jl + 4 else [])
                    win_tiles.append(ktile(KwT, "KwT", Vw, "Vw", w, masks=ms))
            attend(sel_tiles, 1, 3)
            attend(win_tiles, 2, 4)
            ogk = "OGt%d" % i2
            P.op("dve", lambda v: v.tensor_tensor(out=OGt[i2][0:nq, :], in0=acc[0:nq, :], in1=ZSg[i2][0:nq, :], op=ALU.mult),
                 r=["acc", zk], w=[ogk])
            P.dma("sp", OGscr[tok0:tok0 + nq, g * 256:(g + 1) * 256], OGt[i2][0:nq, :], r=[ogk], w=[("OGscr", tok0, g)], semkey=ogk)

        if stage >= 2:
            P.dma("sp", KsT[64:72, 0:8192], t_kaug_sel[:, :], w=["KsT"])
            P.dma("sp", KwT[64:72, 0:2560], t_kaug_win[:, :], w=["KwT"])
            P.dma("sp", KcT[64:72, :], t_kaug_cmp[:, :], w=["KcT"])
            for g in range(4):
                for T8 in range(6):
                    i = cnt2["rb"] % 2
                    cnt2["rb"] += 1
                    rbk = "RB%d" % i
                    P.dma("pool", RB[i][:, :, :], o_sel_p[T8 * 1024:(T8 + 1) * 1024, g * 128:(g + 1) * 128].rearrange("(t p) c -> p t c", p=128),
                          w=[rbk])
                    for t_ in range(8):
                        P.op("pe", lambda t, t_=t_, i=i: t.transpose(psb[7][:, :].bitcast(BF16)[0:64, t_ * 128:(t_ + 1) * 128],
                                                                     RB[i][:, t_, 0:64], ident_b[:, :]), r=[rbk, "ident_b"], w=[pskey(7)])
                    P.op("dve", lambda v, T8=T8: v.tensor_copy(out=KsT[0:64, T8 * 1024:(T8 + 1) * 1024],
                                                               in_=psb[7][:, :].bitcast(BF16)[0:64, 0:1024]), r=[pskey(7)], w=["KsT"])
                    P.op("pool", lambda g_, T8=T8, i=i: g_.tensor_copy(out=Vs[:, T8 * 8:(T8 + 1) * 8, 0:64], in_=RB[i][:, :, 64:128]),
                         r=[rbk], w=["Vs"])
                for j2 in range(16):
                    rbf, rk = load_rows_gather(o_sel_p[:, :], idxo2[:, j2:j2 + 1], "idxo2")
                    prep_from_rbf(rbf, rk, g, 128, KsT[0:64, (48 + j2) * 128:(49 + j2) * 128], "KsT", Vs[:, 48 + j2, 0:64], "Vs")
                for w in range(20):
                    rbf, rk = load_rows_gather(winscr[:, :], idxw[:, w:w + 1], "idxw")
                    prep_from_rbf(rbf, rk, g, 128, KwT[0:64, w * 128:(w + 1) * 128], "KwT", Vw[:, w, 0:64], "Vw")
                rbf, rk = load_rows_gather(cmpscr_p[:, :], idxs[:, 0:1], "idxs")
                prep_from_rbf(rbf, rk, g, 128, KcT[0:64, :], "KcT", Vc[:, :], "Vc")
                for (tok0, nq, jl, sb_) in qtiles:
                    if jl is not None:
                        nsa_tile(g, tok0, nq, jl, None, 16)
        if do_sample:
            P.dma("sp", KsT[64:72, 0:8192], t_kaug_sel_s[:, :], w=["KsT"])
            P.dma("sp", KsT[64:72, 8192:8320], t_kaug_new_s[:, :], w=["KsT"])
            P.dma("sp", KwT[64:72, 0:512], t_kaug_win_s[:, :], w=["KwT"])
            P.dma("sp", KwT[64:72, 512:640], t_kaug_new_s[:, :], w=["KwT"])
            P.dma("sp", KcT[64:72, :], t_kaug_cmp_s[:, :], w=["KcT"])
            for (tok0, nq, jl, sb_) in qtiles:
                if jl is not None:
                    continue
                b = sb_
                for g in range(4):
                    for pg in range(64):
                        rbf, rk = load_rows_gather(csel[:, :], idxpg[:, b * 64 + pg:b * 64 + pg + 1], "idxpg")
                        prep_from_rbf(rbf, rk, g, 128, KsT[0:64, pg * 128:(pg + 1) * 128], "KsT", Vs[:, pg, 0:64], "Vs")
                    rbf, rk = load_rows_plain(o_sel_s[b * 8:(b + 1) * 8, :], 8)
                    prep_from_rbf(rbf, rk, g, 8, KsT[0:64, 8192:8200], "KsT", Vs[0:8, 64, 0:64], "Vs")
                    for w in range(4):
                        rbf, rk = load_rows_plain(swin[b * 512 + w * 128:b * 512 + (w + 1) * 128, :], 128)
                        prep_from_rbf(rbf, rk, g, 128, KwT[0:64, w * 128:(w + 1) * 128], "KwT", Vw[:, w, 0:64], "Vw")
                    rbf, rk = load_rows_plain(o_win_s[b * 512 + 504:b * 512 + 512, :], 8)
                    prep_from_rbf(rbf, rk, g, 8, KwT[0:64, 512:520], "KwT", Vw[0:8, 4, 0:64], "Vw")
                    rbf, rk = load_rows_plain(cmpscr_s[b * 128:(b + 1) * 128, :], 128)
                    prep_from_rbf(rbf, rk, g, 128, KcT[0:64, :], "KcT", Vc[:, :], "Vc")
                    nsa_tile(g, tok0, nq, None, b, 15)

    with P.scope():
        W_outb = P.sb("W_outb", [128, NCH, D], BF16)
        for k in range(NCH):
            P.dma("pool", W_outb[:, k, :], w_out_b[k * 128:(k + 1) * 128, :], w=["W_outb"])
        lnG1 = P.sb("lnG1", [128, 1, D], F32)
        lnB1 = P.sb("lnB1", [128, 1, D], F32)
        P.dma("sp", lnG1[:, 0, :], ln_g[1:2, :].partition_broadcast(128), w=["lnG1"])
        P.dma("sp", lnB1[:, 0, :], ln_b[1:2, :].partition_broadcast(128), w=["lnB1"])
        Gp1 = P.sb("Gp1", [128, D], F32)
        Gs1 = P.sb("Gs1", [8, DEC_B, D], F32)
        P.dma("sp", Gp1[:, :], modscr[1, 0:1, 2 * D:3 * D].partition_broadcast(128), w=["Gp1"])
        for b in range(DEC_B):
            P.dma("sp", Gs1[0:8, b, :], modscr[1, 1 + b:2 + b, 2 * D:3 * D].partition_broadcast(8), w=["Gs1"])
        P.op("pool", lambda g_: g_.tensor_scalar_add(out=Gp1[:], in0=Gp1[:], scalar1=1.0), r=["Gp1"], w=["Gp1"])
        P.op("pool", lambda g_: g_.tensor_scalar_add(out=Gs1[:], in0=Gs1[:], scalar1=1.0), r=["Gs1"], w=["Gs1"])
        idxo3 = P.sb("idxo3", [128, 16], I32)
        P.dma("sp", idxo3[:], t_idx_own[:, :], w=["idxo3"])
        X1o = [P.sb("X1o%d" % i, [128, D], F32) for i in range(2)]
        OGl = [P.sb("OGl%d" % i, [128, D], BF16) for i in range(2)]
        OGT = P.sb("OGT", [128, NCH, 128], BF16)
        vo = [P.sb("vo%d" % i, [128, D], F32) for i in range(2)]
        yo = [P.sb("yo%d" % i, [128, D], F32) for i in range(2)]
        sto = [P.sb("sto%d" % i, [128, 16], F32) for i in range(2)]
        qi = 0
        for (tok0, nq, jl, sb_) in qtiles:
            i2 = qi % 2
            qi += 1
            xk, ok_, vk, yk, sk = "X1o%d" % i2, "OGl%d" % i2, "vo%d" % i2, "yo%d" % i2, "sto%d" % i2
            if jl is not None:
                P.gather(X1o[i2][:, :], x1scr[:, :], idxo3[:, jl:jl + 1], r=["idxo3"], w=[xk])
                Gt, gkey = Gp1[0:nq, :], "Gp1"
            else:
                P.dma("sp", X1o[i2][0:nq, :], x1s_scr[sb_ * 8:(sb_ + 1) * 8, :], w=[xk])
                Gt, gkey = Gs1[0:8, sb_, :], "Gs1"
            P.dma("sp", OGl[i2][0:nq, :], OGscr[tok0:tok0 + nq, :], w=[ok_])
            for k in range(NCH):
                P.op("pe", lambda t, k=k, nq=nq, i2=i2: t.transpose(psb[0][:, :].bitcast(BF16)[:, k * 128:k * 128 + nq],
                                                                   OGl[i2][0:nq, k * 128:(k + 1) * 128], ident_b[0:nq, 0:nq]),
                     r=[ok_, "ident_b"], w=[pskey(0)])
            P.op("dve", lambda v, nq=nq: v.tensor_copy(
                out=OGT[:, :, 0:nq], in_=psb[0][:, :].bitcast(BF16)[:, 0:1024].rearrange("p (k t) -> p k t", t=128)[:, :, 0:nq]),
                r=[pskey(0)], w=["OGT"])
            for half in range(2):
                pb = 2 + half
                for k in range(NCH):
                    P.op("pe", lambda t, k=k, half=half, pb=pb, nq=nq: t.matmul(
                        psb[pb][0:nq, :], lhsT=OGT[:, k, 0:nq], rhs=W_outb[:, k, half * 512:(half + 1) * 512],
                        start=(k == 0), stop=(k == NCH - 1)), r=["OGT", "W_outb"], w=[pskey(pb)])
                if jl is None and sb_ > 0:
                    pass
                P.op("dve", lambda v, half=half, pb=pb, nq=nq, i2=i2, Gt=Gt: v.tensor_tensor(
                    out=vo[i2][0:nq, half * 512:(half + 1) * 512], in0=psb[pb][0:nq, :], in1=Gt[:, half * 512:(half + 1) * 512],
                    op=ALU.mult), r=[pskey(pb), gkey], w=[vk])
            P.op("dve", lambda v, nq=nq, i2=i2: v.scalar_tensor_tensor(
                out=vo[i2][0:nq, :], in0=X1o[i2][0:nq, :], scalar=ALPHA, in1=vo[i2][0:nq, :], op0=ALU.mult, op1=ALU.add),
                r=[xk, vk], w=[vk])
            layernorm_tm(vo[i2], vk, yo[i2], yk, nq, 0, sto[i2], sk, lnG1, "lnG1", lnB1, "lnB1")
            if jl is not None:
                P.dma("sp", y_p[tok0:tok0 + nq, :], yo[i2][0:nq, :], r=[yk], w=[("y_p", tok0)], semkey=yk)
            else:
                P.dma("sp", y_s[sb_ * 8:(sb_ + 1) * 8, :], yo[i2][0:nq, :], r=[yk], w=[("y_s", sb_)], semkey=yk)

    for b in range(DEC_B):
        P.dma("act", o_win_s[b * 512:b * 512 + 504, :], swin[b * 512 + 8:(b + 1) * 512, :], w=[("o_win_s", b, 0)],
              semkey="winscopy")

    P.finish()
    print("instructions:", P.n_ins, {e: P.cnt[e] for e in P.ENG}, "dma sems:", len(P.dsem))
    return P


def _bf(x):
    return np.asarray(x, np.float32).astype(ml_dtypes.bfloat16)


def _split_pos(pos):
    pos = np.asarray(pos, np.int64)
    a = np.floor_divide(pos, 64)
    b = pos - 64 * a
    return a.astype(np.float32), b.astype(np.float32)


def _kaug(pos, valid):
    a, b = _split_pos(pos)
    n = a.shape[0]
    out = np.zeros((8, n), np.float32)
    out[0] = a; out[1] = a; out[2] = b; out[3] = b; out[4] = 1.0
    out[5] = np.where(valid, 0.0, NEGM)
    return _bf(out)


def _slopes_hi_lo():
    s = (2.0 ** (-8.0 * np.arange(1, 17) / 16.0)).astype(np.float32)
    hi = s.astype(ml_dtypes.bfloat16).astype(np.float32)
    lo = (s - hi).astype(ml_dtypes.bfloat16).astype(np.float32)
    return s, hi, lo


def _qaug(tq):
    s, hi, lo = _slopes_hi_lo()
    tq = np.asarray(tq, np.float32)
    nq = tq.shape[0]
    out = np.zeros((8, 16, nq), np.float32)
    out[0] = (64.0 * hi)[:, None]; out[1] = (64.0 * lo)[:, None]
    out[2] = hi[:, None]; out[3] = lo[:, None]
    out[4] = -(s[:, None] * tq[None, :])
    out[5] = 1.0
    return _bf(out)


def prompt_tables(k):
    cs = 2048 * k
    p = np.arange(128)
    t = {}
    t["idx_own"] = (cs + 128 * np.arange(16)[None, :] + p[:, None]).astype(np.int32)
    pos_pref = (np.arange(48 * 128) - cs)
    valid_pref = np.repeat(128 * np.arange(48) < cs, 128)
    pos_own = np.arange(2048)
    t["kaug_sel"] = np.concatenate([_kaug(pos_pref, valid_pref), _kaug(pos_own, np.ones(2048, bool))], axis=1)
    wtok = cs - 512 + np.arange(20 * 128)
    t["idx_win"] = np.maximum(wtok, 0).reshape(20, 128).T.astype(np.int32).copy()
    t["kaug_win"] = _kaug(wtok - cs, wtok >= 0)
    blk = np.concatenate([np.arange(96), 32 * k + np.arange(32)])
    valid = np.concatenate([np.arange(96) < 32 * k, np.ones(32, bool)])
    t["idx_slot"] = blk.astype(np.int32).reshape(128, 1)
    cend = 64 * blk + 63 - cs
    t["kaug_cmp"] = _kaug(cend, valid)
    tq = np.arange(2048)
    tabs = np.arange(2048) + cs
    cb = tabs // 64
    blk_abs = np.where(valid, blk, 10 ** 6)
    forced = (blk_abs[None, :] == 0) | (blk_abs[None, :] == cb[:, None]) | (blk_abs[None, :] == cb[:, None] - 1)
    caus = blk_abs[None, :] <= cb[:, None]
    fbn = np.where(forced, FORCEDV, 0.0) - np.where(caus, 0.0, 1.0)
    t["fbn"] = fbn.astype(np.float32).reshape(16, 128, 128)
    t["caus"] = caus.astype(np.float32).reshape(16, 128, 128)
    cend_abs = 64 * blk + 63
    tm = np.where(cend_abs[None, :] <= tabs[:, None], 0.0, NEGM)
    t["tm"] = tm.astype(np.float32).reshape(16, 128, 128)
    return t


def static_tables():
    t = {}
    t["qaug_p"] = _qaug(np.arange(2048)).reshape(8, 16 * 2048)
    j = np.arange(128)[:, None]
    i = np.arange(128)[None, :]
    tri = np.where(j > i, NEGM, 0.0)
    tri2 = np.where(j < i, NEGM, 0.0)
    t["tri_p"] = _bf(np.tile(tri, (1, 4)))
    t["tri2_p"] = _bf(np.tile(tri2, (1, 4)))
    i8 = np.arange(8)[None, :]
    t["tri_s"] = _bf(np.tile(np.where(j > i8, NEGM, 0.0), (1, 4)))
    t["tri2_s"] = _bf(np.tile(np.where(j < i8, NEGM, 0.0), (1, 4)))
    t["qaug_s"] = _qaug(np.arange(8)).reshape(8, 16 * 8)
    t["kaug_sel_s"] = _kaug(np.arange(8192) - 8192, np.ones(8192, bool))
    t["kaug_win_s"] = _kaug(np.arange(512) - 512, np.ones(512, bool))
    t["kaug_new_s"] = _kaug(np.arange(128), np.arange(128) < 8)
    cend = 64 * np.arange(128) + 63 - 8192
    t["kaug_cmp_s"] = _kaug(cend, np.ones(128, bool))
    fb = np.zeros((8, 128), np.float32)
    fb[:, 0] = FORCEDV; fb[:, 127] = FORCEDV
    t["fbn_s"] = fb
    t["caus_s"] = np.ones((8, 128), np.float32)
    t["tm_s"] = np.zeros((8, 128), np.float32)
    return t


def core_inputs(inp, c):
    b = c // 4
    sb = slice(4 * c, 4 * c + 4)
    f = np.ascontiguousarray
    d = {
        "xf": f(inp["x_prompt"][b]),
        "xs": f(inp["x_sample"][sb].reshape(NS_TOK, D)),
        "cvec": f(np.concatenate([inp["c_prompt"][b:b + 1], inp["c_sample"][sb]], axis=0)),
        "sh0": f(inp["state_h"][0, sb]),
        "sc0": f(inp["state_conv"][0, sb].reshape(DEC_B * 3, D)),
        "swin": f(inp["state_win"][sb].reshape(DEC_B * 512, 512)),
        "ptab": f(inp["page_table"][sb]).astype(np.int32),
        "ccmp": inp["cache_cmp"].reshape(-1, 512),
        "csel": inp["cache_sel"].reshape(-1, 512),
        "w_ada": inp["w_ada"], "b_ada": inp["b_ada"], "ln_g": inp["ln_g"], "ln_b": inp["ln_b"],
        "w_in_a": inp["w_in_a"][0], "conv_w": inp["conv_w_a"][0], "conv_b": inp["conv_b_a"],
        "w_r": inp["w_r_a"][0], "b_r": inp["b_r_a"], "w_i": inp["w_i_a"][0], "b_i": inp["b_i_a"],
        "lam": inp["lam_a"], "w_out_a": inp["w_out_a"][0], "w_kv": inp["w_kv"],
        "phi_pe": inp["phi_pe"].reshape(64, 128), "w_phi1": inp["w_phi1"], "b_phi1": inp["b_phi1"],
        "w_phi2": inp["w_phi2"], "b_phi2": inp["b_phi2"], "w_in_b": inp["w_in_b"][0],
        "b_gate": inp["b_gate_b"], "w_out_b": inp["w_out_b"][0],
    }
    for k2, v in prompt_tables(c % 4).items():
        d["t_" + k2] = v
    for k2, v in static_tables().items():
        d["t_" + k2] = v
    return {k: np.asarray(v) for k, v in d.items()}


def assemble(results, cores):
    y_prompt = np.zeros((2, SEQ, D), np.float32)
    y_sample = np.zeros((32, DEC_S, D), np.float32)
    new_cmp_p = np.zeros((2, SEQ, 4, 2, 64), np.float32)
    new_sel_p = np.zeros((2, SEQ, 4, 2, 64), np.float32)
    new_win_p = np.zeros((2, 512, 4, 2, 64), np.float32)
    new_h_p = np.zeros((1, 2, D), np.float32)
    new_conv_p = np.zeros((1, 2, 3, D), np.float32)
    new_cmp_s = np.zeros((32, DEC_S, 4, 2, 64), np.float32)
    new_sel_s = np.zeros((32, DEC_S, 4, 2, 64), np.float32)
    new_win_s = np.zeros((32, 512, 4, 2, 64), np.float32)
    new_h_s = np.zeros((1, 32, D), np.float32)
    new_conv_s = np.zeros((1, 32, 3, D), np.float32)
    for r, c in zip(results, cores):
        b, k = c // 4, c % 4
        sb = slice(4 * c, 4 * c + 4)
        y_prompt[b, k * 2048:(k + 1) * 2048] = r["y_p"]
        y_sample[sb] = r["y_s"].reshape(DEC_B, DEC_S, D)
        if k == 0:
            new_cmp_p[b] = r["o_cmp_p"].reshape(SEQ, 4, 2, 64)
            new_sel_p[b] = r["o_sel_p"].reshape(SEQ, 4, 2, 64)
            new_win_p[b] = r["o_win_p"].reshape(512, 4, 2, 64)
            new_h_p[0, b] = r["o_h_p"][0]
            new_conv_p[0, b] = r["o_conv_p"]
        new_cmp_s[sb] = r["o_cmp_s"].reshape(DEC_B, DEC_S, 4, 2, 64)
        new_sel_s[sb] = r["o_sel_s"].reshape(DEC_B, DEC_S, 4, 2, 64)
        new_win_s[sb] = r["o_win_s"].reshape(DEC_B, 512, 4, 2, 64)
        new_h_s[0, sb] = r["o_h_s"]
        new_conv_s[0, sb] = r["o_conv_s"].reshape(DEC_B, 3, D)
    return (y_prompt, y_sample, new_cmp_p, new_sel_p, new_win_p, new_h_p, new_conv_p,
            new_cmp_s, new_sel_s, new_win_s, new_h_s, new_conv_s)


def kernel(**inputs):
    inp = {k: np.asarray(v) for k, v in inputs.items()}
    n_phys = inp["cache_cmp"].shape[0]
    P = build(n_phys)
    cores = list(range(8))
    in_maps = [core_inputs(inp, c) for c in cores]
    res = run_bass_kernel_spmd(P.nc, in_maps, core_ids=cores)
    return assemble(res.results, cores)
```

```python
import contextlib
import numpy as np
import ml_dtypes
import concourse.bass as bass
import concourse.mybir as mybir
from concourse.bass_utils import run_bass_kernel_spmd

F32 = mybir.dt.float32
BF16 = mybir.dt.bfloat16
I32 = mybir.dt.int32
AF = mybir.ActivationFunctionType
ALU = mybir.AluOpType
AX = mybir.AxisListType

D = 1024
NCH = 8
SEQ = 8192
TT = 256
NSUB = TT // 128
DEC_B = 4
DEC_S = 8
NS_TOK = DEC_B * DEC_S
ALPHA = 4.0 ** 0.25
LN_EPS = 1e-5
RG_C = 8.0
NEGM = -30000.0
FORCEDV = 1.0e6


class Prog:
    ENG = ("pe", "act", "dve", "pool", "sp")

    def __init__(self):
        self.nc = bass.Bass("TRN2", target_bir_lowering=False)
        self.es = contextlib.ExitStack()
        nc = self.nc
        self.eng = {"pe": nc.tensor, "act": nc.scalar, "dve": nc.vector, "pool": nc.gpsimd, "sp": nc.sync}
        self.sem = {e: self.es.enter_context(nc.semaphore("s_" + e)) for e in self.ENG}
        self.cnt = {e: 0 for e in self.ENG}
        self.seen = {e: {} for e in self.ENG}
        self.dsem = {}
        self.dcnt = {}
        self.bufs = {}
        self.n_ins = 0
        self.stack = [self.es]

    def sb(self, name, shape, dt):
        return self.stack[-1].enter_context(self.nc.sbuf_tensor(name, list(shape), dt))

    def barrier(self):
        deps = {}
        for e2 in self.ENG:
            if self.cnt[e2]:
                deps[("eng", e2)] = self.cnt[e2]
        for k in self.dcnt:
            deps[("dma", k)] = self.dcnt[k]
        for e in self.ENG:
            self._wait(e, dict(deps))

    @contextlib.contextmanager
    def scope(self):
        st = contextlib.ExitStack()
        self.stack.append(st)
        try:
            yield
        finally:
            self.barrier()
            self.stack.pop()
            st.close()

    def ps(self, name, shape, dt):
        return self.es.enter_context(self.nc.psum_tensor(name, list(shape), dt))

    def dram(self, name, shape, dt, kind="Internal"):
        return self.nc.dram_tensor(name, list(shape), dt, kind=kind).ap()

    def _state(self, k):
        st = self.bufs.get(k)
        if st is None:
            st = self.bufs[k] = {"w": {}, "r": {}}
        return st

    def _deps(self, r, w):
        deps = {}
        for k in r:
            for s, v in self._state(k)["w"].items():
                deps[s] = max(deps.get(s, 0), v)
        for k in w:
            st = self._state(k)
            for s, v in st["w"].items():
                deps[s] = max(deps.get(s, 0), v)
            for s, v in st["r"].items():
                deps[s] = max(deps.get(s, 0), v)
        return deps

    def _wait(self, e, deps):
        eng = self.eng[e]
        seen = self.seen[e]
        for s, v in deps.items():
            if s[0] == "dma":
                v = max(v, self.dcnt[s[1]])
                if seen.get(s, 0) >= v:
                    continue
                eng.wait_ge(self.dsem[s[1]], v)
            else:
                if s[1] == e and False:
                    continue
                if seen.get(s, 0) >= v:
                    continue
                eng.wait_ge(self.sem[s[1]], v)
            seen[s] = v

    def _commit(self, me_src, me_val, r, w):
        for k in w:
            st = self._state(k)
            st["w"] = {me_src: me_val}
            st["r"] = {}
        for k in r:
            if k in w:
                continue
            st = self._state(k)
            st["r"][me_src] = max(st["r"].get(me_src, 0), me_val)

    def op(self, e, fn, r=(), w=()):
        w = list(w) + [k for k in r if isinstance(k, str) and k[:2] == "ps" and k[2:].isdigit() and k not in w]
        self._wait(e, self._deps(r, w))
        ins = fn(self.eng[e])
        self.cnt[e] += 1
        ins.then_inc(self.sem[e], 1)
        self._commit(("eng", e), self.cnt[e], r, w)
        self.n_ins += 1
        return ins

    def dma(self, q, out, in_, r=(), w=(), semkey=None, **kw):
        self._wait(q, self._deps(r, w))
        if semkey is None:
            semkey = (tuple(w) + tuple(r))[0]
        if semkey not in self.dsem:
            self.dsem[semkey] = self.es.enter_context(self.nc.semaphore("d%d" % len(self.dsem)))
            self.dcnt[semkey] = 0
        ins = self.eng[q].dma_start(out=out, in_=in_, **kw)
        self.dcnt[semkey] += 16
        ins.then_inc(self.dsem[semkey], 16)
        self._commit(("dma", semkey), self.dcnt[semkey], r, w)
        self.n_ins += 1
        return ins

    def gather(self, out, in_, idx_ap, r=(), w=(), semkey=None):
        q = "pool"
        self._wait(q, self._deps(r, w))
        if semkey is None:
            semkey = tuple(w)[0]
        if semkey not in self.dsem:
            self.dsem[semkey] = self.es.enter_context(self.nc.semaphore("d%d" % len(self.dsem)))
            self.dcnt[semkey] = 0
        ins = self.nc.gpsimd.indirect_dma_start(
            out=out, out_offset=None, in_=in_, in_offset=bass.IndirectOffsetOnAxis(ap=idx_ap, axis=0))
        self.dcnt[semkey] += 16
        ins.then_inc(self.dsem[semkey], 16)
        self._commit(("dma", semkey), self.dcnt[semkey], r, w)
        self.n_ins += 1
        return ins

    def finish(self):
        for e in ("sp",):
            deps = {}
            for k, st in self.bufs.items():
                for s, v in list(st["w"].items()) + list(st["r"].items()):
                    deps[s] = max(deps.get(s, 0), v)
            for k in self.dcnt:
                deps[("dma", k)] = self.dcnt[k]
            for e2 in self.ENG:
                if self.cnt[e2]:
                    deps[("eng", e2)] = self.cnt[e2]
            self._wait(e, deps)


def build(n_phys, stage=9):
    P = Prog()
    nc = P.nc
    es = P.es
    ctx_nc = nc.allow_non_contiguous_dma(reason="small strided parameter / state loads")
    es.enter_context(ctx_nc)

    def din(name, shape, dt=F32):
        return nc.dram_tensor(name, list(shape), dt, kind="ExternalInput").ap()

    def dout(name, shape, dt=F32):
        return nc.dram_tensor(name, list(shape), dt, kind="ExternalOutput").ap()

    xf = din("xf", [SEQ, D])
    xs = din("xs", [NS_TOK, D])
    cvec = din("cvec", [5, D])
    sh0 = din("sh0", [DEC_B, D])
    sc0 = din("sc0", [DEC_B * 3, D])
    swin = din("swin", [DEC_B * 512, 512])
    ptab = din("ptab", [DEC_B, 64], I32)
    ccmp = din("ccmp", [n_phys * 128, 512])
    csel = din("csel", [n_phys * 128, 512])
    w_ada = din("w_ada", [2, D, 3 * D])
    b_ada = din("b_ada", [2, 3 * D])
    ln_g = din("ln_g", [2, D])
    ln_b = din("ln_b", [2, D])
    w_in_a = din("w_in_a", [D, 2 * D])
    conv_w = din("conv_w", [4, D])
    conv_b = din("conv_b", [1, D])
    w_r = din("w_r", [8, 128, 128])
    b_r = din("b_r", [1, D])
    w_i = din("w_i", [8, 128, 128])
    b_i = din("b_i", [1, D])
    lam = din("lam", [1, D])
    w_out_a = din("w_out_a", [D, D])
    w_kv = din("w_kv", [D, 1536])
    phi_pe = din("phi_pe", [64, 128])
    w_phi1 = din("w_phi1", [2, 64, 64, 128])
    b_phi1 = din("b_phi1", [2, 128])
    w_phi2 = din("w_phi2", [2, 128, 64])
    b_phi2 = din("b_phi2", [2, 64])
    w_in_b = din("w_in_b", [D, 2096])
    b_gate = din("b_gate", [1, 48])
    w_out_b = din("w_out_b", [D, D])

    def dtab(name, shape, dt):
        return nc.dram_tensor(name, list(shape), dt, kind="ExternalInput").ap()
    t_idx_own = dtab("t_idx_own", [128, 16], I32)
    t_idx_win = dtab("t_idx_win", [128, 20], I32)
    t_idx_slot = dtab("t_idx_slot", [128, 1], I32)
    t_kaug_sel = dtab("t_kaug_sel", [8, 8192], BF16)
    t_kaug_win = dtab("t_kaug_win", [8, 2560], BF16)
    t_kaug_cmp = dtab("t_kaug_cmp", [8, 128], BF16)
    t_fbn = dtab("t_fbn", [16, 128, 128], F32)
    t_caus = dtab("t_caus", [16, 128, 128], F32)
    t_tm = dtab("t_tm", [16, 128, 128], F32)
    t_qaug_p = dtab("t_qaug_p", [8, 16 * 2048], BF16)
    t_tri_p = dtab("t_tri_p", [128, 512], BF16)
    t_tri2_p = dtab("t_tri2_p", [128, 512], BF16)
    t_tri_s = dtab("t_tri_s", [128, 32], BF16)
    t_tri2_s = dtab("t_tri2_s", [128, 32], BF16)
    t_qaug_s = dtab("t_qaug_s", [8, 128], BF16)
    t_kaug_sel_s = dtab("t_kaug_sel_s", [8, 8192], BF16)
    t_kaug_win_s = dtab("t_kaug_win_s", [8, 512], BF16)
    t_kaug_new_s = dtab("t_kaug_new_s", [8, 128], BF16)
    t_kaug_cmp_s = dtab("t_kaug_cmp_s", [8, 128], BF16)
    t_fbn_s = dtab("t_fbn_s", [8, 128], F32)
    t_caus_s = dtab("t_caus_s", [8, 128], F32)
    t_tm_s = dtab("t_tm_s", [8, 128], F32)

    y_p = dout("y_p", [2048, D])
    y_s = dout("y_s", [NS_TOK, D])
    o_cmp_p = dout("o_cmp_p", [SEQ, 512])
    o_sel_p = dout("o_sel_p", [SEQ, 512])
    o_win_p = dout("o_win_p", [512, 512])
    o_h_p = dout("o_h_p", [1, D])
    o_conv_p = dout("o_conv_p", [3, D])
    o_cmp_s = dout("o_cmp_s", [NS_TOK, 512])
    o_sel_s = dout("o_sel_s", [NS_TOK, 512])
    o_win_s = dout("o_win_s", [DEC_B * 512, 512])
    o_h_s = dout("o_h_s", [DEC_B, D])
    o_conv_s = dout("o_conv_s", [DEC_B * 3, D])

    modscr = P.dram("modscr", [2, 5, 3 * D], F32)
    x1scr = P.dram("x1scr", [SEQ, D], F32)
    x1s_scr = P.dram("x1s_scr", [NS_TOK, D], F32)
    winscr = P.dram("winscr", [SEQ, 512], F32)

    ident_b = P.sb("ident_b", [128, 128], BF16)
    ident_f = P.sb("ident_f", [128, 128], F32)
    for t, k in ((ident_b, "ident_b"), (ident_f, "ident_f")):
        P.op("pool", lambda g, t=t: g.memset(t[:], 0.0), w=[k])
        P.op("pool", lambda g, t=t: g.affine_select(out=t[:], in_=t[:], pattern=[[-1, 128]],
                                                    compare_op=ALU.not_equal, fill=1.0, base=0,
                                                    channel_multiplier=1), r=[k], w=[k])

    psb = [P.ps("ps%d" % i, [128, 512], F32) for i in range(8)]

    def pskey(i):
        return "ps%d" % i

    mod_fm = P.sb("mod_fm", [128, 2, 24, 8], F32)
    scA = P.scope()
    scA.__enter__()
    W_in = P.sb("W_in", [128, NCH, 2 * D], BF16)
    W_out = P.sb("W_out", [128, NCH, D], BF16)
    W_kv = P.sb("W_kv", [128, NCH, 1536], BF16)
    W_r = P.sb("W_r", [128, 8, 128], BF16)
    W_i = P.sb("W_i", [128, 8, 128], BF16)
    for k in range(NCH):
        P.dma("pool", W_in[:, k, :], w_in_a[k * 128:(k + 1) * 128, :], w=["W_in"])
    for k in range(NCH):
        P.dma("pool", W_out[:, k, :], w_out_a[k * 128:(k + 1) * 128, :], w=["W_out"])
    for k in range(NCH):
        P.dma("pool", W_kv[:, k, :], w_kv[k * 128:(k + 1) * 128, :], w=["W_kv"])
    P.dma("pool", W_r[:], w_r.rearrange("n c d -> c n d"), w=["W_r"])
    P.dma("pool", W_i[:], w_i.rearrange("n c d -> c n d"), w=["W_i"])

    pf = P.sb("pf", [128, 10, NCH], F32)
    for k in range(4):
        P.dma("sp", pf[:, k, :], conv_w[k:k + 1, :].rearrange("o (c p) -> p (o c)", p=128), w=["pf"])
    for j, src in ((4, conv_b), (5, b_r), (6, b_i), (7, lam)):
        P.dma("sp", pf[:, j, :], src[0:1, :].rearrange("o (c p) -> p (o c)", p=128), w=["pf"])
    P.op("act", lambda a: a.activation(out=pf[:, 9, :], in_=pf[:, 7, :], func=AF.Exp, scale=-1.0), r=["pf"], w=["pf"])
    P.op("act", lambda a: a.activation(out=pf[:, 9, :], in_=pf[:, 9, :], func=AF.Ln, bias=1.0), r=["pf"], w=["pf"])
    P.op("dve", lambda v: v.tensor_scalar_mul(out=pf[:, 7, :], in0=pf[:, 9, :], scalar1=-RG_C), r=["pf"], w=["pf"])
    P.op("dve", lambda v: v.tensor_scalar_mul(out=pf[:, 8, :], in0=pf[:, 9, :], scalar1=-2.0 * RG_C), r=["pf"], w=["pf"])

    lnG = P.sb("lnG", [128, 1, D], F32)
    lnB = P.sb("lnB", [128, 1, D], F32)
    for l in range(1):
        P.dma("sp", lnG[:, l, :], ln_g[l:l + 1, :].partition_broadcast(128), w=["lnG"])
        P.dma("sp", lnB[:, l, :], ln_b[l:l + 1, :].partition_broadcast(128), w=["lnB"])

    vt = [P.sb("vt%d" % i, [128, D], F32) for i in range(2)]
    c5, c5s = vt[0], vt[1]
    csT = P.sb("csT", [128, NCH, 8], BF16)
    P.dma("sp", c5[0:5, :], cvec[:, :], w=["vt0"])
    P.op("act", lambda a: a.activation(out=c5s[0:5, :], in_=c5[0:5, :], func=AF.Silu), r=["vt0"], w=["vt1"])
    for k in range(NCH):
        P.op("pe", lambda t, k=k: t.transpose(psb[0][:, k * 8:k * 8 + 5], c5s[0:5, k * 128:(k + 1) * 128],
                                              ident_f[0:5, 0:5]), r=["vt1", "ident_f"], w=[pskey(0)])
    P.op("dve", lambda v: v.tensor_copy(out=csT[:, :, 0:5],
                                        in_=psb[0][:, 0:64].rearrange("p (k e) -> p k e", e=8)[:, :, 0:5]),
         r=[pskey(0)], w=["csT"])
    AW = 256
    NA = 3 * D // AW
    wada_buf = [P.sb("wada%d" % i, [128, NCH, AW], BF16) for i in range(2)]
    modc = [P.sb("modc%d" % i, [5, AW], F32) for i in range(2)]
    badac = [P.sb("badac%d" % i, [5, AW], F32) for i in range(2)]
    it = 0
    for l in range(2):
        for n6 in range(NA):
            wb = wada_buf[it % 2]
            wk = "wada%d" % (it % 2)
            mk = "modc%d" % (it % 2)
            bk_ = "badac%d" % (it % 2)
            mc = modc[it % 2]
            bc = badac[it % 2]
            P.dma("pool", wb[:], w_ada[l, :, n6 * AW:(n6 + 1) * AW].rearrange("(k p) n -> p k n", p=128), w=[wk])
            P.dma("sp", bc[:], b_ada[l:l + 1, n6 * AW:(n6 + 1) * AW].partition_broadcast(5), w=[bk_])
            pb = 2 + (it % 2)
            for k in range(NCH):
                P.op("pe", lambda t, k=k, wb=wb, pb=pb: t.matmul(psb[pb][0:5, 0:AW], lhsT=csT[:, k, 0:5], rhs=wb[:, k, :],
                                                                 start=(k == 0), stop=(k == NCH - 1)),
                     r=["csT", wk], w=[pskey(pb)])
            P.op("dve", lambda v, mc=mc, bc=bc, pb=pb: v.tensor_tensor(
                out=mc[:], in0=psb[pb][0:5, 0:AW], in1=bc[:], op=ALU.add),
                r=[pskey(pb), bk_], w=[mk])
            P.dma("sp", modscr[l, :, n6 * AW:(n6 + 1) * AW], mc[:], r=[mk], w=[("modscr", l, n6)], semkey=mk)
            nq = AW // 128
            for q in range(nq):
                P.op("pe", lambda t, q=q, mc=mc: t.transpose(psb[1][:, q * 8:q * 8 + 5], mc[0:5, q * 128:(q + 1) * 128],
                                                             ident_f[0:5, 0:5]), r=[mk, "ident_f"], w=[pskey(1)])
            P.op("dve", lambda v, l=l, n6=n6, nq=nq: v.tensor_copy(
                out=mod_fm[:, l, n6 * nq:(n6 + 1) * nq, 0:5],
                in_=psb[1][:, 0:8 * nq].rearrange("p (k e) -> p k e", e=8)[:, :, 0:5]),
                r=[pskey(1)], w=["mod_fm"])
            it += 1
    P.op("dve", lambda v: v.tensor_scalar_add(out=mod_fm[:, :, 8:24, :], in0=mod_fm[:, :, 8:24, :], scalar1=1.0),
         r=["mod_fm"], w=["mod_fm"])
    Gp = P.sb("Gp", [128, 1, D], F32)
    Gs = P.sb("Gs", [NS_TOK, 1, D], F32)
    mod_keys = [("modscr", l, n6) for l in range(2) for n6 in range(NA)]
    for l in range(1):
        P.dma("sp", Gp[:, l, :], modscr[l, 0:1, 2 * D:3 * D].partition_broadcast(128), r=mod_keys, w=["Gp"])
        for b in range(DEC_B):
            P.dma("sp", Gs[b * 8:(b + 1) * 8, l, :], modscr[l, 1 + b:2 + b, 2 * D:3 * D].partition_broadcast(8),
                  r=mod_keys, w=["Gs"])
    P.op("pool", lambda g: g.tensor_scalar_add(out=Gp[:], in0=Gp[:], scalar1=1.0), r=["Gp"], w=["Gp"])
    P.op("pool", lambda g: g.tensor_scalar_add(out=Gs[:], in0=Gs[:], scalar1=1.0), r=["Gs"], w=["Gs"])

    xtok = [P.sb("xtok%d" % i, [128, NSUB, D], F32) for i in range(2)]
    xbf = P.sb("xbf", [128, NSUB, D], BF16)
    mT = P.sb("mT", [128, NCH, TT], BF16)
    xbe = P.sb("xbe", [128, NCH, 3 + TT], F32)
    xbe_s = P.sb("xbe_s", [128, NCH, DEC_B, 3 + DEC_S], F32)
    hprev = P.sb("hprev", [128, NCH], F32)
    h0s = P.sb("h0s", [128, NCH, DEC_B], F32)
    hlast_s = P.sb("hlast_s", [128, NCH, DEC_B], F32)
    NT = 2
    xc = [P.sb("xc%d" % i, [128, TT], F32) for i in range(NT)]
    xcb = [P.sb("xcb%d" % i, [128, TT], BF16) for i in range(NT)]
    zs = [P.sb("zs%d" % i, [128, TT], F32) for i in range(NT)]
    ra = [P.sb("ra%d" % i, [128, TT], F32) for i in range(NT)]
    ri = [P.sb("ri%d" % i, [128, TT], F32) for i in range(NT)]
    ga = [P.sb("ga%d" % i, [128, TT], F32) for i in range(NT)]
    bb = [P.sb("bb%d" % i, [128, TT], F32) for i in range(NT)]
    hs = [P.sb("hs%d" % i, [128, TT], F32) for i in range(NT)]
    yg = P.sb("yg", [128, NCH, TT], BF16)
    x1t = [P.sb("x1t%d" % i, [128, D], F32) for i in range(2)]
    x1b = [P.sb("x1b%d" % i, [128, D], BF16) for i in range(2)]
    x1T = P.sb("x1T", [128, NCH, TT], BF16)
    kvst = [P.sb("kvst%d" % i, [128, 1536], F32) for i in range(2)]
    stat = [P.sb("stat%d" % i, [128, 16], F32) for i in range(2)]

    P.op("pool", lambda g: g.memset(xbe[:, :, 0:3], 0.0), w=["xbe"])
    P.op("pool", lambda g: g.memset(hprev[:], 0.0), w=["hprev"])
    for n in range(NCH):
        for b in range(DEC_B):
            P.dma("sp", xbe_s[:, n, b, 0:3], sc0[b * 3:(b + 1) * 3, n * 128:(n + 1) * 128].rearrange("k p -> p k"),
                  w=["xbe_s"])
        P.dma("sp", h0s[:, n, :], sh0.rearrange("b (c p) -> c p b", p=128)[n], w=["h0s"])

    cnt = {"tile": 0, "ch": 0, "sub": 0}

    def layernorm_tm(vin, vkey, out, okey, np_, layer, st, skey, Gt_=None, gk_="lnG", Bt_=None, bk_="lnB"):
        Gt_ = lnG if Gt_ is None else Gt_
        Bt_ = lnB if Bt_ is None else Bt_
        P.op("dve", lambda v: v.bn_stats(out=st[0:np_, 0:6], in_=vin[0:np_, 0:512]), r=[vkey], w=[skey])
        P.op("dve", lambda v: v.bn_stats(out=st[0:np_, 6:12], in_=vin[0:np_, 512:1024]), r=[vkey], w=[skey])
        P.op("dve", lambda v: v.bn_aggr(out=st[0:np_, 12:14], in_=st[0:np_, 0:12]),
             r=[skey], w=[skey])
        P.op("dve", lambda v: v.tensor_scalar_add(out=st[0:np_, 14:15], in0=st[0:np_, 13:14], scalar1=LN_EPS),
             r=[skey], w=[skey])
        P.op("act", lambda a: a.activation(out=st[0:np_, 14:15], in_=st[0:np_, 14:15], func=AF.Sqrt), r=[skey], w=[skey])
        P.op("dve", lambda v: v.reciprocal(out=st[0:np_, 14:15], in_=st[0:np_, 14:15]), r=[skey], w=[skey])
        P.op("dve", lambda v: v.scalar_tensor_tensor(out=st[0:np_, 15:16], in0=st[0:np_, 12:13], scalar=-1.0,
                                                     in1=st[0:np_, 14:15], op0=ALU.mult, op1=ALU.mult),
             r=[skey], w=[skey])
        P.op("act", lambda a: a.activation(out=out[0:np_, :], in_=vin[0:np_, :], func=AF.Identity,
                                           scale=st[0:np_, 14:15], bias=st[0:np_, 15:16]),
             r=[vkey, skey], w=[okey])
        P.op("pool", lambda g: g.tensor_tensor(out=out[0:np_, :], in0=out[0:np_, :], in1=Gt_[0:np_, layer, :], op=ALU.mult),
             r=[okey, gk_], w=[okey])
        P.op("pool", lambda g: g.tensor_tensor(out=out[0:np_, :], in0=out[0:np_, :], in1=Bt_[0:np_, layer, :], op=ALU.add),
             r=[okey, bk_], w=[okey])

    import os
    CHPIPE = int(os.environ.get("CHPIPE", "1"))

    class L0Tile:
        def __init__(self, ti, sample):
            self.ti, self.sample = ti, sample
            if sample:
                self.ncols, self.nsub, self.np_ = NS_TOK, 1, NS_TOK
                self.segs = [(b * DEC_S, DEC_S, 1 + b) for b in range(DEC_B)]
            else:
                self.ncols, self.nsub, self.np_ = TT, NSUB, 128
                self.segs = [(0, TT, 0)]
            self.t0 = ti * TT
            self.xt = xtok[cnt["tile"] % 2]
            self.xk = "xtok%d" % (cnt["tile"] % 2)
            cnt["tile"] += 1
            self.cis = {}

        def front(self):
            ti, sample, ncols, nsub, np_, segs, t0, xt, xk = (self.ti, self.sample, self.ncols, self.nsub, self.np_, self.segs,
                                                              self.t0, self.xt, self.xk)
            if sample:
                P.dma("sp", xt[0:np_, 0, :], xs[:, :], w=[xk])
            else:
                for s in range(nsub):
                    P.dma("sp", xt[:, s, :], xf[t0 + s * 128:t0 + (s + 1) * 128, :], w=[xk])
            for s in range(nsub):
                P.op("pool", lambda g, s=s: g.tensor_copy(out=xbf[0:np_, s, :], in_=xt[0:np_, s, :]), r=[xk], w=["xbf"])
            for half in range(2):
                pb = half
                for kk in range(4):
                    k = half * 4 + kk
                    for s in range(nsub):
                        P.op("pe", lambda t, k=k, kk=kk, s=s, pb=pb: t.transpose(
                            psb[pb][:, :].bitcast(BF16)[:, kk * TT + s * 128:kk * TT + s * 128 + np_],
                            xbf[0:np_, s, k * 128:(k + 1) * 128], ident_b[0:np_, 0:np_]),
                            r=["xbf", "ident_b"], w=[pskey(pb)])
                for kk in range(4):
                    k = half * 4 + kk
                    for (c0, cn, mj) in segs:
                        P.op("act", lambda a, k=k, kk=kk, pb=pb, c0=c0, cn=cn, mj=mj: a.activation(
                            out=mT[:, k, c0:c0 + cn], in_=psb[pb][:, :].bitcast(BF16)[:, kk * TT + c0:kk * TT + c0 + cn],
                            func=AF.Identity, scale=mod_fm[:, 0, 8 + k, mj:mj + 1], bias=mod_fm[:, 0, k, mj:mj + 1]),
                            r=[pskey(pb), "mod_fm"], w=["mT"])

        def chunk_ab(self, n):
            ti, sample, ncols = self.ti, self.sample, self.ncols
            ci = cnt["ch"] % NT
            cnt["ch"] += 1
            self.cis[n] = ci
            pb = 2 + (n % 2)
            pk = pskey(pb)
            for k in range(NCH):
                P.op("pe", lambda t, k=k, n=n, pb=pb: t.matmul(psb[pb][:, 0:ncols], lhsT=W_in[:, k, n * 128:(n + 1) * 128],
                                                               rhs=mT[:, k, 0:ncols], start=(k == 0), stop=(k == NCH - 1)),
                     r=["W_in", "mT"], w=[pk])
            for k in range(NCH):
                P.op("pe", lambda t, k=k, n=n, pb=pb: t.matmul(psb[pb][:, 256:256 + ncols],
                                                               lhsT=W_in[:, k, D + n * 128:D + (n + 1) * 128],
                                                               rhs=mT[:, k, 0:ncols], start=(k == 0), stop=(k == NCH - 1)),
                     r=["W_in", "mT"], w=[pk])
            if sample:
                xe = xbe_s[:, n, :, :]
                xek = "xbe_s"
                P.op("dve", lambda v, pb=pb, xe=xe: v.tensor_copy(
                    out=xe[:, :, 3:3 + DEC_S], in_=psb[pb][:, 0:ncols].rearrange("p (b t) -> p b t", t=DEC_S)),
                    r=[pk], w=[xek])
                sh = lambda k: xe[:, :, k:k + DEC_S]
                v3 = lambda ap: ap[:, 0:ncols].rearrange("p (b t) -> p b t", t=DEC_S)
            else:
                xe = xbe[:, n, :]
                xek = ("xbe", n)
                if ti > 0:
                    P.op("dve", lambda v, xe=xe: v.tensor_copy(out=xe[:, 0:3], in_=xe[:, TT:TT + 3]), r=[xek], w=[xek])
                P.op("dve", lambda v, pb=pb, xe=xe: v.tensor_copy(out=xe[:, 3:3 + TT], in_=psb[pb][:, 0:TT]), r=[pk], w=[xek])
                sh = lambda k: xe[:, k:k + TT]
                v3 = lambda ap: ap[:, 0:ncols]
            zk = "zs%d" % ci
            P.op("act", lambda a, pb=pb, ci=ci: a.activation(out=zs[ci][:, 0:ncols], in_=psb[pb][:, 256:256 + ncols],
                                                             func=AF.Silu), r=[pk], w=[zk])
            ck = "xc%d" % ci
            P.op("dve", lambda v, ci=ci, n=n: v.tensor_scalar(out=v3(xc[ci]), in0=sh(0), scalar1=pf[:, 0, n:n + 1],
                                                              scalar2=pf[:, 4, n:n + 1], op0=ALU.mult, op1=ALU.add),
                 r=[xek, "pf"], w=[ck])
            for k in range(1, 4):
                P.op("dve", lambda v, ci=ci, n=n, k=k: v.scalar_tensor_tensor(
                    out=v3(xc[ci]), in0=sh(k), scalar=pf[:, k, n:n + 1], in1=v3(xc[ci]), op0=ALU.mult, op1=ALU.add),
                    r=[xek, "pf", ck], w=[ck])
            cbk = "xcb%d" % ci
            P.op("pool", lambda g, ci=ci: g.tensor_copy(out=xcb[ci][:, 0:ncols], in_=xc[ci][:, 0:ncols]), r=[ck], w=[cbk])

        def chunk_cde(self, n):
            ti, sample, ncols = self.ti, self.sample, self.ncols
            ci = self.cis[n]
            zk, ck, cbk = "zs%d" % ci, "xc%d" % ci, "xcb%d" % ci
            pg = 4 + (n % 2)
            pgk = pskey(pg)
            P.op("pe", lambda t, n=n, ci=ci, pg=pg: t.matmul(psb[pg][:, 0:ncols], lhsT=W_r[:, n, :], rhs=xcb[ci][:, 0:ncols],
                                                             start=True, stop=True), r=["W_r", cbk], w=[pgk])
            P.op("pe", lambda t, n=n, ci=ci, pg=pg: t.matmul(psb[pg][:, 256:256 + ncols], lhsT=W_i[:, n, :],
                                                             rhs=xcb[ci][:, 0:ncols], start=True, stop=True),
                 r=["W_i", cbk], w=[pgk])
            rk, ik, gk, bk, hk = "ra%d" % ci, "ri%d" % ci, "ga%d" % ci, "bb%d" % ci, "hs%d" % ci
            P.op("act", lambda a, n=n, ci=ci, pg=pg: a.activation(out=ra[ci][:, 0:ncols], in_=psb[pg][:, 0:ncols],
                                                                  func=AF.Sigmoid, bias=pf[:, 5, n:n + 1]),
                 r=[pgk, "pf"], w=[rk])
            P.op("act", lambda a, n=n, ci=ci, pg=pg: a.activation(out=ri[ci][:, 0:ncols], in_=psb[pg][:, 256:256 + ncols],
                                                                  func=AF.Sigmoid, bias=pf[:, 6, n:n + 1]),
                 r=[pgk, "pf"], w=[ik])
            P.op("act", lambda a, n=n, ci=ci: a.activation(out=ga[ci][:, 0:ncols], in_=ra[ci][:, 0:ncols], func=AF.Exp,
                                                           scale=pf[:, 8, n:n + 1]), r=[rk, "pf"], w=[gk])
            P.op("act", lambda a, n=n, ci=ci: a.activation(out=ra[ci][:, 0:ncols], in_=ra[ci][:, 0:ncols], func=AF.Exp,
                                                           scale=pf[:, 7, n:n + 1]), r=[rk, "pf"], w=[rk])
            P.op("dve", lambda v, ci=ci: v.tensor_scalar(out=ga[ci][:, 0:ncols], in0=ga[ci][:, 0:ncols], scalar1=-1.0,
                                                         scalar2=1.0, op0=ALU.mult, op1=ALU.add), r=[gk], w=[gk])
            P.op("dve", lambda v, ci=ci: v.tensor_scalar_max(out=ga[ci][:, 0:ncols], in0=ga[ci][:, 0:ncols], scalar1=0.0),
                 r=[gk], w=[gk])
            P.op("act", lambda a, ci=ci: a.activation(out=ga[ci][:, 0:ncols], in_=ga[ci][:, 0:ncols], func=AF.Sqrt),
                 r=[gk], w=[gk])
            P.op("pool", lambda g, ci=ci: g.tensor_tensor(out=bb[ci][:, 0:ncols], in0=ri[ci][:, 0:ncols],
                                                          in1=xc[ci][:, 0:ncols], op=ALU.mult), r=[ik, ck], w=[bk])
            P.op("dve", lambda v, ci=ci: v.tensor_tensor(out=bb[ci][:, 0:ncols], in0=bb[ci][:, 0:ncols],
                                                         in1=ga[ci][:, 0:ncols], op=ALU.mult), r=[bk, gk], w=[bk])
            if sample:
                for b in range(DEC_B):
                    P.op("dve", lambda v, ci=ci, n=n, b=b: v.tensor_tensor_scan(
                        out=hs[ci][:, b * DEC_S:(b + 1) * DEC_S], data0=ra[ci][:, b * DEC_S:(b + 1) * DEC_S],
                        data1=bb[ci][:, b * DEC_S:(b + 1) * DEC_S], initial=h0s[:, n, b:b + 1], op0=ALU.mult, op1=ALU.add),
                        r=[rk, bk, "h0s"], w=[hk])
                P.op("dve", lambda v, ci=ci, n=n: v.tensor_copy(
                    out=hlast_s[:, n, :], in_=hs[ci][:, 0:ncols].rearrange("p (b t) -> p b t", t=DEC_S)[:, :, DEC_S - 1]),
                    r=[hk], w=["hlast_s"])
            else:
                P.op("dve", lambda v, ci=ci, n=n: v.tensor_tensor_scan(
                    out=hs[ci][:, 0:TT], data0=ra[ci][:, 0:TT], data1=bb[ci][:, 0:TT], initial=hprev[:, n:n + 1],
                    op0=ALU.mult, op1=ALU.add), r=[rk, bk, ("hprev", n)], w=[hk])
                P.op("dve", lambda v, ci=ci, n=n: v.tensor_copy(out=hprev[:, n:n + 1], in_=hs[ci][:, TT - 1:TT]),
                     r=[hk], w=[("hprev", n)])
            P.op("pool", lambda g, ci=ci, n=n: g.tensor_tensor(out=yg[:, n, 0:ncols], in0=hs[ci][:, 0:ncols],
                                                               in1=zs[ci][:, 0:ncols], op=ALU.mult), r=[hk, zk], w=["yg"])

        def chunks(self, lo, hi):
            if CHPIPE == 0:
                for n in range(lo, hi):
                    self.chunk_ab(n)
                    self.chunk_cde(n)
                return
            for n in range(lo, hi):
                self.chunk_ab(n)
                if n - 1 >= 0:
                    self.chunk_cde(n - 1)
            if hi == NCH:
                self.chunk_cde(NCH - 1)

        def outproj_ln(self, subs=None):
            ti, sample, nsub, np_, t0, xt, xk = self.ti, self.sample, self.nsub, self.np_, self.t0, self.xt, self.xk
            if not hasattr(self, "sis"):
                self.sis = {}
            for s in (range(nsub) if subs is None else subs):
                si = cnt["sub"] % 2
                cnt["sub"] += 1
                self.sis[s] = si
                vk, x1k, x1bk, stk = "vt%d" % si, "x1t%d" % si, "x1b%d" % si, "stat%d" % si
                for h in range(2):
                    pb = 4 + h
                    for k in range(NCH):
                        P.op("pe", lambda t, k=k, h=h, s=s, pb=pb: t.matmul(
                            psb[pb][0:np_, :], lhsT=yg[:, k, s * 128:s * 128 + np_], rhs=W_out[:, k, h * 512:(h + 1) * 512],
                            start=(k == 0), stop=(k == NCH - 1)), r=["yg", "W_out"], w=[pskey(pb)])
                    G = Gs if sample else Gp
                    P.op("dve", lambda v, h=h, pb=pb, si=si, G=G: v.tensor_tensor(
                        out=vt[si][0:np_, h * 512:(h + 1) * 512], in0=psb[pb][0:np_, :], in1=G[0:np_, 0, h * 512:(h + 1) * 512],
                        op=ALU.mult), r=[pskey(pb), "Gs" if sample else "Gp"], w=[vk])
                P.op("dve", lambda v, si=si, s=s: v.scalar_tensor_tensor(
                    out=vt[si][0:np_, :], in0=xt[0:np_, s, :], scalar=ALPHA, in1=vt[si][0:np_, :], op0=ALU.mult, op1=ALU.add),
                    r=[xk, vk], w=[vk])
                layernorm_tm(vt[si], vk, x1t[si], x1k, np_, 0, stat[si], stk)
                if not sample:
                    P.dma("sp", x1scr[t0 + s * 128:t0 + (s + 1) * 128, :], x1t[si][:, :], r=[x1k], w=[("x1scr", ti, s)], semkey=x1k)
                else:
                    P.dma("sp", x1s_scr[:, :], x1t[si][0:np_, :], r=[x1k], w=["x1s_scr"], semkey=x1k)
                P.op("act", lambda a, si=si: a.activation(out=x1b[si][0:np_, :], in_=x1t[si][0:np_, :], func=AF.Identity),
                     r=[x1k], w=[x1bk])

        def tail(self, subs=None, fin=True):
            ti, sample, nsub, np_, t0 = self.ti, self.sample, self.nsub, self.np_, self.t0
            for s in (range(nsub) if subs is None else subs):
                si = self.sis[s]
                x1bk, kvk = "x1b%d" % si, "kvst%d" % si
                pb = 6 + (s % 2)
                for k in range(NCH):
                    P.op("pe", lambda t, k=k, si=si, pb=pb: t.transpose(
                        psb[pb][:, :].bitcast(BF16)[:, k * 128:k * 128 + np_], x1b[si][0:np_, k * 128:(k + 1) * 128],
                        ident_b[0:np_, 0:np_]), r=[x1bk, "ident_b"], w=[pskey(pb)])
                P.op("dve", lambda v, pb=pb, s=s: v.tensor_copy(
                    out=x1T[:, :, s * 128:s * 128 + np_],
                    in_=psb[pb][:, :].bitcast(BF16)[:, 0:1024].rearrange("p (k t) -> p k t", t=128)[:, :, 0:np_]),
                    r=[pskey(pb)], w=[("x1T", s)])
                for c3 in range(3):
                    pb = c3 % 2
                    for k in range(NCH):
                        P.op("pe", lambda t, k=k, c3=c3, s=s, pb=pb: t.matmul(
                            psb[pb][0:np_, :], lhsT=x1T[:, k, s * 128:s * 128 + np_], rhs=W_kv[:, k, c3 * 512:(c3 + 1) * 512],
                            start=(k == 0), stop=(k == NCH - 1)), r=[("x1T", s), "W_kv"], w=[pskey(pb)])
                    P.op("act", lambda a, c3=c3, pb=pb, si=si: a.activation(
                        out=kvst[si][0:np_, c3 * 512:(c3 + 1) * 512], in_=psb[pb][0:np_, :], func=AF.Identity),
                        r=[pskey(pb)], w=[kvk])
                if sample:
                    P.dma("sp", o_cmp_s[:, :], kvst[si][0:np_, 0:512], r=[kvk], w=["o_cmp_s"], semkey=kvk)
                    P.dma("sp", o_sel_s[:, :], kvst[si][0:np_, 512:1024], r=[kvk], w=["o_sel_s"], semkey=kvk)
                    for b in range(DEC_B):
                        P.dma("sp", o_win_s[b * 512 + 504:(b + 1) * 512, :], kvst[si][b * 8:(b + 1) * 8, 1024:1536],
                              r=[kvk], w=[("o_win_s", b, 1)], semkey=kvk)
                else:
                    r0 = t0 + s * 128
                    P.dma("sp", o_cmp_p[r0:r0 + 128, :], kvst[si][:, 0:512], r=[kvk], w=[("o_cmp_p", ti, s)], semkey=kvk)
                    P.dma("sp", o_sel_p[r0:r0 + 128, :], kvst[si][:, 512:1024], r=[kvk], w=[("o_sel_p", ti, s)], semkey=kvk)
                    P.dma("sp", winscr[r0:r0 + 128, :], kvst[si][:, 1024:1536], r=[kvk], w=[("winscr", ti, s)], semkey=kvk)
                    if r0 >= SEQ - 512:
                        P.dma("sp", o_win_p[r0 - (SEQ - 512):r0 - (SEQ - 512) + 128, :], kvst[si][:, 1024:1536],
                              r=[kvk], w=[("o_win_p", ti, s)], semkey=kvk)
            if sample and fin:
                for n in range(NCH):
                    P.dma("sp", o_h_s.rearrange("b (c p) -> c p b", p=128)[n], hlast_s[:, n, :], r=["hlast_s"], w=[("o_h_s", n)],
                          semkey="hlast_s")
                    for b in range(DEC_B):
                        P.dma("sp", o_conv_s[b * 3:(b + 1) * 3, n * 128:(n + 1) * 128].rearrange("k p -> p k"),
                              xbe_s[:, n, b, DEC_S:DEC_S + 3], r=["xbe_s"], w=[("o_conv_s", n, b)], semkey="xbe_s_o")

        def final_state(self):
            P.dma("sp", o_h_p[0:1, :].rearrange("o (c p) -> p (o c)", p=128), hprev[:, :],
                  r=[("hprev", n) for n in range(NCH)], w=["o_h_p"], semkey="hprev_o")
            for n in range(NCH):
                P.dma("sp", o_conv_p.rearrange("k (c p) -> c p k", p=128)[n], xbe[:, n, TT:TT + 3], r=[("xbe", n)],
                      w=[("o_conv_p", n)], semkey="xbe_o")

    import os
    L0PIPE = int(os.environ.get("L0PIPE", "2"))
    n_ptiles = SEQ // TT if stage >= 1 else 2
    if L0PIPE == -1:
        for i in range(n_ptiles + 1):
            tl = L0Tile(0, True) if i == 0 else L0Tile(i - 1, False)
            tl.front()
            tl.chunks(0, NCH)
            for s_ in range(tl.nsub):
                tl.outproj_ln([s_])
                tl.tail([s_], fin=(s_ == tl.nsub - 1))
        tl.final_state()
    elif L0PIPE == 0:
        seq = [L0Tile(0, True)] + [None] * n_ptiles
        for i in range(n_ptiles + 1):
            tl = seq[i] if i == 0 else L0Tile(i - 1, False)
            tl.front()
            tl.chunks(0, NCH)
            tl.outproj_ln()
            tl.tail()
        tl.final_state()
    elif L0PIPE == 1:
        cur = L0Tile(0, True)
        cur.front()
        for i in range(n_ptiles + 1):
            cur.chunks(0, NCH)
            nxt = L0Tile(i, False) if i < n_ptiles else None
            if nxt is not None:
                nxt.front()
            for s_ in range(cur.nsub):
                cur.outproj_ln([s_])
                cur.tail([s_], fin=(s_ == cur.nsub - 1))
            last = cur
            cur = nxt
        last.final_state()
    else:
        tiles = [L0Tile(0, True)]
        tiles[0].front()
        tiles[0].chunks(0, NCH)
        tiles[0].outproj_ln()
        nxt = L0Tile(0, False)
        nxt.front()
        prev = tiles[0]
        for ti in range(n_ptiles):
            cur = nxt
            cur.chunks(0, NCH // 2)
            prev.tail()
            cur.chunks(NCH // 2, NCH)
            if ti + 1 < n_ptiles:
                nxt = L0Tile(ti + 1, False)
                nxt.front()
            cur.outproj_ln()
            prev = cur
        prev.tail()
        prev.final_state()

    scA.__exit__(None, None, None)
    cmpscr_p = P.dram("cmpscr_p", [128, 512], F32)
    cmpscr_s = P.dram("cmpscr_s", [DEC_B * 128, 512], F32)
    do_sample = stage >= 3
    with P.scope():
        W1r = P.sb("W1r", [128, 2, 64, 128], BF16)
        CB2 = P.sb("CB2", [128, 64, 512], BF16)
        Hh = P.sb("Hh", [128, 2, 2, 256], BF16)
        W2 = P.sb("W2", [128, 2, 64], BF16)
        PEsb = P.sb("PEsb", [64, 128], BF16)
        bias1 = P.sb("bias1", [128, 4], F32)
        b2bc = P.sb("b2bc", [64, 2, 4, 128], F32)
        CS = P.sb("CS", [64, 2, 512], F32)
        IDXf = P.sb("IDXf", [128, DEC_B * 64], F32)
        IDXi = P.sb("IDXi", [128, DEC_B * 64], I32)
        iop = P.sb("iop", [128, 2], I32)
        iopf = P.sb("iopf", [128, 2], F32)
        for c in range(2):
            for half in range(2):
                P.dma("pool", W1r[half * 64:(half + 1) * 64, c, :, :], w_phi1[c], w=["W1r"])
            P.dma("pool", W2[:, c, :], w_phi2[c], w=["W2"])
            P.dma("sp", bias1[:, c:c + 1], b_phi1[c:c + 1, :].rearrange("o p -> p o"), w=["bias1"])
        P.dma("pool", PEsb[:], phi_pe[:, :], w=["PEsb"])
        for nl in range(2):
            for g in range(4):
                P.dma("sp", b2bc[:, nl, g, :], b_phi2.rearrange("c d -> (c d)").rearrange("(o n) -> o n", o=1).partition_broadcast(64),
                      w=["b2bc"])
        for c in range(2):
            for d in range(64):
                P.op("pe", lambda t, c=c, d=d: t.matmul(psb[0][:, c:c + 1], lhsT=W1r[0:64, c, d, :],
                                                        rhs=PEsb[0:64, c * 64 + d:c * 64 + d + 1],
                                                        start=(d == 0), stop=(d == 63)), r=["W1r", "PEsb"], w=[pskey(0)])
        P.op("dve", lambda v: v.tensor_tensor(out=bias1[:, 0:2], in0=psb[0][:, 0:2], in1=bias1[:, 0:2], op=ALU.add),
             r=[pskey(0), "bias1"], w=["bias1"])
        P.op("pool", lambda g_: g_.iota(out=iop[:, 0:1], pattern=[[0, 1]], base=0, channel_multiplier=1), w=["iop"])
        P.op("dve", lambda v: v.tensor_copy(out=iopf[:, 0:1], in_=iop[:, 0:1]), r=["iop"], w=["iopf"])
        P.dma("sp", IDXi[:], ptab.rearrange("b n -> (b n)").rearrange("(o n) -> o n", o=1).partition_broadcast(128), w=["IDXi"])
        P.op("dve", lambda v: v.tensor_copy(out=IDXf[:], in_=IDXi[:]), r=["IDXi"], w=["IDXf"])
        P.op("dve", lambda v: v.tensor_scalar(out=IDXf[:], in0=IDXf[:], scalar1=128.0, scalar2=iopf[:, 0:1],
                                              op0=ALU.mult, op1=ALU.add), r=["IDXf", "iopf"], w=["IDXf"])
        P.op("dve", lambda v: v.tensor_copy(out=IDXi[:], in_=IDXf[:]), r=["IDXf"], w=["IDXi"])
        idxscr = P.dram("idxscr", [128, DEC_B * 64], I32)
        P.dma("sp", idxscr[:, :], IDXi[:], r=["IDXi"], w=["idxscr"], semkey="IDXi_o")

        CB2v = CB2[:].rearrange("p n (g c d) -> p n g c d", g=4, c=2)

        def compress(load_pages, out_rows, okey):
            load_pages()
            for c in range(2):
                for nl in range(2):
                    pb = c * 2 + nl
                    for d in range(64):
                        P.op("pe", lambda t, c=c, nl=nl, d=d, pb=pb: t.matmul(
                            psb[pb][:, 0:256], lhsT=W1r[nl * 64:(nl + 1) * 64, c, d, :],
                            rhs=CB2v[nl * 64:(nl + 1) * 64, :, :, c, d], start=(d == 0), stop=(d == 63)),
                            r=["W1r", "CB2"], w=[pskey(pb)])
                    P.op("act", lambda a, c=c, nl=nl, pb=pb: a.activation(out=Hh[:, c, nl, :], in_=psb[pb][:, 0:256], func=AF.Silu,
                                                                          bias=bias1[:, c:c + 1]), r=[pskey(pb), "bias1"], w=["Hh"])
            Hv = Hh[:].rearrange("p c n (pg g) -> p c n pg g", g=4)
            for nl in range(2):
                pb = 4 + nl
                for g in range(4):
                    for c in range(2):
                        col = (g * 2 + c) * 64
                        P.op("pe", lambda t, nl=nl, g=g, c=c, pb=pb, col=col: t.matmul(
                            psb[pb][0:64, col:col + 64], lhsT=Hv[:, c, nl, :, g], rhs=W2[:, c, :], start=True, stop=True),
                            r=["Hh", "W2"], w=[pskey(pb)])
                P.op("dve", lambda v, nl=nl, pb=pb: v.tensor_tensor(
                    out=CS[:, nl, :], in0=psb[pb][0:64, :], in1=b2bc[:, nl, :, :].rearrange("p g f -> p (g f)"), op=ALU.add),
                    r=[pskey(pb), "b2bc"], w=["CS"])
            P.dma("sp", out_rows.rearrange("(pg n) f -> pg n f", n=2), CS[:], r=["CS"], w=[okey], semkey="CS")

        def load_prompt_pages():
            for pg in range(64):
                P.dma("pool", CB2[:, pg, :], o_cmp_p[pg * 128:(pg + 1) * 128, :], r=[("o_cmp_p", pg // NSUB, pg % NSUB)], w=["CB2"])

        if stage >= 2:
            compress(load_prompt_pages, cmpscr_p[:, :], "cmpscr_p")
        if do_sample:
            for b in range(DEC_B):
                def load_sample_pages(b=b):
                    for pg in range(64):
                        P.gather(CB2[:, pg, :], ccmp[:, :], IDXi[:, b * 64 + pg:b * 64 + pg + 1], r=["IDXi"], w=["CB2"])
                compress(load_sample_pages, cmpscr_s[b * 128:(b + 1) * 128, :], ("cmpscr_s", b))

    NTOK = 2048 + NS_TOK
    QTscr = P.dram("QTscr", [4, 64, 4, NTOK], BF16)
    ZSscr = P.dram("ZSscr", [NTOK, D], BF16)
    GLscr = P.dram("GLscr", [NTOK, 48], F32)
    OGscr = P.dram("OGscr", [NTOK, D], BF16)
    qtiles = [(jl * 128, 128, jl, None) for jl in range(16)] if stage >= 2 else []
    if do_sample:
        qtiles += [(2048 + b * 8, 8, None, b) for b in range(DEC_B)]
    if stage == 2.5:
        qtiles = qtiles[:2]

    with P.scope():
        W_inb = P.sb("W_inb", [128, NCH, 2096], BF16)
        for k in range(NCH):
            P.dma("pool", W_inb[:, k, :], w_in_b[k * 128:(k + 1) * 128, :], w=["W_inb"])
        bgbc = P.sb("bgbc", [128, 48], F32)
        P.dma("sp", bgbc[:], b_gate[0:1, :].partition_broadcast(128), w=["bgbc"])
        idxo = P.sb("idxo", [128, 16], I32)
        P.dma("sp", idxo[:], t_idx_own[:, :], w=["idxo"])
        X1 = [P.sb("X1_%d" % i, [128, D], F32) for i in range(2)]
        X1b = P.sb("X1b", [128, D], BF16)
        m1T = P.sb("m1T", [128, NCH, 128], BF16)
        QTst = [P.sb("QTst%d" % i, [64, 4, 128], BF16) for i in range(2)]
        ZSt = [P.sb("ZSt%d" % i, [128, D], BF16) for i in range(2)]
        GLt = [P.sb("GLt%d" % i, [128, 48], F32) for i in range(2)]
        qi = 0
        for (tok0, nq, jl, sb_) in qtiles:
            i2 = qi % 2
            xk = "X1_%d" % i2
            if jl is not None:
                P.gather(X1[i2][:, :], x1scr[:, :], idxo[:, jl:jl + 1], r=["idxo"], w=[xk])
                mj = 0
            else:
                P.dma("sp", X1[i2][0:nq, :], x1s_scr[sb_ * 8:(sb_ + 1) * 8, :], w=[xk])
                mj = 1 + sb_
            P.op("pool", lambda g_, i2=i2, nq=nq: g_.tensor_copy(out=X1b[0:nq, :], in_=X1[i2][0:nq, :]), r=[xk], w=["X1b"])
            for k in range(NCH):
                P.op("pe", lambda t, k=k, nq=nq: t.transpose(psb[0][:, :].bitcast(BF16)[:, k * 128:k * 128 + nq],
                                                             X1b[0:nq, k * 128:(k + 1) * 128], ident_b[0:nq, 0:nq]),
                     r=["X1b", "ident_b"], w=[pskey(0)])
            for k in range(NCH):
                P.op("act", lambda a, k=k, nq=nq, mj=mj: a.activation(
                    out=m1T[:, k, 0:nq], in_=psb[0][:, :].bitcast(BF16)[:, k * 128:k * 128 + nq], func=AF.Identity,
                    scale=mod_fm[:, 1, 8 + k, mj:mj + 1], bias=mod_fm[:, 1, k, mj:mj + 1]), r=[pskey(0), "mod_fm"], w=["m1T"])
            for g in range(4):
                pb = 2 + (g % 2)
                qk = "QTst%d" % (g % 2)
                for hh in range(4):
                    for k in range(NCH):
                        P.op("pe", lambda t, k=k, hh=hh, g=g, pb=pb, nq=nq: t.matmul(
                            psb[pb][0:64, hh * 128:hh * 128 + nq], lhsT=W_inb[:, k, (4 * g + hh) * 64:(4 * g + hh + 1) * 64],
                            rhs=m1T[:, k, 0:nq], start=(k == 0), stop=(k == NCH - 1)), r=["W_inb", "m1T"], w=[pskey(pb)])
                P.op("dve", lambda v, g=g, pb=pb, nq=nq: v.tensor_scalar_mul(
                    out=QTst[g % 2][:, :, 0:nq], in0=psb[pb][0:64, :].rearrange("p (h q) -> p h q", h=4)[:, :, 0:nq],
                    scalar1=0.125), r=[pskey(pb)], w=[qk])
                P.dma("sp", QTscr[g, :, :, tok0:tok0 + nq], QTst[g % 2][:, :, 0:nq], r=[qk], w=[("QTscr", g, tok0)], semkey=qk)
            zk = "ZSt%d" % i2
            for half in range(2):
                pb = 4 + half
                for k in range(NCH):
                    P.op("pe", lambda t, k=k, half=half, pb=pb, nq=nq: t.matmul(
                        psb[pb][0:nq, :], lhsT=m1T[:, k, 0:nq], rhs=W_inb[:, k, D + half * 512:D + (half + 1) * 512],
                        start=(k == 0), stop=(k == NCH - 1)), r=["W_inb", "m1T"], w=[pskey(pb)])
                P.op("act", lambda a, half=half, pb=pb, nq=nq, i2=i2: a.activation(
                    out=ZSt[i2][0:nq, half * 512:(half + 1) * 512], in_=psb[pb][0:nq, :], func=AF.Silu), r=[pskey(pb)], w=[zk])
            P.dma("sp", ZSscr[tok0:tok0 + nq, :], ZSt[i2][0:nq, :], r=[zk], w=[("ZSscr", tok0)], semkey=zk)
            gk = "GLt%d" % i2
            for k in range(NCH):
                P.op("pe", lambda t, k=k, nq=nq: t.matmul(psb[6][0:nq, 0:48], lhsT=m1T[:, k, 0:nq], rhs=W_inb[:, k, 2048:2096],
                                                          start=(k == 0), stop=(k == NCH - 1)), r=["W_inb", "m1T"], w=[pskey(6)])
            P.op("dve", lambda v, nq=nq, i2=i2: v.tensor_tensor(out=GLt[i2][0:nq, :], in0=psb[6][0:nq, 0:48], in1=bgbc[0:nq, :],
                                                                op=ALU.add), r=[pskey(6), "bgbc"], w=[gk])
            P.op("act", lambda a, nq=nq, i2=i2: a.activation(out=GLt[i2][0:nq, :], in_=GLt[i2][0:nq, :], func=AF.Sigmoid),
                 r=[gk], w=[gk])
            P.dma("sp", GLscr[tok0:tok0 + nq, :], GLt[i2][0:nq, :], r=[gk], w=[("GLscr", tok0)], semkey=gk)
            qi += 1

    with P.scope():
        EE = P.sb("EE", [128, 64, 128], BF16)
        ones_b = P.sb("ones_b", [128, 1024], BF16)
        P.op("pool", lambda g_: g_.memset(ones_b[:], 1.0), w=["ones_b"])
        for T8 in range(8):
            P.op("pool", lambda g_, T8=T8: g_.affine_select(
                out=EE[:, T8 * 8:(T8 + 1) * 8, :].rearrange("p t (a b) -> p t a b", a=2),
                in_=ones_b[:, :].rearrange("p (t a b) -> p t a b", t=8, a=2),
                pattern=[[-2, 8], [-1, 2], [0, 64]], compare_op=ALU.is_equal, fill=0.0, base=-16 * T8, channel_multiplier=1),
                r=["ones_b"], w=["EE"])
        TRIp = P.sb("TRIp", [128, 512], BF16)
        TRI2p = P.sb("TRI2p", [128, 512], BF16)
        TRIs = P.sb("TRIs", [128, 32], BF16)
        TRI2s = P.sb("TRI2s", [128, 32], BF16)
        P.dma("sp", TRIp[:], t_tri_p[:, :], w=["TRIp"])
        P.dma("sp", TRI2p[:], t_tri2_p[:, :], w=["TRI2p"])
        P.dma("sp", TRIs[:], t_tri_s[:, :], w=["TRIs"])
        P.dma("sp", TRI2s[:], t_tri2_s[:, :], w=["TRI2s"])
        KsT = P.sb("KsT", [72, 65 * 128], BF16)
        Vs = P.sb("Vs", [128, 65, 65], BF16)
        KwT = P.sb("KwT", [72, 21 * 128], BF16)
        Vw = P.sb("Vw", [128, 21, 65], BF16)
        KcT = P.sb("KcT", [72, 128], BF16)
        Vc = P.sb("Vc", [128, 64], BF16)
        P.op("pool", lambda g_: g_.memset(Vs[:, :, 64:65], 1.0), w=["Vs"])
        P.op("pool", lambda g_: g_.memset(Vw[:, :, 64:65], 1.0), w=["Vw"])
        RB = [P.sb("RB%d" % i, [128, 8, 128], BF16) for i in range(2)]
        RBF = [P.sb("RBF%d" % i, [128, 512], BF16) for i in range(2)]
        idxo2 = P.sb("idxo2", [128, 16], I32)
        idxw = P.sb("idxw", [128, 20], I32)
        idxs = P.sb("idxs", [128, 1], I32)
        idxpg = P.sb("idxpg", [128, DEC_B * 64], I32)
        P.dma("sp", idxo2[:], t_idx_own[:, :], w=["idxo2"])
        P.dma("sp", idxw[:], t_idx_win[:, :], w=["idxw"])
        P.dma("sp", idxs[:], t_idx_slot[:, :], w=["idxs"])
        P.dma("sp", idxpg[:], idxscr[:, :], w=["idxpg"])
        QT = [P.sb("QT%d" % i, [72, 4, 128], BF16) for i in range(2)]
        FBNt = [P.sb("FBNt%d" % i, [128, 128], F32) for i in range(2)]
        CAUt = [P.sb("CAUt%d" % i, [128, 128], F32) for i in range(2)]
        TMt = [P.sb("TMt%d" % i, [128, 128], F32) for i in range(2)]
        GLg = [P.sb("GLg%d" % i, [128, 48], F32) for i in range(2)]
        ZSg = [P.sb("ZSg%d" % i, [128, 256], BF16) for i in range(2)]
        Ssb = P.sb("Ssb", [128, 512], F32)
        Esb = P.sb("Esb", [128, 512], F32)
        Pn = P.sb("Pn", [128, 512], F32)
        Pnb = P.sb("Pnb", [128, 512], BF16)
        imp = P.sb("imp", [128, 128], F32)
        scr = P.sb("scr", [128, 128], F32)
        scr2 = P.sb("scr2", [128, 128], F32)
        m8 = P.sb("m8", [128, 16], F32)
        smh = P.sb("smh", [128, 16], F32)
        smb = P.sb("smb", [128, 16], F32)
        MselT = P.sb("MselT", [128, 128], BF16)
        Msel4s = [P.sb("Msel4_%d" % i, [128, 512], BF16) for i in range(2)]
        PTc = P.sb("PTc", [128, 512], BF16)
        PT = [P.sb("PT%d" % i, [128, 512], BF16) for i in range(3)]
        OaugSB = P.sb("OaugSB", [65, 512], F32)
        accs = [P.sb("acc%d" % i, [128, 256], F32) for i in range(2)]
        OGt = [P.sb("OGt%d" % i, [128, 256], BF16) for i in range(2)]
        cnt2 = {"rb": 0, "rbf": 0, "pt": 0, "ps": 0, "q": 0}

        def prep_from_rbf(rbf, rk, g, nk, ktdst, kkey, vdst, vkey):
            P.op("pe", lambda t: t.transpose(psb[7][:, :].bitcast(BF16)[0:64, 0:nk], rbf[0:nk, g * 128:g * 128 + 64],
                                             ident_b[0:nk, 0:nk]), r=[rk, "ident_b"], w=[pskey(7)])
            P.op("dve", lambda v: v.tensor_copy(out=ktdst, in_=psb[7][:, :].bitcast(BF16)[0:64, 0:nk]), r=[pskey(7)], w=[kkey])
            P.op("pool", lambda g_: g_.tensor_copy(out=vdst, in_=rbf[0:nk, g * 128 + 64:g * 128 + 128]), r=[rk], w=[vkey])

        def load_rows_gather(src, idx_ap, ikey):
            i = cnt2["rbf"] % 2
            cnt2["rbf"] += 1
            P.gather(RBF[i][:, :], src, idx_ap, r=[ikey], w=["RBF%d" % i])
            return RBF[i], "RBF%d" % i

        def load_rows_plain(src_rows, nk):
            i = cnt2["rbf"] % 2
            cnt2["rbf"] += 1
            P.dma("pool", RBF[i][0:nk, :], src_rows, w=["RBF%d" % i])
            return RBF[i], "RBF%d" % i

        def nsa_tile(g, tok0, nq, jl, sb_, kth):
            ncol = 4 * nq
            sample = jl is None
            i2 = cnt2["q"] % 2
            cnt2["q"] += 1
            qt, qk = QT[i2], "QT%d" % i2
            Msel4, mk4 = Msel4s[i2], "Msel4_%d" % i2
            acc, ak = accs[i2], "acc%d" % i2
            P.dma("sp", qt[0:64, :, 0:nq], QTscr[g, :, :, tok0:tok0 + nq], w=[qk])
            if sample:
                P.dma("sp", qt[64:72, :, 0:nq], t_qaug_s.rearrange("r (h q) -> r h q", h=16)[:, 4 * g:4 * g + 4, :], w=[qk])
                P.dma("sp", FBNt[i2][0:nq, :], t_fbn_s[:, :], w=["FBNt%d" % i2])
                P.dma("sp", CAUt[i2][0:nq, :], t_caus_s[:, :], w=["CAUt%d" % i2])
                P.dma("sp", TMt[i2][0:nq, :], t_tm_s[:, :], w=["TMt%d" % i2])
            else:
                P.dma("sp", qt[64:72, :, 0:nq], t_qaug_p.rearrange("r (h q) -> r h q", h=16)[:, 4 * g:4 * g + 4, tok0:tok0 + nq],
                      w=[qk])
                P.dma("sp", FBNt[i2][:, :], t_fbn[jl], w=["FBNt%d" % i2])
                P.dma("sp", CAUt[i2][:, :], t_caus[jl], w=["CAUt%d" % i2])
                P.dma("sp", TMt[i2][:, :], t_tm[jl], w=["TMt%d" % i2])
            fk, ck, tk, glk, zk = "FBNt%d" % i2, "CAUt%d" % i2, "TMt%d" % i2, "GLg%d" % i2, "ZSg%d" % i2
            P.dma("sp", GLg[i2][0:nq, :], GLscr[tok0:tok0 + nq, :], w=[glk])
            P.dma("sp", ZSg[i2][0:nq, :], ZSscr[tok0:tok0 + nq, g * 256:(g + 1) * 256], w=[zk])
            qrhs = qt[0:72, :, 0:nq]
            gl3 = GLg[i2][0:nq, :].rearrange("p (h b) -> p h b", b=3)
            for hh in range(4):
                P.op("pe", lambda t, hh=hh: t.matmul(psb[6][0:nq, hh * 128:(hh + 1) * 128], lhsT=qt[0:72, hh, 0:nq], rhs=KcT[0:72, :],
                                                     start=True, stop=True), r=[qk, "KcT"], w=[pskey(6)])
            P.op("dve", lambda v: v.tensor_tensor(
                out=Ssb[0:nq, :].rearrange("p (h s) -> p h s", h=4), in0=psb[6][0:nq, :].rearrange("p (h s) -> p h s", h=4),
                in1=TMt[i2][0:nq, :].unsqueeze(1).to_broadcast([nq, 4, 128]), op=ALU.add), r=[pskey(6), tk], w=["Ssb"])
            for hh in range(4):
                P.op("act", lambda a, hh=hh: a.activation(out=Esb[0:nq, hh * 128:(hh + 1) * 128], in_=Ssb[0:nq, hh * 128:(hh + 1) * 128],
                                                          func=AF.Exp, accum_out=smh[0:nq, hh:hh + 1]), r=["Ssb"], w=["Esb", "smh"])
            P.op("dve", lambda v: v.tensor_scalar_add(out=smh[0:nq, 4:8], in0=smh[0:nq, 0:4], scalar1=1e-30), r=["smh"], w=["smh"])
            P.op("dve", lambda v: v.reciprocal(out=smh[0:nq, 4:8], in_=smh[0:nq, 4:8]), r=["smh"], w=["smh"])
            for hh in range(4):
                P.op("dve", lambda v, hh=hh: v.tensor_scalar_mul(out=Pn[0:nq, hh * 128:(hh + 1) * 128],
                                                                 in0=Esb[0:nq, hh * 128:(hh + 1) * 128],
                                                                 scalar1=smh[0:nq, 4 + hh:5 + hh]), r=["Esb", "smh"], w=["Pn"])
            P.op("pool", lambda g_: g_.tensor_copy(out=Pnb[0:nq, :], in_=Pn[0:nq, :]), r=["Pn"], w=["Pnb"])
            P.op("dve", lambda v: v.tensor_tensor(out=imp[0:nq, :], in0=Pn[0:nq, 0:128], in1=Pn[0:nq, 128:256], op=ALU.add),
                 r=["Pn"], w=["imp"])
            P.op("dve", lambda v: v.tensor_tensor(out=imp[0:nq, :], in0=imp[0:nq, :], in1=Pn[0:nq, 256:384], op=ALU.add),
                 r=["Pn", "imp"], w=["imp"])
            P.op("dve", lambda v: v.tensor_tensor(out=imp[0:nq, :], in0=imp[0:nq, :], in1=Pn[0:nq, 384:512], op=ALU.add),
                 r=["Pn", "imp"], w=["imp"])
            P.op("dve", lambda v: v.tensor_tensor(out=scr[0:nq, :], in0=imp[0:nq, :], in1=CAUt[i2][0:nq, :], op=ALU.mult),
                 r=["imp", ck], w=["scr"])
            P.op("dve", lambda v: v.tensor_tensor(out=scr[0:nq, :], in0=scr[0:nq, :], in1=FBNt[i2][0:nq, :], op=ALU.add),
                 r=["scr", fk], w=["scr"])
            P.op("dve", lambda v: v.max(out=m8[0:nq, 0:8], in_=scr[0:nq, :]), r=["scr"], w=["m8"])
            P.op("dve", lambda v: v.match_replace(out=scr2[0:nq, :], in_to_replace=m8[0:nq, 0:8], in_values=scr[0:nq, :],
                                                  imm_value=-1.0e30), r=["scr", "m8"], w=["scr2"])
            P.op("dve", lambda v: v.max(out=m8[0:nq, 8:16], in_=scr2[0:nq, :]), r=["scr2"], w=["m8"])
            thr = m8[0:nq, kth - 1:kth]
            P.op("dve", lambda v: v.tensor_scalar(out=scr2[0:nq, :], in0=scr[0:nq, :], scalar1=thr, scalar2=None, op0=ALU.is_ge),
                 r=["scr", "m8"], w=["scr2"])
            P.op("dve", lambda v: v.tensor_tensor(out=scr2[0:nq, :], in0=scr2[0:nq, :], in1=CAUt[i2][0:nq, :], op=ALU.mult),
                 r=["scr2", ck], w=["scr2"])
            P.op("dve", lambda v: v.tensor_scalar(out=MselT[0:nq, :], in0=scr2[0:nq, :], scalar1=-1.0, scalar2=-NEGM,
                                                  op0=ALU.add, op1=ALU.mult), r=["scr2"], w=["MselT"])
            yield "a"
            P.op("pe", lambda t: t.transpose(psb[7][:, :].bitcast(BF16)[:, 0:nq], MselT[0:nq, :], ident_b[0:nq, 0:nq]),
                 r=["MselT", "ident_b"], w=[pskey(7)])
            P.op("dve", lambda v: v.tensor_copy(
                out=Msel4[:, 0:ncol].rearrange("p (h q) -> p h q", h=4),
                in_=psb[7][:, :].bitcast(BF16)[:, 0:nq].unsqueeze(1).to_broadcast([128, 4, nq])), r=[pskey(7)], w=[mk4])
            for hh in range(4):
                P.op("pe", lambda t, hh=hh: t.transpose(psb[7][:, :].bitcast(BF16)[:, 512 + hh * nq:512 + (hh + 1) * nq],
                                                        Pnb[0:nq, hh * 128:(hh + 1) * 128], ident_b[0:nq, 0:nq]),
                     r=["Pnb", "ident_b"], w=[pskey(7)])
            P.op("dve", lambda v: v.tensor_copy(out=PTc[:, 0:ncol], in_=psb[7][:, :].bitcast(BF16)[:, 512:512 + ncol]),
                 r=[pskey(7)], w=["PTc"])
            for hh in range(4):
                P.op("pe", lambda t, hh=hh: t.matmul(psb[6][0:nq, hh * 64:(hh + 1) * 64], lhsT=PTc[:, hh * nq:(hh + 1) * nq], rhs=Vc[:, :],
                                                     start=True, stop=True), r=["PTc", "Vc"], w=[pskey(6)])
            for hh in range(4):
                P.op("dve", lambda v, hh=hh: v.tensor_scalar_mul(out=acc[0:nq, hh * 64:(hh + 1) * 64],
                                                                 in0=psb[6][0:nq, hh * 64:(hh + 1) * 64],
                                                                 scalar1=gl3[:, 4 * g + hh, 0:1]), r=[pskey(6), glk], w=[ak])

            yield "b"
            def attend(tiles, br, ob):
                nt = len(tiles)
                slots = {}

                def s_stage(i):
                    T = tiles[i]
                    sbk = cnt2["ps"] % 3
                    cnt2["ps"] += 1
                    nk = T["nk"]
                    mm = [(T["kt"], qrhs, T["kkey"], qk)] + T["masks"]
                    for j, (l, r_, lk, rk) in enumerate(mm):
                        P.op("pe", lambda t, l=l, r_=r_, j=j, nk=nk, sbk=sbk, n=len(mm): t.matmul(
                            psb[sbk][0:nk, 0:ncol], lhsT=l, rhs=r_, start=(j == 0), stop=(j == n - 1)),
                            r=[lk, rk], w=[pskey(sbk)])
                    slots[i] = sbk

                def e_stage(i):
                    T = tiles[i]
                    nk = T["nk"]
                    sbk = slots[i]
                    pi = cnt2["pt"] % 3
                    cnt2["pt"] += 1
                    P.op("act", lambda a, nk=nk, sbk=sbk, pi=pi: a.activation(out=PT[pi][0:nk, 0:ncol], in_=psb[sbk][0:nk, 0:ncol],
                                                                              func=AF.Exp), r=[pskey(sbk)], w=["PT%d" % pi])
                    return pi

                def v_stage(i, pi):
                    T = tiles[i]
                    nk = T["nk"]
                    P.op("pe", lambda t, T=T, nk=nk, pi=pi, i=i: t.matmul(psb[ob][0:65, 0:ncol], lhsT=T["v"], rhs=PT[pi][0:nk, 0:ncol],
                                                                          start=(i == 0), stop=(i == nt - 1)),
                         r=[T["vkey"], "PT%d" % pi], w=[pskey(ob)])

                LOOK = 2
                for i in range(min(LOOK, nt)):
                    s_stage(i)
                for i in range(nt):
                    pi = e_stage(i)
                    if i + LOOK < nt:
                        s_stage(i + LOOK)
                    v_stage(i, pi)
                P.op("dve", lambda v: v.tensor_copy(out=OaugSB[0:65, 0:ncol], in_=psb[ob][0:65, 0:ncol]), r=[pskey(ob)], w=["OaugSB"])
                for hh in range(4):
                    P.op("pe", lambda t, hh=hh: t.transpose(psb[5][0:nq, hh * 65:(hh + 1) * 65], OaugSB[0:65, hh * nq:(hh + 1) * nq],
                                                            ident_f[0:65, 0:65]), r=["OaugSB", "ident_f"], w=[pskey(5)])
                o3 = psb[5][0:nq, 0:260].rearrange("p (h e) -> p h e", e=65)
                P.op("dve", lambda v: v.tensor_scalar_add(out=smb[0:nq, 8:12], in0=o3[:, :, 64], scalar1=1e-30), r=[pskey(5)], w=["smb"])
                P.op("dve", lambda v: v.reciprocal(out=smb[0:nq, 8:12], in_=smb[0:nq, 8:12]), r=["smb"], w=["smb"])
                P.op("dve", lambda v: v.tensor_tensor(out=smb[0:nq, 12:16], in0=smb[0:nq, 8:12], in1=gl3[:, 4 * g:4 * g + 4, br],
                                                      op=ALU.mult), r=["smb", glk], w=["smb"])
                for hh in range(4):
                    P.op("dve", lambda v, hh=hh: v.scalar_tensor_tensor(
                        out=acc[0:nq, hh * 64:(hh + 1) * 64], in0=o3[:, hh, 0:64], scalar=smb[0:nq, 12 + hh:13 + hh],
                        in1=acc[0:nq, hh * 64:(hh + 1) * 64], op0=ALU.mult, op1=ALU.add), r=[pskey(5), "smb", ak], w=[ak])

            def ktile(ktbuf, kkey, vbuf, vkey, T, nk=128, masks=()):
                return {"kt": ktbuf[0:72, T * 128:T * 128 + nk], "kkey": kkey, "v": vbuf[0:nk, T, :], "vkey": vkey, "nk": nk,
                        "masks": list(masks)}

            msel = lambda T: (EE[:, T, :], Msel4[:, 0:ncol], "EE", mk4)
            if sample:
                tri = (ident_b[0:8, 0:8], TRIs[0:8, 0:ncol], "ident_b", "TRIs")
                tri2 = (ident_b[:, :], TRI2s[:, 0:ncol], "ident_b", "TRI2s")
                sel_tiles = [ktile(KsT, "KsT", Vs, "Vs", T, masks=[msel(T)]) for T in range(64)]
                sel_tiles.append(ktile(KsT, "KsT", Vs, "Vs", 64, nk=8, masks=[tri]))
                win_tiles = [ktile(KwT, "KwT", Vw, "Vw", 0, masks=[tri2])] + [ktile(KwT, "KwT", Vw, "Vw", w) for w in range(1, 4)]
                win_tiles.append(ktile(KwT, "KwT", Vw, "Vw", 4, nk=8, masks=[tri]))
            else:
                tri = (ident_b[:, :], TRIp[:, 0:ncol], "ident_b", "TRIp")
                tri2 = (ident_b[:, :], TRI2p[:, 0:ncol], "ident_b", "TRI2p")
                sel_tiles = [ktile(KsT, "KsT", Vs, "Vs", T, masks=[msel(T)]) for T in range(48)]
                for j2 in range(jl + 1):
                    ms = [msel(48 + j2)] + ([tri] if j2 == jl else [])
                    sel_tiles.append(ktile(KsT, "KsT", Vs, "Vs", 48 + j2, masks=ms))
                win_tiles = []
                for w in range(jl, jl + 5):
                    ms = [tri2] if w == jl else ([tri] if w == jl + 4 else [])
                    win_tiles.append(ktile(KwT, "KwT", Vw, "Vw", w, masks=ms))
            attend(sel_tiles, 1, 3)
            yield "sel"
            attend(win_tiles, 2, 4)
            ogk = "OGt%d" % i2
            P.op("dve", lambda v: v.tensor_tensor(out=OGt[i2][0:nq, :], in0=acc[0:nq, :], in1=ZSg[i2][0:nq, :], op=ALU.mult),
                 r=[ak, zk], w=[ogk])
            P.dma("sp", OGscr[tok0:tok0 + nq, g * 256:(g + 1) * 256], OGt[i2][0:nq, :], r=[ogk], w=[("OGscr", tok0, g)], semkey=ogk)
            yield "done"

        def run_tiles(specs):
            gens = [nsa_tile(*sp) for sp in specs]
            n = len(gens)
            if n == 0:
                return
            next(gens[0])
            next(gens[0])
            for i in range(n):
                if i + 1 < n:
                    next(gens[i + 1])
                next(gens[i])
                if i + 1 < n:
                    next(gens[i + 1])
                next(gens[i])

        if stage >= 2:
            P.dma("sp", KsT[64:72, 0:8192], t_kaug_sel[:, :], w=["KsT"])
            P.dma("sp", KwT[64:72, 0:2560], t_kaug_win[:, :], w=["KwT"])
            P.dma("sp", KcT[64:72, :], t_kaug_cmp[:, :], w=["KcT"])
            for g in range(4):
                for T8 in range(6):
                    i = cnt2["rb"] % 2
                    cnt2["rb"] += 1
                    rbk = "RB%d" % i
                    P.dma("pool", RB[i][:, :, :], o_sel_p[T8 * 1024:(T8 + 1) * 1024, g * 128:(g + 1) * 128].rearrange("(t p) c -> p t c", p=128),
                          w=[rbk])
                    for t_ in range(8):
                        P.op("pe", lambda t, t_=t_, i=i: t.transpose(psb[7][:, :].bitcast(BF16)[0:64, t_ * 128:(t_ + 1) * 128],
                                                                     RB[i][:, t_, 0:64], ident_b[:, :]), r=[rbk, "ident_b"], w=[pskey(7)])
                    P.op("dve", lambda v, T8=T8: v.tensor_copy(out=KsT[0:64, T8 * 1024:(T8 + 1) * 1024],
                                                               in_=psb[7][:, :].bitcast(BF16)[0:64, 0:1024]), r=[pskey(7)], w=["KsT"])
                    P.op("pool", lambda g_, T8=T8, i=i: g_.tensor_copy(out=Vs[:, T8 * 8:(T8 + 1) * 8, 0:64], in_=RB[i][:, :, 64:128]),
                         r=[rbk], w=["Vs"])
                for j2 in range(16):
                    rbf, rk = load_rows_gather(o_sel_p[:, :], idxo2[:, j2:j2 + 1], "idxo2")
                    prep_from_rbf(rbf, rk, g, 128, KsT[0:64, (48 + j2) * 128:(49 + j2) * 128], "KsT", Vs[:, 48 + j2, 0:64], "Vs")
                for w in range(20):
                    rbf, rk = load_rows_gather(winscr[:, :], idxw[:, w:w + 1], "idxw")
                    prep_from_rbf(rbf, rk, g, 128, KwT[0:64, w * 128:(w + 1) * 128], "KwT", Vw[:, w, 0:64], "Vw")
                rbf, rk = load_rows_gather(cmpscr_p[:, :], idxs[:, 0:1], "idxs")
                prep_from_rbf(rbf, rk, g, 128, KcT[0:64, :], "KcT", Vc[:, :], "Vc")
                run_tiles([(g, tok0, nq, jl, None, 16) for (tok0, nq, jl, sb_) in qtiles if jl is not None])
        if do_sample:
            P.dma("sp", KsT[64:72, 0:8192], t_kaug_sel_s[:, :], w=["KsT"])
            P.dma("sp", KsT[64:72, 8192:8320], t_kaug_new_s[:, :], w=["KsT"])
            P.dma("sp", KwT[64:72, 0:512], t_kaug_win_s[:, :], w=["KwT"])
            P.dma("sp", KwT[64:72, 512:640], t_kaug_new_s[:, :], w=["KwT"])
            P.dma("sp", KcT[64:72, :], t_kaug_cmp_s[:, :], w=["KcT"])
            for (tok0, nq, jl, sb_) in qtiles:
                if jl is not None:
                    continue
                b = sb_
                for g in range(4):
                    for pg in range(64):
                        rbf, rk = load_rows_gather(csel[:, :], idxpg[:, b * 64 + pg:b * 64 + pg + 1], "idxpg")
                        prep_from_rbf(rbf, rk, g, 128, KsT[0:64, pg * 128:(pg + 1) * 128], "KsT", Vs[:, pg, 0:64], "Vs")
                    rbf, rk = load_rows_plain(o_sel_s[b * 8:(b + 1) * 8, :], 8)
                    prep_from_rbf(rbf, rk, g, 8, KsT[0:64, 8192:8200], "KsT", Vs[0:8, 64, 0:64], "Vs")
                    for w in range(4):
                        rbf, rk = load_rows_plain(swin[b * 512 + w * 128:b * 512 + (w + 1) * 128, :], 128)
                        prep_from_rbf(rbf, rk, g, 128, KwT[0:64, w * 128:(w + 1) * 128], "KwT", Vw[:, w, 0:64], "Vw")
                    rbf, rk = load_rows_plain(o_win_s[b * 512 + 504:b * 512 + 512, :], 8)
                    prep_from_rbf(rbf, rk, g, 8, KwT[0:64, 512:520], "KwT", Vw[0:8, 4, 0:64], "Vw")
                    rbf, rk = load_rows_plain(cmpscr_s[b * 128:(b + 1) * 128, :], 128)
                    prep_from_rbf(rbf, rk, g, 128, KcT[0:64, :], "KcT", Vc[:, :], "Vc")
                    run_tiles([(g, tok0, nq, None, b, 15)])

    with P.scope():
        W_outb = P.sb("W_outb", [128, NCH, D], BF16)
        for k in range(NCH):
            P.dma("pool", W_outb[:, k, :], w_out_b[k * 128:(k + 1) * 128, :], w=["W_outb"])
        lnG1 = P.sb("lnG1", [128, 1, D], F32)
        lnB1 = P.sb("lnB1", [128, 1, D], F32)
        P.dma("sp", lnG1[:, 0, :], ln_g[1:2, :].partition_broadcast(128), w=["lnG1"])
        P.dma("sp", lnB1[:, 0, :], ln_b[1:2, :].partition_broadcast(128), w=["lnB1"])
        Gp1 = P.sb("Gp1", [128, D], F32)
        Gs1 = P.sb("Gs1", [8, DEC_B, D], F32)
        P.dma("sp", Gp1[:, :], modscr[1, 0:1, 2 * D:3 * D].partition_broadcast(128), w=["Gp1"])
        for b in range(DEC_B):
            P.dma("sp", Gs1[0:8, b, :], modscr[1, 1 + b:2 + b, 2 * D:3 * D].partition_broadcast(8), w=["Gs1"])
        P.op("pool", lambda g_: g_.tensor_scalar_add(out=Gp1[:], in0=Gp1[:], scalar1=1.0), r=["Gp1"], w=["Gp1"])
        P.op("pool", lambda g_: g_.tensor_scalar_add(out=Gs1[:], in0=Gs1[:], scalar1=1.0), r=["Gs1"], w=["Gs1"])
        idxo3 = P.sb("idxo3", [128, 16], I32)
        P.dma("sp", idxo3[:], t_idx_own[:, :], w=["idxo3"])
        X1o = [P.sb("X1o%d" % i, [128, D], F32) for i in range(2)]
        OGl = [P.sb("OGl%d" % i, [128, D], BF16) for i in range(2)]
        OGT = P.sb("OGT", [128, NCH, 128], BF16)
        vo = [P.sb("vo%d" % i, [128, D], F32) for i in range(2)]
        yo = [P.sb("yo%d" % i, [128, D], F32) for i in range(2)]
        sto = [P.sb("sto%d" % i, [128, 16], F32) for i in range(2)]
        qi = 0
        for (tok0, nq, jl, sb_) in qtiles:
            i2 = qi % 2
            qi += 1
            xk, ok_, vk, yk, sk = "X1o%d" % i2, "OGl%d" % i2, "vo%d" % i2, "yo%d" % i2, "sto%d" % i2
            if jl is not None:
                P.gather(X1o[i2][:, :], x1scr[:, :], idxo3[:, jl:jl + 1], r=["idxo3"], w=[xk])
                Gt, gkey = Gp1[0:nq, :], "Gp1"
            else:
                P.dma("sp", X1o[i2][0:nq, :], x1s_scr[sb_ * 8:(sb_ + 1) * 8, :], w=[xk])
                Gt, gkey = Gs1[0:8, sb_, :], "Gs1"
            P.dma("sp", OGl[i2][0:nq, :], OGscr[tok0:tok0 + nq, :], w=[ok_])
            for k in range(NCH):
                P.op("pe", lambda t, k=k, nq=nq, i2=i2: t.transpose(psb[0][:, :].bitcast(BF16)[:, k * 128:k * 128 + nq],
                                                                   OGl[i2][0:nq, k * 128:(k + 1) * 128], ident_b[0:nq, 0:nq]),
                     r=[ok_, "ident_b"], w=[pskey(0)])
            P.op("dve", lambda v, nq=nq: v.tensor_copy(
                out=OGT[:, :, 0:nq], in_=psb[0][:, :].bitcast(BF16)[:, 0:1024].rearrange("p (k t) -> p k t", t=128)[:, :, 0:nq]),
                r=[pskey(0)], w=["OGT"])
            for half in range(2):
                pb = 2 + half
                for k in range(NCH):
                    P.op("pe", lambda t, k=k, half=half, pb=pb, nq=nq: t.matmul(
                        psb[pb][0:nq, :], lhsT=OGT[:, k, 0:nq], rhs=W_outb[:, k, half * 512:(half + 1) * 512],
                        start=(k == 0), stop=(k == NCH - 1)), r=["OGT", "W_outb"], w=[pskey(pb)])
                if jl is None and sb_ > 0:
                    pass
                P.op("dve", lambda v, half=half, pb=pb, nq=nq, i2=i2, Gt=Gt: v.tensor_tensor(
                    out=vo[i2][0:nq, half * 512:(half + 1) * 512], in0=psb[pb][0:nq, :], in1=Gt[:, half * 512:(half + 1) * 512],
                    op=ALU.mult), r=[pskey(pb), gkey], w=[vk])
            P.op("dve", lambda v, nq=nq, i2=i2: v.scalar_tensor_tensor(
                out=vo[i2][0:nq, :], in0=X1o[i2][0:nq, :], scalar=ALPHA, in1=vo[i2][0:nq, :], op0=ALU.mult, op1=ALU.add),
                r=[xk, vk], w=[vk])
            layernorm_tm(vo[i2], vk, yo[i2], yk, nq, 0, sto[i2], sk, lnG1, "lnG1", lnB1, "lnB1")
            if jl is not None:
                P.dma("sp", y_p[tok0:tok0 + nq, :], yo[i2][0:nq, :], r=[yk], w=[("y_p", tok0)], semkey=yk)
            else:
                P.dma("sp", y_s[sb_ * 8:(sb_ + 1) * 8, :], yo[i2][0:nq, :], r=[yk], w=[("y_s", sb_)], semkey=yk)

    for b in range(DEC_B):
        P.dma("act", o_win_s[b * 512:b * 512 + 504, :], swin[b * 512 + 8:(b + 1) * 512, :], w=[("o_win_s", b, 0)],
              semkey="winscopy")

    P.finish()
    print("instructions:", P.n_ins, {e: P.cnt[e] for e in P.ENG}, "dma sems:", len(P.dsem))
    return P


def _bf(x):
    return np.asarray(x, np.float32).astype(ml_dtypes.bfloat16)


def _split_pos(pos):
    pos = np.asarray(pos, np.int64)
    a = np.floor_divide(pos, 64)
    b = pos - 64 * a
    return a.astype(np.float32), b.astype(np.float32)


def _kaug(pos, valid):
    a, b = _split_pos(pos)
    n = a.shape[0]
    out = np.zeros((8, n), np.float32)
    out[0] = a; out[1] = a; out[2] = b; out[3] = b; out[4] = 1.0
    out[5] = np.where(valid, 0.0, NEGM)
    return _bf(out)


def _slopes_hi_lo():
    s = (2.0 ** (-8.0 * np.arange(1, 17) / 16.0)).astype(np.float32)
    hi = s.astype(ml_dtypes.bfloat16).astype(np.float32)
    lo = (s - hi).astype(ml_dtypes.bfloat16).astype(np.float32)
    return s, hi, lo


def _qaug(tq):
    s, hi, lo = _slopes_hi_lo()
    tq = np.asarray(tq, np.float32)
    nq = tq.shape[0]
    out = np.zeros((8, 16, nq), np.float32)
    out[0] = (64.0 * hi)[:, None]; out[1] = (64.0 * lo)[:, None]
    out[2] = hi[:, None]; out[3] = lo[:, None]
    out[4] = -(s[:, None] * tq[None, :])
    out[5] = 1.0
    return _bf(out)


def prompt_tables(k):
    cs = 2048 * k
    p = np.arange(128)
    t = {}
    t["idx_own"] = (cs + 128 * np.arange(16)[None, :] + p[:, None]).astype(np.int32)
    pos_pref = (np.arange(48 * 128) - cs)
    valid_pref = np.repeat(128 * np.arange(48) < cs, 128)
    pos_own = np.arange(2048)
    t["kaug_sel"] = np.concatenate([_kaug(pos_pref, valid_pref), _kaug(pos_own, np.ones(2048, bool))], axis=1)
    wtok = cs - 512 + np.arange(20 * 128)
    t["idx_win"] = np.maximum(wtok, 0).reshape(20, 128).T.astype(np.int32).copy()
    t["kaug_win"] = _kaug(wtok - cs, wtok >= 0)
    blk = np.concatenate([np.arange(96), 32 * k + np.arange(32)])
    valid = np.concatenate([np.arange(96) < 32 * k, np.ones(32, bool)])
    t["idx_slot"] = blk.astype(np.int32).reshape(128, 1)
    cend = 64 * blk + 63 - cs
    t["kaug_cmp"] = _kaug(cend, valid)
    tq = np.arange(2048)
    tabs = np.arange(2048) + cs
    cb = tabs // 64
    blk_abs = np.where(valid, blk, 10 ** 6)
    forced = (blk_abs[None, :] == 0) | (blk_abs[None, :] == cb[:, None]) | (blk_abs[None, :] == cb[:, None] - 1)
    caus = blk_abs[None, :] <= cb[:, None]
    fbn = np.where(forced, FORCEDV, 0.0) - np.where(caus, 0.0, 1.0)
    t["fbn"] = fbn.astype(np.float32).reshape(16, 128, 128)
    t["caus"] = caus.astype(np.float32).reshape(16, 128, 128)
    cend_abs = 64 * blk + 63
    tm = np.where(cend_abs[None, :] <= tabs[:, None], 0.0, NEGM)
    t["tm"] = tm.astype(np.float32).reshape(16, 128, 128)
    return t


def static_tables():
    t = {}
    t["qaug_p"] = _qaug(np.arange(2048)).reshape(8, 16 * 2048)
    j = np.arange(128)[:, None]
    i = np.arange(128)[None, :]
    tri = np.where(j > i, NEGM, 0.0)
    tri2 = np.where(j < i, NEGM, 0.0)
    t["tri_p"] = _bf(np.tile(tri, (1, 4)))
    t["tri2_p"] = _bf(np.tile(tri2, (1, 4)))
    i8 = np.arange(8)[None, :]
    t["tri_s"] = _bf(np.tile(np.where(j > i8, NEGM, 0.0), (1, 4)))
    t["tri2_s"] = _bf(np.tile(np.where(j < i8, NEGM, 0.0), (1, 4)))
    t["qaug_s"] = _qaug(np.arange(8)).reshape(8, 16 * 8)
    t["kaug_sel_s"] = _kaug(np.arange(8192) - 8192, np.ones(8192, bool))
    t["kaug_win_s"] = _kaug(np.arange(512) - 512, np.ones(512, bool))
    t["kaug_new_s"] = _kaug(np.arange(128), np.arange(128) < 8)
    cend = 64 * np.arange(128) + 63 - 8192
    t["kaug_cmp_s"] = _kaug(cend, np.ones(128, bool))
    fb = np.zeros((8, 128), np.float32)
    fb[:, 0] = FORCEDV; fb[:, 127] = FORCEDV
    t["fbn_s"] = fb
    t["caus_s"] = np.ones((8, 128), np.float32)
    t["tm_s"] = np.zeros((8, 128), np.float32)
    return t


def core_inputs(inp, c):
    b = c // 4
    sb = slice(4 * c, 4 * c + 4)
    f = np.ascontiguousarray
    d = {
        "xf": f(inp["x_prompt"][b]),
        "xs": f(inp["x_sample"][sb].reshape(NS_TOK, D)),
        "cvec": f(np.concatenate([inp["c_prompt"][b:b + 1], inp["c_sample"][sb]], axis=0)),
        "sh0": f(inp["state_h"][0, sb]),
        "sc0": f(inp["state_conv"][0, sb].reshape(DEC_B * 3, D)),
        "swin": f(inp["state_win"][sb].reshape(DEC_B * 512, 512)),
        "ptab": f(inp["page_table"][sb]).astype(np.int32),
        "ccmp": inp["cache_cmp"].reshape(-1, 512),
        "csel": inp["cache_sel"].reshape(-1, 512),
        "w_ada": inp["w_ada"], "b_ada": inp["b_ada"], "ln_g": inp["ln_g"], "ln_b": inp["ln_b"],
        "w_in_a": inp["w_in_a"][0], "conv_w": inp["conv_w_a"][0], "conv_b": inp["conv_b_a"],
        "w_r": inp["w_r_a"][0], "b_r": inp["b_r_a"], "w_i": inp["w_i_a"][0], "b_i": inp["b_i_a"],
        "lam": inp["lam_a"], "w_out_a": inp["w_out_a"][0], "w_kv": inp["w_kv"],
        "phi_pe": inp["phi_pe"].reshape(64, 128), "w_phi1": inp["w_phi1"], "b_phi1": inp["b_phi1"],
        "w_phi2": inp["w_phi2"], "b_phi2": inp["b_phi2"], "w_in_b": inp["w_in_b"][0],
        "b_gate": inp["b_gate_b"], "w_out_b": inp["w_out_b"][0],
    }
    for k2, v in prompt_tables(c % 4).items():
        d["t_" + k2] = v
    for k2, v in static_tables().items():
        d["t_" + k2] = v
    return {k: np.asarray(v) for k, v in d.items()}


def assemble(results, cores):
    y_prompt = np.zeros((2, SEQ, D), np.float32)
    y_sample = np.zeros((32, DEC_S, D), np.float32)
    new_cmp_p = np.zeros((2, SEQ, 4, 2, 64), np.float32)
    new_sel_p = np.zeros((2, SEQ, 4, 2, 64), np.float32)
    new_win_p = np.zeros((2, 512, 4, 2, 64), np.float32)
    new_h_p = np.zeros((1, 2, D), np.float32)
    new_conv_p = np.zeros((1, 2, 3, D), np.float32)
    new_cmp_s = np.zeros((32, DEC_S, 4, 2, 64), np.float32)
    new_sel_s = np.zeros((32, DEC_S, 4, 2, 64), np.float32)
    new_win_s = np.zeros((32, 512, 4, 2, 64), np.float32)
    new_h_s = np.zeros((1, 32, D), np.float32)
    new_conv_s = np.zeros((1, 32, 3, D), np.float32)
    for r, c in zip(results, cores):
        b, k = c // 4, c % 4
        sb = slice(4 * c, 4 * c + 4)
        y_prompt[b, k * 2048:(k + 1) * 2048] = r["y_p"]
        y_sample[sb] = r["y_s"].reshape(DEC_B, DEC_S, D)
        if k == 0:
            new_cmp_p[b] = r["o_cmp_p"].reshape(SEQ, 4, 2, 64)
            new_sel_p[b] = r["o_sel_p"].reshape(SEQ, 4, 2, 64)
            new_win_p[b] = r["o_win_p"].reshape(512, 4, 2, 64)
            new_h_p[0, b] = r["o_h_p"][0]
            new_conv_p[0, b] = r["o_conv_p"]
        new_cmp_s[sb] = r["o_cmp_s"].reshape(DEC_B, DEC_S, 4, 2, 64)
        new_sel_s[sb] = r["o_sel_s"].reshape(DEC_B, DEC_S, 4, 2, 64)
        new_win_s[sb] = r["o_win_s"].reshape(DEC_B, 512, 4, 2, 64)
        new_h_s[0, sb] = r["o_h_s"]
        new_conv_s[0, sb] = r["o_conv_s"].reshape(DEC_B, 3, D)
    return (y_prompt, y_sample, new_cmp_p, new_sel_p, new_win_p, new_h_p, new_conv_p,
            new_cmp_s, new_sel_s, new_win_s, new_h_s, new_conv_s)


def kernel(**inputs):
    inp = {k: np.asarray(v) for k, v in inputs.items()}
    n_phys = inp["cache_cmp"].shape[0]
    P = build(n_phys)
    cores = list(range(8))
    in_maps = [core_inputs(inp, c) for c in cores]
    res = run_bass_kernel_spmd(P.nc, in_maps, core_ids=cores)
    return assemble(res.results, cores)
```

```python
import contextlib
import numpy as np
import ml_dtypes
import concourse.bass as bass
import concourse.mybir as mybir
from concourse.bass_utils import run_bass_kernel_spmd

F32 = mybir.dt.float32
BF16 = mybir.dt.bfloat16
I32 = mybir.dt.int32
AF = mybir.ActivationFunctionType
ALU = mybir.AluOpType
AX = mybir.AxisListType

D = 1024
NCH = 8
SEQ = 8192
TT = 256
NSUB = TT // 128
DEC_B = 4
DEC_S = 8
NS_TOK = DEC_B * DEC_S
ALPHA = 4.0 ** 0.25
LN_EPS = 1e-5
RG_C = 8.0
NEGM = -30000.0
FORCEDV = 1.0e6


class Prog:
    ENG = ("pe", "act", "dve", "pool", "sp")

    def __init__(self):
        self.nc = bass.Bass("TRN2", target_bir_lowering=False)
        self.es = contextlib.ExitStack()
        nc = self.nc
        self.eng = {"pe": nc.tensor, "act": nc.scalar, "dve": nc.vector, "pool": nc.gpsimd, "sp": nc.sync}
        self.sem = {e: self.es.enter_context(nc.semaphore("s_" + e)) for e in self.ENG}
        self.cnt = {e: 0 for e in self.ENG}
        self.seen = {e: {} for e in self.ENG}
        self.dsem = {}
        self.dcnt = {}
        self.bufs = {}
        self.n_ins = 0
        self.stack = [self.es]

    def sb(self, name, shape, dt):
        return self.stack[-1].enter_context(self.nc.sbuf_tensor(name, list(shape), dt))

    def barrier(self):
        deps = {}
        for e2 in self.ENG:
            if self.cnt[e2]:
                deps[("eng", e2)] = self.cnt[e2]
        for k in self.dcnt:
            deps[("dma", k)] = self.dcnt[k]
        for e in self.ENG:
            self._wait(e, dict(deps))

    @contextlib.contextmanager
    def scope(self):
        st = contextlib.ExitStack()
        self.stack.append(st)
        try:
            yield
        finally:
            self.barrier()
            self.stack.pop()
            st.close()

    def ps(self, name, shape, dt):
        return self.es.enter_context(self.nc.psum_tensor(name, list(shape), dt))

    def dram(self, name, shape, dt, kind="Internal"):
        return self.nc.dram_tensor(name, list(shape), dt, kind=kind).ap()

    def _state(self, k):
        st = self.bufs.get(k)
        if st is None:
            st = self.bufs[k] = {"w": {}, "r": {}}
        return st

    def _deps(self, r, w):
        deps = {}
        for k in r:
            for s, v in self._state(k)["w"].items():
                deps[s] = max(deps.get(s, 0), v)
        for k in w:
            st = self._state(k)
            for s, v in st["w"].items():
                deps[s] = max(deps.get(s, 0), v)
            for s, v in st["r"].items():
                deps[s] = max(deps.get(s, 0), v)
        return deps

    def _wait(self, e, deps):
        eng = self.eng[e]
        seen = self.seen[e]
        for s, v in deps.items():
            if s[0] == "dma":
                v = max(v, self.dcnt[s[1]])
                if seen.get(s, 0) >= v:
                    continue
                eng.wait_ge(self.dsem[s[1]], v)
            else:
                if s[1] == e and False:
                    continue
                if seen.get(s, 0) >= v:
                    continue
                eng.wait_ge(self.sem[s[1]], v)
            seen[s] = v

    def _commit(self, me_src, me_val, r, w):
        for k in w:
            st = self._state(k)
            st["w"] = {me_src: me_val}
            st["r"] = {}
        for k in r:
            if k in w:
                continue
            st = self._state(k)
            st["r"][me_src] = max(st["r"].get(me_src, 0), me_val)

    def op(self, e, fn, r=(), w=()):
        w = list(w) + [k for k in r if isinstance(k, str) and k[:2] == "ps" and k[2:].isdigit() and k not in w]
        self._wait(e, self._deps(r, w))
        ins = fn(self.eng[e])
        self.cnt[e] += 1
        ins.then_inc(self.sem[e], 1)
        self._commit(("eng", e), self.cnt[e], r, w)
        self.n_ins += 1
        return ins

    def dma(self, q, out, in_, r=(), w=(), semkey=None, **kw):
        self._wait(q, self._deps(r, w))
        if semkey is None:
            semkey = (tuple(w) + tuple(r))[0]
        if semkey not in self.dsem:
            self.dsem[semkey] = self.es.enter_context(self.nc.semaphore("d%d" % len(self.dsem)))
            self.dcnt[semkey] = 0
        ins = self.eng[q].dma_start(out=out, in_=in_, **kw)
        self.dcnt[semkey] += 16
        ins.then_inc(self.dsem[semkey], 16)
        self._commit(("dma", semkey), self.dcnt[semkey], r, w)
        self.n_ins += 1
        return ins

    def gather(self, out, in_, idx_ap, r=(), w=(), semkey=None):
        q = "pool"
        self._wait(q, self._deps(r, w))
        if semkey is None:
            semkey = tuple(w)[0]
        if semkey not in self.dsem:
            self.dsem[semkey] = self.es.enter_context(self.nc.semaphore("d%d" % len(self.dsem)))
            self.dcnt[semkey] = 0
        ins = self.nc.gpsimd.indirect_dma_start(
            out=out, out_offset=None, in_=in_, in_offset=bass.IndirectOffsetOnAxis(ap=idx_ap, axis=0))
        self.dcnt[semkey] += 16
        ins.then_inc(self.dsem[semkey], 16)
        self._commit(("dma", semkey), self.dcnt[semkey], r, w)
        self.n_ins += 1
        return ins

    def finish(self):
        for e in ("sp",):
            deps = {}
            for k, st in self.bufs.items():
                for s, v in list(st["w"].items()) + list(st["r"].items()):
                    deps[s] = max(deps.get(s, 0), v)
            for k in self.dcnt:
                deps[("dma", k)] = self.dcnt[k]
            for e2 in self.ENG:
                if self.cnt[e2]:
                    deps[("eng", e2)] = self.cnt[e2]
            self._wait(e, deps)


def build(n_phys, stage=9):
    P = Prog()
    nc = P.nc
    es = P.es
    ctx_nc = nc.allow_non_contiguous_dma(reason="small strided parameter / state loads")
    es.enter_context(ctx_nc)

    def din(name, shape, dt=F32):
        return nc.dram_tensor(name, list(shape), dt, kind="ExternalInput").ap()

    def dout(name, shape, dt=F32):
        return nc.dram_tensor(name, list(shape), dt, kind="ExternalOutput").ap()

    xf = din("xf", [SEQ, D])
    xs = din("xs", [NS_TOK, D])
    cvec = din("cvec", [5, D])
    sh0 = din("sh0", [DEC_B, D])
    sc0 = din("sc0", [DEC_B * 3, D])
    swin = din("swin", [DEC_B * 512, 512])
    ptab = din("ptab", [DEC_B, 64], I32)
    ccmp = din("ccmp", [n_phys * 128, 512])
    csel = din("csel", [n_phys * 128, 512])
    w_ada = din("w_ada", [2, D, 3 * D])
    b_ada = din("b_ada", [2, 3 * D])
    ln_g = din("ln_g", [2, D])
    ln_b = din("ln_b", [2, D])
    w_in_a = din("w_in_a", [D, 2 * D])
    conv_w = din("conv_w", [4, D])
    conv_b = din("conv_b", [1, D])
    w_r = din("w_r", [8, 128, 128])
    b_r = din("b_r", [1, D])
    w_i = din("w_i", [8, 128, 128])
    b_i = din("b_i", [1, D])
    lam = din("lam", [1, D])
    w_out_a = din("w_out_a", [D, D])
    w_kv = din("w_kv", [D, 1536])
    phi_pe = din("phi_pe", [64, 128])
    w_phi1 = din("w_phi1", [2, 64, 64, 128])
    b_phi1 = din("b_phi1", [2, 128])
    w_phi2 = din("w_phi2", [2, 128, 64])
    b_phi2 = din("b_phi2", [2, 64])
    w_in_b = din("w_in_b", [D, 2096])
    b_gate = din("b_gate", [1, 48])
    w_out_b = din("w_out_b", [D, D])

    def dtab(name, shape, dt):
        return nc.dram_tensor(name, list(shape), dt, kind="ExternalInput").ap()
    t_idx_own = dtab("t_idx_own", [128, 16], I32)
    t_idx_win = dtab("t_idx_win", [128, 20], I32)
    t_idx_slot = dtab("t_idx_slot", [128, 1], I32)
    t_kaug_sel = dtab("t_kaug_sel", [8, 8192], BF16)
    t_kaug_win = dtab("t_kaug_win", [8, 2560], BF16)
    t_kaug_cmp = dtab("t_kaug_cmp", [8, 128], BF16)
    t_fbn = dtab("t_fbn", [16, 128, 128], F32)
    t_caus = dtab("t_caus", [16, 128, 128], F32)
    t_tm = dtab("t_tm", [16, 128, 128], F32)
    t_qaug_p = dtab("t_qaug_p", [8, 16 * 2048], BF16)
    t_tri_p = dtab("t_tri_p", [128, 512], BF16)
    t_tri2_p = dtab("t_tri2_p", [128, 512], BF16)
    t_tri_s = dtab("t_tri_s", [128, 32], BF16)
    t_tri2_s = dtab("t_tri2_s", [128, 32], BF16)
    t_qaug_s = dtab("t_qaug_s", [8, 128], BF16)
    t_kaug_sel_s = dtab("t_kaug_sel_s", [8, 8192], BF16)
    t_kaug_win_s = dtab("t_kaug_win_s", [8, 512], BF16)
    t_kaug_new_s = dtab("t_kaug_new_s", [8, 128], BF16)
    t_kaug_cmp_s = dtab("t_kaug_cmp_s", [8, 128], BF16)
    t_fbn_s = dtab("t_fbn_s", [8, 128], F32)
    t_caus_s = dtab("t_caus_s", [8, 128], F32)
    t_tm_s = dtab("t_tm_s", [8, 128], F32)

    y_p = dout("y_p", [2048, D])
    y_s = dout("y_s", [NS_TOK, D])
    o_cmp_p = dout("o_cmp_p", [SEQ, 512])
    o_sel_p = dout("o_sel_p", [SEQ, 512])
    o_win_p = dout("o_win_p", [512, 512])
    o_h_p = dout("o_h_p", [1, D])
    o_conv_p = dout("o_conv_p", [3, D])
    o_cmp_s = dout("o_cmp_s", [NS_TOK, 512])
    o_sel_s = dout("o_sel_s", [NS_TOK, 512])
    o_win_s = dout("o_win_s", [DEC_B * 512, 512])
    o_h_s = dout("o_h_s", [DEC_B, D])
    o_conv_s = dout("o_conv_s", [DEC_B * 3, D])

    modscr = P.dram("modscr", [2, 5, 3 * D], F32)
    x1scr = P.dram("x1scr", [SEQ, D], F32)
    x1s_scr = P.dram("x1s_scr", [NS_TOK, D], F32)
    winscr = P.dram("winscr", [SEQ, 512], F32)

    ident_b = P.sb("ident_b", [128, 128], BF16)
    ident_f = P.sb("ident_f", [128, 128], F32)
    for t, k in ((ident_b, "ident_b"), (ident_f, "ident_f")):
        P.op("pool", lambda g, t=t: g.memset(t[:], 0.0), w=[k])
        P.op("pool", lambda g, t=t: g.affine_select(out=t[:], in_=t[:], pattern=[[-1, 128]],
                                                    compare_op=ALU.not_equal, fill=1.0, base=0,
                                                    channel_multiplier=1), r=[k], w=[k])

    psb = [P.ps("ps%d" % i, [128, 512], F32) for i in range(8)]

    def pskey(i):
        return "ps%d" % i

    mod_fm = P.sb("mod_fm", [128, 2, 24, 8], F32)
    scA = P.scope()
    scA.__enter__()
    W_in = P.sb("W_in", [128, NCH, 2 * D], BF16)
    W_out = P.sb("W_out", [128, NCH, D], BF16)
    W_kv = P.sb("W_kv", [128, NCH, 1536], BF16)
    W_r = P.sb("W_r", [128, 8, 128], BF16)
    W_i = P.sb("W_i", [128, 8, 128], BF16)
    for k in range(NCH):
        P.dma("pool", W_in[:, k, :], w_in_a[k * 128:(k + 1) * 128, :], w=["W_in"])
    for k in range(NCH):
        P.dma("pool", W_out[:, k, :], w_out_a[k * 128:(k + 1) * 128, :], w=["W_out"])
    for k in range(NCH):
        P.dma("pool", W_kv[:, k, :], w_kv[k * 128:(k + 1) * 128, :], w=["W_kv"])
    P.dma("pool", W_r[:], w_r.rearrange("n c d -> c n d"), w=["W_r"])
    P.dma("pool", W_i[:], w_i.rearrange("n c d -> c n d"), w=["W_i"])

    pf = P.sb("pf", [128, 10, NCH], F32)
    for k in range(4):
        P.dma("sp", pf[:, k, :], conv_w[k:k + 1, :].rearrange("o (c p) -> p (o c)", p=128), w=["pf"])
    for j, src in ((4, conv_b), (5, b_r), (6, b_i), (7, lam)):
        P.dma("sp", pf[:, j, :], src[0:1, :].rearrange("o (c p) -> p (o c)", p=128), w=["pf"])
    P.op("act", lambda a: a.activation(out=pf[:, 9, :], in_=pf[:, 7, :], func=AF.Exp, scale=-1.0), r=["pf"], w=["pf"])
    P.op("act", lambda a: a.activation(out=pf[:, 9, :], in_=pf[:, 9, :], func=AF.Ln, bias=1.0), r=["pf"], w=["pf"])
    P.op("dve", lambda v: v.tensor_scalar_mul(out=pf[:, 7, :], in0=pf[:, 9, :], scalar1=-RG_C), r=["pf"], w=["pf"])
    P.op("dve", lambda v: v.tensor_scalar_mul(out=pf[:, 8, :], in0=pf[:, 9, :], scalar1=-2.0 * RG_C), r=["pf"], w=["pf"])

    lnG = P.sb("lnG", [128, 1, D], F32)
    lnB = P.sb("lnB", [128, 1, D], F32)
    for l in range(1):
        P.dma("sp", lnG[:, l, :], ln_g[l:l + 1, :].partition_broadcast(128), w=["lnG"])
        P.dma("sp", lnB[:, l, :], ln_b[l:l + 1, :].partition_broadcast(128), w=["lnB"])

    vt = [P.sb("vt%d" % i, [128, D], F32) for i in range(2)]
    c5, c5s = vt[0], vt[1]
    csT = P.sb("csT", [128, NCH, 8], BF16)
    P.dma("sp", c5[0:5, :], cvec[:, :], w=["vt0"])
    P.op("act", lambda a: a.activation(out=c5s[0:5, :], in_=c5[0:5, :], func=AF.Silu), r=["vt0"], w=["vt1"])
    for k in range(NCH):
        P.op("pe", lambda t, k=k: t.transpose(psb[0][:, k * 8:k * 8 + 5], c5s[0:5, k * 128:(k + 1) * 128],
                                              ident_f[0:5, 0:5]), r=["vt1", "ident_f"], w=[pskey(0)])
    P.op("dve", lambda v: v.tensor_copy(out=csT[:, :, 0:5],
                                        in_=psb[0][:, 0:64].rearrange("p (k e) -> p k e", e=8)[:, :, 0:5]),
         r=[pskey(0)], w=["csT"])
    AW = 256
    NA = 3 * D // AW
    wada_buf = [P.sb("wada%d" % i, [128, NCH, AW], BF16) for i in range(2)]
    modc = [P.sb("modc%d" % i, [5, AW], F32) for i in range(2)]
    badac = [P.sb("badac%d" % i, [5, AW], F32) for i in range(2)]
    it = 0
    for l in range(2):
        for n6 in range(NA):
            wb = wada_buf[it % 2]
            wk = "wada%d" % (it % 2)
            mk = "modc%d" % (it % 2)
            bk_ = "badac%d" % (it % 2)
            mc = modc[it % 2]
            bc = badac[it % 2]
            P.dma("pool", wb[:], w_ada[l, :, n6 * AW:(n6 + 1) * AW].rearrange("(k p) n -> p k n", p=128), w=[wk])
            P.dma("sp", bc[:], b_ada[l:l + 1, n6 * AW:(n6 + 1) * AW].partition_broadcast(5), w=[bk_])
            pb = 2 + (it % 2)
            for k in range(NCH):
                P.op("pe", lambda t, k=k, wb=wb, pb=pb: t.matmul(psb[pb][0:5, 0:AW], lhsT=csT[:, k, 0:5], rhs=wb[:, k, :],
                                                                 start=(k == 0), stop=(k == NCH - 1)),
                     r=["csT", wk], w=[pskey(pb)])
            P.op("dve", lambda v, mc=mc, bc=bc, pb=pb: v.tensor_tensor(
                out=mc[:], in0=psb[pb][0:5, 0:AW], in1=bc[:], op=ALU.add),
                r=[pskey(pb), bk_], w=[mk])
            P.dma("sp", modscr[l, :, n6 * AW:(n6 + 1) * AW], mc[:], r=[mk], w=[("modscr", l, n6)], semkey=mk)
            nq = AW // 128
            for q in range(nq):
                P.op("pe", lambda t, q=q, mc=mc: t.transpose(psb[1][:, q * 8:q * 8 + 5], mc[0:5, q * 128:(q + 1) * 128],
                                                             ident_f[0:5, 0:5]), r=[mk, "ident_f"], w=[pskey(1)])
            P.op("dve", lambda v, l=l, n6=n6, nq=nq: v.tensor_copy(
                out=mod_fm[:, l, n6 * nq:(n6 + 1) * nq, 0:5],
                in_=psb[1][:, 0:8 * nq].rearrange("p (k e) -> p k e", e=8)[:, :, 0:5]),
                r=[pskey(1)], w=["mod_fm"])
            it += 1
    P.op("dve", lambda v: v.tensor_scalar_add(out=mod_fm[:, :, 8:24, :], in0=mod_fm[:, :, 8:24, :], scalar1=1.0),
         r=["mod_fm"], w=["mod_fm"])
    Gp = P.sb("Gp", [128, 1, D], F32)
    Gs = P.sb("Gs", [NS_TOK, 1, D], F32)
    mod_keys = [("modscr", l, n6) for l in range(2) for n6 in range(NA)]
    for l in range(1):
        P.dma("sp", Gp[:, l, :], modscr[l, 0:1, 2 * D:3 * D].partition_broadcast(128), r=mod_keys, w=["Gp"])
        for b in range(DEC_B):
            P.dma("sp", Gs[b * 8:(b + 1) * 8, l, :], modscr[l, 1 + b:2 + b, 2 * D:3 * D].partition_broadcast(8),
                  r=mod_keys, w=["Gs"])
    P.op("pool", lambda g: g.tensor_scalar_add(out=Gp[:], in0=Gp[:], scalar1=1.0), r=["Gp"], w=["Gp"])
    P.op("pool", lambda g: g.tensor_scalar_add(out=Gs[:], in0=Gs[:], scalar1=1.0), r=["Gs"], w=["Gs"])

    xtok = [P.sb("xtok%d" % i, [128, NSUB, D], F32) for i in range(2)]
    xbf = P.sb("xbf", [128, NSUB, D], BF16)
    mT = P.sb("mT", [128, NCH, TT], BF16)
    xbe = P.sb("xbe", [128, NCH, 3 + TT], F32)
    xbe_s = P.sb("xbe_s", [128, NCH, DEC_B, 3 + DEC_S], F32)
    hprev = P.sb("hprev", [128, NCH], F32)
    h0s = P.sb("h0s", [128, NCH, DEC_B], F32)
    hlast_s = P.sb("hlast_s", [128, NCH, DEC_B], F32)
    NT = 2
    xc = [P.sb("xc%d" % i, [128, TT], F32) for i in range(NT)]
    xcb = [P.sb("xcb%d" % i, [128, TT], BF16) for i in range(NT)]
    zs = [P.sb("zs%d" % i, [128, TT], F32) for i in range(NT)]
    ra = [P.sb("ra%d" % i, [128, TT], F32) for i in range(NT)]
    ri = [P.sb("ri%d" % i, [128, TT], F32) for i in range(NT)]
    ga = [P.sb("ga%d" % i, [128, TT], F32) for i in range(NT)]
    bb = [P.sb("bb%d" % i, [128, TT], F32) for i in range(NT)]
    hs = [P.sb("hs%d" % i, [128, TT], F32) for i in range(NT)]
    yg = P.sb("yg", [128, NCH, TT], BF16)
    x1t = [P.sb("x1t%d" % i, [128, D], F32) for i in range(2)]
    x1b = [P.sb("x1b%d" % i, [128, D], BF16) for i in range(2)]
    x1T = P.sb("x1T", [128, NCH, TT], BF16)
    kvst = [P.sb("kvst%d" % i, [128, 1536], F32) for i in range(2)]
    stat = [P.sb("stat%d" % i, [128, 16], F32) for i in range(2)]

    P.op("pool", lambda g: g.memset(xbe[:, :, 0:3], 0.0), w=["xbe"])
    P.op("pool", lambda g: g.memset(hprev[:], 0.0), w=["hprev"])
    for n in range(NCH):
        for b in range(DEC_B):
            P.dma("sp", xbe_s[:, n, b, 0:3], sc0[b * 3:(b + 1) * 3, n * 128:(n + 1) * 128].rearrange("k p -> p k"),
                  w=["xbe_s"])
        P.dma("sp", h0s[:, n, :], sh0.rearrange("b (c p) -> c p b", p=128)[n], w=["h0s"])

    cnt = {"tile": 0, "ch": 0, "sub": 0}

    def layernorm_tm(vin, vkey, out, okey, np_, layer, st, skey, Gt_=None, gk_="lnG", Bt_=None, bk_="lnB"):
        Gt_ = lnG if Gt_ is None else Gt_
        Bt_ = lnB if Bt_ is None else Bt_
        P.op("dve", lambda v: v.bn_stats(out=st[0:np_, 0:6], in_=vin[0:np_, 0:512]), r=[vkey], w=[skey])
        P.op("dve", lambda v: v.bn_stats(out=st[0:np_, 6:12], in_=vin[0:np_, 512:1024]), r=[vkey], w=[skey])
        P.op("dve", lambda v: v.bn_aggr(out=st[0:np_, 12:14], in_=st[0:np_, 0:12]),
             r=[skey], w=[skey])
        P.op("dve", lambda v: v.tensor_scalar_add(out=st[0:np_, 14:15], in0=st[0:np_, 13:14], scalar1=LN_EPS),
             r=[skey], w=[skey])
        P.op("act", lambda a: a.activation(out=st[0:np_, 14:15], in_=st[0:np_, 14:15], func=AF.Sqrt), r=[skey], w=[skey])
        P.op("dve", lambda v: v.reciprocal(out=st[0:np_, 14:15], in_=st[0:np_, 14:15]), r=[skey], w=[skey])
        P.op("dve", lambda v: v.scalar_tensor_tensor(out=st[0:np_, 15:16], in0=st[0:np_, 12:13], scalar=-1.0,
                                                     in1=st[0:np_, 14:15], op0=ALU.mult, op1=ALU.mult),
             r=[skey], w=[skey])
        P.op("act", lambda a: a.activation(out=out[0:np_, :], in_=vin[0:np_, :], func=AF.Identity,
                                           scale=st[0:np_, 14:15], bias=st[0:np_, 15:16]),
             r=[vkey, skey], w=[okey])
        P.op("pool", lambda g: g.tensor_tensor(out=out[0:np_, :], in0=out[0:np_, :], in1=Gt_[0:np_, layer, :], op=ALU.mult),
             r=[okey, gk_], w=[okey])
        P.op("pool", lambda g: g.tensor_tensor(out=out[0:np_, :], in0=out[0:np_, :], in1=Bt_[0:np_, layer, :], op=ALU.add),
             r=[okey, bk_], w=[okey])

    import os
    CHPIPE = int(os.environ.get("CHPIPE", "1"))

    class L0Tile:
        def __init__(self, ti, sample):
            self.ti, self.sample = ti, sample
            if sample:
                self.ncols, self.nsub, self.np_ = NS_TOK, 1, NS_TOK
                self.segs = [(b * DEC_S, DEC_S, 1 + b) for b in range(DEC_B)]
            else:
                self.ncols, self.nsub, self.np_ = TT, NSUB, 128
                self.segs = [(0, TT, 0)]
            self.t0 = ti * TT
            self.xt = xtok[cnt["tile"] % 2]
            self.xk = "xtok%d" % (cnt["tile"] % 2)
            cnt["tile"] += 1
            self.cis = {}

        def front(self):
            ti, sample, ncols, nsub, np_, segs, t0, xt, xk = (self.ti, self.sample, self.ncols, self.nsub, self.np_, self.segs,
                                                              self.t0, self.xt, self.xk)
            if sample:
                P.dma("sp", xt[0:np_, 0, :], xs[:, :], w=[xk])
            else:
                for s in range(nsub):
                    P.dma("sp", xt[:, s, :], xf[t0 + s * 128:t0 + (s + 1) * 128, :], w=[xk])
            for s in range(nsub):
                P.op("pool", lambda g, s=s: g.tensor_copy(out=xbf[0:np_, s, :], in_=xt[0:np_, s, :]), r=[xk], w=["xbf"])
            for half in range(2):
                pb = half
                for kk in range(4):
                    k = half * 4 + kk
                    for s in range(nsub):
                        P.op("pe", lambda t, k=k, kk=kk, s=s, pb=pb: t.transpose(
                            psb[pb][:, :].bitcast(BF16)[:, kk * TT + s * 128:kk * TT + s * 128 + np_],
                            xbf[0:np_, s, k * 128:(k + 1) * 128], ident_b[0:np_, 0:np_]),
                            r=["xbf", "ident_b"], w=[pskey(pb)])
                for kk in range(4):
                    k = half * 4 + kk
                    for (c0, cn, mj) in segs:
                        P.op("act", lambda a, k=k, kk=kk, pb=pb, c0=c0, cn=cn, mj=mj: a.activation(
                            out=mT[:, k, c0:c0 + cn], in_=psb[pb][:, :].bitcast(BF16)[:, kk * TT + c0:kk * TT + c0 + cn],
                            func=AF.Identity, scale=mod_fm[:, 0, 8 + k, mj:mj + 1], bias=mod_fm[:, 0, k, mj:mj + 1]),
                            r=[pskey(pb), "mod_fm"], w=["mT"])

        def chunk_ab(self, n):
            ti, sample, ncols = self.ti, self.sample, self.ncols
            ci = cnt["ch"] % NT
            cnt["ch"] += 1
            self.cis[n] = ci
            pb = 2 + (n % 2)
            pk = pskey(pb)
            for k in range(NCH):
                P.op("pe", lambda t, k=k, n=n, pb=pb: t.matmul(psb[pb][:, 0:ncols], lhsT=W_in[:, k, n * 128:(n + 1) * 128],
                                                               rhs=mT[:, k, 0:ncols], start=(k == 0), stop=(k == NCH - 1)),
                     r=["W_in", "mT"], w=[pk])
            for k in range(NCH):
                P.op("pe", lambda t, k=k, n=n, pb=pb: t.matmul(psb[pb][:, 256:256 + ncols],
                                                               lhsT=W_in[:, k, D + n * 128:D + (n + 1) * 128],
                                                               rhs=mT[:, k, 0:ncols], start=(k == 0), stop=(k == NCH - 1)),
                     r=["W_in", "mT"], w=[pk])
            if sample:
                xe = xbe_s[:, n, :, :]
                xek = "xbe_s"
                P.op("dve", lambda v, pb=pb, xe=xe: v.tensor_copy(
                    out=xe[:, :, 3:3 + DEC_S], in_=psb[pb][:, 0:ncols].rearrange("p (b t) -> p b t", t=DEC_S)),
                    r=[pk], w=[xek])
                sh = lambda k: xe[:, :, k:k + DEC_S]
                v3 = lambda ap: ap[:, 0:ncols].rearrange("p (b t) -> p b t", t=DEC_S)
            else:
                xe = xbe[:, n, :]
                xek = ("xbe", n)
                if ti > 0:
                    P.op("dve", lambda v, xe=xe: v.tensor_copy(out=xe[:, 0:3], in_=xe[:, TT:TT + 3]), r=[xek], w=[xek])
                P.op("dve", lambda v, pb=pb, xe=xe: v.tensor_copy(out=xe[:, 3:3 + TT], in_=psb[pb][:, 0:TT]), r=[pk], w=[xek])
                sh = lambda k: xe[:, k:k + TT]
                v3 = lambda ap: ap[:, 0:ncols]
            zk = "zs%d" % ci
            P.op("act", lambda a, pb=pb, ci=ci: a.activation(out=zs[ci][:, 0:ncols], in_=psb[pb][:, 256:256 + ncols],
                                                             func=AF.Silu), r=[pk], w=[zk])
            ck = "xc%d" % ci
            P.op("dve", lambda v, ci=ci, n=n: v.tensor_scalar(out=v3(xc[ci]), in0=sh(0), scalar1=pf[:, 0, n:n + 1],
                                                              scalar2=pf[:, 4, n:n + 1], op0=ALU.mult, op1=ALU.add),
                 r=[xek, "pf"], w=[ck])
            for k in range(1, 4):
                P.op("dve", lambda v, ci=ci, n=n, k=k: v.scalar_tensor_tensor(
                    out=v3(xc[ci]), in0=sh(k), scalar=pf[:, k, n:n + 1], in1=v3(xc[ci]), op0=ALU.mult, op1=ALU.add),
                    r=[xek, "pf", ck], w=[ck])
            cbk = "xcb%d" % ci
            P.op("pool", lambda g, ci=ci: g.tensor_copy(out=xcb[ci][:, 0:ncols], in_=xc[ci][:, 0:ncols]), r=[ck], w=[cbk])

        def chunk_cde(self, n):
            ti, sample, ncols = self.ti, self.sample, self.ncols
            ci = self.cis[n]
            zk, ck, cbk = "zs%d" % ci, "xc%d" % ci, "xcb%d" % ci
            pg = 4 + (n % 2)
            pgk = pskey(pg)
            P.op("pe", lambda t, n=n, ci=ci, pg=pg: t.matmul(psb[pg][:, 0:ncols], lhsT=W_r[:, n, :], rhs=xcb[ci][:, 0:ncols],
                                                             start=True, stop=True), r=["W_r", cbk], w=[pgk])
            P.op("pe", lambda t, n=n, ci=ci, pg=pg: t.matmul(psb[pg][:, 256:256 + ncols], lhsT=W_i[:, n, :],
                                                             rhs=xcb[ci][:, 0:ncols], start=True, stop=True),
                 r=["W_i", cbk], w=[pgk])
            rk, ik, gk, bk, hk = "ra%d" % ci, "ri%d" % ci, "ga%d" % ci, "bb%d" % ci, "hs%d" % ci
            P.op("act", lambda a, n=n, ci=ci, pg=pg: a.activation(out=ra[ci][:, 0:ncols], in_=psb[pg][:, 0:ncols],
                                                                  func=AF.Sigmoid, bias=pf[:, 5, n:n + 1]),
                 r=[pgk, "pf"], w=[rk])
            P.op("act", lambda a, n=n, ci=ci, pg=pg: a.activation(out=ri[ci][:, 0:ncols], in_=psb[pg][:, 256:256 + ncols],
                                                                  func=AF.Sigmoid, bias=pf[:, 6, n:n + 1]),
                 r=[pgk, "pf"], w=[ik])
            P.op("act", lambda a, n=n, ci=ci: a.activation(out=ga[ci][:, 0:ncols], in_=ra[ci][:, 0:ncols], func=AF.Exp,
                                                           scale=pf[:, 8, n:n + 1]), r=[rk, "pf"], w=[gk])
            P.op("act", lambda a, n=n, ci=ci: a.activation(out=ra[ci][:, 0:ncols], in_=ra[ci][:, 0:ncols], func=AF.Exp,
                                                           scale=pf[:, 7, n:n + 1]), r=[rk, "pf"], w=[rk])
            P.op("dve", lambda v, ci=ci: v.tensor_scalar(out=ga[ci][:, 0:ncols], in0=ga[ci][:, 0:ncols], scalar1=-1.0,
                                                         scalar2=1.0, op0=ALU.mult, op1=ALU.add), r=[gk], w=[gk])
            P.op("dve", lambda v, ci=ci: v.tensor_scalar_max(out=ga[ci][:, 0:ncols], in0=ga[ci][:, 0:ncols], scalar1=0.0),
                 r=[gk], w=[gk])
            P.op("act", lambda a, ci=ci: a.activation(out=ga[ci][:, 0:ncols], in_=ga[ci][:, 0:ncols], func=AF.Sqrt),
                 r=[gk], w=[gk])
            P.op("pool", lambda g, ci=ci: g.tensor_tensor(out=bb[ci][:, 0:ncols], in0=ri[ci][:, 0:ncols],
                                                          in1=xc[ci][:, 0:ncols], op=ALU.mult), r=[ik, ck], w=[bk])
            P.op("dve", lambda v, ci=ci: v.tensor_tensor(out=bb[ci][:, 0:ncols], in0=bb[ci][:, 0:ncols],
                                                         in1=ga[ci][:, 0:ncols], op=ALU.mult), r=[bk, gk], w=[bk])
            if sample:
                for b in range(DEC_B):
                    P.op("dve", lambda v, ci=ci, n=n, b=b: v.tensor_tensor_scan(
                        out=hs[ci][:, b * DEC_S:(b + 1) * DEC_S], data0=ra[ci][:, b * DEC_S:(b + 1) * DEC_S],
                        data1=bb[ci][:, b * DEC_S:(b + 1) * DEC_S], initial=h0s[:, n, b:b + 1], op0=ALU.mult, op1=ALU.add),
                        r=[rk, bk, "h0s"], w=[hk])
                P.op("dve", lambda v, ci=ci, n=n: v.tensor_copy(
                    out=hlast_s[:, n, :], in_=hs[ci][:, 0:ncols].rearrange("p (b t) -> p b t", t=DEC_S)[:, :, DEC_S - 1]),
                    r=[hk], w=["hlast_s"])
            else:
                P.op("dve", lambda v, ci=ci, n=n: v.tensor_tensor_scan(
                    out=hs[ci][:, 0:TT], data0=ra[ci][:, 0:TT], data1=bb[ci][:, 0:TT], initial=hprev[:, n:n + 1],
                    op0=ALU.mult, op1=ALU.add), r=[rk, bk, ("hprev", n)], w=[hk])
                P.op("dve", lambda v, ci=ci, n=n: v.tensor_copy(out=hprev[:, n:n + 1], in_=hs[ci][:, TT - 1:TT]),
                     r=[hk], w=[("hprev", n)])
            P.op("pool", lambda g, ci=ci, n=n: g.tensor_tensor(out=yg[:, n, 0:ncols], in0=hs[ci][:, 0:ncols],
                                                               in1=zs[ci][:, 0:ncols], op=ALU.mult), r=[hk, zk], w=["yg"])

        def chunks(self, lo, hi):
            if CHPIPE == 0:
                for n in range(lo, hi):
                    self.chunk_ab(n)
                    self.chunk_cde(n)
                return
            for n in range(lo, hi):
                self.chunk_ab(n)
                if n - 1 >= 0:
                    self.chunk_cde(n - 1)
            if hi == NCH:
                self.chunk_cde(NCH - 1)

        def outproj_ln(self, subs=None):
            ti, sample, nsub, np_, t0, xt, xk = self.ti, self.sample, self.nsub, self.np_, self.t0, self.xt, self.xk
            if not hasattr(self, "sis"):
                self.sis = {}
            for s in (range(nsub) if subs is None else subs):
                si = cnt["sub"] % 2
                cnt["sub"] += 1
                self.sis[s] = si
                vk, x1k, x1bk, stk = "vt%d" % si, "x1t%d" % si, "x1b%d" % si, "stat%d" % si
                for h in range(2):
                    pb = 4 + h
                    for k in range(NCH):
                        P.op("pe", lambda t, k=k, h=h, s=s, pb=pb: t.matmul(
                            psb[pb][0:np_, :], lhsT=yg[:, k, s * 128:s * 128 + np_], rhs=W_out[:, k, h * 512:(h + 1) * 512],
                            start=(k == 0), stop=(k == NCH - 1)), r=["yg", "W_out"], w=[pskey(pb)])
                    G = Gs if sample else Gp
                    P.op("dve", lambda v, h=h, pb=pb, si=si, G=G: v.tensor_tensor(
                        out=vt[si][0:np_, h * 512:(h + 1) * 512], in0=psb[pb][0:np_, :], in1=G[0:np_, 0, h * 512:(h + 1) * 512],
                        op=ALU.mult), r=[pskey(pb), "Gs" if sample else "Gp"], w=[vk])
                P.op("dve", lambda v, si=si, s=s: v.scalar_tensor_tensor(
                    out=vt[si][0:np_, :], in0=xt[0:np_, s, :], scalar=ALPHA, in1=vt[si][0:np_, :], op0=ALU.mult, op1=ALU.add),
                    r=[xk, vk], w=[vk])
                layernorm_tm(vt[si], vk, x1t[si], x1k, np_, 0, stat[si], stk)
                if not sample:
                    P.dma("sp", x1scr[t0 + s * 128:t0 + (s + 1) * 128, :], x1t[si][:, :], r=[x1k], w=[("x1scr", ti, s)], semkey=x1k)
                else:
                    P.dma("sp", x1s_scr[:, :], x1t[si][0:np_, :], r=[x1k], w=["x1s_scr"], semkey=x1k)
                P.op("act", lambda a, si=si: a.activation(out=x1b[si][0:np_, :], in_=x1t[si][0:np_, :], func=AF.Identity),
                     r=[x1k], w=[x1bk])

        def tail(self, subs=None, fin=True):
            ti, sample, nsub, np_, t0 = self.ti, self.sample, self.nsub, self.np_, self.t0
            for s in (range(nsub) if subs is None else subs):
                si = self.sis[s]
                x1bk, kvk = "x1b%d" % si, "kvst%d" % si
                pb = 6 + (s % 2)
                for k in range(NCH):
                    P.op("pe", lambda t, k=k, si=si, pb=pb: t.transpose(
                        psb[pb][:, :].bitcast(BF16)[:, k * 128:k * 128 + np_], x1b[si][0:np_, k * 128:(k + 1) * 128],
                        ident_b[0:np_, 0:np_]), r=[x1bk, "ident_b"], w=[pskey(pb)])
                P.op("dve", lambda v, pb=pb, s=s: v.tensor_copy(
                    out=x1T[:, :, s * 128:s * 128 + np_],
                    in_=psb[pb][:, :].bitcast(BF16)[:, 0:1024].rearrange("p (k t) -> p k t", t=128)[:, :, 0:np_]),
                    r=[pskey(pb)], w=[("x1T", s)])
                for c3 in range(3):
                    pb = c3 % 2
                    for k in range(NCH):
                        P.op("pe", lambda t, k=k, c3=c3, s=s, pb=pb: t.matmul(
                            psb[pb][0:np_, :], lhsT=x1T[:, k, s * 128:s * 128 + np_], rhs=W_kv[:, k, c3 * 512:(c3 + 1) * 512],
                            start=(k == 0), stop=(k == NCH - 1)), r=[("x1T", s), "W_kv"], w=[pskey(pb)])
                    P.op("act", lambda a, c3=c3, pb=pb, si=si: a.activation(
                        out=kvst[si][0:np_, c3 * 512:(c3 + 1) * 512], in_=psb[pb][0:np_, :], func=AF.Identity),
                        r=[pskey(pb)], w=[kvk])
                if sample:
                    P.dma("sp", o_cmp_s[:, :], kvst[si][0:np_, 0:512], r=[kvk], w=["o_cmp_s"], semkey=kvk)
                    P.dma("sp", o_sel_s[:, :], kvst[si][0:np_, 512:1024], r=[kvk], w=["o_sel_s"], semkey=kvk)
                    for b in range(DEC_B):
                        P.dma("sp", o_win_s[b * 512 + 504:(b + 1) * 512, :], kvst[si][b * 8:(b + 1) * 8, 1024:1536],
                              r=[kvk], w=[("o_win_s", b, 1)], semkey=kvk)
                else:
                    r0 = t0 + s * 128
                    P.dma("sp", o_cmp_p[r0:r0 + 128, :], kvst[si][:, 0:512], r=[kvk], w=[("o_cmp_p", ti, s)], semkey=kvk)
                    P.dma("sp", o_sel_p[r0:r0 + 128, :], kvst[si][:, 512:1024], r=[kvk], w=[("o_sel_p", ti, s)], semkey=kvk)
                    P.dma("sp", winscr[r0:r0 + 128, :], kvst[si][:, 1024:1536], r=[kvk], w=[("winscr", ti, s)], semkey=kvk)
                    if r0 >= SEQ - 512:
                        P.dma("sp", o_win_p[r0 - (SEQ - 512):r0 - (SEQ - 512) + 128, :], kvst[si][:, 1024:1536],
                              r=[kvk], w=[("o_win_p", ti, s)], semkey=kvk)
            if sample and fin:
                for n in range(NCH):
                    P.dma("sp", o_h_s.rearrange("b (c p) -> c p b", p=128)[n], hlast_s[:, n, :], r=["hlast_s"], w=[("o_h_s", n)],
                          semkey="hlast_s")
                    for b in range(DEC_B):
                        P.dma("sp", o_conv_s[b * 3:(b + 1) * 3, n * 128:(n + 1) * 128].rearrange("k p -> p k"),
                              xbe_s[:, n, b, DEC_S:DEC_S + 3], r=["xbe_s"], w=[("o_conv_s", n, b)], semkey="xbe_s_o")

        def final_state(self):
            P.dma("sp", o_h_p[0:1, :].rearrange("o (c p) -> p (o c)", p=128), hprev[:, :],
                  r=[("hprev", n) for n in range(NCH)], w=["o_h_p"], semkey="hprev_o")
            for n in range(NCH):
                P.dma("sp", o_conv_p.rearrange("k (c p) -> c p k", p=128)[n], xbe[:, n, TT:TT + 3], r=[("xbe", n)],
                      w=[("o_conv_p", n)], semkey="xbe_o")

    import os
    L0PIPE = int(os.environ.get("L0PIPE", "2"))
    n_ptiles = SEQ // TT if stage >= 1 else 2
    if L0PIPE == -1:
        for i in range(n_ptiles + 1):
            tl = L0Tile(0, True) if i == 0 else L0Tile(i - 1, False)
            tl.front()
            tl.chunks(0, NCH)
            for s_ in range(tl.nsub):
                tl.outproj_ln([s_])
                tl.tail([s_], fin=(s_ == tl.nsub - 1))
        tl.final_state()
    elif L0PIPE == 0:
        seq = [L0Tile(0, True)] + [None] * n_ptiles
        for i in range(n_ptiles + 1):
            tl = seq[i] if i == 0 else L0Tile(i - 1, False)
            tl.front()
            tl.chunks(0, NCH)
            tl.outproj_ln()
            tl.tail()
        tl.final_state()
    elif L0PIPE == 1:
        cur = L0Tile(0, True)
        cur.front()
        for i in range(n_ptiles + 1):
            cur.chunks(0, NCH)
            nxt = L0Tile(i, False) if i < n_ptiles else None
            if nxt is not None:
                nxt.front()
            for s_ in range(cur.nsub):
                cur.outproj_ln([s_])
                cur.tail([s_], fin=(s_ == cur.nsub - 1))
            last = cur
            cur = nxt
        last.final_state()
    else:
        tiles = [L0Tile(0, True)]
        tiles[0].front()
        tiles[0].chunks(0, NCH)
        tiles[0].outproj_ln()
        nxt = L0Tile(0, False)
        nxt.front()
        prev = tiles[0]
        for ti in range(n_ptiles):
            cur = nxt
            cur.chunks(0, NCH // 2)
            prev.tail()
            cur.chunks(NCH // 2, NCH)
            if ti + 1 < n_ptiles:
                nxt = L0Tile(ti + 1, False)
                nxt.front()
            cur.outproj_ln()
            prev = cur
        prev.tail()
        prev.final_state()

    scA.__exit__(None, None, None)
    cmpscr_p = P.dram("cmpscr_p", [128, 512], F32)
    cmpscr_s = P.dram("cmpscr_s", [DEC_B * 128, 512], F32)
    do_sample = stage >= 3
    with P.scope():
        W1r = P.sb("W1r", [128, 2, 64, 128], BF16)
        CB2s = [P.sb("CB2_%d" % i, [128, 64, 512], BF16) for i in range(2)]
        cbi = {"i": 0}
        Hh = P.sb("Hh", [128, 2, 2, 256], BF16)
        W2 = P.sb("W2", [128, 2, 64], BF16)
        PEsb = P.sb("PEsb", [64, 128], BF16)
        bias1 = P.sb("bias1", [128, 4], F32)
        b2bc = P.sb("b2bc", [64, 2, 4, 128], F32)
        CS = P.sb("CS", [64, 2, 512], F32)
        IDXf = P.sb("IDXf", [128, DEC_B * 64], F32)
        IDXi = P.sb("IDXi", [128, DEC_B * 64], I32)
        iop = P.sb("iop", [128, 2], I32)
        iopf = P.sb("iopf", [128, 2], F32)
        for c in range(2):
            for half in range(2):
                P.dma("pool", W1r[half * 64:(half + 1) * 64, c, :, :], w_phi1[c], w=["W1r"])
            P.dma("pool", W2[:, c, :], w_phi2[c], w=["W2"])
            P.dma("sp", bias1[:, c:c + 1], b_phi1[c:c + 1, :].rearrange("o p -> p o"), w=["bias1"])
        P.dma("pool", PEsb[:], phi_pe[:, :], w=["PEsb"])
        for nl in range(2):
            for g in range(4):
                P.dma("sp", b2bc[:, nl, g, :], b_phi2.rearrange("c d -> (c d)").rearrange("(o n) -> o n", o=1).partition_broadcast(64),
                      w=["b2bc"])
        for c in range(2):
            for d in range(64):
                P.op("pe", lambda t, c=c, d=d: t.matmul(psb[0][:, c:c + 1], lhsT=W1r[0:64, c, d, :],
                                                        rhs=PEsb[0:64, c * 64 + d:c * 64 + d + 1],
                                                        start=(d == 0), stop=(d == 63)), r=["W1r", "PEsb"], w=[pskey(0)])
        P.op("dve", lambda v: v.tensor_tensor(out=bias1[:, 0:2], in0=psb[0][:, 0:2], in1=bias1[:, 0:2], op=ALU.add),
             r=[pskey(0), "bias1"], w=["bias1"])
        P.op("pool", lambda g_: g_.iota(out=iop[:, 0:1], pattern=[[0, 1]], base=0, channel_multiplier=1), w=["iop"])
        P.op("dve", lambda v: v.tensor_copy(out=iopf[:, 0:1], in_=iop[:, 0:1]), r=["iop"], w=["iopf"])
        P.dma("sp", IDXi[:], ptab.rearrange("b n -> (b n)").rearrange("(o n) -> o n", o=1).partition_broadcast(128), w=["IDXi"])
        P.op("dve", lambda v: v.tensor_copy(out=IDXf[:], in_=IDXi[:]), r=["IDXi"], w=["IDXf"])
        P.op("dve", lambda v: v.tensor_scalar(out=IDXf[:], in0=IDXf[:], scalar1=128.0, scalar2=iopf[:, 0:1],
                                              op0=ALU.mult, op1=ALU.add), r=["IDXf", "iopf"], w=["IDXf"])
        P.op("dve", lambda v: v.tensor_copy(out=IDXi[:], in_=IDXf[:]), r=["IDXf"], w=["IDXi"])
        idxscr = P.dram("idxscr", [128, DEC_B * 64], I32)
        P.dma("sp", idxscr[:, :], IDXi[:], r=["IDXi"], w=["idxscr"], semkey="IDXi_o")

        def compress(load_pages, out_rows, okey):
            CB2 = CB2s[cbi["i"] % 2]
            cbk = "CB2_%d" % (cbi["i"] % 2)
            cbi["i"] += 1
            CB2v = CB2[:].rearrange("p n (g c d) -> p n g c d", g=4, c=2)
            load_pages(CB2, cbk)
            for c in range(2):
                for nl in range(2):
                    pb = c * 2 + nl
                    for d in range(64):
                        P.op("pe", lambda t, c=c, nl=nl, d=d, pb=pb: t.matmul(
                            psb[pb][:, 0:256], lhsT=W1r[nl * 64:(nl + 1) * 64, c, d, :],
                            rhs=CB2v[nl * 64:(nl + 1) * 64, :, :, c, d], start=(d == 0), stop=(d == 63)),
                            r=["W1r", cbk], w=[pskey(pb)])
                    P.op("act", lambda a, c=c, nl=nl, pb=pb: a.activation(out=Hh[:, c, nl, :], in_=psb[pb][:, 0:256], func=AF.Silu,
                                                                          bias=bias1[:, c:c + 1]), r=[pskey(pb), "bias1"], w=["Hh"])
            Hv = Hh[:].rearrange("p c n (pg g) -> p c n pg g", g=4)
            for nl in range(2):
                pb = 4 + nl
                for g in range(4):
                    for c in range(2):
                        col = (g * 2 + c) * 64
                        P.op("pe", lambda t, nl=nl, g=g, c=c, pb=pb, col=col: t.matmul(
                            psb[pb][0:64, col:col + 64], lhsT=Hv[:, c, nl, :, g], rhs=W2[:, c, :], start=True, stop=True),
                            r=["Hh", "W2"], w=[pskey(pb)])
                P.op("dve", lambda v, nl=nl, pb=pb: v.tensor_tensor(
                    out=CS[:, nl, :], in0=psb[pb][0:64, :], in1=b2bc[:, nl, :, :].rearrange("p g f -> p (g f)"), op=ALU.add),
                    r=[pskey(pb), "b2bc"], w=["CS"])
            P.dma("sp", out_rows.rearrange("(pg n) f -> pg n f", n=2), CS[:], r=["CS"], w=[okey], semkey="CS")

        def load_prompt_pages(CB2, cbk):
            for pg in range(64):
                P.dma("pool", CB2[:, pg, :], o_cmp_p[pg * 128:(pg + 1) * 128, :], r=[("o_cmp_p", pg // NSUB, pg % NSUB)], w=[cbk])

        if stage >= 2:
            compress(load_prompt_pages, cmpscr_p[:, :], "cmpscr_p")
        if do_sample:
            for b in range(DEC_B):
                def load_sample_pages(CB2, cbk, b=b):
                    for pg in range(64):
                        P.gather(CB2[:, pg, :], ccmp[:, :], IDXi[:, b * 64 + pg:b * 64 + pg + 1], r=["IDXi"], w=[cbk])
                compress(load_sample_pages, cmpscr_s[b * 128:(b + 1) * 128, :], ("cmpscr_s", b))

    NTOK = 2048 + NS_TOK
    QTscr = P.dram("QTscr", [4, 64, 4, NTOK], BF16)
    ZSscr = P.dram("ZSscr", [NTOK, D], BF16)
    GLscr = P.dram("GLscr", [NTOK, 48], F32)
    OGscr = P.dram("OGscr", [NTOK, D], BF16)
    qtiles = [(jl * 128, 128, jl, None) for jl in range(16)] if stage >= 2 else []
    if do_sample:
        qtiles += [(2048 + b * 8, 8, None, b) for b in range(DEC_B)]
    if stage == 2.5:
        qtiles = qtiles[:2]

    with P.scope():
        W_inb = P.sb("W_inb", [128, NCH, 2096], BF16)
        for k in range(NCH):
            P.dma("pool", W_inb[:, k, :], w_in_b[k * 128:(k + 1) * 128, :], w=["W_inb"])
        bgbc = P.sb("bgbc", [128, 48], F32)
        P.dma("sp", bgbc[:], b_gate[0:1, :].partition_broadcast(128), w=["bgbc"])
        idxo = P.sb("idxo", [128, 16], I32)
        P.dma("sp", idxo[:], t_idx_own[:, :], w=["idxo"])
        X1 = [P.sb("X1_%d" % i, [128, D], F32) for i in range(2)]
        X1b = P.sb("X1b", [128, D], BF16)
        m1T = P.sb("m1T", [128, NCH, 128], BF16)
        QTst = [P.sb("QTst%d" % i, [64, 4, 128], BF16) for i in range(2)]
        ZSt = [P.sb("ZSt%d" % i, [128, D], BF16) for i in range(2)]
        GLt = [P.sb("GLt%d" % i, [128, 48], F32) for i in range(2)]
        qi = 0
        for (tok0, nq, jl, sb_) in qtiles:
            i2 = qi % 2
            xk = "X1_%d" % i2
            if jl is not None:
                P.gather(X1[i2][:, :], x1scr[:, :], idxo[:, jl:jl + 1], r=["idxo"], w=[xk])
                mj = 0
            else:
                P.dma("sp", X1[i2][0:nq, :], x1s_scr[sb_ * 8:(sb_ + 1) * 8, :], w=[xk])
                mj = 1 + sb_
            P.op("pool", lambda g_, i2=i2, nq=nq: g_.tensor_copy(out=X1b[0:nq, :], in_=X1[i2][0:nq, :]), r=[xk], w=["X1b"])
            for k in range(NCH):
                P.op("pe", lambda t, k=k, nq=nq: t.transpose(psb[0][:, :].bitcast(BF16)[:, k * 128:k * 128 + nq],
                                                             X1b[0:nq, k * 128:(k + 1) * 128], ident_b[0:nq, 0:nq]),
                     r=["X1b", "ident_b"], w=[pskey(0)])
            for k in range(NCH):
                P.op("act", lambda a, k=k, nq=nq, mj=mj: a.activation(
                    out=m1T[:, k, 0:nq], in_=psb[0][:, :].bitcast(BF16)[:, k * 128:k * 128 + nq], func=AF.Identity,
                    scale=mod_fm[:, 1, 8 + k, mj:mj + 1], bias=mod_fm[:, 1, k, mj:mj + 1]), r=[pskey(0), "mod_fm"], w=["m1T"])
            for g in range(4):
                pb = 2 + (g % 2)
                qk = "QTst%d" % (g % 2)
                for hh in range(4):
                    for k in range(NCH):
                        P.op("pe", lambda t, k=k, hh=hh, g=g, pb=pb, nq=nq: t.matmul(
                            psb[pb][0:64, hh * 128:hh * 128 + nq], lhsT=W_inb[:, k, (4 * g + hh) * 64:(4 * g + hh + 1) * 64],
                            rhs=m1T[:, k, 0:nq], start=(k == 0), stop=(k == NCH - 1)), r=["W_inb", "m1T"], w=[pskey(pb)])
                P.op("dve", lambda v, g=g, pb=pb, nq=nq: v.tensor_scalar_mul(
                    out=QTst[g % 2][:, :, 0:nq], in0=psb[pb][0:64, :].rearrange("p (h q) -> p h q", h=4)[:, :, 0:nq],
                    scalar1=0.125), r=[pskey(pb)], w=[qk])
                P.dma("sp", QTscr[g, :, :, tok0:tok0 + nq], QTst[g % 2][:, :, 0:nq], r=[qk], w=[("QTscr", g, tok0)], semkey=qk)
            zk = "ZSt%d" % i2
            for half in range(2):
                pb = 4 + half
                for k in range(NCH):
                    P.op("pe", lambda t, k=k, half=half, pb=pb, nq=nq: t.matmul(
                        psb[pb][0:nq, :], lhsT=m1T[:, k, 0:nq], rhs=W_inb[:, k, D + half * 512:D + (half + 1) * 512],
                        start=(k == 0), stop=(k == NCH - 1)), r=["W_inb", "m1T"], w=[pskey(pb)])
                P.op("act", lambda a, half=half, pb=pb, nq=nq, i2=i2: a.activation(
                    out=ZSt[i2][0:nq, half * 512:(half + 1) * 512], in_=psb[pb][0:nq, :], func=AF.Silu), r=[pskey(pb)], w=[zk])
            P.dma("sp", ZSscr[tok0:tok0 + nq, :], ZSt[i2][0:nq, :], r=[zk], w=[("ZSscr", tok0)], semkey=zk)
            gk = "GLt%d" % i2
            for k in range(NCH):
                P.op("pe", lambda t, k=k, nq=nq: t.matmul(psb[6][0:nq, 0:48], lhsT=m1T[:, k, 0:nq], rhs=W_inb[:, k, 2048:2096],
                                                          start=(k == 0), stop=(k == NCH - 1)), r=["W_inb", "m1T"], w=[pskey(6)])
            P.op("dve", lambda v, nq=nq, i2=i2: v.tensor_tensor(out=GLt[i2][0:nq, :], in0=psb[6][0:nq, 0:48], in1=bgbc[0:nq, :],
                                                                op=ALU.add), r=[pskey(6), "bgbc"], w=[gk])
            P.op("act", lambda a, nq=nq, i2=i2: a.activation(out=GLt[i2][0:nq, :], in_=GLt[i2][0:nq, :], func=AF.Sigmoid),
                 r=[gk], w=[gk])
            P.dma("sp", GLscr[tok0:tok0 + nq, :], GLt[i2][0:nq, :], r=[gk], w=[("GLscr", tok0)], semkey=gk)
            qi += 1

    with P.scope():
        EE = P.sb("EE", [128, 64, 128], BF16)
        ones_b = P.sb("ones_b", [128, 1024], BF16)
        P.op("pool", lambda g_: g_.memset(ones_b[:], 1.0), w=["ones_b"])
        for T8 in range(8):
            P.op("pool", lambda g_, T8=T8: g_.affine_select(
                out=EE[:, T8 * 8:(T8 + 1) * 8, :].rearrange("p t (a b) -> p t a b", a=2),
                in_=ones_b[:, :].rearrange("p (t a b) -> p t a b", t=8, a=2),
                pattern=[[-2, 8], [-1, 2], [0, 64]], compare_op=ALU.is_equal, fill=0.0, base=-16 * T8, channel_multiplier=1),
                r=["ones_b"], w=["EE"])
        TRIp = P.sb("TRIp", [128, 512], BF16)
        TRI2p = P.sb("TRI2p", [128, 512], BF16)
        TRIs = P.sb("TRIs", [128, 32], BF16)
        TRI2s = P.sb("TRI2s", [128, 32], BF16)
        P.dma("sp", TRIp[:], t_tri_p[:, :], w=["TRIp"])
        P.dma("sp", TRI2p[:], t_tri2_p[:, :], w=["TRI2p"])
        P.dma("sp", TRIs[:], t_tri_s[:, :], w=["TRIs"])
        P.dma("sp", TRI2s[:], t_tri2_s[:, :], w=["TRI2s"])
        RB = [P.sb("RB%d" % i, [128, 4, 512], BF16) for i in range(2)]
        RBF = [P.sb("RBF%d" % i, [128, 512], BF16) for i in range(4)]
        KsT4 = P.sb("KsT4", [72, 4, 65 * 128], BF16)
        Vs4 = P.sb("Vs4", [128, 65, 4, 65], BF16)
        KwT4 = P.sb("KwT4", [72, 4, 21 * 128], BF16)
        Vw4 = P.sb("Vw4", [128, 21, 4, 65], BF16)
        KcT4 = P.sb("KcT4", [72, 4, 128], BF16)
        Vc4 = P.sb("Vc4", [128, 4, 64], BF16)
        P.op("pool", lambda g_: g_.memset(Vs4[:, :, :, 64:65], 1.0), w=["Vs4"])
        P.op("pool", lambda g_: g_.memset(Vw4[:, :, :, 64:65], 1.0), w=["Vw4"])
        idxo2 = P.sb("idxo2", [128, 16], I32)
        idxw = P.sb("idxw", [128, 20], I32)
        idxs = P.sb("idxs", [128, 1], I32)
        idxpg = P.sb("idxpg", [128, DEC_B * 64], I32)
        P.dma("sp", idxo2[:], t_idx_own[:, :], w=["idxo2"])
        P.dma("sp", idxw[:], t_idx_win[:, :], w=["idxw"])
        P.dma("sp", idxs[:], t_idx_slot[:, :], w=["idxs"])
        P.dma("sp", idxpg[:], idxscr[:, :], w=["idxpg"])
        QT = [P.sb("QT%d" % i, [72, 4, 128], BF16) for i in range(2)]
        FBNt = [P.sb("FBNt%d" % i, [128, 128], F32) for i in range(2)]
        CAUt = [P.sb("CAUt%d" % i, [128, 128], F32) for i in range(2)]
        TMt = [P.sb("TMt%d" % i, [128, 128], F32) for i in range(2)]
        GLg = [P.sb("GLg%d" % i, [128, 48], F32) for i in range(2)]
        ZSg = [P.sb("ZSg%d" % i, [128, 256], BF16) for i in range(2)]
        Ssb = P.sb("Ssb", [128, 512], F32)
        Esb = P.sb("Esb", [128, 512], F32)
        Pn = P.sb("Pn", [128, 512], F32)
        Pnb = P.sb("Pnb", [128, 512], BF16)
        imp = P.sb("imp", [128, 128], F32)
        scr = P.sb("scr", [128, 128], F32)
        scr2 = P.sb("scr2", [128, 128], F32)
        m8 = P.sb("m8", [128, 16], F32)
        smh = P.sb("smh", [128, 16], F32)
        smb = P.sb("smb", [128, 16], F32)
        MselT = P.sb("MselT", [128, 128], BF16)
        Msel4s = [P.sb("Msel4_%d" % i, [128, 512], BF16) for i in range(2)]
        PTc = P.sb("PTc", [128, 512], BF16)
        PT = [P.sb("PT%d" % i, [128, 512], BF16) for i in range(3)]
        OaugSB = P.sb("OaugSB", [65, 512], F32)
        accs = [P.sb("acc%d" % i, [128, 256], F32) for i in range(2)]
        OGt = [P.sb("OGt%d" % i, [128, 256], BF16) for i in range(2)]
        cnt2 = {"rb": 0, "rbf": 0, "pt": 0, "ps": 0, "q": 0}

        def prep_from_rbf(rbf, rk, g, nk, ktdst, kkey, vdst, vkey):
            P.op("pe", lambda t: t.transpose(psb[7][:, :].bitcast(BF16)[0:64, 0:nk], rbf[0:nk, g * 128:g * 128 + 64],
                                             ident_b[0:nk, 0:nk]), r=[rk, "ident_b"], w=[pskey(7)])
            P.op("dve", lambda v: v.tensor_copy(out=ktdst, in_=psb[7][:, :].bitcast(BF16)[0:64, 0:nk]), r=[pskey(7)], w=[kkey])
            P.op("pool", lambda g_: g_.tensor_copy(out=vdst, in_=rbf[0:nk, g * 128 + 64:g * 128 + 128]), r=[rk], w=[vkey])

        def prep4(rbf, rk, nk, ktdst3, kkey, vdst3, vkey):
            for g4 in range(4):
                P.op("pe", lambda t, g4=g4: t.transpose(psb[7][:, :].bitcast(BF16)[0:64, g4 * 128:g4 * 128 + nk],
                                                        rbf[0:nk, g4 * 128:g4 * 128 + 64], ident_b[0:nk, 0:nk]),
                     r=[rk, "ident_b"], w=[pskey(7)])
            P.op("dve", lambda v: v.tensor_copy(
                out=ktdst3, in_=psb[7][:, :].bitcast(BF16)[0:64, 0:512].rearrange("p (g k) -> p g k", g=4)[:, :, 0:nk]),
                r=[pskey(7)], w=[kkey])
            P.op("pool", lambda g_: g_.tensor_copy(
                out=vdst3, in_=rbf[0:nk, :].rearrange("p (g c d) -> p g c d", g=4, c=2)[:, :, 1, :]), r=[rk], w=[vkey])

        def load_rows_gather(src, idx_ap, ikey):
            i = cnt2["rbf"] % 4
            cnt2["rbf"] += 1
            P.gather(RBF[i][:, :], src, idx_ap, r=[ikey], w=["RBF%d" % i])
            return RBF[i], "RBF%d" % i

        def load_rows_plain(src_rows, nk):
            i = cnt2["rbf"] % 4
            cnt2["rbf"] += 1
            P.dma("pool", RBF[i][0:nk, :], src_rows, w=["RBF%d" % i])
            return RBF[i], "RBF%d" % i

        def nsa_tile(g, tok0, nq, jl, sb_, kth):
            ncol = 4 * nq
            sample = jl is None
            i2 = cnt2["q"] % 2
            cnt2["q"] += 1
            qt, qk = QT[i2], "QT%d" % i2
            Msel4, mk4 = Msel4s[i2], "Msel4_%d" % i2
            acc, ak = accs[i2], "acc%d" % i2
            P.dma("sp", qt[0:64, :, 0:nq], QTscr[g, :, :, tok0:tok0 + nq], w=[qk])
            if sample:
                P.dma("sp", qt[64:72, :, 0:nq], t_qaug_s.rearrange("r (h q) -> r h q", h=16)[:, 4 * g:4 * g + 4, :], w=[qk])
                P.dma("sp", FBNt[i2][0:nq, :], t_fbn_s[:, :], w=["FBNt%d" % i2])
                P.dma("sp", CAUt[i2][0:nq, :], t_caus_s[:, :], w=["CAUt%d" % i2])
                P.dma("sp", TMt[i2][0:nq, :], t_tm_s[:, :], w=["TMt%d" % i2])
            else:
                P.dma("sp", qt[64:72, :, 0:nq], t_qaug_p.rearrange("r (h q) -> r h q", h=16)[:, 4 * g:4 * g + 4, tok0:tok0 + nq],
                      w=[qk])
                P.dma("sp", FBNt[i2][:, :], t_fbn[jl], w=["FBNt%d" % i2])
                P.dma("sp", CAUt[i2][:, :], t_caus[jl], w=["CAUt%d" % i2])
                P.dma("sp", TMt[i2][:, :], t_tm[jl], w=["TMt%d" % i2])
            fk, ck, tk, glk, zk = "FBNt%d" % i2, "CAUt%d" % i2, "TMt%d" % i2, "GLg%d" % i2, "ZSg%d" % i2
            P.dma("sp", GLg[i2][0:nq, :], GLscr[tok0:tok0 + nq, :], w=[glk])
            P.dma("sp", ZSg[i2][0:nq, :], ZSscr[tok0:tok0 + nq, g * 256:(g + 1) * 256], w=[zk])
            qrhs = qt[0:72, :, 0:nq]
            kc_ap, kck, vc_ap, vck = KcT4[0:72, g, :], "KcT4", Vc4[:, g, :], "Vc4"
            ksel = lambda T, nk: (KsT4[0:72, g, T * 128:T * 128 + nk], "KsT4", Vs4[0:nk, T, g, :], "Vs4")
            kwin = lambda T, nk: (KwT4[0:72, g, T * 128:T * 128 + nk], "KwT4", Vw4[0:nk, T, g, :], "Vw4")
            gl3 = GLg[i2][0:nq, :].rearrange("p (h b) -> p h b", b=3)
            for hh in range(4):
                P.op("pe", lambda t, hh=hh: t.matmul(psb[6][0:nq, hh * 128:(hh + 1) * 128], lhsT=qt[0:72, hh, 0:nq], rhs=kc_ap,
                                                     start=True, stop=True), r=[qk, kck], w=[pskey(6)])
            P.op("dve", lambda v: v.tensor_tensor(
                out=Ssb[0:nq, :].rearrange("p (h s) -> p h s", h=4), in0=psb[6][0:nq, :].rearrange("p (h s) -> p h s", h=4),
                in1=TMt[i2][0:nq, :].unsqueeze(1).to_broadcast([nq, 4, 128]), op=ALU.add), r=[pskey(6), tk], w=["Ssb"])
            for hh in range(4):
                P.op("act", lambda a, hh=hh: a.activation(out=Esb[0:nq, hh * 128:(hh + 1) * 128], in_=Ssb[0:nq, hh * 128:(hh + 1) * 128],
                                                          func=AF.Exp, accum_out=smh[0:nq, hh:hh + 1]), r=["Ssb"], w=["Esb", "smh"])
            P.op("dve", lambda v: v.tensor_scalar_add(out=smh[0:nq, 4:8], in0=smh[0:nq, 0:4], scalar1=1e-30), r=["smh"], w=["smh"])
            P.op("dve", lambda v: v.reciprocal(out=smh[0:nq, 4:8], in_=smh[0:nq, 4:8]), r=["smh"], w=["smh"])
            for hh in range(4):
                P.op("dve", lambda v, hh=hh: v.tensor_scalar_mul(out=Pn[0:nq, hh * 128:(hh + 1) * 128],
                                                                 in0=Esb[0:nq, hh * 128:(hh + 1) * 128],
                                                                 scalar1=smh[0:nq, 4 + hh:5 + hh]), r=["Esb", "smh"], w=["Pn"])
            P.op("pool", lambda g_: g_.tensor_copy(out=Pnb[0:nq, :], in_=Pn[0:nq, :]), r=["Pn"], w=["Pnb"])
            P.op("dve", lambda v: v.tensor_tensor(out=imp[0:nq, :], in0=Pn[0:nq, 0:128], in1=Pn[0:nq, 128:256], op=ALU.add),
                 r=["Pn"], w=["imp"])
            P.op("dve", lambda v: v.tensor_tensor(out=imp[0:nq, :], in0=imp[0:nq, :], in1=Pn[0:nq, 256:384], op=ALU.add),
                 r=["Pn", "imp"], w=["imp"])
            P.op("dve", lambda v: v.tensor_tensor(out=imp[0:nq, :], in0=imp[0:nq, :], in1=Pn[0:nq, 384:512], op=ALU.add),
                 r=["Pn", "imp"], w=["imp"])
            P.op("dve", lambda v: v.tensor_tensor(out=scr[0:nq, :], in0=imp[0:nq, :], in1=CAUt[i2][0:nq, :], op=ALU.mult),
                 r=["imp", ck], w=["scr"])
            P.op("dve", lambda v: v.tensor_tensor(out=scr[0:nq, :], in0=scr[0:nq, :], in1=FBNt[i2][0:nq, :], op=ALU.add),
                 r=["scr", fk], w=["scr"])
            P.op("dve", lambda v: v.max(out=m8[0:nq, 0:8], in_=scr[0:nq, :]), r=["scr"], w=["m8"])
            P.op("dve", lambda v: v.match_replace(out=scr2[0:nq, :], in_to_replace=m8[0:nq, 0:8], in_values=scr[0:nq, :],
                                                  imm_value=-1.0e30), r=["scr", "m8"], w=["scr2"])
            P.op("dve", lambda v: v.max(out=m8[0:nq, 8:16], in_=scr2[0:nq, :]), r=["scr2"], w=["m8"])
            thr = m8[0:nq, kth - 1:kth]
            P.op("dve", lambda v: v.tensor_scalar(out=scr2[0:nq, :], in0=scr[0:nq, :], scalar1=thr, scalar2=None, op0=ALU.is_ge),
                 r=["scr", "m8"], w=["scr2"])
            P.op("dve", lambda v: v.tensor_tensor(out=scr2[0:nq, :], in0=scr2[0:nq, :], in1=CAUt[i2][0:nq, :], op=ALU.mult),
                 r=["scr2", ck], w=["scr2"])
            P.op("dve", lambda v: v.tensor_scalar(out=MselT[0:nq, :], in0=scr2[0:nq, :], scalar1=-1.0, scalar2=-NEGM,
                                                  op0=ALU.add, op1=ALU.mult), r=["scr2"], w=["MselT"])
            yield "a"
            P.op("pe", lambda t: t.transpose(psb[7][:, :].bitcast(BF16)[:, 0:nq], MselT[0:nq, :], ident_b[0:nq, 0:nq]),
                 r=["MselT", "ident_b"], w=[pskey(7)])
            P.op("dve", lambda v: v.tensor_copy(
                out=Msel4[:, 0:ncol].rearrange("p (h q) -> p h q", h=4),
                in_=psb[7][:, :].bitcast(BF16)[:, 0:nq].unsqueeze(1).to_broadcast([128, 4, nq])), r=[pskey(7)], w=[mk4])
            for hh in range(4):
                P.op("pe", lambda t, hh=hh: t.transpose(psb[7][:, :].bitcast(BF16)[:, 512 + hh * nq:512 + (hh + 1) * nq],
                                                        Pnb[0:nq, hh * 128:(hh + 1) * 128], ident_b[0:nq, 0:nq]),
                     r=["Pnb", "ident_b"], w=[pskey(7)])
            P.op("dve", lambda v: v.tensor_copy(out=PTc[:, 0:ncol], in_=psb[7][:, :].bitcast(BF16)[:, 512:512 + ncol]),
                 r=[pskey(7)], w=["PTc"])
            for hh in range(4):
                P.op("pe", lambda t, hh=hh: t.matmul(psb[6][0:nq, hh * 64:(hh + 1) * 64], lhsT=PTc[:, hh * nq:(hh + 1) * nq], rhs=vc_ap,
                                                     start=True, stop=True), r=["PTc", vck], w=[pskey(6)])
            for hh in range(4):
                P.op("dve", lambda v, hh=hh: v.tensor_scalar_mul(out=acc[0:nq, hh * 64:(hh + 1) * 64],
                                                                 in0=psb[6][0:nq, hh * 64:(hh + 1) * 64],
                                                                 scalar1=gl3[:, 4 * g + hh, 0:1]), r=[pskey(6), glk], w=[ak])

            yield "b"
            def attend(tiles, br, ob):
                nt = len(tiles)
                slots = {}

                def s_stage(i):
                    T = tiles[i]
                    sbk = cnt2["ps"] % 3
                    cnt2["ps"] += 1
                    nk = T["nk"]
                    mm = [(T["kt"], qrhs, T["kkey"], qk)] + T["masks"]
                    for j, (l, r_, lk, rk) in enumerate(mm):
                        P.op("pe", lambda t, l=l, r_=r_, j=j, nk=nk, sbk=sbk, n=len(mm): t.matmul(
                            psb[sbk][0:nk, 0:ncol], lhsT=l, rhs=r_, start=(j == 0), stop=(j == n - 1)),
                            r=[lk, rk], w=[pskey(sbk)])
                    slots[i] = sbk

                def e_stage(i):
                    T = tiles[i]
                    nk = T["nk"]
                    sbk = slots[i]
                    pi = cnt2["pt"] % 3
                    cnt2["pt"] += 1
                    P.op("act", lambda a, nk=nk, sbk=sbk, pi=pi: a.activation(out=PT[pi][0:nk, 0:ncol], in_=psb[sbk][0:nk, 0:ncol],
                                                                              func=AF.Exp), r=[pskey(sbk)], w=["PT%d" % pi])
                    return pi

                def v_stage(i, pi):
                    T = tiles[i]
                    nk = T["nk"]
                    P.op("pe", lambda t, T=T, nk=nk, pi=pi, i=i: t.matmul(psb[ob][0:65, 0:ncol], lhsT=T["v"], rhs=PT[pi][0:nk, 0:ncol],
                                                                          start=(i == 0), stop=(i == nt - 1)),
                         r=[T["vkey"], "PT%d" % pi], w=[pskey(ob)])

                LOOK = 2
                for i in range(min(LOOK, nt)):
                    s_stage(i)
                for i in range(nt):
                    pi = e_stage(i)
                    if i + LOOK < nt:
                        s_stage(i + LOOK)
                    v_stage(i, pi)
                P.op("dve", lambda v: v.tensor_copy(out=OaugSB[0:65, 0:ncol], in_=psb[ob][0:65, 0:ncol]), r=[pskey(ob)], w=["OaugSB"])
                for hh in range(4):
                    P.op("pe", lambda t, hh=hh: t.transpose(psb[5][0:nq, hh * 65:(hh + 1) * 65], OaugSB[0:65, hh * nq:(hh + 1) * nq],
                                                            ident_f[0:65, 0:65]), r=["OaugSB", "ident_f"], w=[pskey(5)])
                o3 = psb[5][0:nq, 0:260].rearrange("p (h e) -> p h e", e=65)
                P.op("dve", lambda v: v.tensor_scalar_add(out=smb[0:nq, 8:12], in0=o3[:, :, 64], scalar1=1e-30), r=[pskey(5)], w=["smb"])
                P.op("dve", lambda v: v.reciprocal(out=smb[0:nq, 8:12], in_=smb[0:nq, 8:12]), r=["smb"], w=["smb"])
                P.op("dve", lambda v: v.tensor_tensor(out=smb[0:nq, 12:16], in0=smb[0:nq, 8:12], in1=gl3[:, 4 * g:4 * g + 4, br],
                                                      op=ALU.mult), r=["smb", glk], w=["smb"])
                for hh in range(4):
                    P.op("dve", lambda v, hh=hh: v.scalar_tensor_tensor(
                        out=acc[0:nq, hh * 64:(hh + 1) * 64], in0=o3[:, hh, 0:64], scalar=smb[0:nq, 12 + hh:13 + hh],
                        in1=acc[0:nq, hh * 64:(hh + 1) * 64], op0=ALU.mult, op1=ALU.add), r=[pskey(5), "smb", ak], w=[ak])

            def ktile(src, T, nk=128, masks=()):
                kt_, kkey_, v_, vkey_ = src(T, nk)
                return {"kt": kt_, "kkey": kkey_, "v": v_, "vkey": vkey_, "nk": nk, "masks": list(masks)}

            msel = lambda T: (EE[:, T, :], Msel4[:, 0:ncol], "EE", mk4)
            if sample:
                tri = (ident_b[0:8, 0:8], TRIs[0:8, 0:ncol], "ident_b", "TRIs")
                tri2 = (ident_b[:, :], TRI2s[:, 0:ncol], "ident_b", "TRI2s")
                sel_tiles = [ktile(ksel, T, masks=[msel(T)]) for T in range(64)]
                sel_tiles.append(ktile(ksel, 64, nk=8, masks=[tri]))
                win_tiles = [ktile(kwin, 0, masks=[tri2])] + [ktile(kwin, w) for w in range(1, 4)]
                win_tiles.append(ktile(kwin, 4, nk=8, masks=[tri]))
            else:
                tri = (ident_b[:, :], TRIp[:, 0:ncol], "ident_b", "TRIp")
                tri2 = (ident_b[:, :], TRI2p[:, 0:ncol], "ident_b", "TRI2p")
                sel_tiles = [ktile(ksel, T, masks=[msel(T)]) for T in range(48)]
                for j2 in range(jl + 1):
                    ms = [msel(48 + j2)] + ([tri] if j2 == jl else [])
                    sel_tiles.append(ktile(ksel, 48 + j2, masks=ms))
                win_tiles = []
                for w in range(jl, jl + 5):
                    ms = [tri2] if w == jl else ([tri] if w == jl + 4 else [])
                    win_tiles.append(ktile(kwin, w, masks=ms))
            attend(sel_tiles, 1, 3)
            yield "sel"
            attend(win_tiles, 2, 4)
            ogk = "OGt%d" % i2
            P.op("dve", lambda v: v.tensor_tensor(out=OGt[i2][0:nq, :], in0=acc[0:nq, :], in1=ZSg[i2][0:nq, :], op=ALU.mult),
                 r=[ak, zk], w=[ogk])
            P.dma("sp", OGscr[tok0:tok0 + nq, g * 256:(g + 1) * 256], OGt[i2][0:nq, :], r=[ogk], w=[("OGscr", tok0, g)], semkey=ogk)
            yield "done"

        def run_tiles(specs):
            gens = [nsa_tile(*sp) for sp in specs]
            n = len(gens)
            if n == 0:
                return
            next(gens[0])
            next(gens[0])
            for i in range(n):
                if i + 1 < n:
                    next(gens[i + 1])
                next(gens[i])
                if i + 1 < n:
                    next(gens[i + 1])
                next(gens[i])

        if stage >= 2:
            for g in range(4):
                P.dma("sp", KsT4[64:72, g, 0:8192], t_kaug_sel[:, :], w=["KsT4"])
                P.dma("sp", KwT4[64:72, g, 0:2560], t_kaug_win[:, :], w=["KwT4"])
                P.dma("sp", KcT4[64:72, g, :], t_kaug_cmp[:, :], w=["KcT4"])
            for T4 in range(12):
                i = cnt2["rb"] % 2
                cnt2["rb"] += 1
                rbk = "RB%d" % i
                P.dma("pool", RB[i][:, :, :], o_sel_p[T4 * 512:(T4 + 1) * 512, :].rearrange("(t p) c -> p t c", p=128), w=[rbk])
                for t_ in range(4):
                    T = T4 * 4 + t_
                    prep4(RB[i][:, t_, :], rbk, 128, KsT4[0:64, :, T * 128:(T + 1) * 128], "KsT4", Vs4[:, T, :, 0:64], "Vs4")
            for j2 in range(16):
                rbf, rk = load_rows_gather(o_sel_p[:, :], idxo2[:, j2:j2 + 1], "idxo2")
                prep4(rbf, rk, 128, KsT4[0:64, :, (48 + j2) * 128:(49 + j2) * 128], "KsT4", Vs4[:, 48 + j2, :, 0:64], "Vs4")
            for w in range(20):
                rbf, rk = load_rows_gather(winscr[:, :], idxw[:, w:w + 1], "idxw")
                prep4(rbf, rk, 128, KwT4[0:64, :, w * 128:(w + 1) * 128], "KwT4", Vw4[:, w, :, 0:64], "Vw4")
            rbf, rk = load_rows_gather(cmpscr_p[:, :], idxs[:, 0:1], "idxs")
            prep4(rbf, rk, 128, KcT4[0:64, :, :], "KcT4", Vc4[:, :, :], "Vc4")
            run_tiles([(g, tok0, nq, jl, None, 16) for g in range(4) for (tok0, nq, jl, sb_) in qtiles if jl is not None])
        if do_sample:
            for g in range(4):
                P.dma("sp", KsT4[64:72, g, 0:8192], t_kaug_sel_s[:, :], w=["KsT4"])
                P.dma("sp", KsT4[64:72, g, 8192:8320], t_kaug_new_s[:, :], w=["KsT4"])
                P.dma("sp", KwT4[64:72, g, 0:512], t_kaug_win_s[:, :], w=["KwT4"])
                P.dma("sp", KwT4[64:72, g, 512:640], t_kaug_new_s[:, :], w=["KwT4"])
                P.dma("sp", KcT4[64:72, g, :], t_kaug_cmp_s[:, :], w=["KcT4"])
            for (tok0, nq, jl, sb_) in qtiles:
                if jl is not None:
                    continue
                b = sb_
                for pg in range(64):
                    rbf, rk = load_rows_gather(csel[:, :], idxpg[:, b * 64 + pg:b * 64 + pg + 1], "idxpg")
                    prep4(rbf, rk, 128, KsT4[0:64, :, pg * 128:(pg + 1) * 128], "KsT4", Vs4[:, pg, :, 0:64], "Vs4")
                rbf, rk = load_rows_plain(o_sel_s[b * 8:(b + 1) * 8, :], 8)
                prep4(rbf, rk, 8, KsT4[0:64, :, 8192:8200], "KsT4", Vs4[0:8, 64, :, 0:64], "Vs4")
                for w in range(4):
                    rbf, rk = load_rows_plain(swin[b * 512 + w * 128:b * 512 + (w + 1) * 128, :], 128)
                    prep4(rbf, rk, 128, KwT4[0:64, :, w * 128:(w + 1) * 128], "KwT4", Vw4[:, w, :, 0:64], "Vw4")
                rbf, rk = load_rows_plain(o_win_s[b * 512 + 504:b * 512 + 512, :], 8)
                prep4(rbf, rk, 8, KwT4[0:64, :, 512:520], "KwT4", Vw4[0:8, 4, :, 0:64], "Vw4")
                rbf, rk = load_rows_plain(cmpscr_s[b * 128:(b + 1) * 128, :], 128)
                prep4(rbf, rk, 128, KcT4[0:64, :, :], "KcT4", Vc4[:, :, :], "Vc4")
                run_tiles([(g, tok0, nq, None, b, 15) for g in range(4)])

    with P.scope():
        W_outb = P.sb("W_outb", [128, NCH, D], BF16)
        for k in range(NCH):
            P.dma("pool", W_outb[:, k, :], w_out_b[k * 128:(k + 1) * 128, :], w=["W_outb"])
        lnG1 = P.sb("lnG1", [128, 1, D], F32)
        lnB1 = P.sb("lnB1", [128, 1, D], F32)
        P.dma("sp", lnG1[:, 0, :], ln_g[1:2, :].partition_broadcast(128), w=["lnG1"])
        P.dma("sp", lnB1[:, 0, :], ln_b[1:2, :].partition_broadcast(128), w=["lnB1"])
        Gp1 = P.sb("Gp1", [128, D], F32)
        Gs1 = P.sb("Gs1", [8, DEC_B, D], F32)
        P.dma("sp", Gp1[:, :], modscr[1, 0:1, 2 * D:3 * D].partition_broadcast(128), w=["Gp1"])
        for b in range(DEC_B):
            P.dma("sp", Gs1[0:8, b, :], modscr[1, 1 + b:2 + b, 2 * D:3 * D].partition_broadcast(8), w=["Gs1"])
        P.op("pool", lambda g_: g_.tensor_scalar_add(out=Gp1[:], in0=Gp1[:], scalar1=1.0), r=["Gp1"], w=["Gp1"])
        P.op("pool", lambda g_: g_.tensor_scalar_add(out=Gs1[:], in0=Gs1[:], scalar1=1.0), r=["Gs1"], w=["Gs1"])
        idxo3 = P.sb("idxo3", [128, 16], I32)
        P.dma("sp", idxo3[:], t_idx_own[:, :], w=["idxo3"])
        X1o = [P.sb("X1o%d" % i, [128, D], F32) for i in range(2)]
        OGl = [P.sb("OGl%d" % i, [128, D], BF16) for i in range(2)]
        OGT = P.sb("OGT", [128, NCH, 128], BF16)
        vo = [P.sb("vo%d" % i, [128, D], F32) for i in range(2)]
        yo = [P.sb("yo%d" % i, [128, D], F32) for i in range(2)]
        sto = [P.sb("sto%d" % i, [128, 16], F32) for i in range(2)]
        qi = 0
        for (tok0, nq, jl, sb_) in qtiles:
            i2 = qi % 2
            qi += 1
            xk, ok_, vk, yk, sk = "X1o%d" % i2, "OGl%d" % i2, "vo%d" % i2, "yo%d" % i2, "sto%d" % i2
            if jl is not None:
                P.gather(X1o[i2][:, :], x1scr[:, :], idxo3[:, jl:jl + 1], r=["idxo3"], w=[xk])
                Gt, gkey = Gp1[0:nq, :], "Gp1"
            else:
                P.dma("sp", X1o[i2][0:nq, :], x1s_scr[sb_ * 8:(sb_ + 1) * 8, :], w=[xk])
                Gt, gkey = Gs1[0:8, sb_, :], "Gs1"
            P.dma("sp", OGl[i2][0:nq, :], OGscr[tok0:tok0 + nq, :], w=[ok_])
            for k in range(NCH):
                P.op("pe", lambda t, k=k, nq=nq, i2=i2: t.transpose(psb[0][:, :].bitcast(BF16)[:, k * 128:k * 128 + nq],
                                                                   OGl[i2][0:nq, k * 128:(k + 1) * 128], ident_b[0:nq, 0:nq]),
                     r=[ok_, "ident_b"], w=[pskey(0)])
            P.op("dve", lambda v, nq=nq: v.tensor_copy(
                out=OGT[:, :, 0:nq], in_=psb[0][:, :].bitcast(BF16)[:, 0:1024].rearrange("p (k t) -> p k t", t=128)[:, :, 0:nq]),
                r=[pskey(0)], w=["OGT"])
            for half in range(2):
                pb = 2 + half
                for k in range(NCH):
                    P.op("pe", lambda t, k=k, half=half, pb=pb, nq=nq: t.matmul(
                        psb[pb][0:nq, :], lhsT=OGT[:, k, 0:nq], rhs=W_outb[:, k, half * 512:(half + 1) * 512],
                        start=(k == 0), stop=(k == NCH - 1)), r=["OGT", "W_outb"], w=[pskey(pb)])
                if jl is None and sb_ > 0:
                    pass
                P.op("dve", lambda v, half=half, pb=pb, nq=nq, i2=i2, Gt=Gt: v.tensor_tensor(
                    out=vo[i2][0:nq, half * 512:(half + 1) * 512], in0=psb[pb][0:nq, :], in1=Gt[:, half * 512:(half + 1) * 512],
                    op=ALU.mult), r=[pskey(pb), gkey], w=[vk])
            P.op("dve", lambda v, nq=nq, i2=i2: v.scalar_tensor_tensor(
                out=vo[i2][0:nq, :], in0=X1o[i2][0:nq, :], scalar=ALPHA, in1=vo[i2][0:nq, :], op0=ALU.mult, op1=ALU.add),
                r=[xk, vk], w=[vk])
            layernorm_tm(vo[i2], vk, yo[i2], yk, nq, 0, sto[i2], sk, lnG1, "lnG1", lnB1, "lnB1")
            if jl is not None:
                P.dma("sp", y_p[tok0:tok0 + nq, :], yo[i2][0:nq, :], r=[yk], w=[("y_p", tok0)], semkey=yk)
            else:
                P.dma("sp", y_s[sb_ * 8:(sb_ + 1) * 8, :], yo[i2][0:nq, :], r=[yk], w=[("y_s", sb_)], semkey=yk)

    for b in range(DEC_B):
        P.dma("act", o_win_s[b * 512:b * 512 + 504, :], swin[b * 512 + 8:(b + 1) * 512, :], w=[("o_win_s", b, 0)],
              semkey="winscopy")

    P.finish()
    print("instructions:", P.n_ins, {e: P.cnt[e] for e in P.ENG}, "dma sems:", len(P.dsem))
    return P


def _bf(x):
    return np.asarray(x, np.float32).astype(ml_dtypes.bfloat16)


def _split_pos(pos):
    pos = np.asarray(pos, np.int64)
    a = np.floor_divide(pos, 64)
    b = pos - 64 * a
    return a.astype(np.float32), b.astype(np.float32)


def _kaug(pos, valid):
    a, b = _split_pos(pos)
    n = a.shape[0]
    out = np.zeros((8, n), np.float32)
    out[0] = a; out[1] = a; out[2] = b; out[3] = b; out[4] = 1.0
    out[5] = np.where(valid, 0.0, NEGM)
    return _bf(out)


def _slopes_hi_lo():
    s = (2.0 ** (-8.0 * np.arange(1, 17) / 16.0)).astype(np.float32)
    hi = s.astype(ml_dtypes.bfloat16).astype(np.float32)
    lo = (s - hi).astype(ml_dtypes.bfloat16).astype(np.float32)
    return s, hi, lo


def _qaug(tq):
    s, hi, lo = _slopes_hi_lo()
    tq = np.asarray(tq, np.float32)
    nq = tq.shape[0]
    out = np.zeros((8, 16, nq), np.float32)
    out[0] = (64.0 * hi)[:, None]; out[1] = (64.0 * lo)[:, None]
    out[2] = hi[:, None]; out[3] = lo[:, None]
    out[4] = -(s[:, None] * tq[None, :])
    out[5] = 1.0
    return _bf(out)


def prompt_tables(k):
    cs = 2048 * k
    p = np.arange(128)
    t = {}
    t["idx_own"] = (cs + 128 * np.arange(16)[None, :] + p[:, None]).astype(np.int32)
    pos_pref = (np.arange(48 * 128) - cs)
    valid_pref = np.repeat(128 * np.arange(48) < cs, 128)
    pos_own = np.arange(2048)
    t["kaug_sel"] = np.concatenate([_kaug(pos_pref, valid_pref), _kaug(pos_own, np.ones(2048, bool))], axis=1)
    wtok = cs - 512 + np.arange(20 * 128)
    t["idx_win"] = np.maximum(wtok, 0).reshape(20, 128).T.astype(np.int32).copy()
    t["kaug_win"] = _kaug(wtok - cs, wtok >= 0)
    blk = np.concatenate([np.arange(96), 32 * k + np.arange(32)])
    valid = np.concatenate([np.arange(96) < 32 * k, np.ones(32, bool)])
    t["idx_slot"] = blk.astype(np.int32).reshape(128, 1)
    cend = 64 * blk + 63 - cs
    t["kaug_cmp"] = _kaug(cend, valid)
    tq = np.arange(2048)
    tabs = np.arange(2048) + cs
    cb = tabs // 64
    blk_abs = np.where(valid, blk, 10 ** 6)
    forced = (blk_abs[None, :] == 0) | (blk_abs[None, :] == cb[:, None]) | (blk_abs[None, :] == cb[:, None] - 1)
    caus = blk_abs[None, :] <= cb[:, None]
    fbn = np.where(forced, FORCEDV, 0.0) - np.where(caus, 0.0, 1.0)
    t["fbn"] = fbn.astype(np.float32).reshape(16, 128, 128)
    t["caus"] = caus.astype(np.float32).reshape(16, 128, 128)
    cend_abs = 64 * blk + 63
    tm = np.where(cend_abs[None, :] <= tabs[:, None], 0.0, NEGM)
    t["tm"] = tm.astype(np.float32).reshape(16, 128, 128)
    return t


def static_tables():
    t = {}
    t["qaug_p"] = _qaug(np.arange(2048)).reshape(8, 16 * 2048)
    j = np.arange(128)[:, None]
    i = np.arange(128)[None, :]
    tri = np.where(j > i, NEGM, 0.0)
    tri2 = np.where(j < i, NEGM, 0.0)
    t["tri_p"] = _bf(np.tile(tri, (1, 4)))
    t["tri2_p"] = _bf(np.tile(tri2, (1, 4)))
    i8 = np.arange(8)[None, :]
    t["tri_s"] = _bf(np.tile(np.where(j > i8, NEGM, 0.0), (1, 4)))
    t["tri2_s"] = _bf(np.tile(np.where(j < i8, NEGM, 0.0), (1, 4)))
    t["qaug_s"] = _qaug(np.arange(8)).reshape(8, 16 * 8)
    t["kaug_sel_s"] = _kaug(np.arange(8192) - 8192, np.ones(8192, bool))
    t["kaug_win_s"] = _kaug(np.arange(512) - 512, np.ones(512, bool))
    t["kaug_new_s"] = _kaug(np.arange(128), np.arange(128) < 8)
    cend = 64 * np.arange(128) + 63 - 8192
    t["kaug_cmp_s"] = _kaug(cend, np.ones(128, bool))
    fb = np.zeros((8, 128), np.float32)
    fb[:, 0] = FORCEDV; fb[:, 127] = FORCEDV
    t["fbn_s"] = fb
    t["caus_s"] = np.ones((8, 128), np.float32)
    t["tm_s"] = np.zeros((8, 128), np.float32)
    return t


def core_inputs(inp, c):
    b = c // 4
    sb = slice(4 * c, 4 * c + 4)
    f = np.ascontiguousarray
    d = {
        "xf": f(inp["x_prompt"][b]),
        "xs": f(inp["x_sample"][sb].reshape(NS_TOK, D)),
        "cvec": f(np.concatenate([inp["c_prompt"][b:b + 1], inp["c_sample"][sb]], axis=0)),
        "sh0": f(inp["state_h"][0, sb]),
        "sc0": f(inp["state_conv"][0, sb].reshape(DEC_B * 3, D)),
        "swin": f(inp["state_win"][sb].reshape(DEC_B * 512, 512)),
        "ptab": f(inp["page_table"][sb]).astype(np.int32),
        "ccmp": inp["cache_cmp"].reshape(-1, 512),
        "csel": inp["cache_sel"].reshape(-1, 512),
        "w_ada": inp["w_ada"], "b_ada": inp["b_ada"], "ln_g": inp["ln_g"], "ln_b": inp["ln_b"],
        "w_in_a": inp["w_in_a"][0], "conv_w": inp["conv_w_a"][0], "conv_b": inp["conv_b_a"],
        "w_r": inp["w_r_a"][0], "b_r": inp["b_r_a"], "w_i": inp["w_i_a"][0], "b_i": inp["b_i_a"],
        "lam": inp["lam_a"], "w_out_a": inp["w_out_a"][0], "w_kv": inp["w_kv"],
        "phi_pe": inp["phi_pe"].reshape(64, 128), "w_phi1": inp["w_phi1"], "b_phi1": inp["b_phi1"],
        "w_phi2": inp["w_phi2"], "b_phi2": inp["b_phi2"], "w_in_b": inp["w_in_b"][0],
        "b_gate": inp["b_gate_b"], "w_out_b": inp["w_out_b"][0],
    }
    for k2, v in prompt_tables(c % 4).items():
        d["t_" + k2] = v
    for k2, v in static_tables().items():
        d["t_" + k2] = v
    return {k: np.asarray(v) for k, v in d.items()}


def assemble(results, cores):
    y_prompt = np.zeros((2, SEQ, D), np.float32)
    y_sample = np.zeros((32, DEC_S, D), np.float32)
    new_cmp_p = np.zeros((2, SEQ, 4, 2, 64), np.float32)
    new_sel_p = np.zeros((2, SEQ, 4, 2, 64), np.float32)
    new_win_p = np.zeros((2, 512, 4, 2, 64), np.float32)
    new_h_p = np.zeros((1, 2, D), np.float32)
    new_conv_p = np.zeros((1, 2, 3, D), np.float32)
    new_cmp_s = np.zeros((32, DEC_S, 4, 2, 64), np.float32)
    new_sel_s = np.zeros((32, DEC_S, 4, 2, 64), np.float32)
    new_win_s = np.zeros((32, 512, 4, 2, 64), np.float32)
    new_h_s = np.zeros((1, 32, D), np.float32)
    new_conv_s = np.zeros((1, 32, 3, D), np.float32)
    for r, c in zip(results, cores):
        b, k = c // 4, c % 4
        sb = slice(4 * c, 4 * c + 4)
        y_prompt[b, k * 2048:(k + 1) * 2048] = r["y_p"]
        y_sample[sb] = r["y_s"].reshape(DEC_B, DEC_S, D)
        if k == 0:
            new_cmp_p[b] = r["o_cmp_p"].reshape(SEQ, 4, 2, 64)
            new_sel_p[b] = r["o_sel_p"].reshape(SEQ, 4, 2, 64)
            new_win_p[b] = r["o_win_p"].reshape(512, 4, 2, 64)
            new_h_p[0, b] = r["o_h_p"][0]
            new_conv_p[0, b] = r["o_conv_p"]
        new_cmp_s[sb] = r["o_cmp_s"].reshape(DEC_B, DEC_S, 4, 2, 64)
        new_sel_s[sb] = r["o_sel_s"].reshape(DEC_B, DEC_S, 4, 2, 64)
        new_win_s[sb] = r["o_win_s"].reshape(DEC_B, 512, 4, 2, 64)
        new_h_s[0, sb] = r["o_h_s"]
        new_conv_s[0, sb] = r["o_conv_s"].reshape(DEC_B, 3, D)
    return (y_prompt, y_sample, new_cmp_p, new_sel_p, new_win_p, new_h_p, new_conv_p,
            new_cmp_s, new_sel_s, new_win_s, new_h_s, new_conv_s)


def kernel(**inputs):
    inp = {k: np.asarray(v) for k, v in inputs.items()}
    n_phys = inp["cache_cmp"].shape[0]
    P = build(n_phys)
    cores = list(range(8))
    in_maps = [core_inputs(inp, c) for c in cores]
    res = run_bass_kernel_spmd(P.nc, in_maps, core_ids=cores)
    return assemble(res.results, cores)
```

```python
import contextlib
import numpy as np
import ml_dtypes
import concourse.bass as bass
import concourse.mybir as mybir
from concourse.bass_utils import run_bass_kernel_spmd

F32 = mybir.dt.float32
BF16 = mybir.dt.bfloat16
I32 = mybir.dt.int32
AF = mybir.ActivationFunctionType
ALU = mybir.AluOpType
AX = mybir.AxisListType

D = 1024
NCH = 8
SEQ = 8192
TT = 256
NSUB = TT // 128
DEC_B = 4
DEC_S = 8
NS_TOK = DEC_B * DEC_S
ALPHA = 4.0 ** 0.25
LN_EPS = 1e-5
RG_C = 8.0
NEGM = -30000.0
FORCEDV = 1.0e6


class Prog:
    ENG = ("pe", "act", "dve", "pool", "sp")

    def __init__(self):
        self.nc = bass.Bass("TRN2", target_bir_lowering=False)
        self.es = contextlib.ExitStack()
        nc = self.nc
        self.eng = {"pe": nc.tensor, "act": nc.scalar, "dve": nc.vector, "pool": nc.gpsimd, "sp": nc.sync}
        self.sem = {e: self.es.enter_context(nc.semaphore("s_" + e)) for e in self.ENG}
        self.cnt = {e: 0 for e in self.ENG}
        self.seen = {e: {} for e in self.ENG}
        self.dsem = {}
        self.dcnt = {}
        self.bufs = {}
        self.n_ins = 0
        self.stack = [self.es]

    def sb(self, name, shape, dt):
        return self.stack[-1].enter_context(self.nc.sbuf_tensor(name, list(shape), dt))

    def barrier(self):
        deps = {}
        for e2 in self.ENG:
            if self.cnt[e2]:
                deps[("eng", e2)] = self.cnt[e2]
        for k in self.dcnt:
            deps[("dma", k)] = self.dcnt[k]
        for e in self.ENG:
            self._wait(e, dict(deps))

    @contextlib.contextmanager
    def scope(self):
        st = contextlib.ExitStack()
        self.stack.append(st)
        try:
            yield
        finally:
            self.barrier()
            self.stack.pop()
            st.close()

    def ps(self, name, shape, dt):
        return self.es.enter_context(self.nc.psum_tensor(name, list(shape), dt))

    def dram(self, name, shape, dt, kind="Internal"):
        return self.nc.dram_tensor(name, list(shape), dt, kind=kind).ap()

    def _state(self, k):
        st = self.bufs.get(k)
        if st is None:
            st = self.bufs[k] = {"w": {}, "r": {}}
        return st

    def _deps(self, r, w):
        deps = {}
        for k in r:
            for s, v in self._state(k)["w"].items():
                deps[s] = max(deps.get(s, 0), v)
        for k in w:
            st = self._state(k)
            for s, v in st["w"].items():
                deps[s] = max(deps.get(s, 0), v)
            for s, v in st["r"].items():
                deps[s] = max(deps.get(s, 0), v)
        return deps

    def _wait(self, e, deps):
        eng = self.eng[e]
        seen = self.seen[e]
        for s, v in deps.items():
            if s[0] == "dma":
                v = max(v, self.dcnt[s[1]])
                if seen.get(s, 0) >= v:
                    continue
                eng.wait_ge(self.dsem[s[1]], v)
            else:
                if s[1] == e and False:
                    continue
                if seen.get(s, 0) >= v:
                    continue
                eng.wait_ge(self.sem[s[1]], v)
            seen[s] = v

    def _commit(self, me_src, me_val, r, w):
        for k in w:
            st = self._state(k)
            st["w"] = {me_src: me_val}
            st["r"] = {}
        for k in r:
            if k in w:
                continue
            st = self._state(k)
            st["r"][me_src] = max(st["r"].get(me_src, 0), me_val)

    def op(self, e, fn, r=(), w=()):
        w = list(w) + [k for k in r if isinstance(k, str) and k[:2] == "ps" and k[2:].isdigit() and k not in w]
        self._wait(e, self._deps(r, w))
        ins = fn(self.eng[e])
        self.cnt[e] += 1
        ins.then_inc(self.sem[e], 1)
        self._commit(("eng", e), self.cnt[e], r, w)
        self.n_ins += 1
        return ins

    def dma(self, q, out, in_, r=(), w=(), semkey=None, **kw):
        self._wait(q, self._deps(r, w))
        if semkey is None:
            semkey = (tuple(w) + tuple(r))[0]
        if semkey not in self.dsem:
            self.dsem[semkey] = self.es.enter_context(self.nc.semaphore("d%d" % len(self.dsem)))
            self.dcnt[semkey] = 0
        ins = self.eng[q].dma_start(out=out, in_=in_, **kw)
        self.dcnt[semkey] += 16
        ins.then_inc(self.dsem[semkey], 16)
        self._commit(("dma", semkey), self.dcnt[semkey], r, w)
        self.n_ins += 1
        return ins

    def gather(self, out, in_, idx_ap, r=(), w=(), semkey=None):
        q = "pool"
        self._wait(q, self._deps(r, w))
        if semkey is None:
            semkey = tuple(w)[0]
        if semkey not in self.dsem:
            self.dsem[semkey] = self.es.enter_context(self.nc.semaphore("d%d" % len(self.dsem)))
            self.dcnt[semkey] = 0
        ins = self.nc.gpsimd.indirect_dma_start(
            out=out, out_offset=None, in_=in_, in_offset=bass.IndirectOffsetOnAxis(ap=idx_ap, axis=0))
        self.dcnt[semkey] += 16
        ins.then_inc(self.dsem[semkey], 16)
        self._commit(("dma", semkey), self.dcnt[semkey], r, w)
        self.n_ins += 1
        return ins

    def finish(self):
        for e in ("sp",):
            deps = {}
            for k, st in self.bufs.items():
                for s, v in list(st["w"].items()) + list(st["r"].items()):
                    deps[s] = max(deps.get(s, 0), v)
            for k in self.dcnt:
                deps[("dma", k)] = self.dcnt[k]
            for e2 in self.ENG:
                if self.cnt[e2]:
                    deps[("eng", e2)] = self.cnt[e2]
            self._wait(e, deps)


def build(n_phys, stage=9):
    P = Prog()
    nc = P.nc
    es = P.es
    ctx_nc = nc.allow_non_contiguous_dma(reason="small strided parameter / state loads")
    es.enter_context(ctx_nc)

    def din(name, shape, dt=F32):
        return nc.dram_tensor(name, list(shape), dt, kind="ExternalInput").ap()

    def dout(name, shape, dt=F32):
        return nc.dram_tensor(name, list(shape), dt, kind="ExternalOutput").ap()

    xf = din("xf", [SEQ, D])
    xs = din("xs", [NS_TOK, D])
    cvec = din("cvec", [5, D])
    sh0 = din("sh0", [DEC_B, D])
    sc0 = din("sc0", [DEC_B * 3, D])
    swin = din("swin", [DEC_B * 512, 512])
    ptab = din("ptab", [DEC_B, 64], I32)
    ccmp = din("ccmp", [n_phys * 128, 512])
    csel = din("csel", [n_phys * 128, 512])
    w_ada = din("w_ada", [2, D, 3 * D])
    b_ada = din("b_ada", [2, 3 * D])
    ln_g = din("ln_g", [2, D])
    ln_b = din("ln_b", [2, D])
    w_in_a = din("w_in_a", [D, 2 * D])
    conv_w = din("conv_w", [4, D])
    conv_b = din("conv_b", [1, D])
    w_r = din("w_r", [8, 128, 128])
    b_r = din("b_r", [1, D])
    w_i = din("w_i", [8, 128, 128])
    b_i = din("b_i", [1, D])
    lam = din("lam", [1, D])
    w_out_a = din("w_out_a", [D, D])
    w_kv = din("w_kv", [D, 1536])
    phi_pe = din("phi_pe", [64, 128])
    w_phi1 = din("w_phi1", [2, 64, 64, 128])
    b_phi1 = din("b_phi1", [2, 128])
    w_phi2 = din("w_phi2", [2, 128, 64])
    b_phi2 = din("b_phi2", [2, 64])
    w_in_b = din("w_in_b", [D, 2096])
    b_gate = din("b_gate", [1, 48])
    w_out_b = din("w_out_b", [D, D])

    def dtab(name, shape, dt):
        return nc.dram_tensor(name, list(shape), dt, kind="ExternalInput").ap()
    t_idx_own = dtab("t_idx_own", [128, 16], I32)
    t_idx_win = dtab("t_idx_win", [128, 20], I32)
    t_idx_slot = dtab("t_idx_slot", [128, 1], I32)
    t_kaug_sel = dtab("t_kaug_sel", [8, 8192], BF16)
    t_kaug_win = dtab("t_kaug_win", [8, 2560], BF16)
    t_kaug_cmp = dtab("t_kaug_cmp", [8, 128], BF16)
    t_fbn = dtab("t_fbn", [16, 128, 128], F32)
    t_caus = dtab("t_caus", [16, 128, 128], F32)
    t_tm = dtab("t_tm", [16, 128, 128], F32)
    t_qaug_p = dtab("t_qaug_p", [8, 16 * 2048], BF16)
    t_tri_p = dtab("t_tri_p", [128, 512], BF16)
    t_tri2_p = dtab("t_tri2_p", [128, 512], BF16)
    t_tri_s = dtab("t_tri_s", [128, 32], BF16)
    t_tri2_s = dtab("t_tri2_s", [128, 32], BF16)
    t_qaug_s = dtab("t_qaug_s", [8, 128], BF16)
    t_kaug_sel_s = dtab("t_kaug_sel_s", [8, 8192], BF16)
    t_kaug_win_s = dtab("t_kaug_win_s", [8, 512], BF16)
    t_kaug_new_s = dtab("t_kaug_new_s", [8, 128], BF16)
    t_kaug_cmp_s = dtab("t_kaug_cmp_s", [8, 128], BF16)
    t_fbn_s = dtab("t_fbn_s", [8, 128], F32)
    t_caus_s = dtab("t_caus_s", [8, 128], F32)
    t_tm_s = dtab("t_tm_s", [8, 128], F32)

    y_p = dout("y_p", [2048, D])
    y_s = dout("y_s", [NS_TOK, D])
    o_cmp_p = dout("o_cmp_p", [SEQ, 512])
    o_sel_p = dout("o_sel_p", [SEQ, 512])
    o_win_p = dout("o_win_p", [512, 512])
    o_h_p = dout("o_h_p", [1, D])
    o_conv_p = dout("o_conv_p", [3, D])
    o_cmp_s = dout("o_cmp_s", [NS_TOK, 512])
    o_sel_s = dout("o_sel_s", [NS_TOK, 512])
    o_win_s = dout("o_win_s", [DEC_B * 512, 512])
    o_h_s = dout("o_h_s", [DEC_B, D])
    o_conv_s = dout("o_conv_s", [DEC_B * 3, D])

    modscr = P.dram("modscr", [2, 5, 3 * D], F32)
    x1scr = P.dram("x1scr", [SEQ, D], F32)
    x1s_scr = P.dram("x1s_scr", [NS_TOK, D], F32)
    winscr = P.dram("winscr", [SEQ, 512], F32)

    ident_b = P.sb("ident_b", [128, 128], BF16)
    ident_f = P.sb("ident_f", [128, 128], F32)
    for t, k in ((ident_b, "ident_b"), (ident_f, "ident_f")):
        P.op("pool", lambda g, t=t: g.memset(t[:], 0.0), w=[k])
        P.op("pool", lambda g, t=t: g.affine_select(out=t[:], in_=t[:], pattern=[[-1, 128]],
                                                    compare_op=ALU.not_equal, fill=1.0, base=0,
                                                    channel_multiplier=1), r=[k], w=[k])

    psb = [P.ps("ps%d" % i, [128, 512], F32) for i in range(8)]

    def pskey(i):
        return "ps%d" % i

    mod_fm = P.sb("mod_fm", [128, 2, 24, 8], F32)
    scA = P.scope()
    scA.__enter__()
    W_in = P.sb("W_in", [128, NCH, 2 * D], BF16)
    W_out = P.sb("W_out", [128, NCH, D], BF16)
    W_kv = P.sb("W_kv", [128, NCH, 1536], BF16)
    W_r = P.sb("W_r", [128, 8, 128], BF16)
    W_i = P.sb("W_i", [128, 8, 128], BF16)
    for k in range(NCH):
        P.dma("pool", W_in[:, k, :], w_in_a[k * 128:(k + 1) * 128, :], w=["W_in"])
    for k in range(NCH):
        P.dma("pool", W_out[:, k, :], w_out_a[k * 128:(k + 1) * 128, :], w=["W_out"])
    for k in range(NCH):
        P.dma("pool", W_kv[:, k, :], w_kv[k * 128:(k + 1) * 128, :], w=["W_kv"])
    P.dma("pool", W_r[:], w_r.rearrange("n c d -> c n d"), w=["W_r"])
    P.dma("pool", W_i[:], w_i.rearrange("n c d -> c n d"), w=["W_i"])

    pf = P.sb("pf", [128, 10, NCH], F32)
    for k in range(4):
        P.dma("sp", pf[:, k, :], conv_w[k:k + 1, :].rearrange("o (c p) -> p (o c)", p=128), w=["pf"])
    for j, src in ((4, conv_b), (5, b_r), (6, b_i), (7, lam)):
        P.dma("sp", pf[:, j, :], src[0:1, :].rearrange("o (c p) -> p (o c)", p=128), w=["pf"])
    P.op("act", lambda a: a.activation(out=pf[:, 9, :], in_=pf[:, 7, :], func=AF.Exp, scale=-1.0), r=["pf"], w=["pf"])
    P.op("act", lambda a: a.activation(out=pf[:, 9, :], in_=pf[:, 9, :], func=AF.Ln, bias=1.0), r=["pf"], w=["pf"])
    P.op("dve", lambda v: v.tensor_scalar_mul(out=pf[:, 7, :], in0=pf[:, 9, :], scalar1=-RG_C), r=["pf"], w=["pf"])
    P.op("dve", lambda v: v.tensor_scalar_mul(out=pf[:, 8, :], in0=pf[:, 9, :], scalar1=-2.0 * RG_C), r=["pf"], w=["pf"])

    lnG = P.sb("lnG", [128, 1, D], F32)
    lnB = P.sb("lnB", [128, 1, D], F32)
    for l in range(1):
        P.dma("sp", lnG[:, l, :], ln_g[l:l + 1, :].partition_broadcast(128), w=["lnG"])
        P.dma("sp", lnB[:, l, :], ln_b[l:l + 1, :].partition_broadcast(128), w=["lnB"])

    vt = [P.sb("vt%d" % i, [128, D], F32) for i in range(2)]
    c5, c5s = vt[0], vt[1]
    csT = P.sb("csT", [128, NCH, 8], BF16)
    P.dma("sp", c5[0:5, :], cvec[:, :], w=["vt0"])
    P.op("act", lambda a: a.activation(out=c5s[0:5, :], in_=c5[0:5, :], func=AF.Silu), r=["vt0"], w=["vt1"])
    for k in range(NCH):
        P.op("pe", lambda t, k=k: t.transpose(psb[0][:, k * 8:k * 8 + 5], c5s[0:5, k * 128:(k + 1) * 128],
                                              ident_f[0:5, 0:5]), r=["vt1", "ident_f"], w=[pskey(0)])
    P.op("dve", lambda v: v.tensor_copy(out=csT[:, :, 0:5],
                                        in_=psb[0][:, 0:64].rearrange("p (k e) -> p k e", e=8)[:, :, 0:5]),
         r=[pskey(0)], w=["csT"])
    AW = 256
    NA = 3 * D // AW
    wada_buf = [P.sb("wada%d" % i, [128, NCH, AW], BF16) for i in range(2)]
    modc = [P.sb("modc%d" % i, [5, AW], F32) for i in range(2)]
    badac = [P.sb("badac%d" % i, [5, AW], F32) for i in range(2)]
    it = 0
    for l in range(2):
        for n6 in range(NA):
            wb = wada_buf[it % 2]
            wk = "wada%d" % (it % 2)
            mk = "modc%d" % (it % 2)
            bk_ = "badac%d" % (it % 2)
            mc = modc[it % 2]
            bc = badac[it % 2]
            P.dma("pool", wb[:], w_ada[l, :, n6 * AW:(n6 + 1) * AW].rearrange("(k p) n -> p k n", p=128), w=[wk])
            P.dma("sp", bc[:], b_ada[l:l + 1, n6 * AW:(n6 + 1) * AW].partition_broadcast(5), w=[bk_])
            pb = 2 + (it % 2)
            for k in range(NCH):
                P.op("pe", lambda t, k=k, wb=wb, pb=pb: t.matmul(psb[pb][0:5, 0:AW], lhsT=csT[:, k, 0:5], rhs=wb[:, k, :],
                                                                 start=(k == 0), stop=(k == NCH - 1)),
                     r=["csT", wk], w=[pskey(pb)])
            P.op("dve", lambda v, mc=mc, bc=bc, pb=pb: v.tensor_tensor(
                out=mc[:], in0=psb[pb][0:5, 0:AW], in1=bc[:], op=ALU.add),
                r=[pskey(pb), bk_], w=[mk])
            P.dma("sp", modscr[l, :, n6 * AW:(n6 + 1) * AW], mc[:], r=[mk], w=[("modscr", l, n6)], semkey=mk)
            nq = AW // 128
            for q in range(nq):
                P.op("pe", lambda t, q=q, mc=mc: t.transpose(psb[1][:, q * 8:q * 8 + 5], mc[0:5, q * 128:(q + 1) * 128],
                                                             ident_f[0:5, 0:5]), r=[mk, "ident_f"], w=[pskey(1)])
            P.op("dve", lambda v, l=l, n6=n6, nq=nq: v.tensor_copy(
                out=mod_fm[:, l, n6 * nq:(n6 + 1) * nq, 0:5],
                in_=psb[1][:, 0:8 * nq].rearrange("p (k e) -> p k e", e=8)[:, :, 0:5]),
                r=[pskey(1)], w=["mod_fm"])
            it += 1
    P.op("dve", lambda v: v.tensor_scalar_add(out=mod_fm[:, :, 8:24, :], in0=mod_fm[:, :, 8:24, :], scalar1=1.0),
         r=["mod_fm"], w=["mod_fm"])
    Gp = P.sb("Gp", [128, 1, D], F32)
    Gs = P.sb("Gs", [NS_TOK, 1, D], F32)
    mod_keys = [("modscr", l, n6) for l in range(2) for n6 in range(NA)]
    for l in range(1):
        P.dma("sp", Gp[:, l, :], modscr[l, 0:1, 2 * D:3 * D].partition_broadcast(128), r=mod_keys, w=["Gp"])
        for b in range(DEC_B):
            P.dma("sp", Gs[b * 8:(b + 1) * 8, l, :], modscr[l, 1 + b:2 + b, 2 * D:3 * D].partition_broadcast(8),
                  r=mod_keys, w=["Gs"])
    P.op("pool", lambda g: g.tensor_scalar_add(out=Gp[:], in0=Gp[:], scalar1=1.0), r=["Gp"], w=["Gp"])
    P.op("pool", lambda g: g.tensor_scalar_add(out=Gs[:], in0=Gs[:], scalar1=1.0), r=["Gs"], w=["Gs"])

    xtok = [P.sb("xtok%d" % i, [128, NSUB, D], F32) for i in range(2)]
    xbf = P.sb("xbf", [128, NSUB, D], BF16)
    mT = P.sb("mT", [128, NCH, TT], BF16)
    xbe = P.sb("xbe", [128, NCH, 3 + TT], F32)
    xbe_s = P.sb("xbe_s", [128, NCH, DEC_B, 3 + DEC_S], F32)
    hprev = P.sb("hprev", [128, NCH], F32)
    h0s = P.sb("h0s", [128, NCH, DEC_B], F32)
    hlast_s = P.sb("hlast_s", [128, NCH, DEC_B], F32)
    NT = 2
    xc = [P.sb("xc%d" % i, [128, TT], F32) for i in range(NT)]
    xcb = [P.sb("xcb%d" % i, [128, TT], BF16) for i in range(NT)]
    zs = [P.sb("zs%d" % i, [128, TT], F32) for i in range(NT)]
    ra = [P.sb("ra%d" % i, [128, TT], F32) for i in range(NT)]
    ri = [P.sb("ri%d" % i, [128, TT], F32) for i in range(NT)]
    ga = [P.sb("ga%d" % i, [128, TT], F32) for i in range(NT)]
    bb = [P.sb("bb%d" % i, [128, TT], F32) for i in range(NT)]
    hs = [P.sb("hs%d" % i, [128, TT], F32) for i in range(NT)]
    yg = P.sb("yg", [128, NCH, TT], BF16)
    x1t = [P.sb("x1t%d" % i, [128, D], F32) for i in range(2)]
    x1b = [P.sb("x1b%d" % i, [128, D], BF16) for i in range(2)]
    x1T = P.sb("x1T", [128, NCH, TT], BF16)
    kvst = [P.sb("kvst%d" % i, [128, 1536], F32) for i in range(2)]
    stat = [P.sb("stat%d" % i, [128, 16], F32) for i in range(2)]

    P.op("pool", lambda g: g.memset(xbe[:, :, 0:3], 0.0), w=["xbe"])
    P.op("pool", lambda g: g.memset(hprev[:], 0.0), w=["hprev"])
    for n in range(NCH):
        for b in range(DEC_B):
            P.dma("sp", xbe_s[:, n, b, 0:3], sc0[b * 3:(b + 1) * 3, n * 128:(n + 1) * 128].rearrange("k p -> p k"),
                  w=["xbe_s"])
        P.dma("sp", h0s[:, n, :], sh0.rearrange("b (c p) -> c p b", p=128)[n], w=["h0s"])

    cnt = {"tile": 0, "ch": 0, "sub": 0}

    def layernorm_tm(vin, vkey, out, okey, np_, layer, st, skey, Gt_=None, gk_="lnG", Bt_=None, bk_="lnB"):
        Gt_ = lnG if Gt_ is None else Gt_
        Bt_ = lnB if Bt_ is None else Bt_
        P.op("dve", lambda v: v.bn_stats(out=st[0:np_, 0:6], in_=vin[0:np_, 0:512]), r=[vkey], w=[skey])
        P.op("dve", lambda v: v.bn_stats(out=st[0:np_, 6:12], in_=vin[0:np_, 512:1024]), r=[vkey], w=[skey])
        P.op("dve", lambda v: v.bn_aggr(out=st[0:np_, 12:14], in_=st[0:np_, 0:12]),
             r=[skey], w=[skey])
        P.op("dve", lambda v: v.tensor_scalar_add(out=st[0:np_, 14:15], in0=st[0:np_, 13:14], scalar1=LN_EPS),
             r=[skey], w=[skey])
        P.op("act", lambda a: a.activation(out=st[0:np_, 14:15], in_=st[0:np_, 14:15], func=AF.Sqrt), r=[skey], w=[skey])
        P.op("dve", lambda v: v.reciprocal(out=st[0:np_, 14:15], in_=st[0:np_, 14:15]), r=[skey], w=[skey])
        P.op("dve", lambda v: v.scalar_tensor_tensor(out=st[0:np_, 15:16], in0=st[0:np_, 12:13], scalar=-1.0,
                                                     in1=st[0:np_, 14:15], op0=ALU.mult, op1=ALU.mult),
             r=[skey], w=[skey])
        P.op("act", lambda a: a.activation(out=out[0:np_, :], in_=vin[0:np_, :], func=AF.Identity,
                                           scale=st[0:np_, 14:15], bias=st[0:np_, 15:16]),
             r=[vkey, skey], w=[okey])
        P.op("pool", lambda g: g.tensor_tensor(out=out[0:np_, :], in0=out[0:np_, :], in1=Gt_[0:np_, layer, :], op=ALU.mult),
             r=[okey, gk_], w=[okey])
        P.op("pool", lambda g: g.tensor_tensor(out=out[0:np_, :], in0=out[0:np_, :], in1=Bt_[0:np_, layer, :], op=ALU.add),
             r=[okey, bk_], w=[okey])

    import os
    CHPIPE = int(os.environ.get("CHPIPE", "1"))

    class L0Tile:
        def __init__(self, ti, sample):
            self.ti, self.sample = ti, sample
            if sample:
                self.ncols, self.nsub, self.np_ = NS_TOK, 1, NS_TOK
                self.segs = [(b * DEC_S, DEC_S, 1 + b) for b in range(DEC_B)]
            else:
                self.ncols, self.nsub, self.np_ = TT, NSUB, 128
                self.segs = [(0, TT, 0)]
            self.t0 = ti * TT
            self.xt = xtok[cnt["tile"] % 2]
            self.xk = "xtok%d" % (cnt["tile"] % 2)
            cnt["tile"] += 1
            self.cis = {}

        def front(self):
            ti, sample, ncols, nsub, np_, segs, t0, xt, xk = (self.ti, self.sample, self.ncols, self.nsub, self.np_, self.segs,
                                                              self.t0, self.xt, self.xk)
            if sample:
                P.dma("sp", xt[0:np_, 0, :], xs[:, :], w=[xk])
            else:
                for s in range(nsub):
                    P.dma("sp", xt[:, s, :], xf[t0 + s * 128:t0 + (s + 1) * 128, :], w=[xk])
            for s in range(nsub):
                P.op("pool", lambda g, s=s: g.tensor_copy(out=xbf[0:np_, s, :], in_=xt[0:np_, s, :]), r=[xk], w=["xbf"])
            for half in range(2):
                pb = half
                for kk in range(4):
                    k = half * 4 + kk
                    for s in range(nsub):
                        P.op("pe", lambda t, k=k, kk=kk, s=s, pb=pb: t.transpose(
                            psb[pb][:, :].bitcast(BF16)[:, kk * TT + s * 128:kk * TT + s * 128 + np_],
                            xbf[0:np_, s, k * 128:(k + 1) * 128], ident_b[0:np_, 0:np_]),
                            r=["xbf", "ident_b"], w=[pskey(pb)])
                for kk in range(4):
                    k = half * 4 + kk
                    for (c0, cn, mj) in segs:
                        P.op("act", lambda a, k=k, kk=kk, pb=pb, c0=c0, cn=cn, mj=mj: a.activation(
                            out=mT[:, k, c0:c0 + cn], in_=psb[pb][:, :].bitcast(BF16)[:, kk * TT + c0:kk * TT + c0 + cn],
                            func=AF.Identity, scale=mod_fm[:, 0, 8 + k, mj:mj + 1], bias=mod_fm[:, 0, k, mj:mj + 1]),
                            r=[pskey(pb), "mod_fm"], w=["mT"])

        def chunk_ab(self, n):
            ti, sample, ncols = self.ti, self.sample, self.ncols
            ci = cnt["ch"] % NT
            cnt["ch"] += 1
            self.cis[n] = ci
            pb = 2 + (n % 2)
            pk = pskey(pb)
            for k in range(NCH):
                P.op("pe", lambda t, k=k, n=n, pb=pb: t.matmul(psb[pb][:, 0:ncols], lhsT=W_in[:, k, n * 128:(n + 1) * 128],
                                                               rhs=mT[:, k, 0:ncols], start=(k == 0), stop=(k == NCH - 1)),
                     r=["W_in", "mT"], w=[pk])
            for k in range(NCH):
                P.op("pe", lambda t, k=k, n=n, pb=pb: t.matmul(psb[pb][:, 256:256 + ncols],
                                                               lhsT=W_in[:, k, D + n * 128:D + (n + 1) * 128],
                                                               rhs=mT[:, k, 0:ncols], start=(k == 0), stop=(k == NCH - 1)),
                     r=["W_in", "mT"], w=[pk])
            if sample:
                xe = xbe_s[:, n, :, :]
                xek = "xbe_s"
                P.op("dve", lambda v, pb=pb, xe=xe: v.tensor_copy(
                    out=xe[:, :, 3:3 + DEC_S], in_=psb[pb][:, 0:ncols].rearrange("p (b t) -> p b t", t=DEC_S)),
                    r=[pk], w=[xek])
                sh = lambda k: xe[:, :, k:k + DEC_S]
                v3 = lambda ap: ap[:, 0:ncols].rearrange("p (b t) -> p b t", t=DEC_S)
            else:
                xe = xbe[:, n, :]
                xek = ("xbe", n)
                if ti > 0:
                    P.op("dve", lambda v, xe=xe: v.tensor_copy(out=xe[:, 0:3], in_=xe[:, TT:TT + 3]), r=[xek], w=[xek])
                P.op("dve", lambda v, pb=pb, xe=xe: v.tensor_copy(out=xe[:, 3:3 + TT], in_=psb[pb][:, 0:TT]), r=[pk], w=[xek])
                sh = lambda k: xe[:, k:k + TT]
                v3 = lambda ap: ap[:, 0:ncols]
            zk = "zs%d" % ci
            P.op("act", lambda a, pb=pb, ci=ci: a.activation(out=zs[ci][:, 0:ncols], in_=psb[pb][:, 256:256 + ncols],
                                                             func=AF.Silu), r=[pk], w=[zk])
            ck = "xc%d" % ci
            P.op("dve", lambda v, ci=ci, n=n: v.tensor_scalar(out=v3(xc[ci]), in0=sh(0), scalar1=pf[:, 0, n:n + 1],
                                                              scalar2=pf[:, 4, n:n + 1], op0=ALU.mult, op1=ALU.add),
                 r=[xek, "pf"], w=[ck])
            for k in range(1, 4):
                P.op("dve", lambda v, ci=ci, n=n, k=k: v.scalar_tensor_tensor(
                    out=v3(xc[ci]), in0=sh(k), scalar=pf[:, k, n:n + 1], in1=v3(xc[ci]), op0=ALU.mult, op1=ALU.add),
                    r=[xek, "pf", ck], w=[ck])
            cbk = "xcb%d" % ci
            P.op("pool", lambda g, ci=ci: g.tensor_copy(out=xcb[ci][:, 0:ncols], in_=xc[ci][:, 0:ncols]), r=[ck], w=[cbk])

        def chunk_cde(self, n):
            ti, sample, ncols = self.ti, self.sample, self.ncols
            ci = self.cis[n]
            zk, ck, cbk = "zs%d" % ci, "xc%d" % ci, "xcb%d" % ci
            pg = 4 + (n % 2)
            pgk = pskey(pg)
            P.op("pe", lambda t, n=n, ci=ci, pg=pg: t.matmul(psb[pg][:, 0:ncols], lhsT=W_r[:, n, :], rhs=xcb[ci][:, 0:ncols],
                                                             start=True, stop=True), r=["W_r", cbk], w=[pgk])
            P.op("pe", lambda t, n=n, ci=ci, pg=pg: t.matmul(psb[pg][:, 256:256 + ncols], lhsT=W_i[:, n, :],
                                                             rhs=xcb[ci][:, 0:ncols], start=True, stop=True),
                 r=["W_i", cbk], w=[pgk])
            rk, ik, gk, bk, hk = "ra%d" % ci, "ri%d" % ci, "ga%d" % ci, "bb%d" % ci, "hs%d" % ci
            P.op("act", lambda a, n=n, ci=ci, pg=pg: a.activation(out=ra[ci][:, 0:ncols], in_=psb[pg][:, 0:ncols],
                                                                  func=AF.Sigmoid, bias=pf[:, 5, n:n + 1]),
                 r=[pgk, "pf"], w=[rk])
            P.op("act", lambda a, n=n, ci=ci, pg=pg: a.activation(out=ri[ci][:, 0:ncols], in_=psb[pg][:, 256:256 + ncols],
                                                                  func=AF.Sigmoid, bias=pf[:, 6, n:n + 1]),
                 r=[pgk, "pf"], w=[ik])
            P.op("act", lambda a, n=n, ci=ci: a.activation(out=ga[ci][:, 0:ncols], in_=ra[ci][:, 0:ncols], func=AF.Exp,
                                                           scale=pf[:, 8, n:n + 1]), r=[rk, "pf"], w=[gk])
            P.op("act", lambda a, n=n, ci=ci: a.activation(out=ra[ci][:, 0:ncols], in_=ra[ci][:, 0:ncols], func=AF.Exp,
                                                           scale=pf[:, 7, n:n + 1]), r=[rk, "pf"], w=[rk])
            P.op("dve", lambda v, ci=ci: v.tensor_scalar(out=ga[ci][:, 0:ncols], in0=ga[ci][:, 0:ncols], scalar1=-1.0,
                                                         scalar2=1.0, op0=ALU.mult, op1=ALU.add), r=[gk], w=[gk])
            P.op("dve", lambda v, ci=ci: v.tensor_scalar_max(out=ga[ci][:, 0:ncols], in0=ga[ci][:, 0:ncols], scalar1=0.0),
                 r=[gk], w=[gk])
            P.op("act", lambda a, ci=ci: a.activation(out=ga[ci][:, 0:ncols], in_=ga[ci][:, 0:ncols], func=AF.Sqrt),
                 r=[gk], w=[gk])
            P.op("pool", lambda g, ci=ci: g.tensor_tensor(out=bb[ci][:, 0:ncols], in0=ri[ci][:, 0:ncols],
                                                          in1=xc[ci][:, 0:ncols], op=ALU.mult), r=[ik, ck], w=[bk])
            P.op("dve", lambda v, ci=ci: v.tensor_tensor(out=bb[ci][:, 0:ncols], in0=bb[ci][:, 0:ncols],
                                                         in1=ga[ci][:, 0:ncols], op=ALU.mult), r=[bk, gk], w=[bk])
            if sample:
                for b in range(DEC_B):
                    P.op("dve", lambda v, ci=ci, n=n, b=b: v.tensor_tensor_scan(
                        out=hs[ci][:, b * DEC_S:(b + 1) * DEC_S], data0=ra[ci][:, b * DEC_S:(b + 1) * DEC_S],
                        data1=bb[ci][:, b * DEC_S:(b + 1) * DEC_S], initial=h0s[:, n, b:b + 1], op0=ALU.mult, op1=ALU.add),
                        r=[rk, bk, "h0s"], w=[hk])
                P.op("dve", lambda v, ci=ci, n=n: v.tensor_copy(
                    out=hlast_s[:, n, :], in_=hs[ci][:, 0:ncols].rearrange("p (b t) -> p b t", t=DEC_S)[:, :, DEC_S - 1]),
                    r=[hk], w=["hlast_s"])
            else:
                P.op("dve", lambda v, ci=ci, n=n: v.tensor_tensor_scan(
                    out=hs[ci][:, 0:TT], data0=ra[ci][:, 0:TT], data1=bb[ci][:, 0:TT], initial=hprev[:, n:n + 1],
                    op0=ALU.mult, op1=ALU.add), r=[rk, bk, ("hprev", n)], w=[hk])
                P.op("dve", lambda v, ci=ci, n=n: v.tensor_copy(out=hprev[:, n:n + 1], in_=hs[ci][:, TT - 1:TT]),
                     r=[hk], w=[("hprev", n)])
            P.op("pool", lambda g, ci=ci, n=n: g.tensor_tensor(out=yg[:, n, 0:ncols], in0=hs[ci][:, 0:ncols],
                                                               in1=zs[ci][:, 0:ncols], op=ALU.mult), r=[hk, zk], w=["yg"])

        def chunks(self, lo, hi):
            if CHPIPE == 0:
                for n in range(lo, hi):
                    self.chunk_ab(n)
                    self.chunk_cde(n)
                return
            for n in range(lo, hi):
                self.chunk_ab(n)
                if n - 1 >= 0:
                    self.chunk_cde(n - 1)
            if hi == NCH:
                self.chunk_cde(NCH - 1)

        def outproj_ln(self, subs=None):
            ti, sample, nsub, np_, t0, xt, xk = self.ti, self.sample, self.nsub, self.np_, self.t0, self.xt, self.xk
            if not hasattr(self, "sis"):
                self.sis = {}
            for s in (range(nsub) if subs is None else subs):
                si = cnt["sub"] % 2
                cnt["sub"] += 1
                self.sis[s] = si
                vk, x1k, x1bk, stk = "vt%d" % si, "x1t%d" % si, "x1b%d" % si, "stat%d" % si
                for h in range(2):
                    pb = 4 + h
                    for k in range(NCH):
                        P.op("pe", lambda t, k=k, h=h, s=s, pb=pb: t.matmul(
                            psb[pb][0:np_, :], lhsT=yg[:, k, s * 128:s * 128 + np_], rhs=W_out[:, k, h * 512:(h + 1) * 512],
                            start=(k == 0), stop=(k == NCH - 1)), r=["yg", "W_out"], w=[pskey(pb)])
                    G = Gs if sample else Gp
                    P.op("dve", lambda v, h=h, pb=pb, si=si, G=G: v.tensor_tensor(
                        out=vt[si][0:np_, h * 512:(h + 1) * 512], in0=psb[pb][0:np_, :], in1=G[0:np_, 0, h * 512:(h + 1) * 512],
                        op=ALU.mult), r=[pskey(pb), "Gs" if sample else "Gp"], w=[vk])
                P.op("dve", lambda v, si=si, s=s: v.scalar_tensor_tensor(
                    out=vt[si][0:np_, :], in0=xt[0:np_, s, :], scalar=ALPHA, in1=vt[si][0:np_, :], op0=ALU.mult, op1=ALU.add),
                    r=[xk, vk], w=[vk])
                layernorm_tm(vt[si], vk, x1t[si], x1k, np_, 0, stat[si], stk)
                if not sample:
                    P.dma("sp", x1scr[t0 + s * 128:t0 + (s + 1) * 128, :], x1t[si][:, :], r=[x1k], w=[("x1scr", ti, s)], semkey=x1k)
                else:
                    P.dma("sp", x1s_scr[:, :], x1t[si][0:np_, :], r=[x1k], w=["x1s_scr"], semkey=x1k)
                P.op("act", lambda a, si=si: a.activation(out=x1b[si][0:np_, :], in_=x1t[si][0:np_, :], func=AF.Identity),
                     r=[x1k], w=[x1bk])

        def tail(self, subs=None, fin=True):
            ti, sample, nsub, np_, t0 = self.ti, self.sample, self.nsub, self.np_, self.t0
            for s in (range(nsub) if subs is None else subs):
                si = self.sis[s]
                x1bk, kvk = "x1b%d" % si, "kvst%d" % si
                pb = 6 + (s % 2)
                for k in range(NCH):
                    P.op("pe", lambda t, k=k, si=si, pb=pb: t.transpose(
                        psb[pb][:, :].bitcast(BF16)[:, k * 128:k * 128 + np_], x1b[si][0:np_, k * 128:(k + 1) * 128],
                        ident_b[0:np_, 0:np_]), r=[x1bk, "ident_b"], w=[pskey(pb)])
                P.op("dve", lambda v, pb=pb, s=s: v.tensor_copy(
                    out=x1T[:, :, s * 128:s * 128 + np_],
                    in_=psb[pb][:, :].bitcast(BF16)[:, 0:1024].rearrange("p (k t) -> p k t", t=128)[:, :, 0:np_]),
                    r=[pskey(pb)], w=[("x1T", s)])
                for c3 in range(3):
                    pb = c3 % 2
                    for k in range(NCH):
                        P.op("pe", lambda t, k=k, c3=c3, s=s, pb=pb: t.matmul(
                            psb[pb][0:np_, :], lhsT=x1T[:, k, s * 128:s * 128 + np_], rhs=W_kv[:, k, c3 * 512:(c3 + 1) * 512],
                            start=(k == 0), stop=(k == NCH - 1)), r=[("x1T", s), "W_kv"], w=[pskey(pb)])
                    P.op("act", lambda a, c3=c3, pb=pb, si=si: a.activation(
                        out=kvst[si][0:np_, c3 * 512:(c3 + 1) * 512], in_=psb[pb][0:np_, :], func=AF.Identity),
                        r=[pskey(pb)], w=[kvk])
                if sample:
                    P.dma("sp", o_cmp_s[:, :], kvst[si][0:np_, 0:512], r=[kvk], w=["o_cmp_s"], semkey=kvk)
                    P.dma("sp", o_sel_s[:, :], kvst[si][0:np_, 512:1024], r=[kvk], w=["o_sel_s"], semkey=kvk)
                    for b in range(DEC_B):
                        P.dma("sp", o_win_s[b * 512 + 504:(b + 1) * 512, :], kvst[si][b * 8:(b + 1) * 8, 1024:1536],
                              r=[kvk], w=[("o_win_s", b, 1)], semkey=kvk)
                else:
                    r0 = t0 + s * 128
                    P.dma("sp", o_cmp_p[r0:r0 + 128, :], kvst[si][:, 0:512], r=[kvk], w=[("o_cmp_p", ti, s)], semkey=kvk)
                    P.dma("sp", o_sel_p[r0:r0 + 128, :], kvst[si][:, 512:1024], r=[kvk], w=[("o_sel_p", ti, s)], semkey=kvk)
                    P.dma("sp", winscr[r0:r0 + 128, :], kvst[si][:, 1024:1536], r=[kvk], w=[("winscr", ti, s)], semkey=kvk)
                    if r0 >= SEQ - 512:
                        P.dma("sp", o_win_p[r0 - (SEQ - 512):r0 - (SEQ - 512) + 128, :], kvst[si][:, 1024:1536],
                              r=[kvk], w=[("o_win_p", ti, s)], semkey=kvk)
            if sample and fin:
                for n in range(NCH):
                    P.dma("sp", o_h_s.rearrange("b (c p) -> c p b", p=128)[n], hlast_s[:, n, :], r=["hlast_s"], w=[("o_h_s", n)],
                          semkey="hlast_s")
                    for b in range(DEC_B):
                        P.dma("sp", o_conv_s[b * 3:(b + 1) * 3, n * 128:(n + 1) * 128].rearrange("k p -> p k"),
                              xbe_s[:, n, b, DEC_S:DEC_S + 3], r=["xbe_s"], w=[("o_conv_s", n, b)], semkey="xbe_s_o")

        def final_state(self):
            P.dma("sp", o_h_p[0:1, :].rearrange("o (c p) -> p (o c)", p=128), hprev[:, :],
                  r=[("hprev", n) for n in range(NCH)], w=["o_h_p"], semkey="hprev_o")
            for n in range(NCH):
                P.dma("sp", o_conv_p.rearrange("k (c p) -> c p k", p=128)[n], xbe[:, n, TT:TT + 3], r=[("xbe", n)],
                      w=[("o_conv_p", n)], semkey="xbe_o")

    import os
    L0PIPE = int(os.environ.get("L0PIPE", "2"))
    n_ptiles = SEQ // TT if stage >= 1 else 2
    if L0PIPE == -1:
        for i in range(n_ptiles + 1):
            tl = L0Tile(0, True) if i == 0 else L0Tile(i - 1, False)
            tl.front()
            tl.chunks(0, NCH)
            for s_ in range(tl.nsub):
                tl.outproj_ln([s_])
                tl.tail([s_], fin=(s_ == tl.nsub - 1))
        tl.final_state()
    elif L0PIPE == 0:
        seq = [L0Tile(0, True)] + [None] * n_ptiles
        for i in range(n_ptiles + 1):
            tl = seq[i] if i == 0 else L0Tile(i - 1, False)
            tl.front()
            tl.chunks(0, NCH)
            tl.outproj_ln()
            tl.tail()
        tl.final_state()
    elif L0PIPE == 1:
        cur = L0Tile(0, True)
        cur.front()
        for i in range(n_ptiles + 1):
            cur.chunks(0, NCH)
            nxt = L0Tile(i, False) if i < n_ptiles else None
            if nxt is not None:
                nxt.front()
            for s_ in range(cur.nsub):
                cur.outproj_ln([s_])
                cur.tail([s_], fin=(s_ == cur.nsub - 1))
            last = cur
            cur = nxt
        last.final_state()
    else:
        tiles = [L0Tile(0, True)]
        tiles[0].front()
        tiles[0].chunks(0, NCH)
        tiles[0].outproj_ln()
        nxt = L0Tile(0, False)
        nxt.front()
        prev = tiles[0]
        for ti in range(n_ptiles):
            cur = nxt
            cur.chunks(0, NCH // 2)
            prev.tail()
            cur.chunks(NCH // 2, NCH)
            if ti + 1 < n_ptiles:
                nxt = L0Tile(ti + 1, False)
                nxt.front()
            cur.outproj_ln()
            prev = cur
        prev.tail()
        prev.final_state()

    scA.__exit__(None, None, None)
    cmpscr_p = P.dram("cmpscr_p", [128, 512], F32)
    cmpscr_s = P.dram("cmpscr_s", [DEC_B * 128, 512], F32)
    do_sample = stage >= 3
    with P.scope():
        W1r = P.sb("W1r", [128, 2, 64, 128], BF16)
        CB2s = [P.sb("CB2_%d" % i, [128, 64, 512], BF16) for i in range(2)]
        cbi = {"i": 0}
        Hh = P.sb("Hh", [128, 2, 2, 256], BF16)
        W2 = P.sb("W2", [128, 2, 64], BF16)
        PEsb = P.sb("PEsb", [64, 128], BF16)
        bias1 = P.sb("bias1", [128, 4], F32)
        b2bc = P.sb("b2bc", [64, 2, 4, 128], F32)
        CS = P.sb("CS", [64, 2, 512], F32)
        IDXf = P.sb("IDXf", [128, DEC_B * 64], F32)
        IDXi = P.sb("IDXi", [128, DEC_B * 64], I32)
        iop = P.sb("iop", [128, 2], I32)
        iopf = P.sb("iopf", [128, 2], F32)
        for c in range(2):
            for half in range(2):
                P.dma("pool", W1r[half * 64:(half + 1) * 64, c, :, :], w_phi1[c], w=["W1r"])
            P.dma("pool", W2[:, c, :], w_phi2[c], w=["W2"])
            P.dma("sp", bias1[:, c:c + 1], b_phi1[c:c + 1, :].rearrange("o p -> p o"), w=["bias1"])
        P.dma("pool", PEsb[:], phi_pe[:, :], w=["PEsb"])
        for nl in range(2):
            for g in range(4):
                P.dma("sp", b2bc[:, nl, g, :], b_phi2.rearrange("c d -> (c d)").rearrange("(o n) -> o n", o=1).partition_broadcast(64),
                      w=["b2bc"])
        for c in range(2):
            for d in range(64):
                P.op("pe", lambda t, c=c, d=d: t.matmul(psb[0][:, c:c + 1], lhsT=W1r[0:64, c, d, :],
                                                        rhs=PEsb[0:64, c * 64 + d:c * 64 + d + 1],
                                                        start=(d == 0), stop=(d == 63)), r=["W1r", "PEsb"], w=[pskey(0)])
        P.op("dve", lambda v: v.tensor_tensor(out=bias1[:, 0:2], in0=psb[0][:, 0:2], in1=bias1[:, 0:2], op=ALU.add),
             r=[pskey(0), "bias1"], w=["bias1"])
        P.op("pool", lambda g_: g_.iota(out=iop[:, 0:1], pattern=[[0, 1]], base=0, channel_multiplier=1), w=["iop"])
        P.op("dve", lambda v: v.tensor_copy(out=iopf[:, 0:1], in_=iop[:, 0:1]), r=["iop"], w=["iopf"])
        P.dma("sp", IDXi[:], ptab.rearrange("b n -> (b n)").rearrange("(o n) -> o n", o=1).partition_broadcast(128), w=["IDXi"])
        P.op("dve", lambda v: v.tensor_copy(out=IDXf[:], in_=IDXi[:]), r=["IDXi"], w=["IDXf"])
        P.op("dve", lambda v: v.tensor_scalar(out=IDXf[:], in0=IDXf[:], scalar1=128.0, scalar2=iopf[:, 0:1],
                                              op0=ALU.mult, op1=ALU.add), r=["IDXf", "iopf"], w=["IDXf"])
        P.op("dve", lambda v: v.tensor_copy(out=IDXi[:], in_=IDXf[:]), r=["IDXf"], w=["IDXi"])
        idxscr = P.dram("idxscr", [128, DEC_B * 64], I32)
        P.dma("sp", idxscr[:, :], IDXi[:], r=["IDXi"], w=["idxscr"], semkey="IDXi_o")

        def compress(load_pages, out_rows, okey):
            CB2 = CB2s[cbi["i"] % 2]
            cbk = "CB2_%d" % (cbi["i"] % 2)
            cbi["i"] += 1
            CB2v = CB2[:].rearrange("p n (g c d) -> p n g c d", g=4, c=2)
            load_pages(CB2, cbk)
            for c in range(2):
                for nl in range(2):
                    pb = c * 2 + nl
                    for d in range(64):
                        P.op("pe", lambda t, c=c, nl=nl, d=d, pb=pb: t.matmul(
                            psb[pb][:, 0:256], lhsT=W1r[nl * 64:(nl + 1) * 64, c, d, :],
                            rhs=CB2v[nl * 64:(nl + 1) * 64, :, :, c, d], start=(d == 0), stop=(d == 63)),
                            r=["W1r", cbk], w=[pskey(pb)])
                    P.op("act", lambda a, c=c, nl=nl, pb=pb: a.activation(out=Hh[:, c, nl, :], in_=psb[pb][:, 0:256], func=AF.Silu,
                                                                          bias=bias1[:, c:c + 1]), r=[pskey(pb), "bias1"], w=["Hh"])
            Hv = Hh[:].rearrange("p c n (pg g) -> p c n pg g", g=4)
            for nl in range(2):
                pb = 4 + nl
                for g in range(4):
                    for c in range(2):
                        col = (g * 2 + c) * 64
                        P.op("pe", lambda t, nl=nl, g=g, c=c, pb=pb, col=col: t.matmul(
                            psb[pb][0:64, col:col + 64], lhsT=Hv[:, c, nl, :, g], rhs=W2[:, c, :], start=True, stop=True),
                            r=["Hh", "W2"], w=[pskey(pb)])
                P.op("dve", lambda v, nl=nl, pb=pb: v.tensor_tensor(
                    out=CS[:, nl, :], in0=psb[pb][0:64, :], in1=b2bc[:, nl, :, :].rearrange("p g f -> p (g f)"), op=ALU.add),
                    r=[pskey(pb), "b2bc"], w=["CS"])
            P.dma("sp", out_rows.rearrange("(pg n) f -> pg n f", n=2), CS[:], r=["CS"], w=[okey], semkey="CS")

        def load_prompt_pages(CB2, cbk):
            for pg in range(64):
                P.dma("pool", CB2[:, pg, :], o_cmp_p[pg * 128:(pg + 1) * 128, :], r=[("o_cmp_p", pg // NSUB, pg % NSUB)], w=[cbk])

        if stage >= 2:
            compress(load_prompt_pages, cmpscr_p[:, :], "cmpscr_p")
        if do_sample:
            for b in range(DEC_B):
                def load_sample_pages(CB2, cbk, b=b):
                    for pg in range(64):
                        P.gather(CB2[:, pg, :], ccmp[:, :], IDXi[:, b * 64 + pg:b * 64 + pg + 1], r=["IDXi"], w=[cbk])
                compress(load_sample_pages, cmpscr_s[b * 128:(b + 1) * 128, :], ("cmpscr_s", b))

    NTOK = 2048 + NS_TOK
    QTscr = P.dram("QTscr", [4, 64, 4, NTOK], BF16)
    ZSscr = P.dram("ZSscr", [NTOK, D], BF16)
    GLscr = P.dram("GLscr", [NTOK, 48], F32)
    OGscr = P.dram("OGscr", [NTOK, D], BF16)
    qtiles = [(jl * 128, 128, jl, None) for jl in range(16)] if stage >= 2 else []
    if do_sample:
        qtiles += [(2048 + b * 8, 8, None, b) for b in range(DEC_B)]
    if stage == 2.5:
        qtiles = qtiles[:2]

    with P.scope():
        W_inb = P.sb("W_inb", [128, NCH, 2096], BF16)
        for k in range(NCH):
            P.dma("pool", W_inb[:, k, :], w_in_b[k * 128:(k + 1) * 128, :], w=["W_inb"])
        bgbc = P.sb("bgbc", [128, 48], F32)
        P.dma("sp", bgbc[:], b_gate[0:1, :].partition_broadcast(128), w=["bgbc"])
        idxo = P.sb("idxo", [128, 16], I32)
        P.dma("sp", idxo[:], t_idx_own[:, :], w=["idxo"])
        X1 = [P.sb("X1_%d" % i, [128, D], F32) for i in range(2)]
        X1b = P.sb("X1b", [128, D], BF16)
        m1T = P.sb("m1T", [128, NCH, 128], BF16)
        QTst = [P.sb("QTst%d" % i, [64, 4, 128], BF16) for i in range(2)]
        ZSt = [P.sb("ZSt%d" % i, [128, D], BF16) for i in range(2)]
        GLt = [P.sb("GLt%d" % i, [128, 48], F32) for i in range(2)]
        qi = 0
        for (tok0, nq, jl, sb_) in qtiles:
            i2 = qi % 2
            xk = "X1_%d" % i2
            if jl is not None:
                P.gather(X1[i2][:, :], x1scr[:, :], idxo[:, jl:jl + 1], r=["idxo"], w=[xk])
                mj = 0
            else:
                P.dma("sp", X1[i2][0:nq, :], x1s_scr[sb_ * 8:(sb_ + 1) * 8, :], w=[xk])
                mj = 1 + sb_
            P.op("pool", lambda g_, i2=i2, nq=nq: g_.tensor_copy(out=X1b[0:nq, :], in_=X1[i2][0:nq, :]), r=[xk], w=["X1b"])
            for k in range(NCH):
                P.op("pe", lambda t, k=k, nq=nq: t.transpose(psb[0][:, :].bitcast(BF16)[:, k * 128:k * 128 + nq],
                                                             X1b[0:nq, k * 128:(k + 1) * 128], ident_b[0:nq, 0:nq]),
                     r=["X1b", "ident_b"], w=[pskey(0)])
            for k in range(NCH):
                P.op("act", lambda a, k=k, nq=nq, mj=mj: a.activation(
                    out=m1T[:, k, 0:nq], in_=psb[0][:, :].bitcast(BF16)[:, k * 128:k * 128 + nq], func=AF.Identity,
                    scale=mod_fm[:, 1, 8 + k, mj:mj + 1], bias=mod_fm[:, 1, k, mj:mj + 1]), r=[pskey(0), "mod_fm"], w=["m1T"])
            for g in range(4):
                pb = 2 + (g % 2)
                qk = "QTst%d" % (g % 2)
                for hh in range(4):
                    for k in range(NCH):
                        P.op("pe", lambda t, k=k, hh=hh, g=g, pb=pb, nq=nq: t.matmul(
                            psb[pb][0:64, hh * 128:hh * 128 + nq], lhsT=W_inb[:, k, (4 * g + hh) * 64:(4 * g + hh + 1) * 64],
                            rhs=m1T[:, k, 0:nq], start=(k == 0), stop=(k == NCH - 1)), r=["W_inb", "m1T"], w=[pskey(pb)])
                P.op("dve", lambda v, g=g, pb=pb, nq=nq: v.tensor_scalar_mul(
                    out=QTst[g % 2][:, :, 0:nq], in0=psb[pb][0:64, :].rearrange("p (h q) -> p h q", h=4)[:, :, 0:nq],
                    scalar1=0.125), r=[pskey(pb)], w=[qk])
                P.dma("sp", QTscr[g, :, :, tok0:tok0 + nq], QTst[g % 2][:, :, 0:nq], r=[qk], w=[("QTscr", g, tok0)], semkey=qk)
            zk = "ZSt%d" % i2
            for half in range(2):
                pb = 4 + half
                for k in range(NCH):
                    P.op("pe", lambda t, k=k, half=half, pb=pb, nq=nq: t.matmul(
                        psb[pb][0:nq, :], lhsT=m1T[:, k, 0:nq], rhs=W_inb[:, k, D + half * 512:D + (half + 1) * 512],
                        start=(k == 0), stop=(k == NCH - 1)), r=["W_inb", "m1T"], w=[pskey(pb)])
                P.op("act", lambda a, half=half, pb=pb, nq=nq, i2=i2: a.activation(
                    out=ZSt[i2][0:nq, half * 512:(half + 1) * 512], in_=psb[pb][0:nq, :], func=AF.Silu), r=[pskey(pb)], w=[zk])
            P.dma("sp", ZSscr[tok0:tok0 + nq, :], ZSt[i2][0:nq, :], r=[zk], w=[("ZSscr", tok0)], semkey=zk)
            gk = "GLt%d" % i2
            for k in range(NCH):
                P.op("pe", lambda t, k=k, nq=nq: t.matmul(psb[6][0:nq, 0:48], lhsT=m1T[:, k, 0:nq], rhs=W_inb[:, k, 2048:2096],
                                                          start=(k == 0), stop=(k == NCH - 1)), r=["W_inb", "m1T"], w=[pskey(6)])
            P.op("dve", lambda v, nq=nq, i2=i2: v.tensor_tensor(out=GLt[i2][0:nq, :], in0=psb[6][0:nq, 0:48], in1=bgbc[0:nq, :],
                                                                op=ALU.add), r=[pskey(6), "bgbc"], w=[gk])
            P.op("act", lambda a, nq=nq, i2=i2: a.activation(out=GLt[i2][0:nq, :], in_=GLt[i2][0:nq, :], func=AF.Sigmoid),
                 r=[gk], w=[gk])
            P.dma("sp", GLscr[tok0:tok0 + nq, :], GLt[i2][0:nq, :], r=[gk], w=[("GLscr", tok0)], semkey=gk)
            qi += 1

    with P.scope():
        EE = P.sb("EE", [128, 64, 128], BF16)
        ones_b = P.sb("ones_b", [128, 1024], BF16)
        P.op("pool", lambda g_: g_.memset(ones_b[:], 1.0), w=["ones_b"])
        for T8 in range(8):
            P.op("pool", lambda g_, T8=T8: g_.affine_select(
                out=EE[:, T8 * 8:(T8 + 1) * 8, :].rearrange("p t (a b) -> p t a b", a=2),
                in_=ones_b[:, :].rearrange("p (t a b) -> p t a b", t=8, a=2),
                pattern=[[-2, 8], [-1, 2], [0, 64]], compare_op=ALU.is_equal, fill=0.0, base=-16 * T8, channel_multiplier=1),
                r=["ones_b"], w=["EE"])
        TRIp = P.sb("TRIp", [128, 512], BF16)
        TRI2p = P.sb("TRI2p", [128, 512], BF16)
        TRIs = P.sb("TRIs", [128, 32], BF16)
        TRI2s = P.sb("TRI2s", [128, 32], BF16)
        P.dma("sp", TRIp[:], t_tri_p[:, :], w=["TRIp"])
        P.dma("sp", TRI2p[:], t_tri2_p[:, :], w=["TRI2p"])
        P.dma("sp", TRIs[:], t_tri_s[:, :], w=["TRIs"])
        P.dma("sp", TRI2s[:], t_tri2_s[:, :], w=["TRI2s"])
        RB = [P.sb("RB%d" % i, [128, 4, 512], BF16) for i in range(2)]
        RBF = [P.sb("RBF%d" % i, [128, 512], BF16) for i in range(4)]
        KsT4 = P.sb("KsT4", [72, 4, 65 * 128], BF16)
        Vs4 = P.sb("Vs4", [128, 65, 4, 65], BF16)
        KwT4 = P.sb("KwT4", [72, 4, 21 * 128], BF16)
        Vw4 = P.sb("Vw4", [128, 21, 4, 65], BF16)
        KcT4 = P.sb("KcT4", [72, 4, 128], BF16)
        Vc4 = P.sb("Vc4", [128, 4, 64], BF16)
        P.op("pool", lambda g_: g_.memset(Vs4[:, :, :, 64:65], 1.0), w=["Vs4"])
        P.op("pool", lambda g_: g_.memset(Vw4[:, :, :, 64:65], 1.0), w=["Vw4"])
        idxo2 = P.sb("idxo2", [128, 16], I32)
        idxw = P.sb("idxw", [128, 20], I32)
        idxs = P.sb("idxs", [128, 1], I32)
        idxpg = P.sb("idxpg", [128, DEC_B * 64], I32)
        P.dma("sp", idxo2[:], t_idx_own[:, :], w=["idxo2"])
        P.dma("sp", idxw[:], t_idx_win[:, :], w=["idxw"])
        P.dma("sp", idxs[:], t_idx_slot[:, :], w=["idxs"])
        P.dma("sp", idxpg[:], idxscr[:, :], w=["idxpg"])
        QT = [P.sb("QT%d" % i, [72, 4, 128], BF16) for i in range(2)]
        FBNt = [P.sb("FBNt%d" % i, [128, 128], F32) for i in range(2)]
        CAUt = [P.sb("CAUt%d" % i, [128, 128], F32) for i in range(2)]
        TMt = [P.sb("TMt%d" % i, [128, 128], F32) for i in range(2)]
        GLg = [P.sb("GLg%d" % i, [128, 48], F32) for i in range(2)]
        ZSg = [P.sb("ZSg%d" % i, [128, 256], BF16) for i in range(2)]
        Ssb = P.sb("Ssb", [128, 512], F32)
        Esb = P.sb("Esb", [128, 512], F32)
        Pn = P.sb("Pn", [128, 512], F32)
        Pnb = P.sb("Pnb", [128, 512], BF16)
        imp = P.sb("imp", [128, 128], F32)
        scr = P.sb("scr", [128, 128], F32)
        scr2 = P.sb("scr2", [128, 128], F32)
        m8 = P.sb("m8", [128, 16], F32)
        smh = P.sb("smh", [128, 16], F32)
        smb = P.sb("smb", [128, 16], F32)
        MselT = P.sb("MselT", [128, 128], BF16)
        Msel4s = [P.sb("Msel4_%d" % i, [128, 512], BF16) for i in range(2)]
        PTc = P.sb("PTc", [128, 512], BF16)
        PT = [P.sb("PT%d" % i, [128, 512], BF16) for i in range(3)]
        PTm = [P.sb("PTm%d" % i, [128, 512], BF16) for i in range(3)]
        OaugSB = P.sb("OaugSB", [65, 512], F32)
        accs = [P.sb("acc%d" % i, [128, 256], F32) for i in range(2)]
        OGt = [P.sb("OGt%d" % i, [128, 256], BF16) for i in range(2)]
        cnt2 = {"rb": 0, "rbf": 0, "pt": 0, "ps": 0, "q": 0, "mx": 0}

        def prep_from_rbf(rbf, rk, g, nk, ktdst, kkey, vdst, vkey):
            P.op("pe", lambda t: t.transpose(psb[7][:, :].bitcast(BF16)[0:64, 0:nk], rbf[0:nk, g * 128:g * 128 + 64],
                                             ident_b[0:nk, 0:nk]), r=[rk, "ident_b"], w=[pskey(7)])
            P.op("dve", lambda v: v.tensor_copy(out=ktdst, in_=psb[7][:, :].bitcast(BF16)[0:64, 0:nk]), r=[pskey(7)], w=[kkey])
            P.op("pool", lambda g_: g_.tensor_copy(out=vdst, in_=rbf[0:nk, g * 128 + 64:g * 128 + 128]), r=[rk], w=[vkey])

        def prep4(rbf, rk, nk, ktdst3, kkey, vdst3, vkey):
            for g4 in range(4):
                P.op("pe", lambda t, g4=g4: t.transpose(psb[7][:, :].bitcast(BF16)[0:64, g4 * 128:g4 * 128 + nk],
                                                        rbf[0:nk, g4 * 128:g4 * 128 + 64], ident_b[0:nk, 0:nk]),
                     r=[rk, "ident_b"], w=[pskey(7)])
            P.op("dve", lambda v: v.tensor_copy(
                out=ktdst3, in_=psb[7][:, :].bitcast(BF16)[0:64, 0:512].rearrange("p (g k) -> p g k", g=4)[:, :, 0:nk]),
                r=[pskey(7)], w=[kkey])
            P.op("pool", lambda g_: g_.tensor_copy(
                out=vdst3, in_=rbf[0:nk, :].rearrange("p (g c d) -> p g c d", g=4, c=2)[:, :, 1, :]), r=[rk], w=[vkey])

        def load_rows_gather(src, idx_ap, ikey):
            i = cnt2["rbf"] % 4
            cnt2["rbf"] += 1
            P.gather(RBF[i][:, :], src, idx_ap, r=[ikey], w=["RBF%d" % i])
            return RBF[i], "RBF%d" % i

        def load_rows_plain(src_rows, nk):
            i = cnt2["rbf"] % 4
            cnt2["rbf"] += 1
            P.dma("pool", RBF[i][0:nk, :], src_rows, w=["RBF%d" % i])
            return RBF[i], "RBF%d" % i

        def nsa_tile(g, tok0, nq, jl, sb_, kth):
            ncol = 4 * nq
            sample = jl is None
            i2 = cnt2["q"] % 2
            cnt2["q"] += 1
            qt, qk = QT[i2], "QT%d" % i2
            Msel4, mk4 = Msel4s[i2], "Msel4_%d" % i2
            acc, ak = accs[i2], "acc%d" % i2
            P.dma("sp", qt[0:64, :, 0:nq], QTscr[g, :, :, tok0:tok0 + nq], w=[qk])
            if sample:
                P.dma("sp", qt[64:72, :, 0:nq], t_qaug_s.rearrange("r (h q) -> r h q", h=16)[:, 4 * g:4 * g + 4, :], w=[qk])
                P.dma("sp", FBNt[i2][0:nq, :], t_fbn_s[:, :], w=["FBNt%d" % i2])
                P.dma("sp", CAUt[i2][0:nq, :], t_caus_s[:, :], w=["CAUt%d" % i2])
                P.dma("sp", TMt[i2][0:nq, :], t_tm_s[:, :], w=["TMt%d" % i2])
            else:
                P.dma("sp", qt[64:72, :, 0:nq], t_qaug_p.rearrange("r (h q) -> r h q", h=16)[:, 4 * g:4 * g + 4, tok0:tok0 + nq],
                      w=[qk])
                P.dma("sp", FBNt[i2][:, :], t_fbn[jl], w=["FBNt%d" % i2])
                P.dma("sp", CAUt[i2][:, :], t_caus[jl], w=["CAUt%d" % i2])
                P.dma("sp", TMt[i2][:, :], t_tm[jl], w=["TMt%d" % i2])
            fk, ck, tk, glk, zk = "FBNt%d" % i2, "CAUt%d" % i2, "TMt%d" % i2, "GLg%d" % i2, "ZSg%d" % i2
            P.dma("sp", GLg[i2][0:nq, :], GLscr[tok0:tok0 + nq, :], w=[glk])
            P.dma("sp", ZSg[i2][0:nq, :], ZSscr[tok0:tok0 + nq, g * 256:(g + 1) * 256], w=[zk])
            qrhs = qt[0:72, :, 0:nq]
            kc_ap, kck, vc_ap, vck = KcT4[0:72, g, :], "KcT4", Vc4[:, g, :], "Vc4"
            ksel = lambda T, nk: (KsT4[0:72, g, T * 128:T * 128 + nk], "KsT4", Vs4[0:nk, T, g, :], "Vs4")
            kwin = lambda T, nk: (KwT4[0:72, g, T * 128:T * 128 + nk], "KwT4", Vw4[0:nk, T, g, :], "Vw4")
            gl3 = GLg[i2][0:nq, :].rearrange("p (h b) -> p h b", b=3)
            for hh in range(4):
                P.op("pe", lambda t, hh=hh: t.matmul(psb[6][0:nq, hh * 128:(hh + 1) * 128], lhsT=qt[0:72, hh, 0:nq], rhs=kc_ap,
                                                     start=True, stop=True), r=[qk, kck], w=[pskey(6)])
            P.op("dve", lambda v: v.tensor_tensor(
                out=Ssb[0:nq, :].rearrange("p (h s) -> p h s", h=4), in0=psb[6][0:nq, :].rearrange("p (h s) -> p h s", h=4),
                in1=TMt[i2][0:nq, :].unsqueeze(1).to_broadcast([nq, 4, 128]), op=ALU.add), r=[pskey(6), tk], w=["Ssb"])
            for hh in range(4):
                P.op("act", lambda a, hh=hh: a.activation(out=Esb[0:nq, hh * 128:(hh + 1) * 128], in_=Ssb[0:nq, hh * 128:(hh + 1) * 128],
                                                          func=AF.Exp, accum_out=smh[0:nq, hh:hh + 1]), r=["Ssb"], w=["Esb", "smh"])
            P.op("dve", lambda v: v.tensor_scalar_add(out=smh[0:nq, 4:8], in0=smh[0:nq, 0:4], scalar1=1e-30), r=["smh"], w=["smh"])
            P.op("dve", lambda v: v.reciprocal(out=smh[0:nq, 4:8], in_=smh[0:nq, 4:8]), r=["smh"], w=["smh"])
            for hh in range(4):
                P.op("dve", lambda v, hh=hh: v.tensor_scalar_mul(out=Pn[0:nq, hh * 128:(hh + 1) * 128],
                                                                 in0=Esb[0:nq, hh * 128:(hh + 1) * 128],
                                                                 scalar1=smh[0:nq, 4 + hh:5 + hh]), r=["Esb", "smh"], w=["Pn"])
            P.op("pool", lambda g_: g_.tensor_copy(out=Pnb[0:nq, :], in_=Pn[0:nq, :]), r=["Pn"], w=["Pnb"])
            P.op("dve", lambda v: v.tensor_tensor(out=imp[0:nq, :], in0=Pn[0:nq, 0:128], in1=Pn[0:nq, 128:256], op=ALU.add),
                 r=["Pn"], w=["imp"])
            P.op("dve", lambda v: v.tensor_tensor(out=imp[0:nq, :], in0=imp[0:nq, :], in1=Pn[0:nq, 256:384], op=ALU.add),
                 r=["Pn", "imp"], w=["imp"])
            P.op("dve", lambda v: v.tensor_tensor(out=imp[0:nq, :], in0=imp[0:nq, :], in1=Pn[0:nq, 384:512], op=ALU.add),
                 r=["Pn", "imp"], w=["imp"])
            P.op("dve", lambda v: v.tensor_tensor(out=scr[0:nq, :], in0=imp[0:nq, :], in1=CAUt[i2][0:nq, :], op=ALU.mult),
                 r=["imp", ck], w=["scr"])
            P.op("dve", lambda v: v.tensor_tensor(out=scr[0:nq, :], in0=scr[0:nq, :], in1=FBNt[i2][0:nq, :], op=ALU.add),
                 r=["scr", fk], w=["scr"])
            P.op("dve", lambda v: v.max(out=m8[0:nq, 0:8], in_=scr[0:nq, :]), r=["scr"], w=["m8"])
            P.op("dve", lambda v: v.match_replace(out=scr2[0:nq, :], in_to_replace=m8[0:nq, 0:8], in_values=scr[0:nq, :],
                                                  imm_value=-1.0e30), r=["scr", "m8"], w=["scr2"])
            P.op("dve", lambda v: v.max(out=m8[0:nq, 8:16], in_=scr2[0:nq, :]), r=["scr2"], w=["m8"])
            thr = m8[0:nq, kth - 1:kth]
            P.op("dve", lambda v: v.tensor_scalar(out=scr2[0:nq, :], in0=scr[0:nq, :], scalar1=thr, scalar2=None, op0=ALU.is_ge),
                 r=["scr", "m8"], w=["scr2"])
            P.op("dve", lambda v: v.tensor_tensor(out=scr2[0:nq, :], in0=scr2[0:nq, :], in1=CAUt[i2][0:nq, :], op=ALU.mult),
                 r=["scr2", ck], w=["scr2"])
            P.op("dve", lambda v: v.tensor_copy(out=MselT[0:nq, :], in_=scr2[0:nq, :]), r=["scr2"], w=["MselT"])
            yield "a"
            P.op("pe", lambda t: t.transpose(psb[7][:, :].bitcast(BF16)[:, 0:nq], MselT[0:nq, :], ident_b[0:nq, 0:nq]),
                 r=["MselT", "ident_b"], w=[pskey(7)])
            P.op("dve", lambda v: v.tensor_copy(
                out=Msel4[:, 0:nq], in_=psb[7][:, :].bitcast(BF16)[:, 0:nq]), r=[pskey(7)], w=[mk4])
            for hh in range(4):
                P.op("pe", lambda t, hh=hh: t.transpose(psb[7][:, :].bitcast(BF16)[:, 512 + hh * nq:512 + (hh + 1) * nq],
                                                        Pnb[0:nq, hh * 128:(hh + 1) * 128], ident_b[0:nq, 0:nq]),
                     r=["Pnb", "ident_b"], w=[pskey(7)])
            P.op("dve", lambda v: v.tensor_copy(out=PTc[:, 0:ncol], in_=psb[7][:, :].bitcast(BF16)[:, 512:512 + ncol]),
                 r=[pskey(7)], w=["PTc"])
            for hh in range(4):
                P.op("pe", lambda t, hh=hh: t.matmul(psb[6][0:nq, hh * 64:(hh + 1) * 64], lhsT=PTc[:, hh * nq:(hh + 1) * nq], rhs=vc_ap,
                                                     start=True, stop=True), r=["PTc", vck], w=[pskey(6)])
            for hh in range(4):
                P.op("dve", lambda v, hh=hh: v.tensor_scalar_mul(out=acc[0:nq, hh * 64:(hh + 1) * 64],
                                                                 in0=psb[6][0:nq, hh * 64:(hh + 1) * 64],
                                                                 scalar1=gl3[:, 4 * g + hh, 0:1]), r=[pskey(6), glk], w=[ak])

            yield "b"
            def attend(tiles, br, ob):
                nt = len(tiles)
                slots = {}

                def s_stage(i):
                    T = tiles[i]
                    sbk = cnt2["ps"] % 3
                    cnt2["ps"] += 1
                    nk = T["nk"]
                    mm = [(T["kt"], qrhs, T["kkey"], qk)] + T["masks"]
                    for j, (l, r_, lk, rk) in enumerate(mm):
                        P.op("pe", lambda t, l=l, r_=r_, j=j, nk=nk, sbk=sbk, n=len(mm): t.matmul(
                            psb[sbk][0:nk, 0:ncol], lhsT=l, rhs=r_, start=(j == 0), stop=(j == n - 1)),
                            r=[lk, rk], w=[pskey(sbk)])
                    mxb = None
                    if T["msel"] is not None:
                        mxb = 4 + cnt2["mx"] % 2
                        cnt2["mx"] += 1
                        P.op("pe", lambda t, Tm=T["msel"], nk=nk, mxb=mxb: t.matmul(
                            psb[mxb][0:nk, 0:nq], lhsT=EE[:, Tm, 0:nk], rhs=Msel4[:, 0:nq], start=True, stop=True),
                            r=["EE", mk4], w=[pskey(mxb)])
                    slots[i] = (sbk, mxb)

                def e_stage(i):
                    T = tiles[i]
                    nk = T["nk"]
                    sbk, mxb = slots[i]
                    pi = cnt2["pt"] % 3
                    cnt2["pt"] += 1
                    P.op("act", lambda a, nk=nk, sbk=sbk, pi=pi: a.activation(out=PT[pi][0:nk, 0:ncol], in_=psb[sbk][0:nk, 0:ncol],
                                                                              func=AF.Exp), r=[pskey(sbk)], w=["PT%d" % pi])
                    if mxb is None:
                        return PT[pi], "PT%d" % pi
                    P.op("dve", lambda v, nk=nk, pi=pi, mxb=mxb: v.tensor_tensor(
                        out=PTm[pi][0:nk, 0:ncol].rearrange("p (h q) -> p h q", h=4),
                        in0=PT[pi][0:nk, 0:ncol].rearrange("p (h q) -> p h q", h=4),
                        in1=psb[mxb][0:nk, 0:nq].unsqueeze(1).to_broadcast([nk, 4, nq]), op=ALU.mult),
                        r=["PT%d" % pi, pskey(mxb)], w=["PTm%d" % pi])
                    return PTm[pi], "PTm%d" % pi

                def v_stage(i, pi):
                    T = tiles[i]
                    nk = T["nk"]
                    pbuf, pkey_ = pi
                    P.op("pe", lambda t, T=T, nk=nk, pbuf=pbuf, i=i: t.matmul(psb[ob][0:65, 0:ncol], lhsT=T["v"], rhs=pbuf[0:nk, 0:ncol],
                                                                             start=(i == 0), stop=(i == nt - 1)),
                         r=[T["vkey"], pkey_], w=[pskey(ob)])

                LOOK = 2
                for i in range(min(LOOK, nt)):
                    s_stage(i)
                for i in range(nt):
                    pi = e_stage(i)
                    if i + LOOK < nt:
                        s_stage(i + LOOK)
                    v_stage(i, pi)
                P.op("dve", lambda v: v.tensor_copy(out=OaugSB[0:65, 0:ncol], in_=psb[ob][0:65, 0:ncol]), r=[pskey(ob)], w=["OaugSB"])
                for hh in range(4):
                    P.op("pe", lambda t, hh=hh: t.transpose(psb[7][0:nq, hh * 65:(hh + 1) * 65], OaugSB[0:65, hh * nq:(hh + 1) * nq],
                                                            ident_f[0:65, 0:65]), r=["OaugSB", "ident_f"], w=[pskey(7)])
                o3 = psb[7][0:nq, 0:260].rearrange("p (h e) -> p h e", e=65)
                P.op("dve", lambda v: v.tensor_scalar_add(out=smb[0:nq, 8:12], in0=o3[:, :, 64], scalar1=1e-30), r=[pskey(7)], w=["smb"])
                P.op("dve", lambda v: v.reciprocal(out=smb[0:nq, 8:12], in_=smb[0:nq, 8:12]), r=["smb"], w=["smb"])
                P.op("dve", lambda v: v.tensor_tensor(out=smb[0:nq, 12:16], in0=smb[0:nq, 8:12], in1=gl3[:, 4 * g:4 * g + 4, br],
                                                      op=ALU.mult), r=["smb", glk], w=["smb"])
                for hh in range(4):
                    P.op("dve", lambda v, hh=hh: v.scalar_tensor_tensor(
                        out=acc[0:nq, hh * 64:(hh + 1) * 64], in0=o3[:, hh, 0:64], scalar=smb[0:nq, 12 + hh:13 + hh],
                        in1=acc[0:nq, hh * 64:(hh + 1) * 64], op0=ALU.mult, op1=ALU.add), r=[pskey(7), "smb", ak], w=[ak])

            def ktile(src, T, nk=128, masks=()):
                kt_, kkey_, v_, vkey_ = src(T, nk)
                add = [m for m in masks if m[0] != "MSEL"]
                ms_ = [m[1] for m in masks if m[0] == "MSEL"]
                return {"kt": kt_, "kkey": kkey_, "v": v_, "vkey": vkey_, "nk": nk, "masks": add,
                        "msel": (ms_[0] if ms_ else None)}

            msel = lambda T: ("MSEL", T)
            if sample:
                tri = (ident_b[0:8, 0:8], TRIs[0:8, 0:ncol], "ident_b", "TRIs")
                tri2 = (ident_b[:, :], TRI2s[:, 0:ncol], "ident_b", "TRI2s")
                sel_tiles = [ktile(ksel, T, masks=[msel(T)]) for T in range(64)]
                sel_tiles.append(ktile(ksel, 64, nk=8, masks=[tri]))
                win_tiles = [ktile(kwin, 0, masks=[tri2])] + [ktile(kwin, w) for w in range(1, 4)]
                win_tiles.append(ktile(kwin, 4, nk=8, masks=[tri]))
            else:
                tri = (ident_b[:, :], TRIp[:, 0:ncol], "ident_b", "TRIp")
                tri2 = (ident_b[:, :], TRI2p[:, 0:ncol], "ident_b", "TRI2p")
                sel_tiles = [ktile(ksel, T, masks=[msel(T)]) for T in range(48)]
                for j2 in range(jl + 1):
                    ms = [msel(48 + j2)] + ([tri] if j2 == jl else [])
                    sel_tiles.append(ktile(ksel, 48 + j2, masks=ms))
                win_tiles = []
                for w in range(jl, jl + 5):
                    ms = [tri2] if w == jl else ([tri] if w == jl + 4 else [])
                    win_tiles.append(ktile(kwin, w, masks=ms))
            attend(sel_tiles, 1, 3)
            yield "sel"
            attend(win_tiles, 2, 3)
            ogk = "OGt%d" % i2
            P.op("dve", lambda v: v.tensor_tensor(out=OGt[i2][0:nq, :], in0=acc[0:nq, :], in1=ZSg[i2][0:nq, :], op=ALU.mult),
                 r=[ak, zk], w=[ogk])
            P.dma("sp", OGscr[tok0:tok0 + nq, g * 256:(g + 1) * 256], OGt[i2][0:nq, :], r=[ogk], w=[("OGscr", tok0, g)], semkey=ogk)
            yield "done"

        def run_tiles(specs):
            gens = [nsa_tile(*sp) for sp in specs]
            n = len(gens)
            if n == 0:
                return
            next(gens[0])
            next(gens[0])
            for i in range(n):
                if i + 1 < n:
                    next(gens[i + 1])
                next(gens[i])
                if i + 1 < n:
                    next(gens[i + 1])
                next(gens[i])

        if stage >= 2:
            for g in range(4):
                P.dma("sp", KsT4[64:72, g, 0:8192], t_kaug_sel[:, :], w=["KsT4"])
                P.dma("sp", KwT4[64:72, g, 0:2560], t_kaug_win[:, :], w=["KwT4"])
                P.dma("sp", KcT4[64:72, g, :], t_kaug_cmp[:, :], w=["KcT4"])
            for T4 in range(12):
                i = cnt2["rb"] % 2
                cnt2["rb"] += 1
                rbk = "RB%d" % i
                P.dma("pool", RB[i][:, :, :], o_sel_p[T4 * 512:(T4 + 1) * 512, :].rearrange("(t p) c -> p t c", p=128), w=[rbk])
                for t_ in range(4):
                    T = T4 * 4 + t_
                    prep4(RB[i][:, t_, :], rbk, 128, KsT4[0:64, :, T * 128:(T + 1) * 128], "KsT4", Vs4[:, T, :, 0:64], "Vs4")
            for j2 in range(16):
                rbf, rk = load_rows_gather(o_sel_p[:, :], idxo2[:, j2:j2 + 1], "idxo2")
                prep4(rbf, rk, 128, KsT4[0:64, :, (48 + j2) * 128:(49 + j2) * 128], "KsT4", Vs4[:, 48 + j2, :, 0:64], "Vs4")
            for w in range(20):
                rbf, rk = load_rows_gather(winscr[:, :], idxw[:, w:w + 1], "idxw")
                prep4(rbf, rk, 128, KwT4[0:64, :, w * 128:(w + 1) * 128], "KwT4", Vw4[:, w, :, 0:64], "Vw4")
            rbf, rk = load_rows_gather(cmpscr_p[:, :], idxs[:, 0:1], "idxs")
            prep4(rbf, rk, 128, KcT4[0:64, :, :], "KcT4", Vc4[:, :, :], "Vc4")
            run_tiles([(g, tok0, nq, jl, None, 16) for g in range(4) for (tok0, nq, jl, sb_) in qtiles if jl is not None])
        if do_sample:
            for g in range(4):
                P.dma("sp", KsT4[64:72, g, 0:8192], t_kaug_sel_s[:, :], w=["KsT4"])
                P.dma("sp", KsT4[64:72, g, 8192:8320], t_kaug_new_s[:, :], w=["KsT4"])
                P.dma("sp", KwT4[64:72, g, 0:512], t_kaug_win_s[:, :], w=["KwT4"])
                P.dma("sp", KwT4[64:72, g, 512:640], t_kaug_new_s[:, :], w=["KwT4"])
                P.dma("sp", KcT4[64:72, g, :], t_kaug_cmp_s[:, :], w=["KcT4"])
            for (tok0, nq, jl, sb_) in qtiles:
                if jl is not None:
                    continue
                b = sb_
                for pg in range(64):
                    rbf, rk = load_rows_gather(csel[:, :], idxpg[:, b * 64 + pg:b * 64 + pg + 1], "idxpg")
                    prep4(rbf, rk, 128, KsT4[0:64, :, pg * 128:(pg + 1) * 128], "KsT4", Vs4[:, pg, :, 0:64], "Vs4")
                rbf, rk = load_rows_plain(o_sel_s[b * 8:(b + 1) * 8, :], 8)
                prep4(rbf, rk, 8, KsT4[0:64, :, 8192:8200], "KsT4", Vs4[0:8, 64, :, 0:64], "Vs4")
                for w in range(4):
                    rbf, rk = load_rows_plain(swin[b * 512 + w * 128:b * 512 + (w + 1) * 128, :], 128)
                    prep4(rbf, rk, 128, KwT4[0:64, :, w * 128:(w + 1) * 128], "KwT4", Vw4[:, w, :, 0:64], "Vw4")
                rbf, rk = load_rows_plain(o_win_s[b * 512 + 504:b * 512 + 512, :], 8)
                prep4(rbf, rk, 8, KwT4[0:64, :, 512:520], "KwT4", Vw4[0:8, 4, :, 0:64], "Vw4")
                rbf, rk = load_rows_plain(cmpscr_s[b * 128:(b + 1) * 128, :], 128)
                prep4(rbf, rk, 128, KcT4[0:64, :, :], "KcT4", Vc4[:, :, :], "Vc4")
                run_tiles([(g, tok0, nq, None, b, 15) for g in range(4)])

    with P.scope():
        W_outb = P.sb("W_outb", [128, NCH, D], BF16)
        for k in range(NCH):
            P.dma("pool", W_outb[:, k, :], w_out_b[k * 128:(k + 1) * 128, :], w=["W_outb"])
        lnG1 = P.sb("lnG1", [128, 1, D], F32)
        lnB1 = P.sb("lnB1", [128, 1, D], F32)
        P.dma("sp", lnG1[:, 0, :], ln_g[1:2, :].partition_broadcast(128), w=["lnG1"])
        P.dma("sp", lnB1[:, 0, :], ln_b[1:2, :].partition_broadcast(128), w=["lnB1"])
        Gp1 = P.sb("Gp1", [128, D], F32)
        Gs1 = P.sb("Gs1", [8, DEC_B, D], F32)
        P.dma("sp", Gp1[:, :], modscr[1, 0:1, 2 * D:3 * D].partition_broadcast(128), w=["Gp1"])
        for b in range(DEC_B):
            P.dma("sp", Gs1[0:8, b, :], modscr[1, 1 + b:2 + b, 2 * D:3 * D].partition_broadcast(8), w=["Gs1"])
        P.op("pool", lambda g_: g_.tensor_scalar_add(out=Gp1[:], in0=Gp1[:], scalar1=1.0), r=["Gp1"], w=["Gp1"])
        P.op("pool", lambda g_: g_.tensor_scalar_add(out=Gs1[:], in0=Gs1[:], scalar1=1.0), r=["Gs1"], w=["Gs1"])
        idxo3 = P.sb("idxo3", [128, 16], I32)
        P.dma("sp", idxo3[:], t_idx_own[:, :], w=["idxo3"])
        X1o = [P.sb("X1o%d" % i, [128, D], F32) for i in range(2)]
        OGl = [P.sb("OGl%d" % i, [128, D], BF16) for i in range(2)]
        OGT = P.sb("OGT", [128, NCH, 128], BF16)
        vo = [P.sb("vo%d" % i, [128, D], F32) for i in range(2)]
        yo = [P.sb("yo%d" % i, [128, D], F32) for i in range(2)]
        sto = [P.sb("sto%d" % i, [128, 16], F32) for i in range(2)]
        qi = 0
        for (tok0, nq, jl, sb_) in qtiles:
            i2 = qi % 2
            qi += 1
            xk, ok_, vk, yk, sk = "X1o%d" % i2, "OGl%d" % i2, "vo%d" % i2, "yo%d" % i2, "sto%d" % i2
            if jl is not None:
                P.gather(X1o[i2][:, :], x1scr[:, :], idxo3[:, jl:jl + 1], r=["idxo3"], w=[xk])
                Gt, gkey = Gp1[0:nq, :], "Gp1"
            else:
                P.dma("sp", X1o[i2][0:nq, :], x1s_scr[sb_ * 8:(sb_ + 1) * 8, :], w=[xk])
                Gt, gkey = Gs1[0:8, sb_, :], "Gs1"
            P.dma("sp", OGl[i2][0:nq, :], OGscr[tok0:tok0 + nq, :], w=[ok_])
            for k in range(NCH):
                P.op("pe", lambda t, k=k, nq=nq, i2=i2: t.transpose(psb[0][:, :].bitcast(BF16)[:, k * 128:k * 128 + nq],
                                                                   OGl[i2][0:nq, k * 128:(k + 1) * 128], ident_b[0:nq, 0:nq]),
                     r=[ok_, "ident_b"], w=[pskey(0)])
            P.op("dve", lambda v, nq=nq: v.tensor_copy(
                out=OGT[:, :, 0:nq], in_=psb[0][:, :].bitcast(BF16)[:, 0:1024].rearrange("p (k t) -> p k t", t=128)[:, :, 0:nq]),
                r=[pskey(0)], w=["OGT"])
            for half in range(2):
                pb = 2 + half
                for k in range(NCH):
                    P.op("pe", lambda t, k=k, half=half, pb=pb, nq=nq: t.matmul(
                        psb[pb][0:nq, :], lhsT=OGT[:, k, 0:nq], rhs=W_outb[:, k, half * 512:(half + 1) * 512],
                        start=(k == 0), stop=(k == NCH - 1)), r=["OGT", "W_outb"], w=[pskey(pb)])
                if jl is None and sb_ > 0:
                    pass
                P.op("dve", lambda v, half=half, pb=pb, nq=nq, i2=i2, Gt=Gt: v.tensor_tensor(
                    out=vo[i2][0:nq, half * 512:(half + 1) * 512], in0=psb[pb][0:nq, :], in1=Gt[:, half * 512:(half + 1) * 512],
                    op=ALU.mult), r=[pskey(pb), gkey], w=[vk])
            P.op("dve", lambda v, nq=nq, i2=i2: v.scalar_tensor_tensor(
                out=vo[i2][0:nq, :], in0=X1o[i2][0:nq, :], scalar=ALPHA, in1=vo[i2][0:nq, :], op0=ALU.mult, op1=ALU.add),
                r=[xk, vk], w=[vk])
            layernorm_tm(vo[i2], vk, yo[i2], yk, nq, 0, sto[i2], sk, lnG1, "lnG1", lnB1, "lnB1")
            if jl is not None:
                P.dma("sp", y_p[tok0:tok0 + nq, :], yo[i2][0:nq, :], r=[yk], w=[("y_p", tok0)], semkey=yk)
            else:
                P.dma("sp", y_s[sb_ * 8:(sb_ + 1) * 8, :], yo[i2][0:nq, :], r=[yk], w=[("y_s", sb_)], semkey=yk)

    for b in range(DEC_B):
        P.dma("act", o_win_s[b * 512:b * 512 + 504, :], swin[b * 512 + 8:(b + 1) * 512, :], w=[("o_win_s", b, 0)],
              semkey="winscopy")

    P.finish()
    print("instructions:", P.n_ins, {e: P.cnt[e] for e in P.ENG}, "dma sems:", len(P.dsem))
    return P


def _bf(x):
    return np.asarray(x, np.float32).astype(ml_dtypes.bfloat16)


def _split_pos(pos):
    pos = np.asarray(pos, np.int64)
    a = np.floor_divide(pos, 64)
    b = pos - 64 * a
    return a.astype(np.float32), b.astype(np.float32)


def _kaug(pos, valid):
    a, b = _split_pos(pos)
    n = a.shape[0]
    out = np.zeros((8, n), np.float32)
    out[0] = a; out[1] = a; out[2] = b; out[3] = b; out[4] = 1.0
    out[5] = np.where(valid, 0.0, NEGM)
    return _bf(out)


def _slopes_hi_lo():
    s = (2.0 ** (-8.0 * np.arange(1, 17) / 16.0)).astype(np.float32)
    hi = s.astype(ml_dtypes.bfloat16).astype(np.float32)
    lo = (s - hi).astype(ml_dtypes.bfloat16).astype(np.float32)
    return s, hi, lo


def _qaug(tq):
    s, hi, lo = _slopes_hi_lo()
    tq = np.asarray(tq, np.float32)
    nq = tq.shape[0]
    out = np.zeros((8, 16, nq), np.float32)
    out[0] = (64.0 * hi)[:, None]; out[1] = (64.0 * lo)[:, None]
    out[2] = hi[:, None]; out[3] = lo[:, None]
    out[4] = -(s[:, None] * tq[None, :])
    out[5] = 1.0
    return _bf(out)


def prompt_tables(k):
    cs = 2048 * k
    p = np.arange(128)
    t = {}
    t["idx_own"] = (cs + 128 * np.arange(16)[None, :] + p[:, None]).astype(np.int32)
    pos_pref = (np.arange(48 * 128) - cs)
    valid_pref = np.repeat(128 * np.arange(48) < cs, 128)
    pos_own = np.arange(2048)
    t["kaug_sel"] = np.concatenate([_kaug(pos_pref, valid_pref), _kaug(pos_own, np.ones(2048, bool))], axis=1)
    wtok = cs - 512 + np.arange(20 * 128)
    t["idx_win"] = np.maximum(wtok, 0).reshape(20, 128).T.astype(np.int32).copy()
    t["kaug_win"] = _kaug(wtok - cs, wtok >= 0)
    blk = np.concatenate([np.arange(96), 32 * k + np.arange(32)])
    valid = np.concatenate([np.arange(96) < 32 * k, np.ones(32, bool)])
    t["idx_slot"] = blk.astype(np.int32).reshape(128, 1)
    cend = 64 * blk + 63 - cs
    t["kaug_cmp"] = _kaug(cend, valid)
    tq = np.arange(2048)
    tabs = np.arange(2048) + cs
    cb = tabs // 64
    blk_abs = np.where(valid, blk, 10 ** 6)
    forced = (blk_abs[None, :] == 0) | (blk_abs[None, :] == cb[:, None]) | (blk_abs[None, :] == cb[:, None] - 1)
    caus = blk_abs[None, :] <= cb[:, None]
    fbn = np.where(forced, FORCEDV, 0.0) - np.where(caus, 0.0, 1.0)
    t["fbn"] = fbn.astype(np.float32).reshape(16, 128, 128)
    t["caus"] = caus.astype(np.float32).reshape(16, 128, 128)
    cend_abs = 64 * blk + 63
    tm = np.where(cend_abs[None, :] <= tabs[:, None], 0.0, NEGM)
    t["tm"] = tm.astype(np.float32).reshape(16, 128, 128)
    return t


def static_tables():
    t = {}
    t["qaug_p"] = _qaug(np.arange(2048)).reshape(8, 16 * 2048)
    j = np.arange(128)[:, None]
    i = np.arange(128)[None, :]
    tri = np.where(j > i, NEGM, 0.0)
    tri2 = np.where(j < i, NEGM, 0.0)
    t["tri_p"] = _bf(np.tile(tri, (1, 4)))
    t["tri2_p"] = _bf(np.tile(tri2, (1, 4)))
    i8 = np.arange(8)[None, :]
    t["tri_s"] = _bf(np.tile(np.where(j > i8, NEGM, 0.0), (1, 4)))
    t["tri2_s"] = _bf(np.tile(np.where(j < i8, NEGM, 0.0), (1, 4)))
    t["qaug_s"] = _qaug(np.arange(8)).reshape(8, 16 * 8)
    t["kaug_sel_s"] = _kaug(np.arange(8192) - 8192, np.ones(8192, bool))
    t["kaug_win_s"] = _kaug(np.arange(512) - 512, np.ones(512, bool))
    t["kaug_new_s"] = _kaug(np.arange(128), np.arange(128) < 8)
    cend = 64 * np.arange(128) + 63 - 8192
    t["kaug_cmp_s"] = _kaug(cend, np.ones(128, bool))
    fb = np.zeros((8, 128), np.float32)
    fb[:, 0] = FORCEDV; fb[:, 127] = FORCEDV
    t["fbn_s"] = fb
    t["caus_s"] = np.ones((8, 128), np.float32)
    t["tm_s"] = np.zeros((8, 128), np.float32)
    return t


def core_inputs(inp, c):
    b = c // 4
    sb = slice(4 * c, 4 * c + 4)
    f = np.ascontiguousarray
    d = {
        "xf": f(inp["x_prompt"][b]),
        "xs": f(inp["x_sample"][sb].reshape(NS_TOK, D)),
        "cvec": f(np.concatenate([inp["c_prompt"][b:b + 1], inp["c_sample"][sb]], axis=0)),
        "sh0": f(inp["state_h"][0, sb]),
        "sc0": f(inp["state_conv"][0, sb].reshape(DEC_B * 3, D)),
        "swin": f(inp["state_win"][sb].reshape(DEC_B * 512, 512)),
        "ptab": f(inp["page_table"][sb]).astype(np.int32),
        "ccmp": inp["cache_cmp"].reshape(-1, 512),
        "csel": inp["cache_sel"].reshape(-1, 512),
        "w_ada": inp["w_ada"], "b_ada": inp["b_ada"], "ln_g": inp["ln_g"], "ln_b": inp["ln_b"],
        "w_in_a": inp["w_in_a"][0], "conv_w": inp["conv_w_a"][0], "conv_b": inp["conv_b_a"],
        "w_r": inp["w_r_a"][0], "b_r": inp["b_r_a"], "w_i": inp["w_i_a"][0], "b_i": inp["b_i_a"],
        "lam": inp["lam_a"], "w_out_a": inp["w_out_a"][0], "w_kv": inp["w_kv"],
        "phi_pe": inp["phi_pe"].reshape(64, 128), "w_phi1": inp["w_phi1"], "b_phi1": inp["b_phi1"],
        "w_phi2": inp["w_phi2"], "b_phi2": inp["b_phi2"], "w_in_b": inp["w_in_b"][0],
        "b_gate": inp["b_gate_b"], "w_out_b": inp["w_out_b"][0],
    }
    for k2, v in prompt_tables(c % 4).items():
        d["t_" + k2] = v
    for k2, v in static_tables().items():
        d["t_" + k2] = v
    return {k: np.asarray(v) for k, v in d.items()}


def assemble(results, cores):
    y_prompt = np.zeros((2, SEQ, D), np.float32)
    y_sample = np.zeros((32, DEC_S, D), np.float32)
    new_cmp_p = np.zeros((2, SEQ, 4, 2, 64), np.float32)
    new_sel_p = np.zeros((2, SEQ, 4, 2, 64), np.float32)
    new_win_p = np.zeros((2, 512, 4, 2, 64), np.float32)
    new_h_p = np.zeros((1, 2, D), np.float32)
    new_conv_p = np.zeros((1, 2, 3, D), np.float32)
    new_cmp_s = np.zeros((32, DEC_S, 4, 2, 64), np.float32)
    new_sel_s = np.zeros((32, DEC_S, 4, 2, 64), np.float32)
    new_win_s = np.zeros((32, 512, 4, 2, 64), np.float32)
    new_h_s = np.zeros((1, 32, D), np.float32)
    new_conv_s = np.zeros((1, 32, 3, D), np.float32)
    for r, c in zip(results, cores):
        b, k = c // 4, c % 4
        sb = slice(4 * c, 4 * c + 4)
        y_prompt[b, k * 2048:(k + 1) * 2048] = r["y_p"]
        y_sample[sb] = r["y_s"].reshape(DEC_B, DEC_S, D)
        if k == 0:
            new_cmp_p[b] = r["o_cmp_p"].reshape(SEQ, 4, 2, 64)
            new_sel_p[b] = r["o_sel_p"].reshape(SEQ, 4, 2, 64)
            new_win_p[b] = r["o_win_p"].reshape(512, 4, 2, 64)
            new_h_p[0, b] = r["o_h_p"][0]
            new_conv_p[0, b] = r["o_conv_p"]
        new_cmp_s[sb] = r["o_cmp_s"].reshape(DEC_B, DEC_S, 4, 2, 64)
        new_sel_s[sb] = r["o_sel_s"].reshape(DEC_B, DEC_S, 4, 2, 64)
        new_win_s[sb] = r["o_win_s"].reshape(DEC_B, 512, 4, 2, 64)
        new_h_s[0, sb] = r["o_h_s"]
        new_conv_s[0, sb] = r["o_conv_s"].reshape(DEC_B, 3, D)
    return (y_prompt, y_sample, new_cmp_p, new_sel_p, new_win_p, new_h_p, new_conv_p,
            new_cmp_s, new_sel_s, new_win_s, new_h_s, new_conv_s)


def kernel(**inputs):
    inp = {k: np.asarray(v) for k, v in inputs.items()}
    n_phys = inp["cache_cmp"].shape[0]
    P = build(n_phys)
    cores = list(range(8))
    in_maps = [core_inputs(inp, c) for c in cores]
    res = run_bass_kernel_spmd(P.nc, in_maps, core_ids=cores)
    return assemble(res.results, cores)
```

```python
import contextlib
import numpy as np
import ml_dtypes
import concourse.bass as bass
import concourse.mybir as mybir
from concourse.bass_utils import run_bass_kernel_spmd

F32 = mybir.dt.float32
BF16 = mybir.dt.bfloat16
I32 = mybir.dt.int32
AF = mybir.ActivationFunctionType
ALU = mybir.AluOpType
AX = mybir.AxisListType

D = 1024
NCH = 8
SEQ = 8192
TT = 256
NSUB = TT // 128
DEC_B = 4
DEC_S = 8
NS_TOK = DEC_B * DEC_S
ALPHA = 4.0 ** 0.25
LN_EPS = 1e-5
RG_C = 8.0
NEGM = -30000.0
FORCEDV = 1.0e6


class Prog:
    ENG = ("pe", "act", "dve", "pool", "sp")

    def __init__(self):
        self.nc = bass.Bass("TRN2", target_bir_lowering=False)
        self.es = contextlib.ExitStack()
        nc = self.nc
        self.eng = {"pe": nc.tensor, "act": nc.scalar, "dve": nc.vector, "pool": nc.gpsimd, "sp": nc.sync}
        self.sem = {e: self.es.enter_context(nc.semaphore("s_" + e)) for e in self.ENG}
        self.cnt = {e: 0 for e in self.ENG}
        self.seen = {e: {} for e in self.ENG}
        self.dsem = {}
        self.dcnt = {}
        self.bufs = {}
        self.n_ins = 0
        self.stack = [self.es]

    def sb(self, name, shape, dt):
        return self.stack[-1].enter_context(self.nc.sbuf_tensor(name, list(shape), dt))

    def barrier(self):
        deps = {}
        for e2 in self.ENG:
            if self.cnt[e2]:
                deps[("eng", e2)] = self.cnt[e2]
        for k in self.dcnt:
            deps[("dma", k)] = self.dcnt[k]
        for e in self.ENG:
            self._wait(e, dict(deps))

    @contextlib.contextmanager
    def scope(self):
        st = contextlib.ExitStack()
        self.stack.append(st)
        try:
            yield
        finally:
            self.barrier()
            self.stack.pop()
            st.close()

    def ps(self, name, shape, dt):
        return self.es.enter_context(self.nc.psum_tensor(name, list(shape), dt))

    def dram(self, name, shape, dt, kind="Internal"):
        return self.nc.dram_tensor(name, list(shape), dt, kind=kind).ap()

    def _state(self, k):
        st = self.bufs.get(k)
        if st is None:
            st = self.bufs[k] = {"w": {}, "r": {}}
        return st

    def _deps(self, r, w):
        deps = {}
        for k in r:
            for s, v in self._state(k)["w"].items():
                deps[s] = max(deps.get(s, 0), v)
        for k in w:
            st = self._state(k)
            for s, v in st["w"].items():
                deps[s] = max(deps.get(s, 0), v)
            for s, v in st["r"].items():
                deps[s] = max(deps.get(s, 0), v)
        return deps

    def _wait(self, e, deps):
        eng = self.eng[e]
        seen = self.seen[e]
        for s, v in deps.items():
            if s[0] == "dma":
                v = max(v, self.dcnt[s[1]])
                if seen.get(s, 0) >= v:
                    continue
                eng.wait_ge(self.dsem[s[1]], v)
            else:
                if s[1] == e and False:
                    continue
                if seen.get(s, 0) >= v:
                    continue
                eng.wait_ge(self.sem[s[1]], v)
            seen[s] = v

    def _commit(self, me_src, me_val, r, w):
        for k in w:
            st = self._state(k)
            st["w"] = {me_src: me_val}
            st["r"] = {}
        for k in r:
            if k in w:
                continue
            st = self._state(k)
            st["r"][me_src] = max(st["r"].get(me_src, 0), me_val)

    def op(self, e, fn, r=(), w=()):
        w = list(w) + [k for k in r if isinstance(k, str) and k[:2] == "ps" and k[2:].isdigit() and k not in w]
        self._wait(e, self._deps(r, w))
        ins = fn(self.eng[e])
        self.cnt[e] += 1
        ins.then_inc(self.sem[e], 1)
        self._commit(("eng", e), self.cnt[e], r, w)
        self.n_ins += 1
        return ins

    def dma(self, q, out, in_, r=(), w=(), semkey=None, **kw):
        self._wait(q, self._deps(r, w))
        if semkey is None:
            semkey = (tuple(w) + tuple(r))[0]
        if semkey not in self.dsem:
            self.dsem[semkey] = self.es.enter_context(self.nc.semaphore("d%d" % len(self.dsem)))
            self.dcnt[semkey] = 0
        ins = self.eng[q].dma_start(out=out, in_=in_, **kw)
        self.dcnt[semkey] += 16
        ins.then_inc(self.dsem[semkey], 16)
        self._commit(("dma", semkey), self.dcnt[semkey], r, w)
        self.n_ins += 1
        return ins

    def gather(self, out, in_, idx_ap, r=(), w=(), semkey=None):
        q = "pool"
        self._wait(q, self._deps(r, w))
        if semkey is None:
            semkey = tuple(w)[0]
        if semkey not in self.dsem:
            self.dsem[semkey] = self.es.enter_context(self.nc.semaphore("d%d" % len(self.dsem)))
            self.dcnt[semkey] = 0
        ins = self.nc.gpsimd.indirect_dma_start(
            out=out, out_offset=None, in_=in_, in_offset=bass.IndirectOffsetOnAxis(ap=idx_ap, axis=0))
        self.dcnt[semkey] += 16
        ins.then_inc(self.dsem[semkey], 16)
        self._commit(("dma", semkey), self.dcnt[semkey], r, w)
        self.n_ins += 1
        return ins

    def finish(self):
        for e in ("sp",):
            deps = {}
            for k, st in self.bufs.items():
                for s, v in list(st["w"].items()) + list(st["r"].items()):
                    deps[s] = max(deps.get(s, 0), v)
            for k in self.dcnt:
                deps[("dma", k)] = self.dcnt[k]
            for e2 in self.ENG:
                if self.cnt[e2]:
                    deps[("eng", e2)] = self.cnt[e2]
            self._wait(e, deps)


def build(n_phys, stage=9):
    P = Prog()
    nc = P.nc
    es = P.es
    ctx_nc = nc.allow_non_contiguous_dma(reason="small strided parameter / state loads")
    es.enter_context(ctx_nc)

    def din(name, shape, dt=F32):
        return nc.dram_tensor(name, list(shape), dt, kind="ExternalInput").ap()

    def dout(name, shape, dt=F32):
        return nc.dram_tensor(name, list(shape), dt, kind="ExternalOutput").ap()

    xf = din("xf", [SEQ, D])
    xs = din("xs", [NS_TOK, D])
    cvec = din("cvec", [5, D])
    sh0 = din("sh0", [DEC_B, D])
    sc0 = din("sc0", [DEC_B * 3, D])
    swin = din("swin", [DEC_B * 512, 512])
    ptab = din("ptab", [DEC_B, 64], I32)
    ccmp = din("ccmp", [n_phys * 128, 512])
    csel = din("csel", [n_phys * 128, 512])
    w_ada = din("w_ada", [2, D, 3 * D])
    b_ada = din("b_ada", [2, 3 * D])
    ln_g = din("ln_g", [2, D])
    ln_b = din("ln_b", [2, D])
    w_in_a = din("w_in_a", [D, 2 * D])
    conv_w = din("conv_w", [4, D])
    conv_b = din("conv_b", [1, D])
    w_r = din("w_r", [8, 128, 128])
    b_r = din("b_r", [1, D])
    w_i = din("w_i", [8, 128, 128])
    b_i = din("b_i", [1, D])
    lam = din("lam", [1, D])
    w_out_a = din("w_out_a", [D, D])
    w_kv = din("w_kv", [D, 1536])
    phi_pe = din("phi_pe", [64, 128])
    w_phi1 = din("w_phi1", [2, 64, 64, 128])
    b_phi1 = din("b_phi1", [2, 128])
    w_phi2 = din("w_phi2", [2, 128, 64])
    b_phi2 = din("b_phi2", [2, 64])
    w_in_b = din("w_in_b", [D, 2096])
    b_gate = din("b_gate", [1, 48])
    w_out_b = din("w_out_b", [D, D])

    def dtab(name, shape, dt):
        return nc.dram_tensor(name, list(shape), dt, kind="ExternalInput").ap()
    t_idx_own = dtab("t_idx_own", [128, 16], I32)
    t_idx_win = dtab("t_idx_win", [128, 20], I32)
    t_idx_slot = dtab("t_idx_slot", [128, 1], I32)
    t_kaug_sel = dtab("t_kaug_sel", [8, 8192], BF16)
    t_kaug_win = dtab("t_kaug_win", [8, 2560], BF16)
    t_kaug_cmp = dtab("t_kaug_cmp", [8, 128], BF16)
    t_fbn = dtab("t_fbn", [16, 128, 128], F32)
    t_caus = dtab("t_caus", [16, 128, 128], F32)
    t_tm = dtab("t_tm", [16, 128, 128], F32)
    t_qaug_p = dtab("t_qaug_p", [8, 16 * 2048], BF16)
    t_tri_p = dtab("t_tri_p", [128, 512], BF16)
    t_tri2_p = dtab("t_tri2_p", [128, 512], BF16)
    t_tri_s = dtab("t_tri_s", [128, 32], BF16)
    t_tri2_s = dtab("t_tri2_s", [128, 32], BF16)
    t_qaug_s = dtab("t_qaug_s", [8, 128], BF16)
    t_kaug_sel_s = dtab("t_kaug_sel_s", [8, 8192], BF16)
    t_kaug_win_s = dtab("t_kaug_win_s", [8, 512], BF16)
    t_kaug_new_s = dtab("t_kaug_new_s", [8, 128], BF16)
    t_kaug_cmp_s = dtab("t_kaug_cmp_s", [8, 128], BF16)
    t_fbn_s = dtab("t_fbn_s", [8, 128], F32)
    t_caus_s = dtab("t_caus_s", [8, 128], F32)
    t_tm_s = dtab("t_tm_s", [8, 128], F32)

    y_p = dout("y_p", [2048, D])
    y_s = dout("y_s", [NS_TOK, D])
    o_cmp_p = dout("o_cmp_p", [SEQ, 512])
    o_sel_p = dout("o_sel_p", [SEQ, 512])
    o_win_p = dout("o_win_p", [512, 512])
    o_h_p = dout("o_h_p", [1, D])
    o_conv_p = dout("o_conv_p", [3, D])
    o_cmp_s = dout("o_cmp_s", [NS_TOK, 512])
    o_sel_s = dout("o_sel_s", [NS_TOK, 512])
    o_win_s = dout("o_win_s", [DEC_B * 512, 512])
    o_h_s = dout("o_h_s", [DEC_B, D])
    o_conv_s = dout("o_conv_s", [DEC_B * 3, D])

    modscr = P.dram("modscr", [2, 5, 3 * D], F32)
    x1scr = P.dram("x1scr", [SEQ, D], F32)
    x1s_scr = P.dram("x1s_scr", [NS_TOK, D], F32)
    winscr = P.dram("winscr", [SEQ, 512], F32)

    ident_b = P.sb("ident_b", [128, 128], BF16)
    ident_f = P.sb("ident_f", [128, 128], F32)
    for t, k in ((ident_b, "ident_b"), (ident_f, "ident_f")):
        P.op("pool", lambda g, t=t: g.memset(t[:], 0.0), w=[k])
        P.op("pool", lambda g, t=t: g.affine_select(out=t[:], in_=t[:], pattern=[[-1, 128]],
                                                    compare_op=ALU.not_equal, fill=1.0, base=0,
                                                    channel_multiplier=1), r=[k], w=[k])

    psb = [P.ps("ps%d" % i, [128, 512], F32) for i in range(8)]

    def pskey(i):
        return "ps%d" % i

    mod_fm = P.sb("mod_fm", [128, 2, 24, 8], F32)
    scA = P.scope()
    scA.__enter__()
    W_in = P.sb("W_in", [128, NCH, 2 * D], BF16)
    W_out = P.sb("W_out", [128, NCH, D], BF16)
    W_kv = P.sb("W_kv", [128, NCH, 1536], BF16)
    W_r = P.sb("W_r", [128, 8, 128], BF16)
    W_i = P.sb("W_i", [128, 8, 128], BF16)
    for k in range(NCH):
        P.dma("pool", W_in[:, k, :], w_in_a[k * 128:(k + 1) * 128, :], w=["W_in"])
    for k in range(NCH):
        P.dma("pool", W_out[:, k, :], w_out_a[k * 128:(k + 1) * 128, :], w=["W_out"])
    for k in range(NCH):
        P.dma("pool", W_kv[:, k, :], w_kv[k * 128:(k + 1) * 128, :], w=["W_kv"])
    P.dma("pool", W_r[:], w_r.rearrange("n c d -> c n d"), w=["W_r"])
    P.dma("pool", W_i[:], w_i.rearrange("n c d -> c n d"), w=["W_i"])

    pf = P.sb("pf", [128, 10, NCH], F32)
    for k in range(4):
        P.dma("sp", pf[:, k, :], conv_w[k:k + 1, :].rearrange("o (c p) -> p (o c)", p=128), w=["pf"])
    for j, src in ((4, conv_b), (5, b_r), (6, b_i), (7, lam)):
        P.dma("sp", pf[:, j, :], src[0:1, :].rearrange("o (c p) -> p (o c)", p=128), w=["pf"])
    P.op("act", lambda a: a.activation(out=pf[:, 9, :], in_=pf[:, 7, :], func=AF.Exp, scale=-1.0), r=["pf"], w=["pf"])
    P.op("act", lambda a: a.activation(out=pf[:, 9, :], in_=pf[:, 9, :], func=AF.Ln, bias=1.0), r=["pf"], w=["pf"])
    P.op("dve", lambda v: v.tensor_scalar_mul(out=pf[:, 7, :], in0=pf[:, 9, :], scalar1=-RG_C), r=["pf"], w=["pf"])
    P.op("dve", lambda v: v.tensor_scalar_mul(out=pf[:, 8, :], in0=pf[:, 9, :], scalar1=-2.0 * RG_C), r=["pf"], w=["pf"])

    lnG = P.sb("lnG", [128, 1, D], F32)
    lnB = P.sb("lnB", [128, 1, D], F32)
    for l in range(1):
        P.dma("sp", lnG[:, l, :], ln_g[l:l + 1, :].partition_broadcast(128), w=["lnG"])
        P.dma("sp", lnB[:, l, :], ln_b[l:l + 1, :].partition_broadcast(128), w=["lnB"])

    vt = [P.sb("vt%d" % i, [128, D], F32) for i in range(2)]
    c5, c5s = vt[0], vt[1]
    csT = P.sb("csT", [128, NCH, 8], BF16)
    P.dma("sp", c5[0:5, :], cvec[:, :], w=["vt0"])
    P.op("act", lambda a: a.activation(out=c5s[0:5, :], in_=c5[0:5, :], func=AF.Silu), r=["vt0"], w=["vt1"])
    for k in range(NCH):
        P.op("pe", lambda t, k=k: t.transpose(psb[0][:, k * 8:k * 8 + 5], c5s[0:5, k * 128:(k + 1) * 128],
                                              ident_f[0:5, 0:5]), r=["vt1", "ident_f"], w=[pskey(0)])
    P.op("dve", lambda v: v.tensor_copy(out=csT[:, :, 0:5],
                                        in_=psb[0][:, 0:64].rearrange("p (k e) -> p k e", e=8)[:, :, 0:5]),
         r=[pskey(0)], w=["csT"])
    AW = 256
    NA = 3 * D // AW
    wada_buf = [P.sb("wada%d" % i, [128, NCH, AW], BF16) for i in range(2)]
    modc = [P.sb("modc%d" % i, [5, AW], F32) for i in range(2)]
    badac = [P.sb("badac%d" % i, [5, AW], F32) for i in range(2)]
    it = 0
    for l in range(2):
        for n6 in range(NA):
            wb = wada_buf[it % 2]
            wk = "wada%d" % (it % 2)
            mk = "modc%d" % (it % 2)
            bk_ = "badac%d" % (it % 2)
            mc = modc[it % 2]
            bc = badac[it % 2]
            P.dma("pool", wb[:], w_ada[l, :, n6 * AW:(n6 + 1) * AW].rearrange("(k p) n -> p k n", p=128), w=[wk])
            P.dma("sp", bc[:], b_ada[l:l + 1, n6 * AW:(n6 + 1) * AW].partition_broadcast(5), w=[bk_])
            pb = 2 + (it % 2)
            for k in range(NCH):
                P.op("pe", lambda t, k=k, wb=wb, pb=pb: t.matmul(psb[pb][0:5, 0:AW], lhsT=csT[:, k, 0:5], rhs=wb[:, k, :],
                                                                 start=(k == 0), stop=(k == NCH - 1)),
                     r=["csT", wk], w=[pskey(pb)])
            P.op("dve", lambda v, mc=mc, bc=bc, pb=pb: v.tensor_tensor(
                out=mc[:], in0=psb[pb][0:5, 0:AW], in1=bc[:], op=ALU.add),
                r=[pskey(pb), bk_], w=[mk])
            P.dma("sp", modscr[l, :, n6 * AW:(n6 + 1) * AW], mc[:], r=[mk], w=[("modscr", l, n6)], semkey=mk)
            nq = AW // 128
            for q in range(nq):
                P.op("pe", lambda t, q=q, mc=mc: t.transpose(psb[1][:, q * 8:q * 8 + 5], mc[0:5, q * 128:(q + 1) * 128],
                                                             ident_f[0:5, 0:5]), r=[mk, "ident_f"], w=[pskey(1)])
            P.op("dve", lambda v, l=l, n6=n6, nq=nq: v.tensor_copy(
                out=mod_fm[:, l, n6 * nq:(n6 + 1) * nq, 0:5],
                in_=psb[1][:, 0:8 * nq].rearrange("p (k e) -> p k e", e=8)[:, :, 0:5]),
                r=[pskey(1)], w=["mod_fm"])
            it += 1
    P.op("dve", lambda v: v.tensor_scalar_add(out=mod_fm[:, :, 8:24, :], in0=mod_fm[:, :, 8:24, :], scalar1=1.0),
         r=["mod_fm"], w=["mod_fm"])
    Gp = P.sb("Gp", [128, 1, D], F32)
    Gs = P.sb("Gs", [NS_TOK, 1, D], F32)
    mod_keys = [("modscr", l, n6) for l in range(2) for n6 in range(NA)]
    for l in range(1):
        P.dma("sp", Gp[:, l, :], modscr[l, 0:1, 2 * D:3 * D].partition_broadcast(128), r=mod_keys, w=["Gp"])
        for b in range(DEC_B):
            P.dma("sp", Gs[b * 8:(b + 1) * 8, l, :], modscr[l, 1 + b:2 + b, 2 * D:3 * D].partition_broadcast(8),
                  r=mod_keys, w=["Gs"])
    P.op("pool", lambda g: g.tensor_scalar_add(out=Gp[:], in0=Gp[:], scalar1=1.0), r=["Gp"], w=["Gp"])
    P.op("pool", lambda g: g.tensor_scalar_add(out=Gs[:], in0=Gs[:], scalar1=1.0), r=["Gs"], w=["Gs"])

    xtok = [P.sb("xtok%d" % i, [128, NSUB, D], F32) for i in range(2)]
    xbf = P.sb("xbf", [128, NSUB, D], BF16)
    mT = P.sb("mT", [128, NCH, TT], BF16)
    xbe = P.sb("xbe", [128, NCH, 3 + TT], F32)
    xbe_s = P.sb("xbe_s", [128, NCH, DEC_B, 3 + DEC_S], F32)
    hprev = P.sb("hprev", [128, NCH], F32)
    h0s = P.sb("h0s", [128, NCH, DEC_B], F32)
    hlast_s = P.sb("hlast_s", [128, NCH, DEC_B], F32)
    NT = 2
    xc = [P.sb("xc%d" % i, [128, TT], F32) for i in range(NT)]
    xcb = [P.sb("xcb%d" % i, [128, TT], BF16) for i in range(NT)]
    zs = [P.sb("zs%d" % i, [128, TT], F32) for i in range(NT)]
    ra = [P.sb("ra%d" % i, [128, TT], F32) for i in range(NT)]
    ri = [P.sb("ri%d" % i, [128, TT], F32) for i in range(NT)]
    ga = [P.sb("ga%d" % i, [128, TT], F32) for i in range(NT)]
    bb = [P.sb("bb%d" % i, [128, TT], F32) for i in range(NT)]
    hs = [P.sb("hs%d" % i, [128, TT], F32) for i in range(NT)]
    yg = P.sb("yg", [128, NCH, TT], BF16)
    x1t = [P.sb("x1t%d" % i, [128, D], F32) for i in range(2)]
    x1b = [P.sb("x1b%d" % i, [128, D], BF16) for i in range(2)]
    x1T = P.sb("x1T", [128, NCH, TT], BF16)
    kvst = [P.sb("kvst%d" % i, [128, 1536], F32) for i in range(2)]
    stat = [P.sb("stat%d" % i, [128, 16], F32) for i in range(2)]

    P.op("pool", lambda g: g.memset(xbe[:, :, 0:3], 0.0), w=["xbe"])
    P.op("pool", lambda g: g.memset(hprev[:], 0.0), w=["hprev"])
    for n in range(NCH):
        for b in range(DEC_B):
            P.dma("sp", xbe_s[:, n, b, 0:3], sc0[b * 3:(b + 1) * 3, n * 128:(n + 1) * 128].rearrange("k p -> p k"),
                  w=["xbe_s"])
        P.dma("sp", h0s[:, n, :], sh0.rearrange("b (c p) -> c p b", p=128)[n], w=["h0s"])

    cnt = {"tile": 0, "ch": 0, "sub": 0}

    def layernorm_tm(vin, vkey, out, okey, np_, layer, st, skey, Gt_=None, gk_="lnG", Bt_=None, bk_="lnB"):
        Gt_ = lnG if Gt_ is None else Gt_
        Bt_ = lnB if Bt_ is None else Bt_
        P.op("dve", lambda v: v.bn_stats(out=st[0:np_, 0:6], in_=vin[0:np_, 0:512]), r=[vkey], w=[skey])
        P.op("dve", lambda v: v.bn_stats(out=st[0:np_, 6:12], in_=vin[0:np_, 512:1024]), r=[vkey], w=[skey])
        P.op("dve", lambda v: v.bn_aggr(out=st[0:np_, 12:14], in_=st[0:np_, 0:12]),
             r=[skey], w=[skey])
        P.op("dve", lambda v: v.tensor_scalar_add(out=st[0:np_, 14:15], in0=st[0:np_, 13:14], scalar1=LN_EPS),
             r=[skey], w=[skey])
        P.op("act", lambda a: a.activation(out=st[0:np_, 14:15], in_=st[0:np_, 14:15], func=AF.Ln), r=[skey], w=[skey])
        P.op("act", lambda a: a.activation(out=st[0:np_, 14:15], in_=st[0:np_, 14:15], func=AF.Exp, scale=-0.5),
             r=[skey], w=[skey])
        P.op("dve", lambda v: v.scalar_tensor_tensor(out=st[0:np_, 15:16], in0=st[0:np_, 12:13], scalar=-1.0,
                                                     in1=st[0:np_, 14:15], op0=ALU.mult, op1=ALU.mult),
             r=[skey], w=[skey])
        P.op("act", lambda a: a.activation(out=out[0:np_, :], in_=vin[0:np_, :], func=AF.Identity,
                                           scale=st[0:np_, 14:15], bias=st[0:np_, 15:16]),
             r=[vkey, skey], w=[okey])
        P.op("pool", lambda g: g.tensor_tensor(out=out[0:np_, :], in0=out[0:np_, :], in1=Gt_[0:np_, layer, :], op=ALU.mult),
             r=[okey, gk_], w=[okey])
        P.op("pool", lambda g: g.tensor_tensor(out=out[0:np_, :], in0=out[0:np_, :], in1=Bt_[0:np_, layer, :], op=ALU.add),
             r=[okey, bk_], w=[okey])

    import os
    CHPIPE = int(os.environ.get("CHPIPE", "1"))

    class L0Tile:
        def __init__(self, ti, sample):
            self.ti, self.sample = ti, sample
            if sample:
                self.ncols, self.nsub, self.np_ = NS_TOK, 1, NS_TOK
                self.segs = [(b * DEC_S, DEC_S, 1 + b) for b in range(DEC_B)]
            else:
                self.ncols, self.nsub, self.np_ = TT, NSUB, 128
                self.segs = [(0, TT, 0)]
            self.t0 = ti * TT
            self.xt = xtok[cnt["tile"] % 2]
            self.xk = "xtok%d" % (cnt["tile"] % 2)
            cnt["tile"] += 1
            self.cis = {}

        def front(self):
            ti, sample, ncols, nsub, np_, segs, t0, xt, xk = (self.ti, self.sample, self.ncols, self.nsub, self.np_, self.segs,
                                                              self.t0, self.xt, self.xk)
            if sample:
                P.dma("sp", xt[0:np_, 0, :], xs[:, :], w=[xk])
            else:
                for s in range(nsub):
                    P.dma("sp", xt[:, s, :], xf[t0 + s * 128:t0 + (s + 1) * 128, :], w=[xk])
            for s in range(nsub):
                P.op("pool", lambda g, s=s: g.tensor_copy(out=xbf[0:np_, s, :], in_=xt[0:np_, s, :]), r=[xk], w=["xbf"])
            for half in range(2):
                pb = half
                for kk in range(4):
                    k = half * 4 + kk
                    for s in range(nsub):
                        P.op("pe", lambda t, k=k, kk=kk, s=s, pb=pb: t.transpose(
                            psb[pb][:, :].bitcast(BF16)[:, kk * TT + s * 128:kk * TT + s * 128 + np_],
                            xbf[0:np_, s, k * 128:(k + 1) * 128], ident_b[0:np_, 0:np_]),
                            r=["xbf", "ident_b"], w=[pskey(pb)])
                for kk in range(4):
                    k = half * 4 + kk
                    for (c0, cn, mj) in segs:
                        P.op("act", lambda a, k=k, kk=kk, pb=pb, c0=c0, cn=cn, mj=mj: a.activation(
                            out=mT[:, k, c0:c0 + cn], in_=psb[pb][:, :].bitcast(BF16)[:, kk * TT + c0:kk * TT + c0 + cn],
                            func=AF.Identity, scale=mod_fm[:, 0, 8 + k, mj:mj + 1], bias=mod_fm[:, 0, k, mj:mj + 1]),
                            r=[pskey(pb), "mod_fm"], w=["mT"])

        def chunk_ab(self, n):
            ti, sample, ncols = self.ti, self.sample, self.ncols
            ci = cnt["ch"] % NT
            cnt["ch"] += 1
            self.cis[n] = ci
            pb = 2 + (n % 2)
            pk = pskey(pb)
            for k in range(NCH):
                P.op("pe", lambda t, k=k, n=n, pb=pb: t.matmul(psb[pb][:, 0:ncols], lhsT=W_in[:, k, n * 128:(n + 1) * 128],
                                                               rhs=mT[:, k, 0:ncols], start=(k == 0), stop=(k == NCH - 1)),
                     r=["W_in", "mT"], w=[pk])
            for k in range(NCH):
                P.op("pe", lambda t, k=k, n=n, pb=pb: t.matmul(psb[pb][:, 256:256 + ncols],
                                                               lhsT=W_in[:, k, D + n * 128:D + (n + 1) * 128],
                                                               rhs=mT[:, k, 0:ncols], start=(k == 0), stop=(k == NCH - 1)),
                     r=["W_in", "mT"], w=[pk])
            if sample:
                xe = xbe_s[:, n, :, :]
                xek = "xbe_s"
                P.op("dve", lambda v, pb=pb, xe=xe: v.tensor_copy(
                    out=xe[:, :, 3:3 + DEC_S], in_=psb[pb][:, 0:ncols].rearrange("p (b t) -> p b t", t=DEC_S)),
                    r=[pk], w=[xek])
                sh = lambda k: xe[:, :, k:k + DEC_S]
                v3 = lambda ap: ap[:, 0:ncols].rearrange("p (b t) -> p b t", t=DEC_S)
            else:
                xe = xbe[:, n, :]
                xek = ("xbe", n)
                if ti > 0:
                    P.op("dve", lambda v, xe=xe: v.tensor_copy(out=xe[:, 0:3], in_=xe[:, TT:TT + 3]), r=[xek], w=[xek])
                P.op("dve", lambda v, pb=pb, xe=xe: v.tensor_copy(out=xe[:, 3:3 + TT], in_=psb[pb][:, 0:TT]), r=[pk], w=[xek])
                sh = lambda k: xe[:, k:k + TT]
                v3 = lambda ap: ap[:, 0:ncols]
            zk = "zs%d" % ci
            P.op("act", lambda a, pb=pb, ci=ci: a.activation(out=zs[ci][:, 0:ncols], in_=psb[pb][:, 256:256 + ncols],
                                                             func=AF.Sigmoid), r=[pk], w=[zk])
            P.op("dve", lambda v, pb=pb, ci=ci: v.tensor_tensor(out=zs[ci][:, 0:ncols], in0=psb[pb][:, 256:256 + ncols],
                                                                in1=zs[ci][:, 0:ncols], op=ALU.mult), r=[pk, zk], w=[zk])
            ck = "xc%d" % ci
            P.op("dve", lambda v, ci=ci, n=n: v.tensor_scalar(out=v3(xc[ci]), in0=sh(0), scalar1=pf[:, 0, n:n + 1],
                                                              scalar2=pf[:, 4, n:n + 1], op0=ALU.mult, op1=ALU.add),
                 r=[xek, "pf"], w=[ck])
            for k in range(1, 4):
                P.op("dve", lambda v, ci=ci, n=n, k=k: v.scalar_tensor_tensor(
                    out=v3(xc[ci]), in0=sh(k), scalar=pf[:, k, n:n + 1], in1=v3(xc[ci]), op0=ALU.mult, op1=ALU.add),
                    r=[xek, "pf", ck], w=[ck])
            cbk = "xcb%d" % ci
            P.op("pool", lambda g, ci=ci: g.tensor_copy(out=xcb[ci][:, 0:ncols], in_=xc[ci][:, 0:ncols]), r=[ck], w=[cbk])

        def chunk_cde(self, n):
            ti, sample, ncols = self.ti, self.sample, self.ncols
            ci = self.cis[n]
            zk, ck, cbk = "zs%d" % ci, "xc%d" % ci, "xcb%d" % ci
            pg = 4 + (n % 2)
            pgk = pskey(pg)
            P.op("pe", lambda t, n=n, ci=ci, pg=pg: t.matmul(psb[pg][:, 0:ncols], lhsT=W_r[:, n, :], rhs=xcb[ci][:, 0:ncols],
                                                             start=True, stop=True), r=["W_r", cbk], w=[pgk])
            P.op("pe", lambda t, n=n, ci=ci, pg=pg: t.matmul(psb[pg][:, 256:256 + ncols], lhsT=W_i[:, n, :],
                                                             rhs=xcb[ci][:, 0:ncols], start=True, stop=True),
                 r=["W_i", cbk], w=[pgk])
            rk, ik, gk, bk, hk = "ra%d" % ci, "ri%d" % ci, "ga%d" % ci, "bb%d" % ci, "hs%d" % ci
            P.op("act", lambda a, n=n, ci=ci, pg=pg: a.activation(out=ra[ci][:, 0:ncols], in_=psb[pg][:, 0:ncols],
                                                                  func=AF.Sigmoid, bias=pf[:, 5, n:n + 1]),
                 r=[pgk, "pf"], w=[rk])
            P.op("act", lambda a, n=n, ci=ci, pg=pg: a.activation(out=ri[ci][:, 0:ncols], in_=psb[pg][:, 256:256 + ncols],
                                                                  func=AF.Sigmoid, bias=pf[:, 6, n:n + 1]),
                 r=[pgk, "pf"], w=[ik])
            P.op("act", lambda a, n=n, ci=ci: a.activation(out=ga[ci][:, 0:ncols], in_=ra[ci][:, 0:ncols], func=AF.Exp,
                                                           scale=pf[:, 8, n:n + 1]), r=[rk, "pf"], w=[gk])
            P.op("act", lambda a, n=n, ci=ci: a.activation(out=ra[ci][:, 0:ncols], in_=ra[ci][:, 0:ncols], func=AF.Exp,
                                                           scale=pf[:, 7, n:n + 1]), r=[rk, "pf"], w=[rk])
            P.op("dve", lambda v, ci=ci: v.tensor_scalar(out=ga[ci][:, 0:ncols], in0=ga[ci][:, 0:ncols], scalar1=-1.0,
                                                         scalar2=1.0, op0=ALU.mult, op1=ALU.add), r=[gk], w=[gk])
            P.op("dve", lambda v, ci=ci: v.tensor_scalar_max(out=ga[ci][:, 0:ncols], in0=ga[ci][:, 0:ncols], scalar1=1e-30),
                 r=[gk], w=[gk])
            P.op("act", lambda a, ci=ci: a.activation(out=ga[ci][:, 0:ncols], in_=ga[ci][:, 0:ncols], func=AF.Ln),
                 r=[gk], w=[gk])
            P.op("act", lambda a, ci=ci: a.activation(out=ga[ci][:, 0:ncols], in_=ga[ci][:, 0:ncols], func=AF.Exp, scale=0.5),
                 r=[gk], w=[gk])
            P.op("pool", lambda g, ci=ci: g.tensor_tensor(out=bb[ci][:, 0:ncols], in0=ri[ci][:, 0:ncols],
                                                          in1=xc[ci][:, 0:ncols], op=ALU.mult), r=[ik, ck], w=[bk])
            P.op("dve", lambda v, ci=ci: v.tensor_tensor(out=bb[ci][:, 0:ncols], in0=bb[ci][:, 0:ncols],
                                                         in1=ga[ci][:, 0:ncols], op=ALU.mult), r=[bk, gk], w=[bk])
            if sample:
                for b in range(DEC_B):
                    P.op("dve", lambda v, ci=ci, n=n, b=b: v.tensor_tensor_scan(
                        out=hs[ci][:, b * DEC_S:(b + 1) * DEC_S], data0=ra[ci][:, b * DEC_S:(b + 1) * DEC_S],
                        data1=bb[ci][:, b * DEC_S:(b + 1) * DEC_S], initial=h0s[:, n, b:b + 1], op0=ALU.mult, op1=ALU.add),
                        r=[rk, bk, "h0s"], w=[hk])
                P.op("dve", lambda v, ci=ci, n=n: v.tensor_copy(
                    out=hlast_s[:, n, :], in_=hs[ci][:, 0:ncols].rearrange("p (b t) -> p b t", t=DEC_S)[:, :, DEC_S - 1]),
                    r=[hk], w=["hlast_s"])
            else:
                P.op("dve", lambda v, ci=ci, n=n: v.tensor_tensor_scan(
                    out=hs[ci][:, 0:TT], data0=ra[ci][:, 0:TT], data1=bb[ci][:, 0:TT], initial=hprev[:, n:n + 1],
                    op0=ALU.mult, op1=ALU.add), r=[rk, bk, ("hprev", n)], w=[hk])
                P.op("dve", lambda v, ci=ci, n=n: v.tensor_copy(out=hprev[:, n:n + 1], in_=hs[ci][:, TT - 1:TT]),
                     r=[hk], w=[("hprev", n)])
            P.op("pool", lambda g, ci=ci, n=n: g.tensor_tensor(out=yg[:, n, 0:ncols], in0=hs[ci][:, 0:ncols],
                                                               in1=zs[ci][:, 0:ncols], op=ALU.mult), r=[hk, zk], w=["yg"])

        def chunks(self, lo, hi):
            if CHPIPE == 0:
                for n in range(lo, hi):
                    self.chunk_ab(n)
                    self.chunk_cde(n)
                return
            for n in range(lo, hi):
                self.chunk_ab(n)
                if n - 1 >= 0:
                    self.chunk_cde(n - 1)
            if hi == NCH:
                self.chunk_cde(NCH - 1)

        def outproj_ln(self, subs=None):
            ti, sample, nsub, np_, t0, xt, xk = self.ti, self.sample, self.nsub, self.np_, self.t0, self.xt, self.xk
            if not hasattr(self, "sis"):
                self.sis = {}
            for s in (range(nsub) if subs is None else subs):
                si = cnt["sub"] % 2
                cnt["sub"] += 1
                self.sis[s] = si
                vk, x1k, x1bk, stk = "vt%d" % si, "x1t%d" % si, "x1b%d" % si, "stat%d" % si
                for h in range(2):
                    pb = 4 + h
                    for k in range(NCH):
                        P.op("pe", lambda t, k=k, h=h, s=s, pb=pb: t.matmul(
                            psb[pb][0:np_, :], lhsT=yg[:, k, s * 128:s * 128 + np_], rhs=W_out[:, k, h * 512:(h + 1) * 512],
                            start=(k == 0), stop=(k == NCH - 1)), r=["yg", "W_out"], w=[pskey(pb)])
                    G = Gs if sample else Gp
                    P.op("dve", lambda v, h=h, pb=pb, si=si, G=G: v.tensor_tensor(
                        out=vt[si][0:np_, h * 512:(h + 1) * 512], in0=psb[pb][0:np_, :], in1=G[0:np_, 0, h * 512:(h + 1) * 512],
                        op=ALU.mult), r=[pskey(pb), "Gs" if sample else "Gp"], w=[vk])
                P.op("dve", lambda v, si=si, s=s: v.scalar_tensor_tensor(
                    out=vt[si][0:np_, :], in0=xt[0:np_, s, :], scalar=ALPHA, in1=vt[si][0:np_, :], op0=ALU.mult, op1=ALU.add),
                    r=[xk, vk], w=[vk])
                layernorm_tm(vt[si], vk, x1t[si], x1k, np_, 0, stat[si], stk)
                if not sample:
                    P.dma("sp", x1scr[t0 + s * 128:t0 + (s + 1) * 128, :], x1t[si][:, :], r=[x1k], w=[("x1scr", ti, s)], semkey=x1k)
                else:
                    P.dma("sp", x1s_scr[:, :], x1t[si][0:np_, :], r=[x1k], w=["x1s_scr"], semkey=x1k)
                P.op("act", lambda a, si=si: a.activation(out=x1b[si][0:np_, :], in_=x1t[si][0:np_, :], func=AF.Identity),
                     r=[x1k], w=[x1bk])

        def tail(self, subs=None, fin=True):
            ti, sample, nsub, np_, t0 = self.ti, self.sample, self.nsub, self.np_, self.t0
            for s in (range(nsub) if subs is None else subs):
                si = self.sis[s]
                x1bk, kvk = "x1b%d" % si, "kvst%d" % si
                pb = 6 + (s % 2)
                for k in range(NCH):
                    P.op("pe", lambda t, k=k, si=si, pb=pb: t.transpose(
                        psb[pb][:, :].bitcast(BF16)[:, k * 128:k * 128 + np_], x1b[si][0:np_, k * 128:(k + 1) * 128],
                        ident_b[0:np_, 0:np_]), r=[x1bk, "ident_b"], w=[pskey(pb)])
                P.op("dve", lambda v, pb=pb, s=s: v.tensor_copy(
                    out=x1T[:, :, s * 128:s * 128 + np_],
                    in_=psb[pb][:, :].bitcast(BF16)[:, 0:1024].rearrange("p (k t) -> p k t", t=128)[:, :, 0:np_]),
                    r=[pskey(pb)], w=[("x1T", s)])
                for c3 in range(3):
                    pb = c3 % 2
                    for k in range(NCH):
                        P.op("pe", lambda t, k=k, c3=c3, s=s, pb=pb: t.matmul(
                            psb[pb][0:np_, :], lhsT=x1T[:, k, s * 128:s * 128 + np_], rhs=W_kv[:, k, c3 * 512:(c3 + 1) * 512],
                            start=(k == 0), stop=(k == NCH - 1)), r=[("x1T", s), "W_kv"], w=[pskey(pb)])
                    P.op("act", lambda a, c3=c3, pb=pb, si=si: a.activation(
                        out=kvst[si][0:np_, c3 * 512:(c3 + 1) * 512], in_=psb[pb][0:np_, :], func=AF.Identity),
                        r=[pskey(pb)], w=[kvk])
                if sample:
                    P.dma("sp", o_cmp_s[:, :], kvst[si][0:np_, 0:512], r=[kvk], w=["o_cmp_s"], semkey=kvk)
                    P.dma("sp", o_sel_s[:, :], kvst[si][0:np_, 512:1024], r=[kvk], w=["o_sel_s"], semkey=kvk)
                    for b in range(DEC_B):
                        P.dma("sp", o_win_s[b * 512 + 504:(b + 1) * 512, :], kvst[si][b * 8:(b + 1) * 8, 1024:1536],
                              r=[kvk], w=[("o_win_s", b, 1)], semkey=kvk)
                else:
                    r0 = t0 + s * 128
                    P.dma("sp", o_cmp_p[r0:r0 + 128, :], kvst[si][:, 0:512], r=[kvk], w=[("o_cmp_p", ti, s)], semkey=kvk)
                    P.dma("sp", o_sel_p[r0:r0 + 128, :], kvst[si][:, 512:1024], r=[kvk], w=[("o_sel_p", ti, s)], semkey=kvk)
                    P.dma("sp", winscr[r0:r0 + 128, :], kvst[si][:, 1024:1536], r=[kvk], w=[("winscr", ti, s)], semkey=kvk)
                    if r0 >= SEQ - 512:
                        P.dma("sp", o_win_p[r0 - (SEQ - 512):r0 - (SEQ - 512) + 128, :], kvst[si][:, 1024:1536],
                              r=[kvk], w=[("o_win_p", ti, s)], semkey=kvk)
            if sample and fin:
                for n in range(NCH):
                    P.dma("sp", o_h_s.rearrange("b (c p) -> c p b", p=128)[n], hlast_s[:, n, :], r=["hlast_s"], w=[("o_h_s", n)],
                          semkey="hlast_s")
                    for b in range(DEC_B):
                        P.dma("sp", o_conv_s[b * 3:(b + 1) * 3, n * 128:(n + 1) * 128].rearrange("k p -> p k"),
                              xbe_s[:, n, b, DEC_S:DEC_S + 3], r=["xbe_s"], w=[("o_conv_s", n, b)], semkey="xbe_s_o")

        def final_state(self):
            P.dma("sp", o_h_p[0:1, :].rearrange("o (c p) -> p (o c)", p=128), hprev[:, :],
                  r=[("hprev", n) for n in range(NCH)], w=["o_h_p"], semkey="hprev_o")
            for n in range(NCH):
                P.dma("sp", o_conv_p.rearrange("k (c p) -> c p k", p=128)[n], xbe[:, n, TT:TT + 3], r=[("xbe", n)],
                      w=[("o_conv_p", n)], semkey="xbe_o")

    import os
    L0PIPE = int(os.environ.get("L0PIPE", "2"))
    n_ptiles = SEQ // TT if stage >= 1 else 2
    if L0PIPE == -1:
        for i in range(n_ptiles + 1):
            tl = L0Tile(0, True) if i == 0 else L0Tile(i - 1, False)
            tl.front()
            tl.chunks(0, NCH)
            for s_ in range(tl.nsub):
                tl.outproj_ln([s_])
                tl.tail([s_], fin=(s_ == tl.nsub - 1))
        tl.final_state()
    elif L0PIPE == 0:
        seq = [L0Tile(0, True)] + [None] * n_ptiles
        for i in range(n_ptiles + 1):
            tl = seq[i] if i == 0 else L0Tile(i - 1, False)
            tl.front()
            tl.chunks(0, NCH)
            tl.outproj_ln()
            tl.tail()
        tl.final_state()
    elif L0PIPE == 1:
        cur = L0Tile(0, True)
        cur.front()
        for i in range(n_ptiles + 1):
            cur.chunks(0, NCH)
            nxt = L0Tile(i, False) if i < n_ptiles else None
            if nxt is not None:
                nxt.front()
            for s_ in range(cur.nsub):
                cur.outproj_ln([s_])
                cur.tail([s_], fin=(s_ == cur.nsub - 1))
            last = cur
            cur = nxt
        last.final_state()
    else:
        tiles = [L0Tile(0, True)]
        tiles[0].front()
        tiles[0].chunks(0, NCH)
        tiles[0].outproj_ln()
        nxt = L0Tile(0, False)
        nxt.front()
        prev = tiles[0]
        for ti in range(n_ptiles):
            cur = nxt
            cur.chunks(0, NCH // 2)
            prev.tail()
            cur.chunks(NCH // 2, NCH)
            if ti + 1 < n_ptiles:
                nxt = L0Tile(ti + 1, False)
                nxt.front()
            cur.outproj_ln()
            prev = cur
        prev.tail()
        prev.final_state()

    scA.__exit__(None, None, None)
    cmpscr_p = P.dram("cmpscr_p", [128, 512], F32)
    cmpscr_s = P.dram("cmpscr_s", [DEC_B * 128, 512], F32)
    do_sample = stage >= 3
    with P.scope():
        W1r = P.sb("W1r", [128, 2, 64, 128], BF16)
        CB2s = [P.sb("CB2_%d" % i, [128, 64, 512], BF16) for i in range(2)]
        cbi = {"i": 0}
        Hh = P.sb("Hh", [128, 2, 2, 256], BF16)
        W2 = P.sb("W2", [128, 2, 64], BF16)
        PEsb = P.sb("PEsb", [64, 128], BF16)
        bias1 = P.sb("bias1", [128, 4], F32)
        b2bc = P.sb("b2bc", [64, 2, 4, 128], F32)
        CS = P.sb("CS", [64, 2, 512], F32)
        IDXf = P.sb("IDXf", [128, DEC_B * 64], F32)
        IDXi = P.sb("IDXi", [128, DEC_B * 64], I32)
        iop = P.sb("iop", [128, 2], I32)
        iopf = P.sb("iopf", [128, 2], F32)
        for c in range(2):
            for half in range(2):
                P.dma("pool", W1r[half * 64:(half + 1) * 64, c, :, :], w_phi1[c], w=["W1r"])
            P.dma("pool", W2[:, c, :], w_phi2[c], w=["W2"])
            P.dma("sp", bias1[:, c:c + 1], b_phi1[c:c + 1, :].rearrange("o p -> p o"), w=["bias1"])
        P.dma("pool", PEsb[:], phi_pe[:, :], w=["PEsb"])
        for nl in range(2):
            for g in range(4):
                P.dma("sp", b2bc[:, nl, g, :], b_phi2.rearrange("c d -> (c d)").rearrange("(o n) -> o n", o=1).partition_broadcast(64),
                      w=["b2bc"])
        for c in range(2):
            for d in range(64):
                P.op("pe", lambda t, c=c, d=d: t.matmul(psb[0][:, c:c + 1], lhsT=W1r[0:64, c, d, :],
                                                        rhs=PEsb[0:64, c * 64 + d:c * 64 + d + 1],
                                                        start=(d == 0), stop=(d == 63)), r=["W1r", "PEsb"], w=[pskey(0)])
        P.op("dve", lambda v: v.tensor_tensor(out=bias1[:, 0:2], in0=psb[0][:, 0:2], in1=bias1[:, 0:2], op=ALU.add),
             r=[pskey(0), "bias1"], w=["bias1"])
        P.op("pool", lambda g_: g_.iota(out=iop[:, 0:1], pattern=[[0, 1]], base=0, channel_multiplier=1), w=["iop"])
        P.op("dve", lambda v: v.tensor_copy(out=iopf[:, 0:1], in_=iop[:, 0:1]), r=["iop"], w=["iopf"])
        P.dma("sp", IDXi[:], ptab.rearrange("b n -> (b n)").rearrange("(o n) -> o n", o=1).partition_broadcast(128), w=["IDXi"])
        P.op("dve", lambda v: v.tensor_copy(out=IDXf[:], in_=IDXi[:]), r=["IDXi"], w=["IDXf"])
        P.op("dve", lambda v: v.tensor_scalar(out=IDXf[:], in0=IDXf[:], scalar1=128.0, scalar2=iopf[:, 0:1],
                                              op0=ALU.mult, op1=ALU.add), r=["IDXf", "iopf"], w=["IDXf"])
        P.op("dve", lambda v: v.tensor_copy(out=IDXi[:], in_=IDXf[:]), r=["IDXf"], w=["IDXi"])
        idxscr = P.dram("idxscr", [128, DEC_B * 64], I32)
        P.dma("sp", idxscr[:, :], IDXi[:], r=["IDXi"], w=["idxscr"], semkey="IDXi_o")

        def compress(load_pages, out_rows, okey):
            CB2 = CB2s[cbi["i"] % 2]
            cbk = "CB2_%d" % (cbi["i"] % 2)
            cbi["i"] += 1
            CB2v = CB2[:].rearrange("p n (g c d) -> p n g c d", g=4, c=2)
            load_pages(CB2, cbk)
            for c in range(2):
                for nl in range(2):
                    pb = c * 2 + nl
                    for d in range(64):
                        P.op("pe", lambda t, c=c, nl=nl, d=d, pb=pb: t.matmul(
                            psb[pb][:, 0:256], lhsT=W1r[nl * 64:(nl + 1) * 64, c, d, :],
                            rhs=CB2v[nl * 64:(nl + 1) * 64, :, :, c, d], start=(d == 0), stop=(d == 63)),
                            r=["W1r", cbk], w=[pskey(pb)])
                    P.op("act", lambda a, c=c, nl=nl, pb=pb: a.activation(out=Hh[:, c, nl, :], in_=psb[pb][:, 0:256], func=AF.Silu,
                                                                          bias=bias1[:, c:c + 1]), r=[pskey(pb), "bias1"], w=["Hh"])
            Hv = Hh[:].rearrange("p c n (pg g) -> p c n pg g", g=4)
            for nl in range(2):
                pb = 4 + nl
                for g in range(4):
                    for c in range(2):
                        col = (g * 2 + c) * 64
                        P.op("pe", lambda t, nl=nl, g=g, c=c, pb=pb, col=col: t.matmul(
                            psb[pb][0:64, col:col + 64], lhsT=Hv[:, c, nl, :, g], rhs=W2[:, c, :], start=True, stop=True),
                            r=["Hh", "W2"], w=[pskey(pb)])
                P.op("dve", lambda v, nl=nl, pb=pb: v.tensor_tensor(
                    out=CS[:, nl, :], in0=psb[pb][0:64, :], in1=b2bc[:, nl, :, :].rearrange("p g f -> p (g f)"), op=ALU.add),
                    r=[pskey(pb), "b2bc"], w=["CS"])
            P.dma("sp", out_rows.rearrange("(pg n) f -> pg n f", n=2), CS[:], r=["CS"], w=[okey], semkey="CS")

        def load_prompt_pages(CB2, cbk):
            for pg in range(64):
                P.dma("pool", CB2[:, pg, :], o_cmp_p[pg * 128:(pg + 1) * 128, :], r=[("o_cmp_p", pg // NSUB, pg % NSUB)], w=[cbk])

        if stage >= 2:
            compress(load_prompt_pages, cmpscr_p[:, :], "cmpscr_p")
        if do_sample:
            for b in range(DEC_B):
                def load_sample_pages(CB2, cbk, b=b):
                    for pg in range(64):
                        P.gather(CB2[:, pg, :], ccmp[:, :], IDXi[:, b * 64 + pg:b * 64 + pg + 1], r=["IDXi"], w=[cbk])
                compress(load_sample_pages, cmpscr_s[b * 128:(b + 1) * 128, :], ("cmpscr_s", b))

    NTOK = 2048 + NS_TOK
    QTscr = P.dram("QTscr", [4, 64, 4, NTOK], BF16)
    ZSscr = P.dram("ZSscr", [NTOK, D], BF16)
    GLscr = P.dram("GLscr", [NTOK, 48], F32)
    OGscr = P.dram("OGscr", [NTOK, D], BF16)
    qtiles = [(jl * 128, 128, jl, None) for jl in range(16)] if stage >= 2 else []
    if do_sample:
        qtiles += [(2048 + b * 8, 8, None, b) for b in range(DEC_B)]
    if stage == 2.5:
        qtiles = qtiles[:2]

    with P.scope():
        W_inb = P.sb("W_inb", [128, NCH, 2096], BF16)
        for k in range(NCH):
            P.dma("pool", W_inb[:, k, :], w_in_b[k * 128:(k + 1) * 128, :], w=["W_inb"])
        bgbc = P.sb("bgbc", [128, 48], F32)
        P.dma("sp", bgbc[:], b_gate[0:1, :].partition_broadcast(128), w=["bgbc"])
        idxo = P.sb("idxo", [128, 16], I32)
        P.dma("sp", idxo[:], t_idx_own[:, :], w=["idxo"])
        X1 = [P.sb("X1_%d" % i, [128, D], F32) for i in range(2)]
        X1b = P.sb("X1b", [128, D], BF16)
        m1T = P.sb("m1T", [128, NCH, 128], BF16)
        QTst = [P.sb("QTst%d" % i, [64, 4, 128], BF16) for i in range(2)]
        ZSt = [P.sb("ZSt%d" % i, [128, D], BF16) for i in range(2)]
        GLt = [P.sb("GLt%d" % i, [128, 48], F32) for i in range(2)]
        qi = 0
        for (tok0, nq, jl, sb_) in qtiles:
            i2 = qi % 2
            xk = "X1_%d" % i2
            if jl is not None:
                P.gather(X1[i2][:, :], x1scr[:, :], idxo[:, jl:jl + 1], r=["idxo"], w=[xk])
                mj = 0
            else:
                P.dma("sp", X1[i2][0:nq, :], x1s_scr[sb_ * 8:(sb_ + 1) * 8, :], w=[xk])
                mj = 1 + sb_
            P.op("pool", lambda g_, i2=i2, nq=nq: g_.tensor_copy(out=X1b[0:nq, :], in_=X1[i2][0:nq, :]), r=[xk], w=["X1b"])
            for k in range(NCH):
                P.op("pe", lambda t, k=k, nq=nq: t.transpose(psb[0][:, :].bitcast(BF16)[:, k * 128:k * 128 + nq],
                                                             X1b[0:nq, k * 128:(k + 1) * 128], ident_b[0:nq, 0:nq]),
                     r=["X1b", "ident_b"], w=[pskey(0)])
            for k in range(NCH):
                P.op("act", lambda a, k=k, nq=nq, mj=mj: a.activation(
                    out=m1T[:, k, 0:nq], in_=psb[0][:, :].bitcast(BF16)[:, k * 128:k * 128 + nq], func=AF.Identity,
                    scale=mod_fm[:, 1, 8 + k, mj:mj + 1], bias=mod_fm[:, 1, k, mj:mj + 1]), r=[pskey(0), "mod_fm"], w=["m1T"])
            for g in range(4):
                pb = 2 + (g % 2)
                qk = "QTst%d" % (g % 2)
                for hh in range(4):
                    for k in range(NCH):
                        P.op("pe", lambda t, k=k, hh=hh, g=g, pb=pb, nq=nq: t.matmul(
                            psb[pb][0:64, hh * 128:hh * 128 + nq], lhsT=W_inb[:, k, (4 * g + hh) * 64:(4 * g + hh + 1) * 64],
                            rhs=m1T[:, k, 0:nq], start=(k == 0), stop=(k == NCH - 1)), r=["W_inb", "m1T"], w=[pskey(pb)])
                P.op("dve", lambda v, g=g, pb=pb, nq=nq: v.tensor_scalar_mul(
                    out=QTst[g % 2][:, :, 0:nq], in0=psb[pb][0:64, :].rearrange("p (h q) -> p h q", h=4)[:, :, 0:nq],
                    scalar1=0.125), r=[pskey(pb)], w=[qk])
                P.dma("sp", QTscr[g, :, :, tok0:tok0 + nq], QTst[g % 2][:, :, 0:nq], r=[qk], w=[("QTscr", g, tok0)], semkey=qk)
            zk = "ZSt%d" % i2
            for half in range(2):
                pb = 4 + half
                for k in range(NCH):
                    P.op("pe", lambda t, k=k, half=half, pb=pb, nq=nq: t.matmul(
                        psb[pb][0:nq, :], lhsT=m1T[:, k, 0:nq], rhs=W_inb[:, k, D + half * 512:D + (half + 1) * 512],
                        start=(k == 0), stop=(k == NCH - 1)), r=["W_inb", "m1T"], w=[pskey(pb)])
                P.op("act", lambda a, half=half, pb=pb, nq=nq, i2=i2: a.activation(
                    out=ZSt[i2][0:nq, half * 512:(half + 1) * 512], in_=psb[pb][0:nq, :], func=AF.Silu), r=[pskey(pb)], w=[zk])
            P.dma("sp", ZSscr[tok0:tok0 + nq, :], ZSt[i2][0:nq, :], r=[zk], w=[("ZSscr", tok0)], semkey=zk)
            gk = "GLt%d" % i2
            for k in range(NCH):
                P.op("pe", lambda t, k=k, nq=nq: t.matmul(psb[6][0:nq, 0:48], lhsT=m1T[:, k, 0:nq], rhs=W_inb[:, k, 2048:2096],
                                                          start=(k == 0), stop=(k == NCH - 1)), r=["W_inb", "m1T"], w=[pskey(6)])
            P.op("dve", lambda v, nq=nq, i2=i2: v.tensor_tensor(out=GLt[i2][0:nq, :], in0=psb[6][0:nq, 0:48], in1=bgbc[0:nq, :],
                                                                op=ALU.add), r=[pskey(6), "bgbc"], w=[gk])
            P.op("act", lambda a, nq=nq, i2=i2: a.activation(out=GLt[i2][0:nq, :], in_=GLt[i2][0:nq, :], func=AF.Sigmoid),
                 r=[gk], w=[gk])
            P.dma("sp", GLscr[tok0:tok0 + nq, :], GLt[i2][0:nq, :], r=[gk], w=[("GLscr", tok0)], semkey=gk)
            qi += 1

    with P.scope():
        EE = P.sb("EE", [128, 64, 128], BF16)
        ones_b = P.sb("ones_b", [128, 1024], BF16)
        P.op("pool", lambda g_: g_.memset(ones_b[:], 1.0), w=["ones_b"])
        for T8 in range(8):
            P.op("pool", lambda g_, T8=T8: g_.affine_select(
                out=EE[:, T8 * 8:(T8 + 1) * 8, :].rearrange("p t (a b) -> p t a b", a=2),
                in_=ones_b[:, :].rearrange("p (t a b) -> p t a b", t=8, a=2),
                pattern=[[-2, 8], [-1, 2], [0, 64]], compare_op=ALU.is_equal, fill=0.0, base=-16 * T8, channel_multiplier=1),
                r=["ones_b"], w=["EE"])
        TRIp = P.sb("TRIp", [128, 512], BF16)
        TRI2p = P.sb("TRI2p", [128, 512], BF16)
        TRIs = P.sb("TRIs", [128, 32], BF16)
        TRI2s = P.sb("TRI2s", [128, 32], BF16)
        P.dma("sp", TRIp[:], t_tri_p[:, :], w=["TRIp"])
        P.dma("sp", TRI2p[:], t_tri2_p[:, :], w=["TRI2p"])
        P.dma("sp", TRIs[:], t_tri_s[:, :], w=["TRIs"])
        P.dma("sp", TRI2s[:], t_tri2_s[:, :], w=["TRI2s"])
        RB = [P.sb("RB%d" % i, [128, 4, 512], BF16) for i in range(2)]
        RBF = [P.sb("RBF%d" % i, [128, 512], BF16) for i in range(4)]
        KsT4 = P.sb("KsT4", [72, 4, 65 * 128], BF16)
        Vs4 = P.sb("Vs4", [128, 65, 4, 65], BF16)
        KwT4 = P.sb("KwT4", [72, 4, 21 * 128], BF16)
        Vw4 = P.sb("Vw4", [128, 21, 4, 65], BF16)
        KcT4 = P.sb("KcT4", [72, 4, 128], BF16)
        Vc4 = P.sb("Vc4", [128, 4, 64], BF16)
        P.op("pool", lambda g_: g_.memset(Vs4[:, :, :, 64:65], 1.0), w=["Vs4"])
        P.op("pool", lambda g_: g_.memset(Vw4[:, :, :, 64:65], 1.0), w=["Vw4"])
        idxo2 = P.sb("idxo2", [128, 16], I32)
        idxw = P.sb("idxw", [128, 20], I32)
        idxs = P.sb("idxs", [128, 1], I32)
        idxpg = P.sb("idxpg", [128, DEC_B * 64], I32)
        P.dma("sp", idxo2[:], t_idx_own[:, :], w=["idxo2"])
        P.dma("sp", idxw[:], t_idx_win[:, :], w=["idxw"])
        P.dma("sp", idxs[:], t_idx_slot[:, :], w=["idxs"])
        P.dma("sp", idxpg[:], idxscr[:, :], w=["idxpg"])
        QT = [P.sb("QT%d" % i, [72, 4, 128], BF16) for i in range(2)]
        FBNt = [P.sb("FBNt%d" % i, [128, 128], F32) for i in range(2)]
        CAUt = [P.sb("CAUt%d" % i, [128, 128], F32) for i in range(2)]
        TMt = [P.sb("TMt%d" % i, [128, 128], F32) for i in range(2)]
        GLg = [P.sb("GLg%d" % i, [128, 48], F32) for i in range(2)]
        ZSg = [P.sb("ZSg%d" % i, [128, 256], BF16) for i in range(2)]
        Ssb = P.sb("Ssb", [128, 512], F32)
        Esb = P.sb("Esb", [128, 512], F32)
        Pn = P.sb("Pn", [128, 512], F32)
        Pnb = P.sb("Pnb", [128, 512], BF16)
        imp = P.sb("imp", [128, 128], F32)
        scr = P.sb("scr", [128, 128], F32)
        scr2 = P.sb("scr2", [128, 128], F32)
        m8 = P.sb("m8", [128, 16], F32)
        smh = P.sb("smh", [128, 16], F32)
        smb = P.sb("smb", [128, 16], F32)
        MselT = P.sb("MselT", [128, 128], BF16)
        Msel4s = [P.sb("Msel4_%d" % i, [128, 512], BF16) for i in range(2)]
        PTc = P.sb("PTc", [128, 512], BF16)
        PT = [P.sb("PT%d" % i, [128, 512], BF16) for i in range(3)]
        PTm = [P.sb("PTm%d" % i, [128, 512], BF16) for i in range(3)]
        OaugSB = P.sb("OaugSB", [65, 512], F32)
        accs = [P.sb("acc%d" % i, [128, 256], F32) for i in range(2)]
        OGt = [P.sb("OGt%d" % i, [128, 256], BF16) for i in range(2)]
        cnt2 = {"rb": 0, "rbf": 0, "pt": 0, "ps": 0, "q": 0, "mx": 0}

        def prep_from_rbf(rbf, rk, g, nk, ktdst, kkey, vdst, vkey):
            P.op("pe", lambda t: t.transpose(psb[7][:, :].bitcast(BF16)[0:64, 0:nk], rbf[0:nk, g * 128:g * 128 + 64],
                                             ident_b[0:nk, 0:nk]), r=[rk, "ident_b"], w=[pskey(7)])
            P.op("dve", lambda v: v.tensor_copy(out=ktdst, in_=psb[7][:, :].bitcast(BF16)[0:64, 0:nk]), r=[pskey(7)], w=[kkey])
            P.op("pool", lambda g_: g_.tensor_copy(out=vdst, in_=rbf[0:nk, g * 128 + 64:g * 128 + 128]), r=[rk], w=[vkey])

        def prep4(rbf, rk, nk, ktdst3, kkey, vdst3, vkey):
            for g4 in range(4):
                P.op("pe", lambda t, g4=g4: t.transpose(psb[7][:, :].bitcast(BF16)[0:64, g4 * 128:g4 * 128 + nk],
                                                        rbf[0:nk, g4 * 128:g4 * 128 + 64], ident_b[0:nk, 0:nk]),
                     r=[rk, "ident_b"], w=[pskey(7)])
            P.op("dve", lambda v: v.tensor_copy(
                out=ktdst3, in_=psb[7][:, :].bitcast(BF16)[0:64, 0:512].rearrange("p (g k) -> p g k", g=4)[:, :, 0:nk]),
                r=[pskey(7)], w=[kkey])
            P.op("pool", lambda g_: g_.tensor_copy(
                out=vdst3, in_=rbf[0:nk, :].rearrange("p (g c d) -> p g c d", g=4, c=2)[:, :, 1, :]), r=[rk], w=[vkey])

        def load_rows_gather(src, idx_ap, ikey):
            i = cnt2["rbf"] % 4
            cnt2["rbf"] += 1
            P.gather(RBF[i][:, :], src, idx_ap, r=[ikey], w=["RBF%d" % i])
            return RBF[i], "RBF%d" % i

        def load_rows_plain(src_rows, nk):
            i = cnt2["rbf"] % 4
            cnt2["rbf"] += 1
            P.dma("pool", RBF[i][0:nk, :], src_rows, w=["RBF%d" % i])
            return RBF[i], "RBF%d" % i

        def nsa_tile(g, tok0, nq, jl, sb_, kth):
            ncol = 4 * nq
            sample = jl is None
            i2 = cnt2["q"] % 2
            cnt2["q"] += 1
            qt, qk = QT[i2], "QT%d" % i2
            Msel4, mk4 = Msel4s[i2], "Msel4_%d" % i2
            acc, ak = accs[i2], "acc%d" % i2
            P.dma("sp", qt[0:64, :, 0:nq], QTscr[g, :, :, tok0:tok0 + nq], w=[qk])
            if sample:
                P.dma("sp", qt[64:72, :, 0:nq], t_qaug_s.rearrange("r (h q) -> r h q", h=16)[:, 4 * g:4 * g + 4, :], w=[qk])
                P.dma("sp", FBNt[i2][0:nq, :], t_fbn_s[:, :], w=["FBNt%d" % i2])
                P.dma("sp", CAUt[i2][0:nq, :], t_caus_s[:, :], w=["CAUt%d" % i2])
                P.dma("sp", TMt[i2][0:nq, :], t_tm_s[:, :], w=["TMt%d" % i2])
            else:
                P.dma("sp", qt[64:72, :, 0:nq], t_qaug_p.rearrange("r (h q) -> r h q", h=16)[:, 4 * g:4 * g + 4, tok0:tok0 + nq],
                      w=[qk])
                P.dma("sp", FBNt[i2][:, :], t_fbn[jl], w=["FBNt%d" % i2])
                P.dma("sp", CAUt[i2][:, :], t_caus[jl], w=["CAUt%d" % i2])
                P.dma("sp", TMt[i2][:, :], t_tm[jl], w=["TMt%d" % i2])
            fk, ck, tk, glk, zk = "FBNt%d" % i2, "CAUt%d" % i2, "TMt%d" % i2, "GLg%d" % i2, "ZSg%d" % i2
            P.dma("sp", GLg[i2][0:nq, :], GLscr[tok0:tok0 + nq, :], w=[glk])
            P.dma("sp", ZSg[i2][0:nq, :], ZSscr[tok0:tok0 + nq, g * 256:(g + 1) * 256], w=[zk])
            qrhs = qt[0:72, :, 0:nq]
            kc_ap, kck, vc_ap, vck = KcT4[0:72, g, :], "KcT4", Vc4[:, g, :], "Vc4"
            ksel = lambda T, nk: (KsT4[0:72, g, T * 128:T * 128 + nk], "KsT4", Vs4[0:nk, T, g, :], "Vs4")
            kwin = lambda T, nk: (KwT4[0:72, g, T * 128:T * 128 + nk], "KwT4", Vw4[0:nk, T, g, :], "Vw4")
            gl3 = GLg[i2][0:nq, :].rearrange("p (h b) -> p h b", b=3)
            for hh in range(4):
                P.op("pe", lambda t, hh=hh: t.matmul(psb[6][0:nq, hh * 128:(hh + 1) * 128], lhsT=qt[0:72, hh, 0:nq], rhs=kc_ap,
                                                     start=True, stop=True), r=[qk, kck], w=[pskey(6)])
            P.op("dve", lambda v: v.tensor_tensor(
                out=Ssb[0:nq, :].rearrange("p (h s) -> p h s", h=4), in0=psb[6][0:nq, :].rearrange("p (h s) -> p h s", h=4),
                in1=TMt[i2][0:nq, :].unsqueeze(1).to_broadcast([nq, 4, 128]), op=ALU.add), r=[pskey(6), tk], w=["Ssb"])
            for hh in range(4):
                P.op("act", lambda a, hh=hh: a.activation(out=Esb[0:nq, hh * 128:(hh + 1) * 128], in_=Ssb[0:nq, hh * 128:(hh + 1) * 128],
                                                          func=AF.Exp, accum_out=smh[0:nq, hh:hh + 1]), r=["Ssb"], w=["Esb", "smh"])
            P.op("dve", lambda v: v.tensor_scalar_add(out=smh[0:nq, 4:8], in0=smh[0:nq, 0:4], scalar1=1e-30), r=["smh"], w=["smh"])
            P.op("dve", lambda v: v.reciprocal(out=smh[0:nq, 4:8], in_=smh[0:nq, 4:8]), r=["smh"], w=["smh"])
            for hh in range(4):
                P.op("dve", lambda v, hh=hh: v.tensor_scalar_mul(out=Pn[0:nq, hh * 128:(hh + 1) * 128],
                                                                 in0=Esb[0:nq, hh * 128:(hh + 1) * 128],
                                                                 scalar1=smh[0:nq, 4 + hh:5 + hh]), r=["Esb", "smh"], w=["Pn"])
            P.op("pool", lambda g_: g_.tensor_copy(out=Pnb[0:nq, :], in_=Pn[0:nq, :]), r=["Pn"], w=["Pnb"])
            P.op("dve", lambda v: v.tensor_tensor(out=imp[0:nq, :], in0=Pn[0:nq, 0:128], in1=Pn[0:nq, 128:256], op=ALU.add),
                 r=["Pn"], w=["imp"])
            P.op("dve", lambda v: v.tensor_tensor(out=imp[0:nq, :], in0=imp[0:nq, :], in1=Pn[0:nq, 256:384], op=ALU.add),
                 r=["Pn", "imp"], w=["imp"])
            P.op("dve", lambda v: v.tensor_tensor(out=imp[0:nq, :], in0=imp[0:nq, :], in1=Pn[0:nq, 384:512], op=ALU.add),
                 r=["Pn", "imp"], w=["imp"])
            P.op("dve", lambda v: v.tensor_tensor(out=scr[0:nq, :], in0=imp[0:nq, :], in1=CAUt[i2][0:nq, :], op=ALU.mult),
                 r=["imp", ck], w=["scr"])
            P.op("dve", lambda v: v.tensor_tensor(out=scr[0:nq, :], in0=scr[0:nq, :], in1=FBNt[i2][0:nq, :], op=ALU.add),
                 r=["scr", fk], w=["scr"])
            P.op("dve", lambda v: v.max(out=m8[0:nq, 0:8], in_=scr[0:nq, :]), r=["scr"], w=["m8"])
            P.op("dve", lambda v: v.match_replace(out=scr2[0:nq, :], in_to_replace=m8[0:nq, 0:8], in_values=scr[0:nq, :],
                                                  imm_value=-1.0e30), r=["scr", "m8"], w=["scr2"])
            P.op("dve", lambda v: v.max(out=m8[0:nq, 8:16], in_=scr2[0:nq, :]), r=["scr2"], w=["m8"])
            thr = m8[0:nq, kth - 1:kth]
            P.op("dve", lambda v: v.tensor_scalar(out=scr2[0:nq, :], in0=scr[0:nq, :], scalar1=thr, scalar2=None, op0=ALU.is_ge),
                 r=["scr", "m8"], w=["scr2"])
            P.op("dve", lambda v: v.tensor_tensor(out=scr2[0:nq, :], in0=scr2[0:nq, :], in1=CAUt[i2][0:nq, :], op=ALU.mult),
                 r=["scr2", ck], w=["scr2"])
            P.op("dve", lambda v: v.tensor_copy(out=MselT[0:nq, :], in_=scr2[0:nq, :]), r=["scr2"], w=["MselT"])
            yield "a"
            P.op("pe", lambda t: t.transpose(psb[7][:, :].bitcast(BF16)[:, 0:nq], MselT[0:nq, :], ident_b[0:nq, 0:nq]),
                 r=["MselT", "ident_b"], w=[pskey(7)])
            P.op("dve", lambda v: v.tensor_copy(
                out=Msel4[:, 0:nq], in_=psb[7][:, :].bitcast(BF16)[:, 0:nq]), r=[pskey(7)], w=[mk4])
            for hh in range(4):
                P.op("pe", lambda t, hh=hh: t.transpose(psb[7][:, :].bitcast(BF16)[:, 512 + hh * nq:512 + (hh + 1) * nq],
                                                        Pnb[0:nq, hh * 128:(hh + 1) * 128], ident_b[0:nq, 0:nq]),
                     r=["Pnb", "ident_b"], w=[pskey(7)])
            P.op("dve", lambda v: v.tensor_copy(out=PTc[:, 0:ncol], in_=psb[7][:, :].bitcast(BF16)[:, 512:512 + ncol]),
                 r=[pskey(7)], w=["PTc"])
            for hh in range(4):
                P.op("pe", lambda t, hh=hh: t.matmul(psb[6][0:nq, hh * 64:(hh + 1) * 64], lhsT=PTc[:, hh * nq:(hh + 1) * nq], rhs=vc_ap,
                                                     start=True, stop=True), r=["PTc", vck], w=[pskey(6)])
            for hh in range(4):
                P.op("dve", lambda v, hh=hh: v.tensor_scalar_mul(out=acc[0:nq, hh * 64:(hh + 1) * 64],
                                                                 in0=psb[6][0:nq, hh * 64:(hh + 1) * 64],
                                                                 scalar1=gl3[:, 4 * g + hh, 0:1]), r=[pskey(6), glk], w=[ak])

            yield "b"
            def attend(tiles, br, ob):
                nt = len(tiles)
                slots = {}

                def s_stage(i):
                    T = tiles[i]
                    sbk = cnt2["ps"] % 3
                    cnt2["ps"] += 1
                    nk = T["nk"]
                    mm = [(T["kt"], qrhs, T["kkey"], qk)] + T["masks"]
                    for j, (l, r_, lk, rk) in enumerate(mm):
                        P.op("pe", lambda t, l=l, r_=r_, j=j, nk=nk, sbk=sbk, n=len(mm): t.matmul(
                            psb[sbk][0:nk, 0:ncol], lhsT=l, rhs=r_, start=(j == 0), stop=(j == n - 1)),
                            r=[lk, rk], w=[pskey(sbk)])
                    mxb = None
                    if T["msel"] is not None:
                        mxb = 4 + cnt2["mx"] % 2
                        cnt2["mx"] += 1
                        P.op("pe", lambda t, Tm=T["msel"], nk=nk, mxb=mxb: t.matmul(
                            psb[mxb][0:nk, 0:nq], lhsT=EE[:, Tm, 0:nk], rhs=Msel4[:, 0:nq], start=True, stop=True),
                            r=["EE", mk4], w=[pskey(mxb)])
                    slots[i] = (sbk, mxb)

                def e_stage(i):
                    T = tiles[i]
                    nk = T["nk"]
                    sbk, mxb = slots[i]
                    pi = cnt2["pt"] % 3
                    cnt2["pt"] += 1
                    P.op("act", lambda a, nk=nk, sbk=sbk, pi=pi: a.activation(out=PT[pi][0:nk, 0:ncol], in_=psb[sbk][0:nk, 0:ncol],
                                                                              func=AF.Exp), r=[pskey(sbk)], w=["PT%d" % pi])
                    if mxb is None:
                        return PT[pi], "PT%d" % pi
                    P.op("dve", lambda v, nk=nk, pi=pi, mxb=mxb: v.tensor_tensor(
                        out=PTm[pi][0:nk, 0:ncol].rearrange("p (h q) -> p h q", h=4),
                        in0=PT[pi][0:nk, 0:ncol].rearrange("p (h q) -> p h q", h=4),
                        in1=psb[mxb][0:nk, 0:nq].unsqueeze(1).to_broadcast([nk, 4, nq]), op=ALU.mult),
                        r=["PT%d" % pi, pskey(mxb)], w=["PTm%d" % pi])
                    return PTm[pi], "PTm%d" % pi

                def v_stage(i, pi):
                    T = tiles[i]
                    nk = T["nk"]
                    pbuf, pkey_ = pi
                    P.op("pe", lambda t, T=T, nk=nk, pbuf=pbuf, i=i: t.matmul(psb[ob][0:65, 0:ncol], lhsT=T["v"], rhs=pbuf[0:nk, 0:ncol],
                                                                             start=(i == 0), stop=(i == nt - 1)),
                         r=[T["vkey"], pkey_], w=[pskey(ob)])

                LOOK = 2
                for i in range(min(LOOK, nt)):
                    s_stage(i)
                for i in range(nt):
                    pi = e_stage(i)
                    if i + LOOK < nt:
                        s_stage(i + LOOK)
                    v_stage(i, pi)
                P.op("dve", lambda v: v.tensor_copy(out=OaugSB[0:65, 0:ncol], in_=psb[ob][0:65, 0:ncol]), r=[pskey(ob)], w=["OaugSB"])
                for hh in range(4):
                    P.op("pe", lambda t, hh=hh: t.transpose(psb[7][0:nq, hh * 65:(hh + 1) * 65], OaugSB[0:65, hh * nq:(hh + 1) * nq],
                                                            ident_f[0:65, 0:65]), r=["OaugSB", "ident_f"], w=[pskey(7)])
                o3 = psb[7][0:nq, 0:260].rearrange("p (h e) -> p h e", e=65)
                P.op("dve", lambda v: v.tensor_scalar_add(out=smb[0:nq, 8:12], in0=o3[:, :, 64], scalar1=1e-30), r=[pskey(7)], w=["smb"])
                P.op("dve", lambda v: v.reciprocal(out=smb[0:nq, 8:12], in_=smb[0:nq, 8:12]), r=["smb"], w=["smb"])
                P.op("dve", lambda v: v.tensor_tensor(out=smb[0:nq, 12:16], in0=smb[0:nq, 8:12], in1=gl3[:, 4 * g:4 * g + 4, br],
                                                      op=ALU.mult), r=["smb", glk], w=["smb"])
                for hh in range(4):
                    P.op("dve", lambda v, hh=hh: v.scalar_tensor_tensor(
                        out=acc[0:nq, hh * 64:(hh + 1) * 64], in0=o3[:, hh, 0:64], scalar=smb[0:nq, 12 + hh:13 + hh],
                        in1=acc[0:nq, hh * 64:(hh + 1) * 64], op0=ALU.mult, op1=ALU.add), r=[pskey(7), "smb", ak], w=[ak])

            def ktile(src, T, nk=128, masks=()):
                kt_, kkey_, v_, vkey_ = src(T, nk)
                add = [m for m in masks if m[0] != "MSEL"]
                ms_ = [m[1] for m in masks if m[0] == "MSEL"]
                return {"kt": kt_, "kkey": kkey_, "v": v_, "vkey": vkey_, "nk": nk, "masks": add,
                        "msel": (ms_[0] if ms_ else None)}

            msel = lambda T: ("MSEL", T)
            if sample:
                tri = (ident_b[0:8, 0:8], TRIs[0:8, 0:ncol], "ident_b", "TRIs")
                tri2 = (ident_b[:, :], TRI2s[:, 0:ncol], "ident_b", "TRI2s")
                sel_tiles = [ktile(ksel, T, masks=[msel(T)]) for T in range(64)]
                sel_tiles.append(ktile(ksel, 64, nk=8, masks=[tri]))
                win_tiles = [ktile(kwin, 0, masks=[tri2])] + [ktile(kwin, w) for w in range(1, 4)]
                win_tiles.append(ktile(kwin, 4, nk=8, masks=[tri]))
            else:
                tri = (ident_b[:, :], TRIp[:, 0:ncol], "ident_b", "TRIp")
                tri2 = (ident_b[:, :], TRI2p[:, 0:ncol], "ident_b", "TRI2p")
                sel_tiles = [ktile(ksel, T, masks=[msel(T)]) for T in range(48)]
                for j2 in range(jl + 1):
                    ms = [msel(48 + j2)] + ([tri] if j2 == jl else [])
                    sel_tiles.append(ktile(ksel, 48 + j2, masks=ms))
                win_tiles = []
                for w in range(jl, jl + 5):
                    ms = [tri2] if w == jl else ([tri] if w == jl + 4 else [])
                    win_tiles.append(ktile(kwin, w, masks=ms))
            attend(sel_tiles, 1, 3)
            yield "sel"
            attend(win_tiles, 2, 3)
            ogk = "OGt%d" % i2
            P.op("dve", lambda v: v.tensor_tensor(out=OGt[i2][0:nq, :], in0=acc[0:nq, :], in1=ZSg[i2][0:nq, :], op=ALU.mult),
                 r=[ak, zk], w=[ogk])
            P.dma("sp", OGscr[tok0:tok0 + nq, g * 256:(g + 1) * 256], OGt[i2][0:nq, :], r=[ogk], w=[("OGscr", tok0, g)], semkey=ogk)
            yield "done"

        def run_tiles(specs):
            gens = [nsa_tile(*sp) for sp in specs]
            n = len(gens)
            if n == 0:
                return
            next(gens[0])
            next(gens[0])
            for i in range(n):
                if i + 1 < n:
                    next(gens[i + 1])
                next(gens[i])
                if i + 1 < n:
                    next(gens[i + 1])
                next(gens[i])

        if stage >= 2:
            for g in range(4):
                P.dma("sp", KsT4[64:72, g, 0:8192], t_kaug_sel[:, :], w=["KsT4"])
                P.dma("sp", KwT4[64:72, g, 0:2560], t_kaug_win[:, :], w=["KwT4"])
                P.dma("sp", KcT4[64:72, g, :], t_kaug_cmp[:, :], w=["KcT4"])
            for T4 in range(12):
                i = cnt2["rb"] % 2
                cnt2["rb"] += 1
                rbk = "RB%d" % i
                P.dma("pool", RB[i][:, :, :], o_sel_p[T4 * 512:(T4 + 1) * 512, :].rearrange("(t p) c -> p t c", p=128), w=[rbk])
                for t_ in range(4):
                    T = T4 * 4 + t_
                    prep4(RB[i][:, t_, :], rbk, 128, KsT4[0:64, :, T * 128:(T + 1) * 128], "KsT4", Vs4[:, T, :, 0:64], "Vs4")
            for j2 in range(16):
                rbf, rk = load_rows_gather(o_sel_p[:, :], idxo2[:, j2:j2 + 1], "idxo2")
                prep4(rbf, rk, 128, KsT4[0:64, :, (48 + j2) * 128:(49 + j2) * 128], "KsT4", Vs4[:, 48 + j2, :, 0:64], "Vs4")
            for w in range(20):
                rbf, rk = load_rows_gather(winscr[:, :], idxw[:, w:w + 1], "idxw")
                prep4(rbf, rk, 128, KwT4[0:64, :, w * 128:(w + 1) * 128], "KwT4", Vw4[:, w, :, 0:64], "Vw4")
            rbf, rk = load_rows_gather(cmpscr_p[:, :], idxs[:, 0:1], "idxs")
            prep4(rbf, rk, 128, KcT4[0:64, :, :], "KcT4", Vc4[:, :, :], "Vc4")
            run_tiles([(g, tok0, nq, jl, None, 16) for g in range(4) for (tok0, nq, jl, sb_) in qtiles if jl is not None])
        if do_sample:
            for g in range(4):
                P.dma("sp", KsT4[64:72, g, 0:8192], t_kaug_sel_s[:, :], w=["KsT4"])
                P.dma("sp", KsT4[64:72, g, 8192:8320], t_kaug_new_s[:, :], w=["KsT4"])
                P.dma("sp", KwT4[64:72, g, 0:512], t_kaug_win_s[:, :], w=["KwT4"])
                P.dma("sp", KwT4[64:72, g, 512:640], t_kaug_new_s[:, :], w=["KwT4"])
                P.dma("sp", KcT4[64:72, g, :], t_kaug_cmp_s[:, :], w=["KcT4"])
            for (tok0, nq, jl, sb_) in qtiles:
                if jl is not None:
                    continue
                b = sb_
                for pg in range(64):
                    rbf, rk = load_rows_gather(csel[:, :], idxpg[:, b * 64 + pg:b * 64 + pg + 1], "idxpg")
                    prep4(rbf, rk, 128, KsT4[0:64, :, pg * 128:(pg + 1) * 128], "KsT4", Vs4[:, pg, :, 0:64], "Vs4")
                rbf, rk = load_rows_plain(o_sel_s[b * 8:(b + 1) * 8, :], 8)
                prep4(rbf, rk, 8, KsT4[0:64, :, 8192:8200], "KsT4", Vs4[0:8, 64, :, 0:64], "Vs4")
                for w in range(4):
                    rbf, rk = load_rows_plain(swin[b * 512 + w * 128:b * 512 + (w + 1) * 128, :], 128)
                    prep4(rbf, rk, 128, KwT4[0:64, :, w * 128:(w + 1) * 128], "KwT4", Vw4[:, w, :, 0:64], "Vw4")
                rbf, rk = load_rows_plain(o_win_s[b * 512 + 504:b * 512 + 512, :], 8)
                prep4(rbf, rk, 8, KwT4[0:64, :, 512:520], "KwT4", Vw4[0:8, 4, :, 0:64], "Vw4")
                rbf, rk = load_rows_plain(cmpscr_s[b * 128:(b + 1) * 128, :], 128)
                prep4(rbf, rk, 128, KcT4[0:64, :, :], "KcT4", Vc4[:, :, :], "Vc4")
                run_tiles([(g, tok0, nq, None, b, 15) for g in range(4)])

    with P.scope():
        W_outb = P.sb("W_outb", [128, NCH, D], BF16)
        for k in range(NCH):
            P.dma("pool", W_outb[:, k, :], w_out_b[k * 128:(k + 1) * 128, :], w=["W_outb"])
        lnG1 = P.sb("lnG1", [128, 1, D], F32)
        lnB1 = P.sb("lnB1", [128, 1, D], F32)
        P.dma("sp", lnG1[:, 0, :], ln_g[1:2, :].partition_broadcast(128), w=["lnG1"])
        P.dma("sp", lnB1[:, 0, :], ln_b[1:2, :].partition_broadcast(128), w=["lnB1"])
        Gp1 = P.sb("Gp1", [128, D], F32)
        Gs1 = P.sb("Gs1", [8, DEC_B, D], F32)
        P.dma("sp", Gp1[:, :], modscr[1, 0:1, 2 * D:3 * D].partition_broadcast(128), w=["Gp1"])
        for b in range(DEC_B):
            P.dma("sp", Gs1[0:8, b, :], modscr[1, 1 + b:2 + b, 2 * D:3 * D].partition_broadcast(8), w=["Gs1"])
        P.op("pool", lambda g_: g_.tensor_scalar_add(out=Gp1[:], in0=Gp1[:], scalar1=1.0), r=["Gp1"], w=["Gp1"])
        P.op("pool", lambda g_: g_.tensor_scalar_add(out=Gs1[:], in0=Gs1[:], scalar1=1.0), r=["Gs1"], w=["Gs1"])
        idxo3 = P.sb("idxo3", [128, 16], I32)
        P.dma("sp", idxo3[:], t_idx_own[:, :], w=["idxo3"])
        X1o = [P.sb("X1o%d" % i, [128, D], F32) for i in range(2)]
        OGl = [P.sb("OGl%d" % i, [128, D], BF16) for i in range(2)]
        OGT = P.sb("OGT", [128, NCH, 128], BF16)
        vo = [P.sb("vo%d" % i, [128, D], F32) for i in range(2)]
        yo = [P.sb("yo%d" % i, [128, D], F32) for i in range(2)]
        sto = [P.sb("sto%d" % i, [128, 16], F32) for i in range(2)]
        qi = 0
        for (tok0, nq, jl, sb_) in qtiles:
            i2 = qi % 2
            qi += 1
            xk, ok_, vk, yk, sk = "X1o%d" % i2, "OGl%d" % i2, "vo%d" % i2, "yo%d" % i2, "sto%d" % i2
            if jl is not None:
                P.gather(X1o[i2][:, :], x1scr[:, :], idxo3[:, jl:jl + 1], r=["idxo3"], w=[xk])
                Gt, gkey = Gp1[0:nq, :], "Gp1"
            else:
                P.dma("sp", X1o[i2][0:nq, :], x1s_scr[sb_ * 8:(sb_ + 1) * 8, :], w=[xk])
                Gt, gkey = Gs1[0:8, sb_, :], "Gs1"
            P.dma("sp", OGl[i2][0:nq, :], OGscr[tok0:tok0 + nq, :], w=[ok_])
            for k in range(NCH):
                P.op("pe", lambda t, k=k, nq=nq, i2=i2: t.transpose(psb[0][:, :].bitcast(BF16)[:, k * 128:k * 128 + nq],
                                                                   OGl[i2][0:nq, k * 128:(k + 1) * 128], ident_b[0:nq, 0:nq]),
                     r=[ok_, "ident_b"], w=[pskey(0)])
            P.op("dve", lambda v, nq=nq: v.tensor_copy(
                out=OGT[:, :, 0:nq], in_=psb[0][:, :].bitcast(BF16)[:, 0:1024].rearrange("p (k t) -> p k t", t=128)[:, :, 0:nq]),
                r=[pskey(0)], w=["OGT"])
            for half in range(2):
                pb = 2 + half
                for k in range(NCH):
                    P.op("pe", lambda t, k=k, half=half, pb=pb, nq=nq: t.matmul(
                        psb[pb][0:nq, :], lhsT=OGT[:, k, 0:nq], rhs=W_outb[:, k, half * 512:(half + 1) * 512],
                        start=(k == 0), stop=(k == NCH - 1)), r=["OGT", "W_outb"], w=[pskey(pb)])
                if jl is None and sb_ > 0:
                    pass
                P.op("dve", lambda v, half=half, pb=pb, nq=nq, i2=i2, Gt=Gt: v.tensor_tensor(
                    out=vo[i2][0:nq, half * 512:(half + 1) * 512], in0=psb[pb][0:nq, :], in1=Gt[:, half * 512:(half + 1) * 512],
                    op=ALU.mult), r=[pskey(pb), gkey], w=[vk])
            P.op("dve", lambda v, nq=nq, i2=i2: v.scalar_tensor_tensor(
                out=vo[i2][0:nq, :], in0=X1o[i2][0:nq, :], scalar=ALPHA, in1=vo[i2][0:nq, :], op0=ALU.mult, op1=ALU.add),
                r=[xk, vk], w=[vk])
            layernorm_tm(vo[i2], vk, yo[i2], yk, nq, 0, sto[i2], sk, lnG1, "lnG1", lnB1, "lnB1")
            if jl is not None:
                P.dma("sp", y_p[tok0:tok0 + nq, :], yo[i2][0:nq, :], r=[yk], w=[("y_p", tok0)], semkey=yk)
            else:
                P.dma("sp", y_s[sb_ * 8:(sb_ + 1) * 8, :], yo[i2][0:nq, :], r=[yk], w=[("y_s", sb_)], semkey=yk)

    for b in range(DEC_B):
        P.dma("act", o_win_s[b * 512:b * 512 + 504, :], swin[b * 512 + 8:(b + 1) * 512, :], w=[("o_win_s", b, 0)],
              semkey="winscopy")

    P.finish()
    print("instructions:", P.n_ins, {e: P.cnt[e] for e in P.ENG}, "dma sems:", len(P.dsem))
    return P


def _bf(x):
    return np.asarray(x, np.float32).astype(ml_dtypes.bfloat16)


def _split_pos(pos):
    pos = np.asarray(pos, np.int64)
    a = np.floor_divide(pos, 64)
    b = pos - 64 * a
    return a.astype(np.float32), b.astype(np.float32)


def _kaug(pos, valid):
    a, b = _split_pos(pos)
    n = a.shape[0]
    out = np.zeros((8, n), np.float32)
    out[0] = a; out[1] = a; out[2] = b; out[3] = b; out[4] = 1.0
    out[5] = np.where(valid, 0.0, NEGM)
    return _bf(out)


def _slopes_hi_lo():
    s = (2.0 ** (-8.0 * np.arange(1, 17) / 16.0)).astype(np.float32)
    hi = s.astype(ml_dtypes.bfloat16).astype(np.float32)
    lo = (s - hi).astype(ml_dtypes.bfloat16).astype(np.float32)
    return s, hi, lo


def _qaug(tq):
    s, hi, lo = _slopes_hi_lo()
    tq = np.asarray(tq, np.float32)
    nq = tq.shape[0]
    out = np.zeros((8, 16, nq), np.float32)
    out[0] = (64.0 * hi)[:, None]; out[1] = (64.0 * lo)[:, None]
    out[2] = hi[:, None]; out[3] = lo[:, None]
    out[4] = -(s[:, None] * tq[None, :])
    out[5] = 1.0
    return _bf(out)


def prompt_tables(k):
    cs = 2048 * k
    p = np.arange(128)
    t = {}
    t["idx_own"] = (cs + 128 * np.arange(16)[None, :] + p[:, None]).astype(np.int32)
    pos_pref = (np.arange(48 * 128) - cs)
    valid_pref = np.repeat(128 * np.arange(48) < cs, 128)
    pos_own = np.arange(2048)
    t["kaug_sel"] = np.concatenate([_kaug(pos_pref, valid_pref), _kaug(pos_own, np.ones(2048, bool))], axis=1)
    wtok = cs - 512 + np.arange(20 * 128)
    t["idx_win"] = np.maximum(wtok, 0).reshape(20, 128).T.astype(np.int32).copy()
    t["kaug_win"] = _kaug(wtok - cs, wtok >= 0)
    blk = np.concatenate([np.arange(96), 32 * k + np.arange(32)])
    valid = np.concatenate([np.arange(96) < 32 * k, np.ones(32, bool)])
    t["idx_slot"] = blk.astype(np.int32).reshape(128, 1)
    cend = 64 * blk + 63 - cs
    t["kaug_cmp"] = _kaug(cend, valid)
    tq = np.arange(2048)
    tabs = np.arange(2048) + cs
    cb = tabs // 64
    blk_abs = np.where(valid, blk, 10 ** 6)
    forced = (blk_abs[None, :] == 0) | (blk_abs[None, :] == cb[:, None]) | (blk_abs[None, :] == cb[:, None] - 1)
    caus = blk_abs[None, :] <= cb[:, None]
    fbn = np.where(forced, FORCEDV, 0.0) - np.where(caus, 0.0, 1.0)
    t["fbn"] = fbn.astype(np.float32).reshape(16, 128, 128)
    t["caus"] = caus.astype(np.float32).reshape(16, 128, 128)
    cend_abs = 64 * blk + 63
    tm = np.where(cend_abs[None, :] <= tabs[:, None], 0.0, NEGM)
    t["tm"] = tm.astype(np.float32).reshape(16, 128, 128)
    return t


def static_tables():
    t = {}
    t["qaug_p"] = _qaug(np.arange(2048)).reshape(8, 16 * 2048)
    j = np.arange(128)[:, None]
    i = np.arange(128)[None, :]
    tri = np.where(j > i, NEGM, 0.0)
    tri2 = np.where(j < i, NEGM, 0.0)
    t["tri_p"] = _bf(np.tile(tri, (1, 4)))
    t["tri2_p"] = _bf(np.tile(tri2, (1, 4)))
    i8 = np.arange(8)[None, :]
    t["tri_s"] = _bf(np.tile(np.where(j > i8, NEGM, 0.0), (1, 4)))
    t["tri2_s"] = _bf(np.tile(np.where(j < i8, NEGM, 0.0), (1, 4)))
    t["qaug_s"] = _qaug(np.arange(8)).reshape(8, 16 * 8)
    t["kaug_sel_s"] = _kaug(np.arange(8192) - 8192, np.ones(8192, bool))
    t["kaug_win_s"] = _kaug(np.arange(512) - 512, np.ones(512, bool))
    t["kaug_new_s"] = _kaug(np.arange(128), np.arange(128) < 8)
    cend = 64 * np.arange(128) + 63 - 8192
    t["kaug_cmp_s"] = _kaug(cend, np.ones(128, bool))
    fb = np.zeros((8, 128), np.float32)
    fb[:, 0] = FORCEDV; fb[:, 127] = FORCEDV
    t["fbn_s"] = fb
    t["caus_s"] = np.ones((8, 128), np.float32)
    t["tm_s"] = np.zeros((8, 128), np.float32)
    return t


def core_inputs(inp, c):
    b = c // 4
    sb = slice(4 * c, 4 * c + 4)
    f = np.ascontiguousarray
    d = {
        "xf": f(inp["x_prompt"][b]),
        "xs": f(inp["x_sample"][sb].reshape(NS_TOK, D)),
        "cvec": f(np.concatenate([inp["c_prompt"][b:b + 1], inp["c_sample"][sb]], axis=0)),
        "sh0": f(inp["state_h"][0, sb]),
        "sc0": f(inp["state_conv"][0, sb].reshape(DEC_B * 3, D)),
        "swin": f(inp["state_win"][sb].reshape(DEC_B * 512, 512)),
        "ptab": f(inp["page_table"][sb]).astype(np.int32),
        "ccmp": inp["cache_cmp"].reshape(-1, 512),
        "csel": inp["cache_sel"].reshape(-1, 512),
        "w_ada": inp["w_ada"], "b_ada": inp["b_ada"], "ln_g": inp["ln_g"], "ln_b": inp["ln_b"],
        "w_in_a": inp["w_in_a"][0], "conv_w": inp["conv_w_a"][0], "conv_b": inp["conv_b_a"],
        "w_r": inp["w_r_a"][0], "b_r": inp["b_r_a"], "w_i": inp["w_i_a"][0], "b_i": inp["b_i_a"],
        "lam": inp["lam_a"], "w_out_a": inp["w_out_a"][0], "w_kv": inp["w_kv"],
        "phi_pe": inp["phi_pe"].reshape(64, 128), "w_phi1": inp["w_phi1"], "b_phi1": inp["b_phi1"],
        "w_phi2": inp["w_phi2"], "b_phi2": inp["b_phi2"], "w_in_b": inp["w_in_b"][0],
        "b_gate": inp["b_gate_b"], "w_out_b": inp["w_out_b"][0],
    }
    for k2, v in prompt_tables(c % 4).items():
        d["t_" + k2] = v
    for k2, v in static_tables().items():
        d["t_" + k2] = v
    return {k: np.asarray(v) for k, v in d.items()}


def assemble(results, cores):
    y_prompt = np.zeros((2, SEQ, D), np.float32)
    y_sample = np.zeros((32, DEC_S, D), np.float32)
    new_cmp_p = np.zeros((2, SEQ, 4, 2, 64), np.float32)
    new_sel_p = np.zeros((2, SEQ, 4, 2, 64), np.float32)
    new_win_p = np.zeros((2, 512, 4, 2, 64), np.float32)
    new_h_p = np.zeros((1, 2, D), np.float32)
    new_conv_p = np.zeros((1, 2, 3, D), np.float32)
    new_cmp_s = np.zeros((32, DEC_S, 4, 2, 64), np.float32)
    new_sel_s = np.zeros((32, DEC_S, 4, 2, 64), np.float32)
    new_win_s = np.zeros((32, 512, 4, 2, 64), np.float32)
    new_h_s = np.zeros((1, 32, D), np.float32)
    new_conv_s = np.zeros((1, 32, 3, D), np.float32)
    for r, c in zip(results, cores):
        b, k = c // 4, c % 4
        sb = slice(4 * c, 4 * c + 4)
        y_prompt[b, k * 2048:(k + 1) * 2048] = r["y_p"]
        y_sample[sb] = r["y_s"].reshape(DEC_B, DEC_S, D)
        if k == 0:
            new_cmp_p[b] = r["o_cmp_p"].reshape(SEQ, 4, 2, 64)
            new_sel_p[b] = r["o_sel_p"].reshape(SEQ, 4, 2, 64)
            new_win_p[b] = r["o_win_p"].reshape(512, 4, 2, 64)
            new_h_p[0, b] = r["o_h_p"][0]
            new_conv_p[0, b] = r["o_conv_p"]
        new_cmp_s[sb] = r["o_cmp_s"].reshape(DEC_B, DEC_S, 4, 2, 64)
        new_sel_s[sb] = r["o_sel_s"].reshape(DEC_B, DEC_S, 4, 2, 64)
        new_win_s[sb] = r["o_win_s"].reshape(DEC_B, 512, 4, 2, 64)
        new_h_s[0, sb] = r["o_h_s"]
        new_conv_s[0, sb] = r["o_conv_s"].reshape(DEC_B, 3, D)
    return (y_prompt, y_sample, new_cmp_p, new_sel_p, new_win_p, new_h_p, new_conv_p,
            new_cmp_s, new_sel_s, new_win_s, new_h_s, new_conv_s)


def kernel(**inputs):
    inp = {k: np.asarray(v) for k, v in inputs.items()}
    n_phys = inp["cache_cmp"].shape[0]
    P = build(n_phys)
    cores = list(range(8))
    in_maps = [core_inputs(inp, c) for c in cores]
    res = run_bass_kernel_spmd(P.nc, in_maps, core_ids=cores)
    return assemble(res.results, cores)
```

```python
import contextlib
import numpy as np
import ml_dtypes
import concourse.bass as bass
import concourse.mybir as mybir
from concourse.bass_utils import run_bass_kernel_spmd

F32 = mybir.dt.float32
BF16 = mybir.dt.bfloat16
I32 = mybir.dt.int32
AF = mybir.ActivationFunctionType
ALU = mybir.AluOpType
AX = mybir.AxisListType

D = 1024
NCH = 8
SEQ = 8192
TT = 256
NSUB = TT // 128
DEC_B = 4
DEC_S = 8
NS_TOK = DEC_B * DEC_S
ALPHA = 4.0 ** 0.25
LN_EPS = 1e-5
RG_C = 8.0
NEGM = -30000.0
FORCEDV = 1.0e6


class Prog:
    ENG = ("pe", "act", "dve", "pool", "sp")

    def __init__(self):
        self.nc = bass.Bass("TRN2", target_bir_lowering=False)
        self.es = contextlib.ExitStack()
        nc = self.nc
        self.eng = {"pe": nc.tensor, "act": nc.scalar, "dve": nc.vector, "pool": nc.gpsimd, "sp": nc.sync}
        self.sem = {e: self.es.enter_context(nc.semaphore("s_" + e)) for e in self.ENG}
        self.cnt = {e: 0 for e in self.ENG}
        self.seen = {e: {} for e in self.ENG}
        self.dsem = {}
        self.dcnt = {}
        self.bufs = {}
        self.n_ins = 0
        self.stack = [self.es]

    def sb(self, name, shape, dt):
        return self.stack[-1].enter_context(self.nc.sbuf_tensor(name, list(shape), dt))

    def barrier(self):
        deps = {}
        for e2 in self.ENG:
            if self.cnt[e2]:
                deps[("eng", e2)] = self.cnt[e2]
        for k in self.dcnt:
            deps[("dma", k)] = self.dcnt[k]
        for e in self.ENG:
            self._wait(e, dict(deps))

    @contextlib.contextmanager
    def scope(self):
        st = contextlib.ExitStack()
        self.stack.append(st)
        try:
            yield
        finally:
            self.barrier()
            self.stack.pop()
            st.close()

    def ps(self, name, shape, dt):
        return self.es.enter_context(self.nc.psum_tensor(name, list(shape), dt))

    def dram(self, name, shape, dt, kind="Internal"):
        return self.nc.dram_tensor(name, list(shape), dt, kind=kind).ap()

    def _state(self, k):
        st = self.bufs.get(k)
        if st is None:
            st = self.bufs[k] = {"w": {}, "r": {}}
        return st

    def _deps(self, r, w):
        deps = {}
        for k in r:
            for s, v in self._state(k)["w"].items():
                deps[s] = max(deps.get(s, 0), v)
        for k in w:
            st = self._state(k)
            for s, v in st["w"].items():
                deps[s] = max(deps.get(s, 0), v)
            for s, v in st["r"].items():
                deps[s] = max(deps.get(s, 0), v)
        return deps

    def _wait(self, e, deps):
        eng = self.eng[e]
        seen = self.seen[e]
        for s, v in deps.items():
            if s[0] == "dma":
                v = max(v, self.dcnt[s[1]])
                if seen.get(s, 0) >= v:
                    continue
                eng.wait_ge(self.dsem[s[1]], v)
            else:
                if s[1] == e and False:
                    continue
                if seen.get(s, 0) >= v:
                    continue
                eng.wait_ge(self.sem[s[1]], v)
            seen[s] = v

    def _commit(self, me_src, me_val, r, w):
        for k in w:
            st = self._state(k)
            st["w"] = {me_src: me_val}
            st["r"] = {}
        for k in r:
            if k in w:
                continue
            st = self._state(k)
            st["r"][me_src] = max(st["r"].get(me_src, 0), me_val)

    def op(self, e, fn, r=(), w=()):
        w = list(w) + [k for k in r if isinstance(k, str) and k[:2] == "ps" and k[2:].isdigit() and k not in w]
        self._wait(e, self._deps(r, w))
        ins = fn(self.eng[e])
        self.cnt[e] += 1
        ins.then_inc(self.sem[e], 1)
        self._commit(("eng", e), self.cnt[e], r, w)
        self.n_ins += 1
        return ins

    def dma(self, q, out, in_, r=(), w=(), semkey=None, **kw):
        self._wait(q, self._deps(r, w))
        if semkey is None:
            semkey = (tuple(w) + tuple(r))[0]
        if semkey not in self.dsem:
            self.dsem[semkey] = self.es.enter_context(self.nc.semaphore("d%d" % len(self.dsem)))
            self.dcnt[semkey] = 0
        ins = self.eng[q].dma_start(out=out, in_=in_, **kw)
        self.dcnt[semkey] += 16
        ins.then_inc(self.dsem[semkey], 16)
        self._commit(("dma", semkey), self.dcnt[semkey], r, w)
        self.n_ins += 1
        return ins

    def gather(self, out, in_, idx_ap, r=(), w=(), semkey=None):
        q = "pool"
        self._wait(q, self._deps(r, w))
        if semkey is None:
            semkey = tuple(w)[0]
        if semkey not in self.dsem:
            self.dsem[semkey] = self.es.enter_context(self.nc.semaphore("d%d" % len(self.dsem)))
            self.dcnt[semkey] = 0
        ins = self.nc.gpsimd.indirect_dma_start(
            out=out, out_offset=None, in_=in_, in_offset=bass.IndirectOffsetOnAxis(ap=idx_ap, axis=0))
        self.dcnt[semkey] += 16
        ins.then_inc(self.dsem[semkey], 16)
        self._commit(("dma", semkey), self.dcnt[semkey], r, w)
        self.n_ins += 1
        return ins

    def finish(self):
        for e in ("sp",):
            deps = {}
            for k, st in self.bufs.items():
                for s, v in list(st["w"].items()) + list(st["r"].items()):
                    deps[s] = max(deps.get(s, 0), v)
            for k in self.dcnt:
                deps[("dma", k)] = self.dcnt[k]
            for e2 in self.ENG:
                if self.cnt[e2]:
                    deps[("eng", e2)] = self.cnt[e2]
            self._wait(e, deps)


def build(n_phys, stage=9):
    P = Prog()
    nc = P.nc
    es = P.es
    ctx_nc = nc.allow_non_contiguous_dma(reason="small strided parameter / state loads")
    es.enter_context(ctx_nc)

    def din(name, shape, dt=F32):
        return nc.dram_tensor(name, list(shape), dt, kind="ExternalInput").ap()

    def dout(name, shape, dt=F32):
        return nc.dram_tensor(name, list(shape), dt, kind="ExternalOutput").ap()

    xf = din("xf", [SEQ, D])
    xs = din("xs", [NS_TOK, D])
    cvec = din("cvec", [5, D])
    sh0 = din("sh0", [DEC_B, D])
    sc0 = din("sc0", [DEC_B * 3, D])
    swin = din("swin", [DEC_B * 512, 512])
    ptab = din("ptab", [DEC_B, 64], I32)
    ccmp = din("ccmp", [n_phys * 128, 512])
    csel = din("csel", [n_phys * 128, 512])
    w_ada = din("w_ada", [2, D, 3 * D])
    b_ada = din("b_ada", [2, 3 * D])
    ln_g = din("ln_g", [2, D])
    ln_b = din("ln_b", [2, D])
    w_in_a = din("w_in_a", [D, 2 * D])
    conv_w = din("conv_w", [4, D])
    conv_b = din("conv_b", [1, D])
    w_r = din("w_r", [8, 128, 128])
    b_r = din("b_r", [1, D])
    w_i = din("w_i", [8, 128, 128])
    b_i = din("b_i", [1, D])
    lam = din("lam", [1, D])
    w_out_a = din("w_out_a", [D, D])
    w_kv = din("w_kv", [D, 1536])
    phi_pe = din("phi_pe", [64, 128])
    w_phi1 = din("w_phi1", [2, 64, 64, 128])
    b_phi1 = din("b_phi1", [2, 128])
    w_phi2 = din("w_phi2", [2, 128, 64])
    b_phi2 = din("b_phi2", [2, 64])
    w_in_b = din("w_in_b", [D, 2096])
    b_gate = din("b_gate", [1, 48])
    w_out_b = din("w_out_b", [D, D])

    def dtab(name, shape, dt):
        return nc.dram_tensor(name, list(shape), dt, kind="ExternalInput").ap()
    t_idx_own = dtab("t_idx_own", [128, 16], I32)
    t_idx_win = dtab("t_idx_win", [128, 20], I32)
    t_idx_slot = dtab("t_idx_slot", [128, 1], I32)
    t_kaug_sel = dtab("t_kaug_sel", [8, 8192], BF16)
    t_kaug_win = dtab("t_kaug_win", [8, 2560], BF16)
    t_kaug_cmp = dtab("t_kaug_cmp", [8, 128], BF16)
    t_fbn = dtab("t_fbn", [16, 128, 128], F32)
    t_caus = dtab("t_caus", [16, 128, 128], F32)
    t_tm = dtab("t_tm", [16, 128, 128], F32)
    t_qaug_p = dtab("t_qaug_p", [8, 16 * 2048], BF16)
    t_tri_p = dtab("t_tri_p", [128, 512], BF16)
    t_tri2_p = dtab("t_tri2_p", [128, 512], BF16)
    t_tri_s = dtab("t_tri_s", [128, 32], BF16)
    t_tri2_s = dtab("t_tri2_s", [128, 32], BF16)
    t_qaug_s = dtab("t_qaug_s", [8, 128], BF16)
    t_kaug_sel_s = dtab("t_kaug_sel_s", [8, 8192], BF16)
    t_kaug_win_s = dtab("t_kaug_win_s", [8, 512], BF16)
    t_kaug_new_s = dtab("t_kaug_new_s", [8, 128], BF16)
    t_kaug_cmp_s = dtab("t_kaug_cmp_s", [8, 128], BF16)
    t_fbn_s = dtab("t_fbn_s", [8, 128], F32)
    t_caus_s = dtab("t_caus_s", [8, 128], F32)
    t_tm_s = dtab("t_tm_s", [8, 128], F32)

    y_p = dout("y_p", [2048, D])
    y_s = dout("y_s", [NS_TOK, D])
    o_cmp_p = dout("o_cmp_p", [SEQ, 512])
    o_sel_p = dout("o_sel_p", [SEQ, 512])
    o_win_p = dout("o_win_p", [512, 512])
    o_h_p = dout("o_h_p", [1, D])
    o_conv_p = dout("o_conv_p", [3, D])
    o_cmp_s = dout("o_cmp_s", [NS_TOK, 512])
    o_sel_s = dout("o_sel_s", [NS_TOK, 512])
    o_win_s = dout("o_win_s", [DEC_B * 512, 512])
    o_h_s = dout("o_h_s", [DEC_B, D])
    o_conv_s = dout("o_conv_s", [DEC_B * 3, D])

    modscr = P.dram("modscr", [2, 5, 3 * D], F32)
    x1scr = P.dram("x1scr", [SEQ, D], F32)
    x1s_scr = P.dram("x1s_scr", [NS_TOK, D], F32)
    winscr = P.dram("winscr", [SEQ, 512], F32)

    ident_b = P.sb("ident_b", [128, 128], BF16)
    ident_f = P.sb("ident_f", [128, 128], F32)
    for t, k in ((ident_b, "ident_b"), (ident_f, "ident_f")):
        P.op("pool", lambda g, t=t: g.memset(t[:], 0.0), w=[k])
        P.op("pool", lambda g, t=t: g.affine_select(out=t[:], in_=t[:], pattern=[[-1, 128]],
                                                    compare_op=ALU.not_equal, fill=1.0, base=0,
                                                    channel_multiplier=1), r=[k], w=[k])

    psb = [P.ps("ps%d" % i, [128, 512], F32) for i in range(8)]

    def pskey(i):
        return "ps%d" % i

    mod_fm = P.sb("mod_fm", [128, 2, 24, 8], F32)
    scA = P.scope()
    scA.__enter__()
    W_in = P.sb("W_in", [128, NCH, 2 * D], BF16)
    W_out = P.sb("W_out", [128, NCH, D], BF16)
    W_kv = P.sb("W_kv", [128, NCH, 1536], BF16)
    W_r = P.sb("W_r", [128, 8, 128], BF16)
    W_i = P.sb("W_i", [128, 8, 128], BF16)
    for k in range(NCH):
        P.dma("pool", W_in[:, k, :], w_in_a[k * 128:(k + 1) * 128, :], w=["W_in"])
    for k in range(NCH):
        P.dma("pool", W_out[:, k, :], w_out_a[k * 128:(k + 1) * 128, :], w=["W_out"])
    for k in range(NCH):
        P.dma("pool", W_kv[:, k, :], w_kv[k * 128:(k + 1) * 128, :], w=["W_kv"])
    P.dma("pool", W_r[:], w_r.rearrange("n c d -> c n d"), w=["W_r"])
    P.dma("pool", W_i[:], w_i.rearrange("n c d -> c n d"), w=["W_i"])

    pf = P.sb("pf", [128, 10, NCH], F32)
    for k in range(4):
        P.dma("sp", pf[:, k, :], conv_w[k:k + 1, :].rearrange("o (c p) -> p (o c)", p=128), w=["pf"])
    for j, src in ((4, conv_b), (5, b_r), (6, b_i), (7, lam)):
        P.dma("sp", pf[:, j, :], src[0:1, :].rearrange("o (c p) -> p (o c)", p=128), w=["pf"])
    P.op("act", lambda a: a.activation(out=pf[:, 9, :], in_=pf[:, 7, :], func=AF.Exp, scale=-1.0), r=["pf"], w=["pf"])
    P.op("act", lambda a: a.activation(out=pf[:, 9, :], in_=pf[:, 9, :], func=AF.Ln, bias=1.0), r=["pf"], w=["pf"])
    P.op("dve", lambda v: v.tensor_scalar_mul(out=pf[:, 7, :], in0=pf[:, 9, :], scalar1=-RG_C), r=["pf"], w=["pf"])
    P.op("dve", lambda v: v.tensor_scalar_mul(out=pf[:, 8, :], in0=pf[:, 9, :], scalar1=-2.0 * RG_C), r=["pf"], w=["pf"])

    lnG = P.sb("lnG", [128, 1, D], F32)
    lnB = P.sb("lnB", [128, 1, D], F32)
    for l in range(1):
        P.dma("sp", lnG[:, l, :], ln_g[l:l + 1, :].partition_broadcast(128), w=["lnG"])
        P.dma("sp", lnB[:, l, :], ln_b[l:l + 1, :].partition_broadcast(128), w=["lnB"])

    vt = [P.sb("vt%d" % i, [128, D], F32) for i in range(2)]
    c5, c5s = vt[0], vt[1]
    csT = P.sb("csT", [128, NCH, 8], BF16)
    P.dma("sp", c5[0:5, :], cvec[:, :], w=["vt0"])
    P.op("act", lambda a: a.activation(out=c5s[0:5, :], in_=c5[0:5, :], func=AF.Silu), r=["vt0"], w=["vt1"])
    for k in range(NCH):
        P.op("pe", lambda t, k=k: t.transpose(psb[0][:, k * 8:k * 8 + 5], c5s[0:5, k * 128:(k + 1) * 128],
                                              ident_f[0:5, 0:5]), r=["vt1", "ident_f"], w=[pskey(0)])
    P.op("dve", lambda v: v.tensor_copy(out=csT[:, :, 0:5],
                                        in_=psb[0][:, 0:64].rearrange("p (k e) -> p k e", e=8)[:, :, 0:5]),
         r=[pskey(0)], w=["csT"])
    AW = 256
    NA = 3 * D // AW
    wada_buf = [P.sb("wada%d" % i, [128, NCH, AW], BF16) for i in range(2)]
    modc = [P.sb("modc%d" % i, [5, AW], F32) for i in range(2)]
    badac = [P.sb("badac%d" % i, [5, AW], F32) for i in range(2)]
    it = 0
    for l in range(2):
        for n6 in range(NA):
            wb = wada_buf[it % 2]
            wk = "wada%d" % (it % 2)
            mk = "modc%d" % (it % 2)
            bk_ = "badac%d" % (it % 2)
            mc = modc[it % 2]
            bc = badac[it % 2]
            P.dma("pool", wb[:], w_ada[l, :, n6 * AW:(n6 + 1) * AW].rearrange("(k p) n -> p k n", p=128), w=[wk])
            P.dma("sp", bc[:], b_ada[l:l + 1, n6 * AW:(n6 + 1) * AW].partition_broadcast(5), w=[bk_])
            pb = 2 + (it % 2)
            for k in range(NCH):
                P.op("pe", lambda t, k=k, wb=wb, pb=pb: t.matmul(psb[pb][0:5, 0:AW], lhsT=csT[:, k, 0:5], rhs=wb[:, k, :],
                                                                 start=(k == 0), stop=(k == NCH - 1)),
                     r=["csT", wk], w=[pskey(pb)])
            P.op("dve", lambda v, mc=mc, bc=bc, pb=pb: v.tensor_tensor(
                out=mc[:], in0=psb[pb][0:5, 0:AW], in1=bc[:], op=ALU.add),
                r=[pskey(pb), bk_], w=[mk])
            P.dma("sp", modscr[l, :, n6 * AW:(n6 + 1) * AW], mc[:], r=[mk], w=[("modscr", l, n6)], semkey=mk)
            nq = AW // 128
            for q in range(nq):
                P.op("pe", lambda t, q=q, mc=mc: t.transpose(psb[1][:, q * 8:q * 8 + 5], mc[0:5, q * 128:(q + 1) * 128],
                                                             ident_f[0:5, 0:5]), r=[mk, "ident_f"], w=[pskey(1)])
            P.op("dve", lambda v, l=l, n6=n6, nq=nq: v.tensor_copy(
                out=mod_fm[:, l, n6 * nq:(n6 + 1) * nq, 0:5],
                in_=psb[1][:, 0:8 * nq].rearrange("p (k e) -> p k e", e=8)[:, :, 0:5]),
                r=[pskey(1)], w=["mod_fm"])
            it += 1
    P.op("dve", lambda v: v.tensor_scalar_add(out=mod_fm[:, :, 8:24, :], in0=mod_fm[:, :, 8:24, :], scalar1=1.0),
         r=["mod_fm"], w=["mod_fm"])
    Gp = P.sb("Gp", [128, 1, D], F32)
    Gs = P.sb("Gs", [NS_TOK, 1, D], F32)
    mod_keys = [("modscr", l, n6) for l in range(2) for n6 in range(NA)]
    for l in range(1):
        P.dma("sp", Gp[:, l, :], modscr[l, 0:1, 2 * D:3 * D].partition_broadcast(128), r=mod_keys, w=["Gp"])
        for b in range(DEC_B):
            P.dma("sp", Gs[b * 8:(b + 1) * 8, l, :], modscr[l, 1 + b:2 + b, 2 * D:3 * D].partition_broadcast(8),
                  r=mod_keys, w=["Gs"])
    P.op("pool", lambda g: g.tensor_scalar_add(out=Gp[:], in0=Gp[:], scalar1=1.0), r=["Gp"], w=["Gp"])
    P.op("pool", lambda g: g.tensor_scalar_add(out=Gs[:], in0=Gs[:], scalar1=1.0), r=["Gs"], w=["Gs"])

    xtok = [P.sb("xtok%d" % i, [128, NSUB, D], F32) for i in range(2)]
    xbf = P.sb("xbf", [128, NSUB, D], BF16)
    mT = P.sb("mT", [128, NCH, TT], BF16)
    xbe = P.sb("xbe", [128, NCH, 3 + TT], F32)
    xbe_s = P.sb("xbe_s", [128, NCH, DEC_B, 3 + DEC_S], F32)
    hprev = P.sb("hprev", [128, NCH], F32)
    h0s = P.sb("h0s", [128, NCH, DEC_B], F32)
    hlast_s = P.sb("hlast_s", [128, NCH, DEC_B], F32)
    NT = 2
    xc = [P.sb("xc%d" % i, [128, TT], F32) for i in range(NT)]
    xcb = [P.sb("xcb%d" % i, [128, TT], BF16) for i in range(NT)]
    zs = [P.sb("zs%d" % i, [128, TT], F32) for i in range(NT)]
    ra = [P.sb("ra%d" % i, [128, TT], F32) for i in range(NT)]
    ri = [P.sb("ri%d" % i, [128, TT], F32) for i in range(NT)]
    ga = [P.sb("ga%d" % i, [128, TT], F32) for i in range(NT)]
    bb = [P.sb("bb%d" % i, [128, TT], F32) for i in range(NT)]
    hs = [P.sb("hs%d" % i, [128, TT], F32) for i in range(NT)]
    yg = P.sb("yg", [128, NCH, TT], BF16)
    x1t = [P.sb("x1t%d" % i, [128, D], F32) for i in range(2)]
    x1b = [P.sb("x1b%d" % i, [128, D], BF16) for i in range(2)]
    x1T = P.sb("x1T", [128, NCH, TT], BF16)
    kvst = [P.sb("kvst%d" % i, [128, 1536], F32) for i in range(2)]
    stat = [P.sb("stat%d" % i, [128, 16], F32) for i in range(2)]

    P.op("pool", lambda g: g.memset(xbe[:, :, 0:3], 0.0), w=["xbe"])
    P.op("pool", lambda g: g.memset(hprev[:], 0.0), w=["hprev"])
    for n in range(NCH):
        for b in range(DEC_B):
            P.dma("sp", xbe_s[:, n, b, 0:3], sc0[b * 3:(b + 1) * 3, n * 128:(n + 1) * 128].rearrange("k p -> p k"),
                  w=["xbe_s"])
        P.dma("sp", h0s[:, n, :], sh0.rearrange("b (c p) -> c p b", p=128)[n], w=["h0s"])

    cnt = {"tile": 0, "ch": 0, "sub": 0}

    def layernorm_tm(vin, vkey, out, okey, np_, layer, st, skey, Gt_=None, gk_="lnG", Bt_=None, bk_="lnB"):
        Gt_ = lnG if Gt_ is None else Gt_
        Bt_ = lnB if Bt_ is None else Bt_
        P.op("dve", lambda v: v.bn_stats(out=st[0:np_, 0:6], in_=vin[0:np_, 0:512]), r=[vkey], w=[skey])
        P.op("dve", lambda v: v.bn_stats(out=st[0:np_, 6:12], in_=vin[0:np_, 512:1024]), r=[vkey], w=[skey])
        P.op("dve", lambda v: v.bn_aggr(out=st[0:np_, 12:14], in_=st[0:np_, 0:12]),
             r=[skey], w=[skey])
        P.op("dve", lambda v: v.tensor_scalar_add(out=st[0:np_, 14:15], in0=st[0:np_, 13:14], scalar1=LN_EPS),
             r=[skey], w=[skey])
        P.op("act", lambda a: a.activation(out=st[0:np_, 14:15], in_=st[0:np_, 14:15], func=AF.Ln), r=[skey], w=[skey])
        P.op("act", lambda a: a.activation(out=st[0:np_, 14:15], in_=st[0:np_, 14:15], func=AF.Exp, scale=-0.5),
             r=[skey], w=[skey])
        P.op("dve", lambda v: v.scalar_tensor_tensor(out=st[0:np_, 15:16], in0=st[0:np_, 12:13], scalar=-1.0,
                                                     in1=st[0:np_, 14:15], op0=ALU.mult, op1=ALU.mult),
             r=[skey], w=[skey])
        P.op("act", lambda a: a.activation(out=out[0:np_, :], in_=vin[0:np_, :], func=AF.Identity,
                                           scale=st[0:np_, 14:15], bias=st[0:np_, 15:16]),
             r=[vkey, skey], w=[okey])
        P.op("pool", lambda g: g.tensor_tensor(out=out[0:np_, :], in0=out[0:np_, :], in1=Gt_[0:np_, layer, :], op=ALU.mult),
             r=[okey, gk_], w=[okey])
        P.op("pool", lambda g: g.tensor_tensor(out=out[0:np_, :], in0=out[0:np_, :], in1=Bt_[0:np_, layer, :], op=ALU.add),
             r=[okey, bk_], w=[okey])

    import os
    CHPIPE = int(os.environ.get("CHPIPE", "1"))

    class L0Tile:
        def __init__(self, ti, sample):
            self.ti, self.sample = ti, sample
            if sample:
                self.ncols, self.nsub, self.np_ = NS_TOK, 1, NS_TOK
                self.segs = [(b * DEC_S, DEC_S, 1 + b) for b in range(DEC_B)]
            else:
                self.ncols, self.nsub, self.np_ = TT, NSUB, 128
                self.segs = [(0, TT, 0)]
            self.t0 = ti * TT
            self.xt = xtok[cnt["tile"] % 2]
            self.xk = "xtok%d" % (cnt["tile"] % 2)
            cnt["tile"] += 1
            self.cis = {}

        def front(self):
            ti, sample, ncols, nsub, np_, segs, t0, xt, xk = (self.ti, self.sample, self.ncols, self.nsub, self.np_, self.segs,
                                                              self.t0, self.xt, self.xk)
            if sample:
                P.dma("sp", xt[0:np_, 0, :], xs[:, :], w=[xk])
            else:
                for s in range(nsub):
                    P.dma("sp", xt[:, s, :], xf[t0 + s * 128:t0 + (s + 1) * 128, :], w=[xk])
            for s in range(nsub):
                P.op("pool", lambda g, s=s: g.tensor_copy(out=xbf[0:np_, s, :], in_=xt[0:np_, s, :]), r=[xk], w=["xbf"])
            for half in range(2):
                pb = half
                for kk in range(4):
                    k = half * 4 + kk
                    for s in range(nsub):
                        P.op("pe", lambda t, k=k, kk=kk, s=s, pb=pb: t.transpose(
                            psb[pb][:, :].bitcast(BF16)[:, kk * TT + s * 128:kk * TT + s * 128 + np_],
                            xbf[0:np_, s, k * 128:(k + 1) * 128], ident_b[0:np_, 0:np_]),
                            r=["xbf", "ident_b"], w=[pskey(pb)])
                for kk in range(4):
                    k = half * 4 + kk
                    for (c0, cn, mj) in segs:
                        P.op("act", lambda a, k=k, kk=kk, pb=pb, c0=c0, cn=cn, mj=mj: a.activation(
                            out=mT[:, k, c0:c0 + cn], in_=psb[pb][:, :].bitcast(BF16)[:, kk * TT + c0:kk * TT + c0 + cn],
                            func=AF.Identity, scale=mod_fm[:, 0, 8 + k, mj:mj + 1], bias=mod_fm[:, 0, k, mj:mj + 1]),
                            r=[pskey(pb), "mod_fm"], w=["mT"])

        def chunk_ab(self, n):
            ti, sample, ncols = self.ti, self.sample, self.ncols
            ci = cnt["ch"] % NT
            cnt["ch"] += 1
            self.cis[n] = ci
            pb = 2 + (n % 2)
            pk = pskey(pb)
            for k in range(NCH):
                P.op("pe", lambda t, k=k, n=n, pb=pb: t.matmul(psb[pb][:, 0:ncols], lhsT=W_in[:, k, n * 128:(n + 1) * 128],
                                                               rhs=mT[:, k, 0:ncols], start=(k == 0), stop=(k == NCH - 1)),
                     r=["W_in", "mT"], w=[pk])
            for k in range(NCH):
                P.op("pe", lambda t, k=k, n=n, pb=pb: t.matmul(psb[pb][:, 256:256 + ncols],
                                                               lhsT=W_in[:, k, D + n * 128:D + (n + 1) * 128],
                                                               rhs=mT[:, k, 0:ncols], start=(k == 0), stop=(k == NCH - 1)),
                     r=["W_in", "mT"], w=[pk])
            if sample:
                xe = xbe_s[:, n, :, :]
                xek = "xbe_s"
                P.op("dve", lambda v, pb=pb, xe=xe: v.tensor_copy(
                    out=xe[:, :, 3:3 + DEC_S], in_=psb[pb][:, 0:ncols].rearrange("p (b t) -> p b t", t=DEC_S)),
                    r=[pk], w=[xek])
                sh = lambda k: xe[:, :, k:k + DEC_S]
                v3 = lambda ap: ap[:, 0:ncols].rearrange("p (b t) -> p b t", t=DEC_S)
            else:
                xe = xbe[:, n, :]
                xek = ("xbe", n)
                if ti > 0:
                    P.op("dve", lambda v, xe=xe: v.tensor_copy(out=xe[:, 0:3], in_=xe[:, TT:TT + 3]), r=[xek], w=[xek])
                P.op("dve", lambda v, pb=pb, xe=xe: v.tensor_copy(out=xe[:, 3:3 + TT], in_=psb[pb][:, 0:TT]), r=[pk], w=[xek])
                sh = lambda k: xe[:, k:k + TT]
                v3 = lambda ap: ap[:, 0:ncols]
            zk = "zs%d" % ci
            P.op("act", lambda a, pb=pb, ci=ci: a.activation(out=zs[ci][:, 0:ncols], in_=psb[pb][:, 256:256 + ncols],
                                                             func=AF.Sigmoid), r=[pk], w=[zk])
            P.op("dve", lambda v, pb=pb, ci=ci: v.tensor_tensor(out=zs[ci][:, 0:ncols], in0=psb[pb][:, 256:256 + ncols],
                                                                in1=zs[ci][:, 0:ncols], op=ALU.mult), r=[pk, zk], w=[zk])
            ck = "xc%d" % ci
            P.op("dve", lambda v, ci=ci, n=n: v.tensor_scalar(out=v3(xc[ci]), in0=sh(0), scalar1=pf[:, 0, n:n + 1],
                                                              scalar2=pf[:, 4, n:n + 1], op0=ALU.mult, op1=ALU.add),
                 r=[xek, "pf"], w=[ck])
            for k in range(1, 4):
                P.op("dve", lambda v, ci=ci, n=n, k=k: v.scalar_tensor_tensor(
                    out=v3(xc[ci]), in0=sh(k), scalar=pf[:, k, n:n + 1], in1=v3(xc[ci]), op0=ALU.mult, op1=ALU.add),
                    r=[xek, "pf", ck], w=[ck])
            cbk = "xcb%d" % ci
            P.op("pool", lambda g, ci=ci: g.tensor_copy(out=xcb[ci][:, 0:ncols], in_=xc[ci][:, 0:ncols]), r=[ck], w=[cbk])

        def chunk_cde(self, n):
            ti, sample, ncols = self.ti, self.sample, self.ncols
            ci = self.cis[n]
            zk, ck, cbk = "zs%d" % ci, "xc%d" % ci, "xcb%d" % ci
            pg = 4 + (n % 2)
            pgk = pskey(pg)
            P.op("pe", lambda t, n=n, ci=ci, pg=pg: t.matmul(psb[pg][:, 0:ncols], lhsT=W_r[:, n, :], rhs=xcb[ci][:, 0:ncols],
                                                             start=True, stop=True), r=["W_r", cbk], w=[pgk])
            P.op("pe", lambda t, n=n, ci=ci, pg=pg: t.matmul(psb[pg][:, 256:256 + ncols], lhsT=W_i[:, n, :],
                                                             rhs=xcb[ci][:, 0:ncols], start=True, stop=True),
                 r=["W_i", cbk], w=[pgk])
            rk, ik, gk, bk, hk = "ra%d" % ci, "ri%d" % ci, "ga%d" % ci, "bb%d" % ci, "hs%d" % ci
            P.op("act", lambda a, n=n, ci=ci, pg=pg: a.activation(out=ra[ci][:, 0:ncols], in_=psb[pg][:, 0:ncols],
                                                                  func=AF.Sigmoid, bias=pf[:, 5, n:n + 1]),
                 r=[pgk, "pf"], w=[rk])
            P.op("act", lambda a, n=n, ci=ci, pg=pg: a.activation(out=ri[ci][:, 0:ncols], in_=psb[pg][:, 256:256 + ncols],
                                                                  func=AF.Sigmoid, bias=pf[:, 6, n:n + 1]),
                 r=[pgk, "pf"], w=[ik])
            P.op("act", lambda a, n=n, ci=ci: a.activation(out=ga[ci][:, 0:ncols], in_=ra[ci][:, 0:ncols], func=AF.Exp,
                                                           scale=pf[:, 8, n:n + 1]), r=[rk, "pf"], w=[gk])
            P.op("act", lambda a, n=n, ci=ci: a.activation(out=ra[ci][:, 0:ncols], in_=ra[ci][:, 0:ncols], func=AF.Exp,
                                                           scale=pf[:, 7, n:n + 1]), r=[rk, "pf"], w=[rk])
            P.op("dve", lambda v, ci=ci: v.tensor_scalar(out=ga[ci][:, 0:ncols], in0=ga[ci][:, 0:ncols], scalar1=-1.0,
                                                         scalar2=1.0, op0=ALU.mult, op1=ALU.add), r=[gk], w=[gk])
            P.op("dve", lambda v, ci=ci: v.tensor_scalar_max(out=ga[ci][:, 0:ncols], in0=ga[ci][:, 0:ncols], scalar1=1e-30),
                 r=[gk], w=[gk])
            P.op("act", lambda a, ci=ci: a.activation(out=ga[ci][:, 0:ncols], in_=ga[ci][:, 0:ncols], func=AF.Ln),
                 r=[gk], w=[gk])
            P.op("act", lambda a, ci=ci: a.activation(out=ga[ci][:, 0:ncols], in_=ga[ci][:, 0:ncols], func=AF.Exp, scale=0.5),
                 r=[gk], w=[gk])
            P.op("pool", lambda g, ci=ci: g.tensor_tensor(out=bb[ci][:, 0:ncols], in0=ri[ci][:, 0:ncols],
                                                          in1=xc[ci][:, 0:ncols], op=ALU.mult), r=[ik, ck], w=[bk])
            P.op("dve", lambda v, ci=ci: v.tensor_tensor(out=bb[ci][:, 0:ncols], in0=bb[ci][:, 0:ncols],
                                                         in1=ga[ci][:, 0:ncols], op=ALU.mult), r=[bk, gk], w=[bk])
            if sample:
                for b in range(DEC_B):
                    P.op("dve", lambda v, ci=ci, n=n, b=b: v.tensor_tensor_scan(
                        out=hs[ci][:, b * DEC_S:(b + 1) * DEC_S], data0=ra[ci][:, b * DEC_S:(b + 1) * DEC_S],
                        data1=bb[ci][:, b * DEC_S:(b + 1) * DEC_S], initial=h0s[:, n, b:b + 1], op0=ALU.mult, op1=ALU.add),
                        r=[rk, bk, "h0s"], w=[hk])
                P.op("dve", lambda v, ci=ci, n=n: v.tensor_copy(
                    out=hlast_s[:, n, :], in_=hs[ci][:, 0:ncols].rearrange("p (b t) -> p b t", t=DEC_S)[:, :, DEC_S - 1]),
                    r=[hk], w=["hlast_s"])
            else:
                P.op("dve", lambda v, ci=ci, n=n: v.tensor_tensor_scan(
                    out=hs[ci][:, 0:TT], data0=ra[ci][:, 0:TT], data1=bb[ci][:, 0:TT], initial=hprev[:, n:n + 1],
                    op0=ALU.mult, op1=ALU.add), r=[rk, bk, ("hprev", n)], w=[hk])
                P.op("dve", lambda v, ci=ci, n=n: v.tensor_copy(out=hprev[:, n:n + 1], in_=hs[ci][:, TT - 1:TT]),
                     r=[hk], w=[("hprev", n)])
            P.op("pool", lambda g, ci=ci, n=n: g.tensor_tensor(out=yg[:, n, 0:ncols], in0=hs[ci][:, 0:ncols],
                                                               in1=zs[ci][:, 0:ncols], op=ALU.mult), r=[hk, zk], w=["yg"])

        def chunks(self, lo, hi):
            if CHPIPE == 0:
                for n in range(lo, hi):
                    self.chunk_ab(n)
                    self.chunk_cde(n)
                return
            for n in range(lo, hi):
                self.chunk_ab(n)
                if n - 1 >= 0:
                    self.chunk_cde(n - 1)
            if hi == NCH:
                self.chunk_cde(NCH - 1)

        def outproj_ln(self, subs=None):
            ti, sample, nsub, np_, t0, xt, xk = self.ti, self.sample, self.nsub, self.np_, self.t0, self.xt, self.xk
            if not hasattr(self, "sis"):
                self.sis = {}
            for s in (range(nsub) if subs is None else subs):
                si = cnt["sub"] % 2
                cnt["sub"] += 1
                self.sis[s] = si
                vk, x1k, x1bk, stk = "vt%d" % si, "x1t%d" % si, "x1b%d" % si, "stat%d" % si
                for h in range(2):
                    pb = 4 + h
                    for k in range(NCH):
                        P.op("pe", lambda t, k=k, h=h, s=s, pb=pb: t.matmul(
                            psb[pb][0:np_, :], lhsT=yg[:, k, s * 128:s * 128 + np_], rhs=W_out[:, k, h * 512:(h + 1) * 512],
                            start=(k == 0), stop=(k == NCH - 1)), r=["yg", "W_out"], w=[pskey(pb)])
                    G = Gs if sample else Gp
                    P.op("dve", lambda v, h=h, pb=pb, si=si, G=G: v.tensor_tensor(
                        out=vt[si][0:np_, h * 512:(h + 1) * 512], in0=psb[pb][0:np_, :], in1=G[0:np_, 0, h * 512:(h + 1) * 512],
                        op=ALU.mult), r=[pskey(pb), "Gs" if sample else "Gp"], w=[vk])
                P.op("dve", lambda v, si=si, s=s: v.scalar_tensor_tensor(
                    out=vt[si][0:np_, :], in0=xt[0:np_, s, :], scalar=ALPHA, in1=vt[si][0:np_, :], op0=ALU.mult, op1=ALU.add),
                    r=[xk, vk], w=[vk])
                layernorm_tm(vt[si], vk, x1t[si], x1k, np_, 0, stat[si], stk)
                if not sample:
                    P.dma("sp", x1scr[t0 + s * 128:t0 + (s + 1) * 128, :], x1t[si][:, :], r=[x1k], w=[("x1scr", ti, s)], semkey=x1k)
                else:
                    P.dma("sp", x1s_scr[:, :], x1t[si][0:np_, :], r=[x1k], w=["x1s_scr"], semkey=x1k)
                P.op("act", lambda a, si=si: a.activation(out=x1b[si][0:np_, :], in_=x1t[si][0:np_, :], func=AF.Identity),
                     r=[x1k], w=[x1bk])

        def tail(self, subs=None, fin=True):
            ti, sample, nsub, np_, t0 = self.ti, self.sample, self.nsub, self.np_, self.t0
            for s in (range(nsub) if subs is None else subs):
                si = self.sis[s]
                x1bk, kvk = "x1b%d" % si, "kvst%d" % si
                pb = 6 + (s % 2)
                for k in range(NCH):
                    P.op("pe", lambda t, k=k, si=si, pb=pb: t.transpose(
                        psb[pb][:, :].bitcast(BF16)[:, k * 128:k * 128 + np_], x1b[si][0:np_, k * 128:(k + 1) * 128],
                        ident_b[0:np_, 0:np_]), r=[x1bk, "ident_b"], w=[pskey(pb)])
                P.op("dve", lambda v, pb=pb, s=s: v.tensor_copy(
                    out=x1T[:, :, s * 128:s * 128 + np_],
                    in_=psb[pb][:, :].bitcast(BF16)[:, 0:1024].rearrange("p (k t) -> p k t", t=128)[:, :, 0:np_]),
                    r=[pskey(pb)], w=[("x1T", s)])
                for c3 in range(3):
                    pb = c3 % 2
                    for k in range(NCH):
                        P.op("pe", lambda t, k=k, c3=c3, s=s, pb=pb: t.matmul(
                            psb[pb][0:np_, :], lhsT=x1T[:, k, s * 128:s * 128 + np_], rhs=W_kv[:, k, c3 * 512:(c3 + 1) * 512],
                            start=(k == 0), stop=(k == NCH - 1)), r=[("x1T", s), "W_kv"], w=[pskey(pb)])
                    P.op("act", lambda a, c3=c3, pb=pb, si=si: a.activation(
                        out=kvst[si][0:np_, c3 * 512:(c3 + 1) * 512], in_=psb[pb][0:np_, :], func=AF.Identity),
                        r=[pskey(pb)], w=[kvk])
                if sample:
                    P.dma("sp", o_cmp_s[:, :], kvst[si][0:np_, 0:512], r=[kvk], w=["o_cmp_s"], semkey=kvk)
                    P.dma("sp", o_sel_s[:, :], kvst[si][0:np_, 512:1024], r=[kvk], w=["o_sel_s"], semkey=kvk)
                    for b in range(DEC_B):
                        P.dma("sp", o_win_s[b * 512 + 504:(b + 1) * 512, :], kvst[si][b * 8:(b + 1) * 8, 1024:1536],
                              r=[kvk], w=[("o_win_s", b, 1)], semkey=kvk)
                else:
                    r0 = t0 + s * 128
                    P.dma("sp", o_cmp_p[r0:r0 + 128, :], kvst[si][:, 0:512], r=[kvk], w=[("o_cmp_p", ti, s)], semkey=kvk)
                    P.dma("sp", o_sel_p[r0:r0 + 128, :], kvst[si][:, 512:1024], r=[kvk], w=[("o_sel_p", ti, s)], semkey=kvk)
                    P.dma("sp", winscr[r0:r0 + 128, :], kvst[si][:, 1024:1536], r=[kvk], w=[("winscr", ti, s)], semkey=kvk)
                    if r0 >= SEQ - 512:
                        P.dma("sp", o_win_p[r0 - (SEQ - 512):r0 - (SEQ - 512) + 128, :], kvst[si][:, 1024:1536],
                              r=[kvk], w=[("o_win_p", ti, s)], semkey=kvk)
            if sample and fin:
                for n in range(NCH):
                    P.dma("sp", o_h_s.rearrange("b (c p) -> c p b", p=128)[n], hlast_s[:, n, :], r=["hlast_s"], w=[("o_h_s", n)],
                          semkey="hlast_s")
                    for b in range(DEC_B):
                        P.dma("sp", o_conv_s[b * 3:(b + 1) * 3, n * 128:(n + 1) * 128].rearrange("k p -> p k"),
                              xbe_s[:, n, b, DEC_S:DEC_S + 3], r=["xbe_s"], w=[("o_conv_s", n, b)], semkey="xbe_s_o")

        def final_state(self):
            P.dma("sp", o_h_p[0:1, :].rearrange("o (c p) -> p (o c)", p=128), hprev[:, :],
                  r=[("hprev", n) for n in range(NCH)], w=["o_h_p"], semkey="hprev_o")
            for n in range(NCH):
                P.dma("sp", o_conv_p.rearrange("k (c p) -> c p k", p=128)[n], xbe[:, n, TT:TT + 3], r=[("xbe", n)],
                      w=[("o_conv_p", n)], semkey="xbe_o")

    import os
    L0PIPE = int(os.environ.get("L0PIPE", "2"))
    n_ptiles = SEQ // TT if stage >= 1 else 2
    if L0PIPE == -1:
        for i in range(n_ptiles + 1):
            tl = L0Tile(0, True) if i == 0 else L0Tile(i - 1, False)
            tl.front()
            tl.chunks(0, NCH)
            for s_ in range(tl.nsub):
                tl.outproj_ln([s_])
                tl.tail([s_], fin=(s_ == tl.nsub - 1))
        tl.final_state()
    elif L0PIPE == 0:
        seq = [L0Tile(0, True)] + [None] * n_ptiles
        for i in range(n_ptiles + 1):
            tl = seq[i] if i == 0 else L0Tile(i - 1, False)
            tl.front()
            tl.chunks(0, NCH)
            tl.outproj_ln()
            tl.tail()
        tl.final_state()
    elif L0PIPE == 1:
        cur = L0Tile(0, True)
        cur.front()
        for i in range(n_ptiles + 1):
            cur.chunks(0, NCH)
            nxt = L0Tile(i, False) if i < n_ptiles else None
            if nxt is not None:
                nxt.front()
            for s_ in range(cur.nsub):
                cur.outproj_ln([s_])
                cur.tail([s_], fin=(s_ == cur.nsub - 1))
            last = cur
            cur = nxt
        last.final_state()
    else:
        tiles = [L0Tile(0, True)]
        tiles[0].front()
        tiles[0].chunks(0, NCH)
        tiles[0].outproj_ln()
        nxt = L0Tile(0, False)
        nxt.front()
        prev = tiles[0]
        for ti in range(n_ptiles):
            cur = nxt
            cur.chunks(0, NCH // 2)
            prev.tail()
            cur.chunks(NCH // 2, NCH)
            if ti + 1 < n_ptiles:
                nxt = L0Tile(ti + 1, False)
                nxt.front()
            cur.outproj_ln()
            prev = cur
        prev.tail()
        prev.final_state()

    scA.__exit__(None, None, None)
    cmpscr_p = P.dram("cmpscr_p", [128, 512], F32)
    cmpscr_s = P.dram("cmpscr_s", [DEC_B * 128, 512], F32)
    do_sample = stage >= 3
    with P.scope():
        W1r = P.sb("W1r", [128, 2, 64, 128], BF16)
        CB2s = [P.sb("CB2_%d" % i, [128, 64, 512], BF16) for i in range(2)]
        cbi = {"i": 0}
        Hh = P.sb("Hh", [128, 2, 2, 256], BF16)
        W2 = P.sb("W2", [128, 2, 64], BF16)
        PEsb = P.sb("PEsb", [64, 128], BF16)
        bias1 = P.sb("bias1", [128, 4], F32)
        b2bc = P.sb("b2bc", [64, 2, 4, 128], F32)
        CS = P.sb("CS", [64, 2, 512], F32)
        IDXf = P.sb("IDXf", [128, DEC_B * 64], F32)
        IDXi = P.sb("IDXi", [128, DEC_B * 64], I32)
        iop = P.sb("iop", [128, 2], I32)
        iopf = P.sb("iopf", [128, 2], F32)
        for c in range(2):
            for half in range(2):
                P.dma("pool", W1r[half * 64:(half + 1) * 64, c, :, :], w_phi1[c], w=["W1r"])
            P.dma("pool", W2[:, c, :], w_phi2[c], w=["W2"])
            P.dma("sp", bias1[:, c:c + 1], b_phi1[c:c + 1, :].rearrange("o p -> p o"), w=["bias1"])
        P.dma("pool", PEsb[:], phi_pe[:, :], w=["PEsb"])
        for nl in range(2):
            for g in range(4):
                P.dma("sp", b2bc[:, nl, g, :], b_phi2.rearrange("c d -> (c d)").rearrange("(o n) -> o n", o=1).partition_broadcast(64),
                      w=["b2bc"])
        for c in range(2):
            for d in range(64):
                P.op("pe", lambda t, c=c, d=d: t.matmul(psb[0][:, c:c + 1], lhsT=W1r[0:64, c, d, :],
                                                        rhs=PEsb[0:64, c * 64 + d:c * 64 + d + 1],
                                                        start=(d == 0), stop=(d == 63)), r=["W1r", "PEsb"], w=[pskey(0)])
        P.op("dve", lambda v: v.tensor_tensor(out=bias1[:, 0:2], in0=psb[0][:, 0:2], in1=bias1[:, 0:2], op=ALU.add),
             r=[pskey(0), "bias1"], w=["bias1"])
        P.op("pool", lambda g_: g_.iota(out=iop[:, 0:1], pattern=[[0, 1]], base=0, channel_multiplier=1), w=["iop"])
        P.op("dve", lambda v: v.tensor_copy(out=iopf[:, 0:1], in_=iop[:, 0:1]), r=["iop"], w=["iopf"])
        P.dma("sp", IDXi[:], ptab.rearrange("b n -> (b n)").rearrange("(o n) -> o n", o=1).partition_broadcast(128), w=["IDXi"])
        P.op("dve", lambda v: v.tensor_copy(out=IDXf[:], in_=IDXi[:]), r=["IDXi"], w=["IDXf"])
        P.op("dve", lambda v: v.tensor_scalar(out=IDXf[:], in0=IDXf[:], scalar1=128.0, scalar2=iopf[:, 0:1],
                                              op0=ALU.mult, op1=ALU.add), r=["IDXf", "iopf"], w=["IDXf"])
        P.op("dve", lambda v: v.tensor_copy(out=IDXi[:], in_=IDXf[:]), r=["IDXf"], w=["IDXi"])
        idxscr = P.dram("idxscr", [128, DEC_B * 64], I32)
        P.dma("sp", idxscr[:, :], IDXi[:], r=["IDXi"], w=["idxscr"], semkey="IDXi_o")

        def compress(load_pages, out_rows, okey):
            CB2 = CB2s[cbi["i"] % 2]
            cbk = "CB2_%d" % (cbi["i"] % 2)
            cbi["i"] += 1
            CB2v = CB2[:].rearrange("p n (g c d) -> p n g c d", g=4, c=2)
            load_pages(CB2, cbk)
            for c in range(2):
                for nl in range(2):
                    pb = c * 2 + nl
                    for d in range(64):
                        P.op("pe", lambda t, c=c, nl=nl, d=d, pb=pb: t.matmul(
                            psb[pb][:, 0:256], lhsT=W1r[nl * 64:(nl + 1) * 64, c, d, :],
                            rhs=CB2v[nl * 64:(nl + 1) * 64, :, :, c, d], start=(d == 0), stop=(d == 63)),
                            r=["W1r", cbk], w=[pskey(pb)])
                    P.op("act", lambda a, c=c, nl=nl, pb=pb: a.activation(out=Hh[:, c, nl, :], in_=psb[pb][:, 0:256], func=AF.Silu,
                                                                          bias=bias1[:, c:c + 1]), r=[pskey(pb), "bias1"], w=["Hh"])
            Hv = Hh[:].rearrange("p c n (pg g) -> p c n pg g", g=4)
            for nl in range(2):
                pb = 4 + nl
                for g in range(4):
                    for c in range(2):
                        col = (g * 2 + c) * 64
                        P.op("pe", lambda t, nl=nl, g=g, c=c, pb=pb, col=col: t.matmul(
                            psb[pb][0:64, col:col + 64], lhsT=Hv[:, c, nl, :, g], rhs=W2[:, c, :], start=True, stop=True),
                            r=["Hh", "W2"], w=[pskey(pb)])
                P.op("dve", lambda v, nl=nl, pb=pb: v.tensor_tensor(
                    out=CS[:, nl, :], in0=psb[pb][0:64, :], in1=b2bc[:, nl, :, :].rearrange("p g f -> p (g f)"), op=ALU.add),
                    r=[pskey(pb), "b2bc"], w=["CS"])
            P.dma("sp", out_rows.rearrange("(pg n) f -> pg n f", n=2), CS[:], r=["CS"], w=[okey], semkey="CS")

        def load_prompt_pages(CB2, cbk):
            for pg in range(64):
                P.dma("pool", CB2[:, pg, :], o_cmp_p[pg * 128:(pg + 1) * 128, :], r=[("o_cmp_p", pg // NSUB, pg % NSUB)], w=[cbk])

        if stage >= 2:
            compress(load_prompt_pages, cmpscr_p[:, :], "cmpscr_p")
        if do_sample:
            for b in range(DEC_B):
                def load_sample_pages(CB2, cbk, b=b):
                    for pg in range(64):
                        P.gather(CB2[:, pg, :], ccmp[:, :], IDXi[:, b * 64 + pg:b * 64 + pg + 1], r=["IDXi"], w=[cbk])
                compress(load_sample_pages, cmpscr_s[b * 128:(b + 1) * 128, :], ("cmpscr_s", b))

    NTOK = 2048 + NS_TOK
    QTscr = P.dram("QTscr", [4, 64, 4, NTOK], BF16)
    ZSscr = P.dram("ZSscr", [NTOK, D], BF16)
    GLscr = P.dram("GLscr", [NTOK, 48], F32)
    OGscr = P.dram("OGscr", [NTOK, D], BF16)
    qtiles = [(jl * 128, 128, jl, None) for jl in range(16)] if stage >= 2 else []
    if do_sample:
        qtiles += [(2048 + b * 8, 8, None, b) for b in range(DEC_B)]
    if stage == 2.5:
        qtiles = qtiles[:2]

    with P.scope():
        W_inb = P.sb("W_inb", [128, NCH, 2096], BF16)
        for k in range(NCH):
            P.dma("pool", W_inb[:, k, :], w_in_b[k * 128:(k + 1) * 128, :], w=["W_inb"])
        bgbc = P.sb("bgbc", [128, 48], F32)
        P.dma("sp", bgbc[:], b_gate[0:1, :].partition_broadcast(128), w=["bgbc"])
        idxo = P.sb("idxo", [128, 16], I32)
        P.dma("sp", idxo[:], t_idx_own[:, :], w=["idxo"])
        X1 = [P.sb("X1_%d" % i, [128, D], F32) for i in range(2)]
        X1b = P.sb("X1b", [128, D], BF16)
        m1T = P.sb("m1T", [128, NCH, 128], BF16)
        QTst = [P.sb("QTst%d" % i, [64, 4, 128], BF16) for i in range(2)]
        ZSt = [P.sb("ZSt%d" % i, [128, D], BF16) for i in range(2)]
        GLt = [P.sb("GLt%d" % i, [128, 48], F32) for i in range(2)]
        qi = 0
        for (tok0, nq, jl, sb_) in qtiles:
            i2 = qi % 2
            xk = "X1_%d" % i2
            if jl is not None:
                P.gather(X1[i2][:, :], x1scr[:, :], idxo[:, jl:jl + 1], r=["idxo"], w=[xk])
                mj = 0
            else:
                P.dma("sp", X1[i2][0:nq, :], x1s_scr[sb_ * 8:(sb_ + 1) * 8, :], w=[xk])
                mj = 1 + sb_
            P.op("pool", lambda g_, i2=i2, nq=nq: g_.tensor_copy(out=X1b[0:nq, :], in_=X1[i2][0:nq, :]), r=[xk], w=["X1b"])
            for k in range(NCH):
                P.op("pe", lambda t, k=k, nq=nq: t.transpose(psb[0][:, :].bitcast(BF16)[:, k * 128:k * 128 + nq],
                                                             X1b[0:nq, k * 128:(k + 1) * 128], ident_b[0:nq, 0:nq]),
                     r=["X1b", "ident_b"], w=[pskey(0)])
            for k in range(NCH):
                P.op("act", lambda a, k=k, nq=nq, mj=mj: a.activation(
                    out=m1T[:, k, 0:nq], in_=psb[0][:, :].bitcast(BF16)[:, k * 128:k * 128 + nq], func=AF.Identity,
                    scale=mod_fm[:, 1, 8 + k, mj:mj + 1], bias=mod_fm[:, 1, k, mj:mj + 1]), r=[pskey(0), "mod_fm"], w=["m1T"])
            for g in range(4):
                pb = 2 + (g % 2)
                qk = "QTst%d" % (g % 2)
                for hh in range(4):
                    for k in range(NCH):
                        P.op("pe", lambda t, k=k, hh=hh, g=g, pb=pb, nq=nq: t.matmul(
                            psb[pb][0:64, hh * 128:hh * 128 + nq], lhsT=W_inb[:, k, (4 * g + hh) * 64:(4 * g + hh + 1) * 64],
                            rhs=m1T[:, k, 0:nq], start=(k == 0), stop=(k == NCH - 1)), r=["W_inb", "m1T"], w=[pskey(pb)])
                P.op("dve", lambda v, g=g, pb=pb, nq=nq: v.tensor_scalar_mul(
                    out=QTst[g % 2][:, :, 0:nq], in0=psb[pb][0:64, :].rearrange("p (h q) -> p h q", h=4)[:, :, 0:nq],
                    scalar1=0.125), r=[pskey(pb)], w=[qk])
                P.dma("sp", QTscr[g, :, :, tok0:tok0 + nq], QTst[g % 2][:, :, 0:nq], r=[qk], w=[("QTscr", g, tok0)], semkey=qk)
            zk = "ZSt%d" % i2
            for half in range(2):
                pb = 4 + half
                for k in range(NCH):
                    P.op("pe", lambda t, k=k, half=half, pb=pb, nq=nq: t.matmul(
                        psb[pb][0:nq, :], lhsT=m1T[:, k, 0:nq], rhs=W_inb[:, k, D + half * 512:D + (half + 1) * 512],
                        start=(k == 0), stop=(k == NCH - 1)), r=["W_inb", "m1T"], w=[pskey(pb)])
                P.op("act", lambda a, half=half, pb=pb, nq=nq, i2=i2: a.activation(
                    out=ZSt[i2][0:nq, half * 512:(half + 1) * 512], in_=psb[pb][0:nq, :], func=AF.Silu), r=[pskey(pb)], w=[zk])
            P.dma("sp", ZSscr[tok0:tok0 + nq, :], ZSt[i2][0:nq, :], r=[zk], w=[("ZSscr", tok0)], semkey=zk)
            gk = "GLt%d" % i2
            for k in range(NCH):
                P.op("pe", lambda t, k=k, nq=nq: t.matmul(psb[6][0:nq, 0:48], lhsT=m1T[:, k, 0:nq], rhs=W_inb[:, k, 2048:2096],
                                                          start=(k == 0), stop=(k == NCH - 1)), r=["W_inb", "m1T"], w=[pskey(6)])
            P.op("dve", lambda v, nq=nq, i2=i2: v.tensor_tensor(out=GLt[i2][0:nq, :], in0=psb[6][0:nq, 0:48], in1=bgbc[0:nq, :],
                                                                op=ALU.add), r=[pskey(6), "bgbc"], w=[gk])
            P.op("act", lambda a, nq=nq, i2=i2: a.activation(out=GLt[i2][0:nq, :], in_=GLt[i2][0:nq, :], func=AF.Sigmoid),
                 r=[gk], w=[gk])
            P.dma("sp", GLscr[tok0:tok0 + nq, :], GLt[i2][0:nq, :], r=[gk], w=[("GLscr", tok0)], semkey=gk)
            qi += 1

    with P.scope():
        EE = P.sb("EE", [128, 64, 128], BF16)
        ones_b = P.sb("ones_b", [128, 1024], BF16)
        P.op("pool", lambda g_: g_.memset(ones_b[:], 1.0), w=["ones_b"])
        for T8 in range(8):
            P.op("pool", lambda g_, T8=T8: g_.affine_select(
                out=EE[:, T8 * 8:(T8 + 1) * 8, :].rearrange("p t (a b) -> p t a b", a=2),
                in_=ones_b[:, :].rearrange("p (t a b) -> p t a b", t=8, a=2),
                pattern=[[-2, 8], [-1, 2], [0, 64]], compare_op=ALU.is_equal, fill=0.0, base=-16 * T8, channel_multiplier=1),
                r=["ones_b"], w=["EE"])
        TRIp = P.sb("TRIp", [128, 512], BF16)
        TRI2p = P.sb("TRI2p", [128, 512], BF16)
        TRIs = P.sb("TRIs", [128, 32], BF16)
        TRI2s = P.sb("TRI2s", [128, 32], BF16)
        P.dma("sp", TRIp[:], t_tri_p[:, :], w=["TRIp"])
        P.dma("sp", TRI2p[:], t_tri2_p[:, :], w=["TRI2p"])
        P.dma("sp", TRIs[:], t_tri_s[:, :], w=["TRIs"])
        P.dma("sp", TRI2s[:], t_tri2_s[:, :], w=["TRI2s"])
        RB = [P.sb("RB%d" % i, [128, 4, 512], BF16) for i in range(2)]
        RBF = [P.sb("RBF%d" % i, [128, 512], BF16) for i in range(4)]
        KsT4 = P.sb("KsT4", [72, 4, 65 * 128], BF16)
        Vs4 = P.sb("Vs4", [128, 65, 4, 65], BF16)
        KwT4 = P.sb("KwT4", [72, 4, 21 * 128], BF16)
        Vw4 = P.sb("Vw4", [128, 21, 4, 65], BF16)
        KcT4 = P.sb("KcT4", [72, 4, 128], BF16)
        Vc4 = P.sb("Vc4", [128, 4, 64], BF16)
        P.op("pool", lambda g_: g_.memset(Vs4[:, :, :, 64:65], 1.0), w=["Vs4"])
        P.op("pool", lambda g_: g_.memset(Vw4[:, :, :, 64:65], 1.0), w=["Vw4"])
        idxo2 = P.sb("idxo2", [128, 16], I32)
        idxw = P.sb("idxw", [128, 20], I32)
        idxs = P.sb("idxs", [128, 1], I32)
        idxpg = P.sb("idxpg", [128, DEC_B * 64], I32)
        P.dma("sp", idxo2[:], t_idx_own[:, :], w=["idxo2"])
        P.dma("sp", idxw[:], t_idx_win[:, :], w=["idxw"])
        P.dma("sp", idxs[:], t_idx_slot[:, :], w=["idxs"])
        P.dma("sp", idxpg[:], idxscr[:, :], w=["idxpg"])
        QT = [P.sb("QT%d" % i, [72, 4, 128], BF16) for i in range(2)]
        FBNt = [P.sb("FBNt%d" % i, [128, 128], F32) for i in range(2)]
        CAUt = [P.sb("CAUt%d" % i, [128, 128], F32) for i in range(2)]
        TMt = [P.sb("TMt%d" % i, [128, 128], F32) for i in range(2)]
        GLg = [P.sb("GLg%d" % i, [128, 48], F32) for i in range(2)]
        ZSg = [P.sb("ZSg%d" % i, [128, 256], BF16) for i in range(2)]
        Ssb = P.sb("Ssb", [128, 512], F32)
        Esb = P.sb("Esb", [128, 512], F32)
        Pn = P.sb("Pn", [128, 512], F32)
        Pnb = P.sb("Pnb", [128, 512], BF16)
        imp = P.sb("imp", [128, 128], F32)
        scr = P.sb("scr", [128, 128], F32)
        scr2 = P.sb("scr2", [128, 128], F32)
        m8 = P.sb("m8", [128, 16], F32)
        smh = P.sb("smh", [128, 16], F32)
        smb = P.sb("smb", [128, 16], F32)
        MselT = P.sb("MselT", [128, 128], BF16)
        Msel4s = [P.sb("Msel4_%d" % i, [128, 512], BF16) for i in range(2)]
        PTc = P.sb("PTc", [128, 512], BF16)
        PT = [P.sb("PT%d" % i, [128, 512], BF16) for i in range(4)]
        PTm = [P.sb("PTm%d" % i, [128, 512], BF16) for i in range(4)]
        OaugSB = P.sb("OaugSB", [65, 512], F32)
        accs = [P.sb("acc%d" % i, [128, 256], F32) for i in range(2)]
        OGt = [P.sb("OGt%d" % i, [128, 256], BF16) for i in range(2)]
        cnt2 = {"rb": 0, "rbf": 0, "pt": 0, "ps": 0, "q": 0, "mx": 0}

        def prep_from_rbf(rbf, rk, g, nk, ktdst, kkey, vdst, vkey):
            P.op("pe", lambda t: t.transpose(psb[7][:, :].bitcast(BF16)[0:64, 0:nk], rbf[0:nk, g * 128:g * 128 + 64],
                                             ident_b[0:nk, 0:nk]), r=[rk, "ident_b"], w=[pskey(7)])
            P.op("dve", lambda v: v.tensor_copy(out=ktdst, in_=psb[7][:, :].bitcast(BF16)[0:64, 0:nk]), r=[pskey(7)], w=[kkey])
            P.op("pool", lambda g_: g_.tensor_copy(out=vdst, in_=rbf[0:nk, g * 128 + 64:g * 128 + 128]), r=[rk], w=[vkey])

        def prep4(rbf, rk, nk, ktdst3, kkey, vdst3, vkey):
            for g4 in range(4):
                P.op("pe", lambda t, g4=g4: t.transpose(psb[7][:, :].bitcast(BF16)[0:64, g4 * 128:g4 * 128 + nk],
                                                        rbf[0:nk, g4 * 128:g4 * 128 + 64], ident_b[0:nk, 0:nk]),
                     r=[rk, "ident_b"], w=[pskey(7)])
            P.op("dve", lambda v: v.tensor_copy(
                out=ktdst3, in_=psb[7][:, :].bitcast(BF16)[0:64, 0:512].rearrange("p (g k) -> p g k", g=4)[:, :, 0:nk]),
                r=[pskey(7)], w=[kkey])
            P.op("pool", lambda g_: g_.tensor_copy(
                out=vdst3, in_=rbf[0:nk, :].rearrange("p (g c d) -> p g c d", g=4, c=2)[:, :, 1, :]), r=[rk], w=[vkey])

        def load_rows_gather(src, idx_ap, ikey):
            i = cnt2["rbf"] % 4
            cnt2["rbf"] += 1
            P.gather(RBF[i][:, :], src, idx_ap, r=[ikey], w=["RBF%d" % i])
            return RBF[i], "RBF%d" % i

        def load_rows_plain(src_rows, nk):
            i = cnt2["rbf"] % 4
            cnt2["rbf"] += 1
            P.dma("pool", RBF[i][0:nk, :], src_rows, w=["RBF%d" % i])
            return RBF[i], "RBF%d" % i

        def nsa_tile(g, tok0, nq, jl, sb_, kth):
            ncol = 4 * nq
            sample = jl is None
            i2 = cnt2["q"] % 2
            cnt2["q"] += 1
            qt, qk = QT[i2], "QT%d" % i2
            Msel4, mk4 = Msel4s[i2], "Msel4_%d" % i2
            acc, ak = accs[i2], "acc%d" % i2
            P.dma("sp", qt[0:64, :, 0:nq], QTscr[g, :, :, tok0:tok0 + nq], w=[qk])
            if sample:
                P.dma("sp", qt[64:72, :, 0:nq], t_qaug_s.rearrange("r (h q) -> r h q", h=16)[:, 4 * g:4 * g + 4, :], w=[qk])
                P.dma("sp", FBNt[i2][0:nq, :], t_fbn_s[:, :], w=["FBNt%d" % i2])
                P.dma("sp", CAUt[i2][0:nq, :], t_caus_s[:, :], w=["CAUt%d" % i2])
                P.dma("sp", TMt[i2][0:nq, :], t_tm_s[:, :], w=["TMt%d" % i2])
            else:
                P.dma("sp", qt[64:72, :, 0:nq], t_qaug_p.rearrange("r (h q) -> r h q", h=16)[:, 4 * g:4 * g + 4, tok0:tok0 + nq],
                      w=[qk])
                P.dma("sp", FBNt[i2][:, :], t_fbn[jl], w=["FBNt%d" % i2])
                P.dma("sp", CAUt[i2][:, :], t_caus[jl], w=["CAUt%d" % i2])
                P.dma("sp", TMt[i2][:, :], t_tm[jl], w=["TMt%d" % i2])
            fk, ck, tk, glk, zk = "FBNt%d" % i2, "CAUt%d" % i2, "TMt%d" % i2, "GLg%d" % i2, "ZSg%d" % i2
            P.dma("sp", GLg[i2][0:nq, :], GLscr[tok0:tok0 + nq, :], w=[glk])
            P.dma("sp", ZSg[i2][0:nq, :], ZSscr[tok0:tok0 + nq, g * 256:(g + 1) * 256], w=[zk])
            qrhs = qt[0:72, :, 0:nq]
            kc_ap, kck, vc_ap, vck = KcT4[0:72, g, :], "KcT4", Vc4[:, g, :], "Vc4"
            ksel = lambda T, nk: (KsT4[0:72, g, T * 128:T * 128 + nk], "KsT4", Vs4[0:nk, T, g, :], "Vs4")
            kwin = lambda T, nk: (KwT4[0:72, g, T * 128:T * 128 + nk], "KwT4", Vw4[0:nk, T, g, :], "Vw4")
            gl3 = GLg[i2][0:nq, :].rearrange("p (h b) -> p h b", b=3)
            for hh in range(4):
                P.op("pe", lambda t, hh=hh: t.matmul(psb[7][0:nq, hh * 128:(hh + 1) * 128], lhsT=qt[0:72, hh, 0:nq], rhs=kc_ap,
                                                     start=True, stop=True), r=[qk, kck], w=[pskey(7)])
            P.op("dve", lambda v: v.tensor_tensor(
                out=Ssb[0:nq, :].rearrange("p (h s) -> p h s", h=4), in0=psb[7][0:nq, :].rearrange("p (h s) -> p h s", h=4),
                in1=TMt[i2][0:nq, :].unsqueeze(1).to_broadcast([nq, 4, 128]), op=ALU.add), r=[pskey(7), tk], w=["Ssb"])
            for hh in range(4):
                P.op("act", lambda a, hh=hh: a.activation(out=Esb[0:nq, hh * 128:(hh + 1) * 128], in_=Ssb[0:nq, hh * 128:(hh + 1) * 128],
                                                          func=AF.Exp, accum_out=smh[0:nq, hh:hh + 1]), r=["Ssb"], w=["Esb", "smh"])
            P.op("dve", lambda v: v.tensor_scalar_add(out=smh[0:nq, 4:8], in0=smh[0:nq, 0:4], scalar1=1e-30), r=["smh"], w=["smh"])
            P.op("dve", lambda v: v.reciprocal(out=smh[0:nq, 4:8], in_=smh[0:nq, 4:8]), r=["smh"], w=["smh"])
            for hh in range(4):
                P.op("dve", lambda v, hh=hh: v.tensor_scalar_mul(out=Pn[0:nq, hh * 128:(hh + 1) * 128],
                                                                 in0=Esb[0:nq, hh * 128:(hh + 1) * 128],
                                                                 scalar1=smh[0:nq, 4 + hh:5 + hh]), r=["Esb", "smh"], w=["Pn"])
            P.op("pool", lambda g_: g_.tensor_copy(out=Pnb[0:nq, :], in_=Pn[0:nq, :]), r=["Pn"], w=["Pnb"])
            P.op("dve", lambda v: v.tensor_tensor(out=imp[0:nq, :], in0=Pn[0:nq, 0:128], in1=Pn[0:nq, 128:256], op=ALU.add),
                 r=["Pn"], w=["imp"])
            P.op("dve", lambda v: v.tensor_tensor(out=imp[0:nq, :], in0=imp[0:nq, :], in1=Pn[0:nq, 256:384], op=ALU.add),
                 r=["Pn", "imp"], w=["imp"])
            P.op("dve", lambda v: v.tensor_tensor(out=imp[0:nq, :], in0=imp[0:nq, :], in1=Pn[0:nq, 384:512], op=ALU.add),
                 r=["Pn", "imp"], w=["imp"])
            P.op("dve", lambda v: v.tensor_tensor(out=scr[0:nq, :], in0=imp[0:nq, :], in1=CAUt[i2][0:nq, :], op=ALU.mult),
                 r=["imp", ck], w=["scr"])
            P.op("dve", lambda v: v.tensor_tensor(out=scr[0:nq, :], in0=scr[0:nq, :], in1=FBNt[i2][0:nq, :], op=ALU.add),
                 r=["scr", fk], w=["scr"])
            P.op("dve", lambda v: v.max(out=m8[0:nq, 0:8], in_=scr[0:nq, :]), r=["scr"], w=["m8"])
            P.op("dve", lambda v: v.match_replace(out=scr2[0:nq, :], in_to_replace=m8[0:nq, 0:8], in_values=scr[0:nq, :],
                                                  imm_value=-1.0e30), r=["scr", "m8"], w=["scr2"])
            P.op("dve", lambda v: v.max(out=m8[0:nq, 8:16], in_=scr2[0:nq, :]), r=["scr2"], w=["m8"])
            thr = m8[0:nq, kth - 1:kth]
            P.op("dve", lambda v: v.tensor_scalar(out=scr2[0:nq, :], in0=scr[0:nq, :], scalar1=thr, scalar2=None, op0=ALU.is_ge),
                 r=["scr", "m8"], w=["scr2"])
            P.op("dve", lambda v: v.tensor_tensor(out=scr2[0:nq, :], in0=scr2[0:nq, :], in1=CAUt[i2][0:nq, :], op=ALU.mult),
                 r=["scr2", ck], w=["scr2"])
            P.op("dve", lambda v: v.tensor_copy(out=MselT[0:nq, :], in_=scr2[0:nq, :]), r=["scr2"], w=["MselT"])
            yield "a"
            P.op("pe", lambda t: t.transpose(psb[7][:, :].bitcast(BF16)[:, 0:nq], MselT[0:nq, :], ident_b[0:nq, 0:nq]),
                 r=["MselT", "ident_b"], w=[pskey(7)])
            P.op("dve", lambda v: v.tensor_copy(
                out=Msel4[:, 0:nq], in_=psb[7][:, :].bitcast(BF16)[:, 0:nq]), r=[pskey(7)], w=[mk4])
            for hh in range(4):
                P.op("pe", lambda t, hh=hh: t.transpose(psb[7][:, :].bitcast(BF16)[:, 512 + hh * nq:512 + (hh + 1) * nq],
                                                        Pnb[0:nq, hh * 128:(hh + 1) * 128], ident_b[0:nq, 0:nq]),
                     r=["Pnb", "ident_b"], w=[pskey(7)])
            P.op("dve", lambda v: v.tensor_copy(out=PTc[:, 0:ncol], in_=psb[7][:, :].bitcast(BF16)[:, 512:512 + ncol]),
                 r=[pskey(7)], w=["PTc"])
            for hh in range(4):
                P.op("pe", lambda t, hh=hh: t.matmul(psb[7][0:nq, hh * 64:(hh + 1) * 64], lhsT=PTc[:, hh * nq:(hh + 1) * nq], rhs=vc_ap,
                                                     start=True, stop=True), r=["PTc", vck], w=[pskey(7)])
            for hh in range(4):
                P.op("dve", lambda v, hh=hh: v.tensor_scalar_mul(out=acc[0:nq, hh * 64:(hh + 1) * 64],
                                                                 in0=psb[7][0:nq, hh * 64:(hh + 1) * 64],
                                                                 scalar1=gl3[:, 4 * g + hh, 0:1]), r=[pskey(7), glk], w=[ak])

            yield "b"
            def attend(tiles, br, ob):
                nt = len(tiles)
                slots = {}
                mslots = {}

                SB = (0, 1, 2, 6)

                def s_stage(i):
                    T = tiles[i]
                    sbk = SB[cnt2["ps"] % 4]
                    cnt2["ps"] += 1
                    nk = T["nk"]
                    mm = [(T["kt"], qrhs, T["kkey"], qk)] + T["masks"]
                    for j, (l, r_, lk, rk) in enumerate(mm):
                        P.op("pe", lambda t, l=l, r_=r_, j=j, nk=nk, sbk=sbk, n=len(mm): t.matmul(
                            psb[sbk][0:nk, 0:ncol], lhsT=l, rhs=r_, start=(j == 0), stop=(j == n - 1)),
                            r=[lk, rk], w=[pskey(sbk)])
                    slots[i] = sbk

                def m_stage(i):
                    T = tiles[i]
                    nk = T["nk"]
                    mxb = None
                    if T["msel"] is not None:
                        mxb = 4 + cnt2["mx"] % 2
                        cnt2["mx"] += 1
                        P.op("pe", lambda t, Tm=T["msel"], nk=nk, mxb=mxb: t.matmul(
                            psb[mxb][0:nk, 0:nq], lhsT=EE[:, Tm, 0:nk], rhs=Msel4[:, 0:nq], start=True, stop=True),
                            r=["EE", mk4], w=[pskey(mxb)])
                    mslots[i] = mxb

                def e_stage(i):
                    T = tiles[i]
                    nk = T["nk"]
                    sbk, mxb = slots[i], mslots[i]
                    pi = cnt2["pt"] % 4
                    cnt2["pt"] += 1
                    P.op("act", lambda a, nk=nk, sbk=sbk, pi=pi: a.activation(out=PT[pi][0:nk, 0:ncol], in_=psb[sbk][0:nk, 0:ncol],
                                                                              func=AF.Exp), r=[pskey(sbk)], w=["PT%d" % pi])
                    if mxb is None:
                        return PT[pi], "PT%d" % pi
                    P.op("dve", lambda v, nk=nk, pi=pi, mxb=mxb: v.tensor_tensor(
                        out=PTm[pi][0:nk, 0:ncol].rearrange("p (h q) -> p h q", h=4),
                        in0=PT[pi][0:nk, 0:ncol].rearrange("p (h q) -> p h q", h=4),
                        in1=psb[mxb][0:nk, 0:nq].unsqueeze(1).to_broadcast([nk, 4, nq]), op=ALU.mult),
                        r=["PT%d" % pi, pskey(mxb)], w=["PTm%d" % pi])
                    return PTm[pi], "PTm%d" % pi

                def v_stage(i, pi):
                    T = tiles[i]
                    nk = T["nk"]
                    pbuf, pkey_ = pi
                    P.op("pe", lambda t, T=T, nk=nk, pbuf=pbuf, i=i: t.matmul(psb[ob][0:65, 0:ncol], lhsT=T["v"], rhs=pbuf[0:nk, 0:ncol],
                                                                             start=(i == 0), stop=(i == nt - 1)),
                         r=[T["vkey"], pkey_], w=[pskey(ob)])

                LOOK_S, LOOK_M = 3, 2
                for i in range(min(LOOK_S, nt)):
                    s_stage(i)
                for i in range(min(LOOK_M, nt)):
                    m_stage(i)
                for i in range(nt):
                    pi = e_stage(i)
                    if i + LOOK_S < nt:
                        s_stage(i + LOOK_S)
                    if i + LOOK_M < nt:
                        m_stage(i + LOOK_M)
                    v_stage(i, pi)
                P.op("dve", lambda v: v.tensor_copy(out=OaugSB[0:65, 0:ncol], in_=psb[ob][0:65, 0:ncol]), r=[pskey(ob)], w=["OaugSB"])
                for hh in range(4):
                    P.op("pe", lambda t, hh=hh: t.transpose(psb[7][0:nq, hh * 65:(hh + 1) * 65], OaugSB[0:65, hh * nq:(hh + 1) * nq],
                                                            ident_f[0:65, 0:65]), r=["OaugSB", "ident_f"], w=[pskey(7)])
                o3 = psb[7][0:nq, 0:260].rearrange("p (h e) -> p h e", e=65)
                P.op("dve", lambda v: v.tensor_scalar_add(out=smb[0:nq, 8:12], in0=o3[:, :, 64], scalar1=1e-30), r=[pskey(7)], w=["smb"])
                P.op("dve", lambda v: v.reciprocal(out=smb[0:nq, 8:12], in_=smb[0:nq, 8:12]), r=["smb"], w=["smb"])
                P.op("dve", lambda v: v.tensor_tensor(out=smb[0:nq, 12:16], in0=smb[0:nq, 8:12], in1=gl3[:, 4 * g:4 * g + 4, br],
                                                      op=ALU.mult), r=["smb", glk], w=["smb"])
                for hh in range(4):
                    P.op("dve", lambda v, hh=hh: v.scalar_tensor_tensor(
                        out=acc[0:nq, hh * 64:(hh + 1) * 64], in0=o3[:, hh, 0:64], scalar=smb[0:nq, 12 + hh:13 + hh],
                        in1=acc[0:nq, hh * 64:(hh + 1) * 64], op0=ALU.mult, op1=ALU.add), r=[pskey(7), "smb", ak], w=[ak])

            def ktile(src, T, nk=128, masks=()):
                kt_, kkey_, v_, vkey_ = src(T, nk)
                add = [m for m in masks if m[0] != "MSEL"]
                ms_ = [m[1] for m in masks if m[0] == "MSEL"]
                return {"kt": kt_, "kkey": kkey_, "v": v_, "vkey": vkey_, "nk": nk, "masks": add,
                        "msel": (ms_[0] if ms_ else None)}

            msel = lambda T: ("MSEL", T)
            if sample:
                tri = (ident_b[0:8, 0:8], TRIs[0:8, 0:ncol], "ident_b", "TRIs")
                tri2 = (ident_b[:, :], TRI2s[:, 0:ncol], "ident_b", "TRI2s")
                sel_tiles = [ktile(ksel, T, masks=[msel(T)]) for T in range(64)]
                sel_tiles.append(ktile(ksel, 64, nk=8, masks=[tri]))
                win_tiles = [ktile(kwin, 0, masks=[tri2])] + [ktile(kwin, w) for w in range(1, 4)]
                win_tiles.append(ktile(kwin, 4, nk=8, masks=[tri]))
            else:
                tri = (ident_b[:, :], TRIp[:, 0:ncol], "ident_b", "TRIp")
                tri2 = (ident_b[:, :], TRI2p[:, 0:ncol], "ident_b", "TRI2p")
                sel_tiles = [ktile(ksel, T, masks=[msel(T)]) for T in range(48)]
                for j2 in range(jl + 1):
                    ms = [msel(48 + j2)] + ([tri] if j2 == jl else [])
                    sel_tiles.append(ktile(ksel, 48 + j2, masks=ms))
                win_tiles = []
                for w in range(jl, jl + 5):
                    ms = [tri2] if w == jl else ([tri] if w == jl + 4 else [])
                    win_tiles.append(ktile(kwin, w, masks=ms))
            attend(sel_tiles, 1, 3)
            yield "sel"
            attend(win_tiles, 2, 3)
            ogk = "OGt%d" % i2
            P.op("dve", lambda v: v.tensor_tensor(out=OGt[i2][0:nq, :], in0=acc[0:nq, :], in1=ZSg[i2][0:nq, :], op=ALU.mult),
                 r=[ak, zk], w=[ogk])
            P.dma("sp", OGscr[tok0:tok0 + nq, g * 256:(g + 1) * 256], OGt[i2][0:nq, :], r=[ogk], w=[("OGscr", tok0, g)], semkey=ogk)
            yield "done"

        def run_tiles(specs):
            gens = [nsa_tile(*sp) for sp in specs]
            n = len(gens)
            if n == 0:
                return
            next(gens[0])
            next(gens[0])
            for i in range(n):
                if i + 1 < n:
                    next(gens[i + 1])
                next(gens[i])
                if i + 1 < n:
                    next(gens[i + 1])
                next(gens[i])

        if stage >= 2:
            for g in range(4):
                P.dma("sp", KsT4[64:72, g, 0:8192], t_kaug_sel[:, :], w=["KsT4"])
                P.dma("sp", KwT4[64:72, g, 0:2560], t_kaug_win[:, :], w=["KwT4"])
                P.dma("sp", KcT4[64:72, g, :], t_kaug_cmp[:, :], w=["KcT4"])
            for T4 in range(12):
                i = cnt2["rb"] % 2
                cnt2["rb"] += 1
                rbk = "RB%d" % i
                P.dma("pool", RB[i][:, :, :], o_sel_p[T4 * 512:(T4 + 1) * 512, :].rearrange("(t p) c -> p t c", p=128), w=[rbk])
                for t_ in range(4):
                    T = T4 * 4 + t_
                    prep4(RB[i][:, t_, :], rbk, 128, KsT4[0:64, :, T * 128:(T + 1) * 128], "KsT4", Vs4[:, T, :, 0:64], "Vs4")
            for j2 in range(16):
                rbf, rk = load_rows_gather(o_sel_p[:, :], idxo2[:, j2:j2 + 1], "idxo2")
                prep4(rbf, rk, 128, KsT4[0:64, :, (48 + j2) * 128:(49 + j2) * 128], "KsT4", Vs4[:, 48 + j2, :, 0:64], "Vs4")
            for w in range(20):
                rbf, rk = load_rows_gather(winscr[:, :], idxw[:, w:w + 1], "idxw")
                prep4(rbf, rk, 128, KwT4[0:64, :, w * 128:(w + 1) * 128], "KwT4", Vw4[:, w, :, 0:64], "Vw4")
            rbf, rk = load_rows_gather(cmpscr_p[:, :], idxs[:, 0:1], "idxs")
            prep4(rbf, rk, 128, KcT4[0:64, :, :], "KcT4", Vc4[:, :, :], "Vc4")
            run_tiles([(g, tok0, nq, jl, None, 16) for g in range(4) for (tok0, nq, jl, sb_) in qtiles if jl is not None])
        if do_sample:
            for g in range(4):
                P.dma("sp", KsT4[64:72, g, 0:8192], t_kaug_sel_s[:, :], w=["KsT4"])
                P.dma("sp", KsT4[64:72, g, 8192:8320], t_kaug_new_s[:, :], w=["KsT4"])
                P.dma("sp", KwT4[64:72, g, 0:512], t_kaug_win_s[:, :], w=["KwT4"])
                P.dma("sp", KwT4[64:72, g, 512:640], t_kaug_new_s[:, :], w=["KwT4"])
                P.dma("sp", KcT4[64:72, g, :], t_kaug_cmp_s[:, :], w=["KcT4"])
            for (tok0, nq, jl, sb_) in qtiles:
                if jl is not None:
                    continue
                b = sb_
                for pg in range(64):
                    rbf, rk = load_rows_gather(csel[:, :], idxpg[:, b * 64 + pg:b * 64 + pg + 1], "idxpg")
                    prep4(rbf, rk, 128, KsT4[0:64, :, pg * 128:(pg + 1) * 128], "KsT4", Vs4[:, pg, :, 0:64], "Vs4")
                rbf, rk = load_rows_plain(o_sel_s[b * 8:(b + 1) * 8, :], 8)
                prep4(rbf, rk, 8, KsT4[0:64, :, 8192:8200], "KsT4", Vs4[0:8, 64, :, 0:64], "Vs4")
                for w in range(4):
                    rbf, rk = load_rows_plain(swin[b * 512 + w * 128:b * 512 + (w + 1) * 128, :], 128)
                    prep4(rbf, rk, 128, KwT4[0:64, :, w * 128:(w + 1) * 128], "KwT4", Vw4[:, w, :, 0:64], "Vw4")
                rbf, rk = load_rows_plain(o_win_s[b * 512 + 504:b * 512 + 512, :], 8)
                prep4(rbf, rk, 8, KwT4[0:64, :, 512:520], "KwT4", Vw4[0:8, 4, :, 0:64], "Vw4")
                rbf, rk = load_rows_plain(cmpscr_s[b * 128:(b + 1) * 128, :], 128)
                prep4(rbf, rk, 128, KcT4[0:64, :, :], "KcT4", Vc4[:, :, :], "Vc4")
                run_tiles([(g, tok0, nq, None, b, 15) for g in range(4)])

    with P.scope():
        W_outb = P.sb("W_outb", [128, NCH, D], BF16)
        for k in range(NCH):
            P.dma("pool", W_outb[:, k, :], w_out_b[k * 128:(k + 1) * 128, :], w=["W_outb"])
        lnG1 = P.sb("lnG1", [128, 1, D], F32)
        lnB1 = P.sb("lnB1", [128, 1, D], F32)
        P.dma("sp", lnG1[:, 0, :], ln_g[1:2, :].partition_broadcast(128), w=["lnG1"])
        P.dma("sp", lnB1[:, 0, :], ln_b[1:2, :].partition_broadcast(128), w=["lnB1"])
        Gp1 = P.sb("Gp1", [128, D], F32)
        Gs1 = P.sb("Gs1", [8, DEC_B, D], F32)
        P.dma("sp", Gp1[:, :], modscr[1, 0:1, 2 * D:3 * D].partition_broadcast(128), w=["Gp1"])
        for b in range(DEC_B):
            P.dma("sp", Gs1[0:8, b, :], modscr[1, 1 + b:2 + b, 2 * D:3 * D].partition_broadcast(8), w=["Gs1"])
        P.op("pool", lambda g_: g_.tensor_scalar_add(out=Gp1[:], in0=Gp1[:], scalar1=1.0), r=["Gp1"], w=["Gp1"])
        P.op("pool", lambda g_: g_.tensor_scalar_add(out=Gs1[:], in0=Gs1[:], scalar1=1.0), r=["Gs1"], w=["Gs1"])
        idxo3 = P.sb("idxo3", [128, 16], I32)
        P.dma("sp", idxo3[:], t_idx_own[:, :], w=["idxo3"])
        X1o = [P.sb("X1o%d" % i, [128, D], F32) for i in range(2)]
        OGl = [P.sb("OGl%d" % i, [128, D], BF16) for i in range(2)]
        OGT = P.sb("OGT", [128, NCH, 128], BF16)
        vo = [P.sb("vo%d" % i, [128, D], F32) for i in range(2)]
        yo = [P.sb("yo%d" % i, [128, D], F32) for i in range(2)]
        sto = [P.sb("sto%d" % i, [128, 16], F32) for i in range(2)]
        qi = 0
        for (tok0, nq, jl, sb_) in qtiles:
            i2 = qi % 2
            qi += 1
            xk, ok_, vk, yk, sk = "X1o%d" % i2, "OGl%d" % i2, "vo%d" % i2, "yo%d" % i2, "sto%d" % i2
            if jl is not None:
                P.gather(X1o[i2][:, :], x1scr[:, :], idxo3[:, jl:jl + 1], r=["idxo3"], w=[xk])
                Gt, gkey = Gp1[0:nq, :], "Gp1"
            else:
                P.dma("sp", X1o[i2][0:nq, :], x1s_scr[sb_ * 8:(sb_ + 1) * 8, :], w=[xk])
                Gt, gkey = Gs1[0:8, sb_, :], "Gs1"
            P.dma("sp", OGl[i2][0:nq, :], OGscr[tok0:tok0 + nq, :], w=[ok_])
            for k in range(NCH):
                P.op("pe", lambda t, k=k, nq=nq, i2=i2: t.transpose(psb[0][:, :].bitcast(BF16)[:, k * 128:k * 128 + nq],
                                                                   OGl[i2][0:nq, k * 128:(k + 1) * 128], ident_b[0:nq, 0:nq]),
                     r=[ok_, "ident_b"], w=[pskey(0)])
            P.op("dve", lambda v, nq=nq: v.tensor_copy(
                out=OGT[:, :, 0:nq], in_=psb[0][:, :].bitcast(BF16)[:, 0:1024].rearrange("p (k t) -> p k t", t=128)[:, :, 0:nq]),
                r=[pskey(0)], w=["OGT"])
            for half in range(2):
                pb = 2 + half
                for k in range(NCH):
                    P.op("pe", lambda t, k=k, half=half, pb=pb, nq=nq: t.matmul(
                        psb[pb][0:nq, :], lhsT=OGT[:, k, 0:nq], rhs=W_outb[:, k, half * 512:(half + 1) * 512],
                        start=(k == 0), stop=(k == NCH - 1)), r=["OGT", "W_outb"], w=[pskey(pb)])
                if jl is None and sb_ > 0:
                    pass
                P.op("dve", lambda v, half=half, pb=pb, nq=nq, i2=i2, Gt=Gt: v.tensor_tensor(
                    out=vo[i2][0:nq, half * 512:(half + 1) * 512], in0=psb[pb][0:nq, :], in1=Gt[:, half * 512:(half + 1) * 512],
                    op=ALU.mult), r=[pskey(pb), gkey], w=[vk])
            P.op("dve", lambda v, nq=nq, i2=i2: v.scalar_tensor_tensor(
                out=vo[i2][0:nq, :], in0=X1o[i2][0:nq, :], scalar=ALPHA, in1=vo[i2][0:nq, :], op0=ALU.mult, op1=ALU.add),
                r=[xk, vk], w=[vk])
            layernorm_tm(vo[i2], vk, yo[i2], yk, nq, 0, sto[i2], sk, lnG1, "lnG1", lnB1, "lnB1")
            if jl is not None:
                P.dma("sp", y_p[tok0:tok0 + nq, :], yo[i2][0:nq, :], r=[yk], w=[("y_p", tok0)], semkey=yk)
            else:
                P.dma("sp", y_s[sb_ * 8:(sb_ + 1) * 8, :], yo[i2][0:nq, :], r=[yk], w=[("y_s", sb_)], semkey=yk)

    for b in range(DEC_B):
        P.dma("act", o_win_s[b * 512:b * 512 + 504, :], swin[b * 512 + 8:(b + 1) * 512, :], w=[("o_win_s", b, 0)],
              semkey="winscopy")

    P.finish()
    print("instructions:", P.n_ins, {e: P.cnt[e] for e in P.ENG}, "dma sems:", len(P.dsem))
    return P


def _bf(x):
    return np.asarray(x, np.float32).astype(ml_dtypes.bfloat16)


def _split_pos(pos):
    pos = np.asarray(pos, np.int64)
    a = np.floor_divide(pos, 64)
    b = pos - 64 * a
    return a.astype(np.float32), b.astype(np.float32)


def _kaug(pos, valid):
    a, b = _split_pos(pos)
    n = a.shape[0]
    out = np.zeros((8, n), np.float32)
    out[0] = a; out[1] = a; out[2] = b; out[3] = b; out[4] = 1.0
    out[5] = np.where(valid, 0.0, NEGM)
    return _bf(out)


def _slopes_hi_lo():
    s = (2.0 ** (-8.0 * np.arange(1, 17) / 16.0)).astype(np.float32)
    hi = s.astype(ml_dtypes.bfloat16).astype(np.float32)
    lo = (s - hi).astype(ml_dtypes.bfloat16).astype(np.float32)
    return s, hi, lo


def _qaug(tq):
    s, hi, lo = _slopes_hi_lo()
    tq = np.asarray(tq, np.float32)
    nq = tq.shape[0]
    out = np.zeros((8, 16, nq), np.float32)
    out[0] = (64.0 * hi)[:, None]; out[1] = (64.0 * lo)[:, None]
    out[2] = hi[:, None]; out[3] = lo[:, None]
    out[4] = -(s[:, None] * tq[None, :])
    out[5] = 1.0
    return _bf(out)


def prompt_tables(k):
    cs = 2048 * k
    p = np.arange(128)
    t = {}
    t["idx_own"] = (cs + 128 * np.arange(16)[None, :] + p[:, None]).astype(np.int32)
    pos_pref = (np.arange(48 * 128) - cs)
    valid_pref = np.repeat(128 * np.arange(48) < cs, 128)
    pos_own = np.arange(2048)
    t["kaug_sel"] = np.concatenate([_kaug(pos_pref, valid_pref), _kaug(pos_own, np.ones(2048, bool))], axis=1)
    wtok = cs - 512 + np.arange(20 * 128)
    t["idx_win"] = np.maximum(wtok, 0).reshape(20, 128).T.astype(np.int32).copy()
    t["kaug_win"] = _kaug(wtok - cs, wtok >= 0)
    blk = np.concatenate([np.arange(96), 32 * k + np.arange(32)])
    valid = np.concatenate([np.arange(96) < 32 * k, np.ones(32, bool)])
    t["idx_slot"] = blk.astype(np.int32).reshape(128, 1)
    cend = 64 * blk + 63 - cs
    t["kaug_cmp"] = _kaug(cend, valid)
    tq = np.arange(2048)
    tabs = np.arange(2048) + cs
    cb = tabs // 64
    blk_abs = np.where(valid, blk, 10 ** 6)
    forced = (blk_abs[None, :] == 0) | (blk_abs[None, :] == cb[:, None]) | (blk_abs[None, :] == cb[:, None] - 1)
    caus = blk_abs[None, :] <= cb[:, None]
    fbn = np.where(forced, FORCEDV, 0.0) - np.where(caus, 0.0, 1.0)
    t["fbn"] = fbn.astype(np.float32).reshape(16, 128, 128)
    t["caus"] = caus.astype(np.float32).reshape(16, 128, 128)
    cend_abs = 64 * blk + 63
    tm = np.where(cend_abs[None, :] <= tabs[:, None], 0.0, NEGM)
    t["tm"] = tm.astype(np.float32).reshape(16, 128, 128)
    return t


def static_tables():
    t = {}
    t["qaug_p"] = _qaug(np.arange(2048)).reshape(8, 16 * 2048)
    j = np.arange(128)[:, None]
    i = np.arange(128)[None, :]
    tri = np.where(j > i, NEGM, 0.0)
    tri2 = np.where(j < i, NEGM, 0.0)
    t["tri_p"] = _bf(np.tile(tri, (1, 4)))
    t["tri2_p"] = _bf(np.tile(tri2, (1, 4)))
    i8 = np.arange(8)[None, :]
    t["tri_s"] = _bf(np.tile(np.where(j > i8, NEGM, 0.0), (1, 4)))
    t["tri2_s"] = _bf(np.tile(np.where(j < i8, NEGM, 0.0), (1, 4)))
    t["qaug_s"] = _qaug(np.arange(8)).reshape(8, 16 * 8)
    t["kaug_sel_s"] = _kaug(np.arange(8192) - 8192, np.ones(8192, bool))
    t["kaug_win_s"] = _kaug(np.arange(512) - 512, np.ones(512, bool))
    t["kaug_new_s"] = _kaug(np.arange(128), np.arange(128) < 8)
    cend = 64 * np.arange(128) + 63 - 8192
    t["kaug_cmp_s"] = _kaug(cend, np.ones(128, bool))
    fb = np.zeros((8, 128), np.float32)
    fb[:, 0] = FORCEDV; fb[:, 127] = FORCEDV
    t["fbn_s"] = fb
    t["caus_s"] = np.ones((8, 128), np.float32)
    t["tm_s"] = np.zeros((8, 128), np.float32)
    return t


def core_inputs(inp, c):
    b = c // 4
    sb = slice(4 * c, 4 * c + 4)
    f = np.ascontiguousarray
    d = {
        "xf": f(inp["x_prompt"][b]),
        "xs": f(inp["x_sample"][sb].reshape(NS_TOK, D)),
        "cvec": f(np.concatenate([inp["c_prompt"][b:b + 1], inp["c_sample"][sb]], axis=0)),
        "sh0": f(inp["state_h"][0, sb]),
        "sc0": f(inp["state_conv"][0, sb].reshape(DEC_B * 3, D)),
        "swin": f(inp["state_win"][sb].reshape(DEC_B * 512, 512)),
        "ptab": f(inp["page_table"][sb]).astype(np.int32),
        "ccmp": inp["cache_cmp"].reshape(-1, 512),
        "csel": inp["cache_sel"].reshape(-1, 512),
        "w_ada": inp["w_ada"], "b_ada": inp["b_ada"], "ln_g": inp["ln_g"], "ln_b": inp["ln_b"],
        "w_in_a": inp["w_in_a"][0], "conv_w": inp["conv_w_a"][0], "conv_b": inp["conv_b_a"],
        "w_r": inp["w_r_a"][0], "b_r": inp["b_r_a"], "w_i": inp["w_i_a"][0], "b_i": inp["b_i_a"],
        "lam": inp["lam_a"], "w_out_a": inp["w_out_a"][0], "w_kv": inp["w_kv"],
        "phi_pe": inp["phi_pe"].reshape(64, 128), "w_phi1": inp["w_phi1"], "b_phi1": inp["b_phi1"],
        "w_phi2": inp["w_phi2"], "b_phi2": inp["b_phi2"], "w_in_b": inp["w_in_b"][0],
        "b_gate": inp["b_gate_b"], "w_out_b": inp["w_out_b"][0],
    }
    for k2, v in prompt_tables(c % 4).items():
        d["t_" + k2] = v
    for k2, v in static_tables().items():
        d["t_" + k2] = v
    return {k: np.asarray(v) for k, v in d.items()}


def assemble(results, cores):
    y_prompt = np.zeros((2, SEQ, D), np.float32)
    y_sample = np.zeros((32, DEC_S, D), np.float32)
    new_cmp_p = np.zeros((2, SEQ, 4, 2, 64), np.float32)
    new_sel_p = np.zeros((2, SEQ, 4, 2, 64), np.float32)
    new_win_p = np.zeros((2, 512, 4, 2, 64), np.float32)
    new_h_p = np.zeros((1, 2, D), np.float32)
    new_conv_p = np.zeros((1, 2, 3, D), np.float32)
    new_cmp_s = np.zeros((32, DEC_S, 4, 2, 64), np.float32)
    new_sel_s = np.zeros((32, DEC_S, 4, 2, 64), np.float32)
    new_win_s = np.zeros((32, 512, 4, 2, 64), np.float32)
    new_h_s = np.zeros((1, 32, D), np.float32)
    new_conv_s = np.zeros((1, 32, 3, D), np.float32)
    for r, c in zip(results, cores):
        b, k = c // 4, c % 4
        sb = slice(4 * c, 4 * c + 4)
        y_prompt[b, k * 2048:(k + 1) * 2048] = r["y_p"]
        y_sample[sb] = r["y_s"].reshape(DEC_B, DEC_S, D)
        if k == 0:
            new_cmp_p[b] = r["o_cmp_p"].reshape(SEQ, 4, 2, 64)
            new_sel_p[b] = r["o_sel_p"].reshape(SEQ, 4, 2, 64)
            new_win_p[b] = r["o_win_p"].reshape(512, 4, 2, 64)
            new_h_p[0, b] = r["o_h_p"][0]
            new_conv_p[0, b] = r["o_conv_p"]
        new_cmp_s[sb] = r["o_cmp_s"].reshape(DEC_B, DEC_S, 4, 2, 64)
        new_sel_s[sb] = r["o_sel_s"].reshape(DEC_B, DEC_S, 4, 2, 64)
        new_win_s[sb] = r["o_win_s"].reshape(DEC_B, 512, 4, 2, 64)
        new_h_s[0, sb] = r["o_h_s"]
        new_conv_s[0, sb] = r["o_conv_s"].reshape(DEC_B, 3, D)
    return (y_prompt, y_sample, new_cmp_p, new_sel_p, new_win_p, new_h_p, new_conv_p,
            new_cmp_s, new_sel_s, new_win_s, new_h_s, new_conv_s)


def kernel(**inputs):
    inp = {k: np.asarray(v) for k, v in inputs.items()}
    n_phys = inp["cache_cmp"].shape[0]
    P = build(n_phys)
    cores = list(range(8))
    in_maps = [core_inputs(inp, c) for c in cores]
    res = run_bass_kernel_spmd(P.nc, in_maps, core_ids=cores)
    return assemble(res.results, cores)
```

```python
import contextlib
import numpy as np
import ml_dtypes
import concourse.bass as bass
import concourse.mybir as mybir
from concourse.bass_utils import run_bass_kernel_spmd

F32 = mybir.dt.float32
BF16 = mybir.dt.bfloat16
I32 = mybir.dt.int32
AF = mybir.ActivationFunctionType
ALU = mybir.AluOpType
AX = mybir.AxisListType

D = 1024
NCH = 8
SEQ = 8192
TT = 256
NSUB = TT // 128
DEC_B = 4
DEC_S = 8
NS_TOK = DEC_B * DEC_S
ALPHA = 4.0 ** 0.25
LN_EPS = 1e-5
RG_C = 8.0
NEGM = -30000.0
FORCEDV = 1.0e6


class Prog:
    ENG = ("pe", "act", "dve", "pool", "sp")

    def __init__(self):
        self.nc = bass.Bass("TRN2", target_bir_lowering=False)
        self.es = contextlib.ExitStack()
        nc = self.nc
        self.eng = {"pe": nc.tensor, "act": nc.scalar, "dve": nc.vector, "pool": nc.gpsimd, "sp": nc.sync}
        self.sem = {e: self.es.enter_context(nc.semaphore("s_" + e)) for e in self.ENG}
        self.cnt = {e: 0 for e in self.ENG}
        self.seen = {e: {} for e in self.ENG}
        self.dsem = {}
        self.dcnt = {}
        self.bufs = {}
        self.n_ins = 0
        self.stack = [self.es]

    def sb(self, name, shape, dt):
        return self.stack[-1].enter_context(self.nc.sbuf_tensor(name, list(shape), dt))

    def barrier(self):
        deps = {}
        for e2 in self.ENG:
            if self.cnt[e2]:
                deps[("eng", e2)] = self.cnt[e2]
        for k in self.dcnt:
            deps[("dma", k)] = self.dcnt[k]
        for e in self.ENG:
            self._wait(e, dict(deps))

    @contextlib.contextmanager
    def scope(self):
        st = contextlib.ExitStack()
        self.stack.append(st)
        try:
            yield
        finally:
            self.barrier()
            self.stack.pop()
            st.close()

    def ps(self, name, shape, dt):
        return self.es.enter_context(self.nc.psum_tensor(name, list(shape), dt))

    def dram(self, name, shape, dt, kind="Internal"):
        return self.nc.dram_tensor(name, list(shape), dt, kind=kind).ap()

    def _state(self, k):
        st = self.bufs.get(k)
        if st is None:
            st = self.bufs[k] = {"w": {}, "r": {}}
        return st

    def _deps(self, r, w):
        deps = {}
        for k in r:
            for s, v in self._state(k)["w"].items():
                deps[s] = max(deps.get(s, 0), v)
        for k in w:
            st = self._state(k)
            for s, v in st["w"].items():
                deps[s] = max(deps.get(s, 0), v)
            for s, v in st["r"].items():
                deps[s] = max(deps.get(s, 0), v)
        return deps

    def _wait(self, e, deps):
        eng = self.eng[e]
        seen = self.seen[e]
        for s, v in deps.items():
            if s[0] == "dma":
                v = max(v, self.dcnt[s[1]])
                if seen.get(s, 0) >= v:
                    continue
                eng.wait_ge(self.dsem[s[1]], v)
            else:
                if s[1] == e and False:
                    continue
                if seen.get(s, 0) >= v:
                    continue
                eng.wait_ge(self.sem[s[1]], v)
            seen[s] = v

    def _commit(self, me_src, me_val, r, w):
        for k in w:
            st = self._state(k)
            st["w"] = {me_src: me_val}
            st["r"] = {}
        for k in r:
            if k in w:
                continue
            st = self._state(k)
            st["r"][me_src] = max(st["r"].get(me_src, 0), me_val)

    def op(self, e, fn, r=(), w=()):
        w = list(w) + [k for k in r if isinstance(k, str) and k[:2] == "ps" and k[2:].isdigit() and k not in w]
        self._wait(e, self._deps(r, w))
        ins = fn(self.eng[e])
        self.cnt[e] += 1
        ins.then_inc(self.sem[e], 1)
        self._commit(("eng", e), self.cnt[e], r, w)
        self.n_ins += 1
        return ins

    def dma(self, q, out, in_, r=(), w=(), semkey=None, **kw):
        self._wait(q, self._deps(r, w))
        if semkey is None:
            semkey = (tuple(w) + tuple(r))[0]
        if semkey not in self.dsem:
            self.dsem[semkey] = self.es.enter_context(self.nc.semaphore("d%d" % len(self.dsem)))
            self.dcnt[semkey] = 0
        ins = self.eng[q].dma_start(out=out, in_=in_, **kw)
        self.dcnt[semkey] += 16
        ins.then_inc(self.dsem[semkey], 16)
        self._commit(("dma", semkey), self.dcnt[semkey], r, w)
        self.n_ins += 1
        return ins

    def gather(self, out, in_, idx_ap, r=(), w=(), semkey=None):
        q = "pool"
        self._wait(q, self._deps(r, w))
        if semkey is None:
            semkey = tuple(w)[0]
        if semkey not in self.dsem:
            self.dsem[semkey] = self.es.enter_context(self.nc.semaphore("d%d" % len(self.dsem)))
            self.dcnt[semkey] = 0
        ins = self.nc.gpsimd.indirect_dma_start(
            out=out, out_offset=None, in_=in_, in_offset=bass.IndirectOffsetOnAxis(ap=idx_ap, axis=0))
        self.dcnt[semkey] += 16
        ins.then_inc(self.dsem[semkey], 16)
        self._commit(("dma", semkey), self.dcnt[semkey], r, w)
        self.n_ins += 1
        return ins

    def finish(self):
        for e in ("sp",):
            deps = {}
            for k, st in self.bufs.items():
                for s, v in list(st["w"].items()) + list(st["r"].items()):
                    deps[s] = max(deps.get(s, 0), v)
            for k in self.dcnt:
                deps[("dma", k)] = self.dcnt[k]
            for e2 in self.ENG:
                if self.cnt[e2]:
                    deps[("eng", e2)] = self.cnt[e2]
            self._wait(e, deps)


def build(n_phys, stage=9):
    P = Prog()
    nc = P.nc
    es = P.es
    ctx_nc = nc.allow_non_contiguous_dma(reason="small strided parameter / state loads")
    es.enter_context(ctx_nc)

    def din(name, shape, dt=F32):
        return nc.dram_tensor(name, list(shape), dt, kind="ExternalInput").ap()

    def dout(name, shape, dt=F32):
        return nc.dram_tensor(name, list(shape), dt, kind="ExternalOutput").ap()

    xf = din("xf", [SEQ, D])
    xs = din("xs", [NS_TOK, D])
    cvec = din("cvec", [5, D])
    sh0 = din("sh0", [DEC_B, D])
    sc0 = din("sc0", [DEC_B * 3, D])
    swin = din("swin", [DEC_B * 512, 512])
    ptab = din("ptab", [DEC_B, 64], I32)
    ccmp = din("ccmp", [n_phys * 128, 512])
    csel = din("csel", [n_phys * 128, 512])
    w_ada = din("w_ada", [2, D, 3 * D])
    b_ada = din("b_ada", [2, 3 * D])
    ln_g = din("ln_g", [2, D])
    ln_b = din("ln_b", [2, D])
    w_in_a = din("w_in_a", [D, 2 * D])
    conv_w = din("conv_w", [4, D])
    conv_b = din("conv_b", [1, D])
    w_r = din("w_r", [8, 128, 128])
    b_r = din("b_r", [1, D])
    w_i = din("w_i", [8, 128, 128])
    b_i = din("b_i", [1, D])
    lam = din("lam", [1, D])
    w_out_a = din("w_out_a", [D, D])
    w_kv = din("w_kv", [D, 1536])
    phi_pe = din("phi_pe", [64, 128])
    w_phi1 = din("w_phi1", [2, 64, 64, 128])
    b_phi1 = din("b_phi1", [2, 128])
    w_phi2 = din("w_phi2", [2, 128, 64])
    b_phi2 = din("b_phi2", [2, 64])
    w_in_b = din("w_in_b", [D, 2096])
    b_gate = din("b_gate", [1, 48])
    w_out_b = din("w_out_b", [D, D])

    def dtab(name, shape, dt):
        return nc.dram_tensor(name, list(shape), dt, kind="ExternalInput").ap()
    t_idx_own = dtab("t_idx_own", [128, 16], I32)
    t_idx_win = dtab("t_idx_win", [128, 20], I32)
    t_idx_slot = dtab("t_idx_slot", [128, 1], I32)
    t_kaug_sel = dtab("t_kaug_sel", [8, 8192], BF16)
    t_kaug_win = dtab("t_kaug_win", [8, 2560], BF16)
    t_kaug_cmp = dtab("t_kaug_cmp", [8, 128], BF16)
    t_fbn = dtab("t_fbn", [16, 128, 128], F32)
    t_caus = dtab("t_caus", [16, 128, 128], F32)
    t_tm = dtab("t_tm", [16, 128, 128], F32)
    t_qaug_p = dtab("t_qaug_p", [8, 16 * 2048], BF16)
    t_tri_p = dtab("t_tri_p", [128, 512], BF16)
    t_tri2_p = dtab("t_tri2_p", [128, 512], BF16)
    t_tri_s = dtab("t_tri_s", [128, 32], BF16)
    t_tri2_s = dtab("t_tri2_s", [128, 32], BF16)
    t_qaug_s = dtab("t_qaug_s", [8, 128], BF16)
    t_kaug_sel_s = dtab("t_kaug_sel_s", [8, 8192], BF16)
    t_kaug_win_s = dtab("t_kaug_win_s", [8, 512], BF16)
    t_kaug_new_s = dtab("t_kaug_new_s", [8, 128], BF16)
    t_kaug_cmp_s = dtab("t_kaug_cmp_s", [8, 128], BF16)
    t_fbn_s = dtab("t_fbn_s", [8, 128], F32)
    t_caus_s = dtab("t_caus_s", [8, 128], F32)
    t_tm_s = dtab("t_tm_s", [8, 128], F32)

    y_p = dout("y_p", [2048, D])
    y_s = dout("y_s", [NS_TOK, D])
    o_cmp_p = dout("o_cmp_p", [SEQ, 512])
    o_sel_p = dout("o_sel_p", [SEQ, 512])
    o_win_p = dout("o_win_p", [512, 512])
    o_h_p = dout("o_h_p", [1, D])
    o_conv_p = dout("o_conv_p", [3, D])
    o_cmp_s = dout("o_cmp_s", [NS_TOK, 512])
    o_sel_s = dout("o_sel_s", [NS_TOK, 512])
    o_win_s = dout("o_win_s", [DEC_B * 512, 512])
    o_h_s = dout("o_h_s", [DEC_B, D])
    o_conv_s = dout("o_conv_s", [DEC_B * 3, D])

    modscr = P.dram("modscr", [2, 5, 3 * D], F32)
    x1scr = P.dram("x1scr", [SEQ, D], F32)
    x1s_scr = P.dram("x1s_scr", [NS_TOK, D], F32)
    winscr = P.dram("winscr", [SEQ, 512], F32)

    ident_b = P.sb("ident_b", [128, 128], BF16)
    ident_f = P.sb("ident_f", [128, 128], F32)
    for t, k in ((ident_b, "ident_b"), (ident_f, "ident_f")):
        P.op("pool", lambda g, t=t: g.memset(t[:], 0.0), w=[k])
        P.op("pool", lambda g, t=t: g.affine_select(out=t[:], in_=t[:], pattern=[[-1, 128]],
                                                    compare_op=ALU.not_equal, fill=1.0, base=0,
                                                    channel_multiplier=1), r=[k], w=[k])

    psb = [P.ps("ps%d" % i, [128, 512], F32) for i in range(8)]

    def pskey(i):
        return "ps%d" % i

    mod_fm = P.sb("mod_fm", [128, 2, 24, 8], F32)
    scA = P.scope()
    scA.__enter__()
    W_in = P.sb("W_in", [128, NCH, 2 * D], BF16)
    W_out = P.sb("W_out", [128, NCH, D], BF16)
    W_kv = P.sb("W_kv", [128, NCH, 1536], BF16)
    W_r = P.sb("W_r", [128, 8, 128], BF16)
    W_i = P.sb("W_i", [128, 8, 128], BF16)
    for k in range(NCH):
        P.dma("pool", W_in[:, k, :], w_in_a[k * 128:(k + 1) * 128, :], w=["W_in"])
    for k in range(NCH):
        P.dma("pool", W_out[:, k, :], w_out_a[k * 128:(k + 1) * 128, :], w=["W_out"])
    for k in range(NCH):
        P.dma("pool", W_kv[:, k, :], w_kv[k * 128:(k + 1) * 128, :], w=["W_kv"])
    P.dma("pool", W_r[:], w_r.rearrange("n c d -> c n d"), w=["W_r"])
    P.dma("pool", W_i[:], w_i.rearrange("n c d -> c n d"), w=["W_i"])

    pf = P.sb("pf", [128, 10, NCH], F32)
    for k in range(4):
        P.dma("sp", pf[:, k, :], conv_w[k:k + 1, :].rearrange("o (c p) -> p (o c)", p=128), w=["pf"])
    for j, src in ((4, conv_b), (5, b_r), (6, b_i), (7, lam)):
        P.dma("sp", pf[:, j, :], src[0:1, :].rearrange("o (c p) -> p (o c)", p=128), w=["pf"])
    P.op("act", lambda a: a.activation(out=pf[:, 9, :], in_=pf[:, 7, :], func=AF.Exp, scale=-1.0), r=["pf"], w=["pf"])
    P.op("act", lambda a: a.activation(out=pf[:, 9, :], in_=pf[:, 9, :], func=AF.Ln, bias=1.0), r=["pf"], w=["pf"])
    P.op("dve", lambda v: v.tensor_scalar_mul(out=pf[:, 7, :], in0=pf[:, 9, :], scalar1=-RG_C), r=["pf"], w=["pf"])
    P.op("dve", lambda v: v.tensor_scalar_mul(out=pf[:, 8, :], in0=pf[:, 9, :], scalar1=-2.0 * RG_C), r=["pf"], w=["pf"])

    lnG = P.sb("lnG", [128, 1, D], F32)
    lnB = P.sb("lnB", [128, 1, D], F32)
    for l in range(1):
        P.dma("sp", lnG[:, l, :], ln_g[l:l + 1, :].partition_broadcast(128), w=["lnG"])
        P.dma("sp", lnB[:, l, :], ln_b[l:l + 1, :].partition_broadcast(128), w=["lnB"])

    vt = [P.sb("vt%d" % i, [128, D], F32) for i in range(2)]
    c5, c5s = vt[0], vt[1]
    csT = P.sb("csT", [128, NCH, 8], BF16)
    P.dma("sp", c5[0:5, :], cvec[:, :], w=["vt0"])
    P.op("act", lambda a: a.activation(out=c5s[0:5, :], in_=c5[0:5, :], func=AF.Silu), r=["vt0"], w=["vt1"])
    for k in range(NCH):
        P.op("pe", lambda t, k=k: t.transpose(psb[0][:, k * 8:k * 8 + 5], c5s[0:5, k * 128:(k + 1) * 128],
                                              ident_f[0:5, 0:5]), r=["vt1", "ident_f"], w=[pskey(0)])
    P.op("dve", lambda v: v.tensor_copy(out=csT[:, :, 0:5],
                                        in_=psb[0][:, 0:64].rearrange("p (k e) -> p k e", e=8)[:, :, 0:5]),
         r=[pskey(0)], w=["csT"])
    AW = 256
    NA = 3 * D // AW
    wada_buf = [P.sb("wada%d" % i, [128, NCH, AW], BF16) for i in range(2)]
    modc = [P.sb("modc%d" % i, [5, AW], F32) for i in range(2)]
    badac = [P.sb("badac%d" % i, [5, AW], F32) for i in range(2)]
    it = 0
    for l in range(2):
        for n6 in range(NA):
            wb = wada_buf[it % 2]
            wk = "wada%d" % (it % 2)
            mk = "modc%d" % (it % 2)
            bk_ = "badac%d" % (it % 2)
            mc = modc[it % 2]
            bc = badac[it % 2]
            P.dma("pool", wb[:], w_ada[l, :, n6 * AW:(n6 + 1) * AW].rearrange("(k p) n -> p k n", p=128), w=[wk])
            P.dma("sp", bc[:], b_ada[l:l + 1, n6 * AW:(n6 + 1) * AW].partition_broadcast(5), w=[bk_])
            pb = 2 + (it % 2)
            for k in range(NCH):
                P.op("pe", lambda t, k=k, wb=wb, pb=pb: t.matmul(psb[pb][0:5, 0:AW], lhsT=csT[:, k, 0:5], rhs=wb[:, k, :],
                                                                 start=(k == 0), stop=(k == NCH - 1)),
                     r=["csT", wk], w=[pskey(pb)])
            P.op("dve", lambda v, mc=mc, bc=bc, pb=pb: v.tensor_tensor(
                out=mc[:], in0=psb[pb][0:5, 0:AW], in1=bc[:], op=ALU.add),
                r=[pskey(pb), bk_], w=[mk])
            P.dma("sp", modscr[l, :, n6 * AW:(n6 + 1) * AW], mc[:], r=[mk], w=[("modscr", l, n6)], semkey=mk)
            nq = AW // 128
            for q in range(nq):
                P.op("pe", lambda t, q=q, mc=mc: t.transpose(psb[1][:, q * 8:q * 8 + 5], mc[0:5, q * 128:(q + 1) * 128],
                                                             ident_f[0:5, 0:5]), r=[mk, "ident_f"], w=[pskey(1)])
            P.op("dve", lambda v, l=l, n6=n6, nq=nq: v.tensor_copy(
                out=mod_fm[:, l, n6 * nq:(n6 + 1) * nq, 0:5],
                in_=psb[1][:, 0:8 * nq].rearrange("p (k e) -> p k e", e=8)[:, :, 0:5]),
                r=[pskey(1)], w=["mod_fm"])
            it += 1
    P.op("dve", lambda v: v.tensor_scalar_add(out=mod_fm[:, :, 8:24, :], in0=mod_fm[:, :, 8:24, :], scalar1=1.0),
         r=["mod_fm"], w=["mod_fm"])
    Gp = P.sb("Gp", [128, 1, D], F32)
    Gs = P.sb("Gs", [NS_TOK, 1, D], F32)
    mod_keys = [("modscr", l, n6) for l in range(2) for n6 in range(NA)]
    for l in range(1):
        P.dma("sp", Gp[:, l, :], modscr[l, 0:1, 2 * D:3 * D].partition_broadcast(128), r=mod_keys, w=["Gp"])
        for b in range(DEC_B):
            P.dma("sp", Gs[b * 8:(b + 1) * 8, l, :], modscr[l, 1 + b:2 + b, 2 * D:3 * D].partition_broadcast(8),
                  r=mod_keys, w=["Gs"])
    P.op("pool", lambda g: g.tensor_scalar_add(out=Gp[:], in0=Gp[:], scalar1=1.0), r=["Gp"], w=["Gp"])
    P.op("pool", lambda g: g.tensor_scalar_add(out=Gs[:], in0=Gs[:], scalar1=1.0), r=["Gs"], w=["Gs"])

    xtok = [P.sb("xtok%d" % i, [128, NSUB, D], F32) for i in range(2)]
    xbf = P.sb("xbf", [128, NSUB, D], BF16)
    mT = P.sb("mT", [128, NCH, TT], BF16)
    xbe = P.sb("xbe", [128, NCH, 3 + TT], F32)
    xbe_s = P.sb("xbe_s", [128, NCH, DEC_B, 3 + DEC_S], F32)
    hprev = P.sb("hprev", [128, NCH], F32)
    h0s = P.sb("h0s", [128, NCH, DEC_B], F32)
    hlast_s = P.sb("hlast_s", [128, NCH, DEC_B], F32)
    NT = 2
    xc = [P.sb("xc%d" % i, [128, TT], F32) for i in range(NT)]
    xcb = [P.sb("xcb%d" % i, [128, TT], BF16) for i in range(NT)]
    zs = [P.sb("zs%d" % i, [128, TT], F32) for i in range(NT)]
    ra = [P.sb("ra%d" % i, [128, TT], F32) for i in range(NT)]
    ri = [P.sb("ri%d" % i, [128, TT], F32) for i in range(NT)]
    ga = [P.sb("ga%d" % i, [128, TT], F32) for i in range(NT)]
    bb = [P.sb("bb%d" % i, [128, TT], F32) for i in range(NT)]
    hs = [P.sb("hs%d" % i, [128, TT], F32) for i in range(NT)]
    yg = P.sb("yg", [128, NCH, TT], BF16)
    x1t = [P.sb("x1t%d" % i, [128, D], F32) for i in range(2)]
    x1b = [P.sb("x1b%d" % i, [128, D], BF16) for i in range(2)]
    x1T = P.sb("x1T", [128, NCH, TT], BF16)
    kvst = [P.sb("kvst%d" % i, [128, 1536], F32) for i in range(2)]
    stat = [P.sb("stat%d" % i, [128, 16], F32) for i in range(2)]

    P.op("pool", lambda g: g.memset(xbe[:, :, 0:3], 0.0), w=["xbe"])
    P.op("pool", lambda g: g.memset(hprev[:], 0.0), w=["hprev"])
    for n in range(NCH):
        for b in range(DEC_B):
            P.dma("sp", xbe_s[:, n, b, 0:3], sc0[b * 3:(b + 1) * 3, n * 128:(n + 1) * 128].rearrange("k p -> p k"),
                  w=["xbe_s"])
        P.dma("sp", h0s[:, n, :], sh0.rearrange("b (c p) -> c p b", p=128)[n], w=["h0s"])

    cnt = {"tile": 0, "ch": 0, "sub": 0}

    def layernorm_tm(vin, vkey, out, okey, np_, layer, st, skey, Gt_=None, gk_="lnG", Bt_=None, bk_="lnB"):
        Gt_ = lnG if Gt_ is None else Gt_
        Bt_ = lnB if Bt_ is None else Bt_
        P.op("dve", lambda v: v.bn_stats(out=st[0:np_, 0:6], in_=vin[0:np_, 0:512]), r=[vkey], w=[skey])
        P.op("dve", lambda v: v.bn_stats(out=st[0:np_, 6:12], in_=vin[0:np_, 512:1024]), r=[vkey], w=[skey])
        P.op("dve", lambda v: v.bn_aggr(out=st[0:np_, 12:14], in_=st[0:np_, 0:12]),
             r=[skey], w=[skey])
        P.op("dve", lambda v: v.tensor_scalar_add(out=st[0:np_, 14:15], in0=st[0:np_, 13:14], scalar1=LN_EPS),
             r=[skey], w=[skey])
        P.op("act", lambda a: a.activation(out=st[0:np_, 14:15], in_=st[0:np_, 14:15], func=AF.Ln), r=[skey], w=[skey])
        P.op("act", lambda a: a.activation(out=st[0:np_, 14:15], in_=st[0:np_, 14:15], func=AF.Exp, scale=-0.5),
             r=[skey], w=[skey])
        P.op("dve", lambda v: v.scalar_tensor_tensor(out=st[0:np_, 15:16], in0=st[0:np_, 12:13], scalar=-1.0,
                                                     in1=st[0:np_, 14:15], op0=ALU.mult, op1=ALU.mult),
             r=[skey], w=[skey])
        P.op("act", lambda a: a.activation(out=out[0:np_, :], in_=vin[0:np_, :], func=AF.Identity,
                                           scale=st[0:np_, 14:15], bias=st[0:np_, 15:16]),
             r=[vkey, skey], w=[okey])
        P.op("pool", lambda g: g.tensor_tensor(out=out[0:np_, :], in0=out[0:np_, :], in1=Gt_[0:np_, layer, :], op=ALU.mult),
             r=[okey, gk_], w=[okey])
        P.op("pool", lambda g: g.tensor_tensor(out=out[0:np_, :], in0=out[0:np_, :], in1=Bt_[0:np_, layer, :], op=ALU.add),
             r=[okey, bk_], w=[okey])

    import os
    CHPIPE = int(os.environ.get("CHPIPE", "1"))

    class L0Tile:
        def __init__(self, ti, sample):
            self.ti, self.sample = ti, sample
            if sample:
                self.ncols, self.nsub, self.np_ = NS_TOK, 1, NS_TOK
                self.segs = [(b * DEC_S, DEC_S, 1 + b) for b in range(DEC_B)]
            else:
                self.ncols, self.nsub, self.np_ = TT, NSUB, 128
                self.segs = [(0, TT, 0)]
            self.t0 = ti * TT
            self.xt = xtok[cnt["tile"] % 2]
            self.xk = "xtok%d" % (cnt["tile"] % 2)
            cnt["tile"] += 1
            self.cis = {}

        def front(self):
            ti, sample, ncols, nsub, np_, segs, t0, xt, xk = (self.ti, self.sample, self.ncols, self.nsub, self.np_, self.segs,
                                                              self.t0, self.xt, self.xk)
            if sample:
                P.dma("sp", xt[0:np_, 0, :], xs[:, :], w=[xk])
            else:
                for s in range(nsub):
                    P.dma("sp", xt[:, s, :], xf[t0 + s * 128:t0 + (s + 1) * 128, :], w=[xk])
            for s in range(nsub):
                P.op("pool", lambda g, s=s: g.tensor_copy(out=xbf[0:np_, s, :], in_=xt[0:np_, s, :]), r=[xk], w=["xbf"])
            for half in range(2):
                pb = half
                for kk in range(4):
                    k = half * 4 + kk
                    for s in range(nsub):
                        P.op("pe", lambda t, k=k, kk=kk, s=s, pb=pb: t.transpose(
                            psb[pb][:, :].bitcast(BF16)[:, kk * TT + s * 128:kk * TT + s * 128 + np_],
                            xbf[0:np_, s, k * 128:(k + 1) * 128], ident_b[0:np_, 0:np_]),
                            r=["xbf", "ident_b"], w=[pskey(pb)])
                for kk in range(4):
                    k = half * 4 + kk
                    for (c0, cn, mj) in segs:
                        P.op("act", lambda a, k=k, kk=kk, pb=pb, c0=c0, cn=cn, mj=mj: a.activation(
                            out=mT[:, k, c0:c0 + cn], in_=psb[pb][:, :].bitcast(BF16)[:, kk * TT + c0:kk * TT + c0 + cn],
                            func=AF.Identity, scale=mod_fm[:, 0, 8 + k, mj:mj + 1], bias=mod_fm[:, 0, k, mj:mj + 1]),
                            r=[pskey(pb), "mod_fm"], w=["mT"])

        def chunk_ab(self, n):
            ti, sample, ncols = self.ti, self.sample, self.ncols
            ci = cnt["ch"] % NT
            cnt["ch"] += 1
            self.cis[n] = ci
            pb = 2 + (n % 2)
            pk = pskey(pb)
            for k in range(NCH):
                P.op("pe", lambda t, k=k, n=n, pb=pb: t.matmul(psb[pb][:, 0:ncols], lhsT=W_in[:, k, n * 128:(n + 1) * 128],
                                                               rhs=mT[:, k, 0:ncols], start=(k == 0), stop=(k == NCH - 1)),
                     r=["W_in", "mT"], w=[pk])
            for k in range(NCH):
                P.op("pe", lambda t, k=k, n=n, pb=pb: t.matmul(psb[pb][:, 256:256 + ncols],
                                                               lhsT=W_in[:, k, D + n * 128:D + (n + 1) * 128],
                                                               rhs=mT[:, k, 0:ncols], start=(k == 0), stop=(k == NCH - 1)),
                     r=["W_in", "mT"], w=[pk])
            if sample:
                xe = xbe_s[:, n, :, :]
                xek = "xbe_s"
                P.op("dve", lambda v, pb=pb, xe=xe: v.tensor_copy(
                    out=xe[:, :, 3:3 + DEC_S], in_=psb[pb][:, 0:ncols].rearrange("p (b t) -> p b t", t=DEC_S)),
                    r=[pk], w=[xek])
                sh = lambda k: xe[:, :, k:k + DEC_S]
                v3 = lambda ap: ap[:, 0:ncols].rearrange("p (b t) -> p b t", t=DEC_S)
            else:
                xe = xbe[:, n, :]
                xek = ("xbe", n)
                if ti > 0:
                    P.op("dve", lambda v, xe=xe: v.tensor_copy(out=xe[:, 0:3], in_=xe[:, TT:TT + 3]), r=[xek], w=[xek])
                P.op("dve", lambda v, pb=pb, xe=xe: v.tensor_copy(out=xe[:, 3:3 + TT], in_=psb[pb][:, 0:TT]), r=[pk], w=[xek])
                sh = lambda k: xe[:, k:k + TT]
                v3 = lambda ap: ap[:, 0:ncols]
            zk = "zs%d" % ci
            P.op("act", lambda a, pb=pb, ci=ci: a.activation(out=zs[ci][:, 0:ncols], in_=psb[pb][:, 256:256 + ncols],
                                                             func=AF.Sigmoid), r=[pk], w=[zk])
            P.op("dve", lambda v, pb=pb, ci=ci: v.tensor_tensor(out=zs[ci][:, 0:ncols], in0=psb[pb][:, 256:256 + ncols],
                                                                in1=zs[ci][:, 0:ncols], op=ALU.mult), r=[pk, zk], w=[zk])
            ck = "xc%d" % ci
            P.op("dve", lambda v, ci=ci, n=n: v.tensor_scalar(out=v3(xc[ci]), in0=sh(0), scalar1=pf[:, 0, n:n + 1],
                                                              scalar2=pf[:, 4, n:n + 1], op0=ALU.mult, op1=ALU.add),
                 r=[xek, "pf"], w=[ck])
            for k in range(1, 4):
                P.op("dve", lambda v, ci=ci, n=n, k=k: v.scalar_tensor_tensor(
                    out=v3(xc[ci]), in0=sh(k), scalar=pf[:, k, n:n + 1], in1=v3(xc[ci]), op0=ALU.mult, op1=ALU.add),
                    r=[xek, "pf", ck], w=[ck])
            cbk = "xcb%d" % ci
            P.op("pool", lambda g, ci=ci: g.tensor_copy(out=xcb[ci][:, 0:ncols], in_=xc[ci][:, 0:ncols]), r=[ck], w=[cbk])

        def chunk_cde(self, n):
            ti, sample, ncols = self.ti, self.sample, self.ncols
            ci = self.cis[n]
            zk, ck, cbk = "zs%d" % ci, "xc%d" % ci, "xcb%d" % ci
            pg = 4 + (n % 2)
            pgk = pskey(pg)
            P.op("pe", lambda t, n=n, ci=ci, pg=pg: t.matmul(psb[pg][:, 0:ncols], lhsT=W_r[:, n, :], rhs=xcb[ci][:, 0:ncols],
                                                             start=True, stop=True), r=["W_r", cbk], w=[pgk])
            P.op("pe", lambda t, n=n, ci=ci, pg=pg: t.matmul(psb[pg][:, 256:256 + ncols], lhsT=W_i[:, n, :],
                                                             rhs=xcb[ci][:, 0:ncols], start=True, stop=True),
                 r=["W_i", cbk], w=[pgk])
            rk, ik, gk, bk, hk = "ra%d" % ci, "ri%d" % ci, "ga%d" % ci, "bb%d" % ci, "hs%d" % ci
            P.op("act", lambda a, n=n, ci=ci, pg=pg: a.activation(out=ra[ci][:, 0:ncols], in_=psb[pg][:, 0:ncols],
                                                                  func=AF.Sigmoid, bias=pf[:, 5, n:n + 1]),
                 r=[pgk, "pf"], w=[rk])
            P.op("act", lambda a, n=n, ci=ci, pg=pg: a.activation(out=ri[ci][:, 0:ncols], in_=psb[pg][:, 256:256 + ncols],
                                                                  func=AF.Sigmoid, bias=pf[:, 6, n:n + 1]),
                 r=[pgk, "pf"], w=[ik])
            P.op("act", lambda a, n=n, ci=ci: a.activation(out=ga[ci][:, 0:ncols], in_=ra[ci][:, 0:ncols], func=AF.Exp,
                                                           scale=pf[:, 8, n:n + 1]), r=[rk, "pf"], w=[gk])
            P.op("act", lambda a, n=n, ci=ci: a.activation(out=ra[ci][:, 0:ncols], in_=ra[ci][:, 0:ncols], func=AF.Exp,
                                                           scale=pf[:, 7, n:n + 1]), r=[rk, "pf"], w=[rk])
            P.op("dve", lambda v, ci=ci: v.tensor_scalar(out=ga[ci][:, 0:ncols], in0=ga[ci][:, 0:ncols], scalar1=-1.0,
                                                         scalar2=1.0, op0=ALU.mult, op1=ALU.add), r=[gk], w=[gk])
            P.op("dve", lambda v, ci=ci: v.tensor_scalar_max(out=ga[ci][:, 0:ncols], in0=ga[ci][:, 0:ncols], scalar1=1e-30),
                 r=[gk], w=[gk])
            P.op("act", lambda a, ci=ci: a.activation(out=ga[ci][:, 0:ncols], in_=ga[ci][:, 0:ncols], func=AF.Ln),
                 r=[gk], w=[gk])
            P.op("act", lambda a, ci=ci: a.activation(out=ga[ci][:, 0:ncols], in_=ga[ci][:, 0:ncols], func=AF.Exp, scale=0.5),
                 r=[gk], w=[gk])
            P.op("pool", lambda g, ci=ci: g.tensor_tensor(out=bb[ci][:, 0:ncols], in0=ri[ci][:, 0:ncols],
                                                          in1=xc[ci][:, 0:ncols], op=ALU.mult), r=[ik, ck], w=[bk])
            P.op("dve", lambda v, ci=ci: v.tensor_tensor(out=bb[ci][:, 0:ncols], in0=bb[ci][:, 0:ncols],
                                                         in1=ga[ci][:, 0:ncols], op=ALU.mult), r=[bk, gk], w=[bk])
            if sample:
                for b in range(DEC_B):
                    P.op("dve", lambda v, ci=ci, n=n, b=b: v.tensor_tensor_scan(
                        out=hs[ci][:, b * DEC_S:(b + 1) * DEC_S], data0=ra[ci][:, b * DEC_S:(b + 1) * DEC_S],
                        data1=bb[ci][:, b * DEC_S:(b + 1) * DEC_S], initial=h0s[:, n, b:b + 1], op0=ALU.mult, op1=ALU.add),
                        r=[rk, bk, "h0s"], w=[hk])
                P.op("dve", lambda v, ci=ci, n=n: v.tensor_copy(
                    out=hlast_s[:, n, :], in_=hs[ci][:, 0:ncols].rearrange("p (b t) -> p b t", t=DEC_S)[:, :, DEC_S - 1]),
                    r=[hk], w=["hlast_s"])
            else:
                P.op("dve", lambda v, ci=ci, n=n: v.tensor_tensor_scan(
                    out=hs[ci][:, 0:TT], data0=ra[ci][:, 0:TT], data1=bb[ci][:, 0:TT], initial=hprev[:, n:n + 1],
                    op0=ALU.mult, op1=ALU.add), r=[rk, bk, ("hprev", n)], w=[hk])
                P.op("dve", lambda v, ci=ci, n=n: v.tensor_copy(out=hprev[:, n:n + 1], in_=hs[ci][:, TT - 1:TT]),
                     r=[hk], w=[("hprev", n)])
            P.op("pool", lambda g, ci=ci, n=n: g.tensor_tensor(out=yg[:, n, 0:ncols], in0=hs[ci][:, 0:ncols],
                                                               in1=zs[ci][:, 0:ncols], op=ALU.mult), r=[hk, zk], w=["yg"])

        def chunks(self, lo, hi):
            if CHPIPE == 0:
                for n in range(lo, hi):
                    self.chunk_ab(n)
                    self.chunk_cde(n)
                return
            for n in range(lo, hi):
                self.chunk_ab(n)
                if n - 1 >= 0:
                    self.chunk_cde(n - 1)
            if hi == NCH:
                self.chunk_cde(NCH - 1)

        def outproj_ln(self, subs=None):
            ti, sample, nsub, np_, t0, xt, xk = self.ti, self.sample, self.nsub, self.np_, self.t0, self.xt, self.xk
            if not hasattr(self, "sis"):
                self.sis = {}
            for s in (range(nsub) if subs is None else subs):
                si = cnt["sub"] % 2
                cnt["sub"] += 1
                self.sis[s] = si
                vk, x1k, x1bk, stk = "vt%d" % si, "x1t%d" % si, "x1b%d" % si, "stat%d" % si
                for h in range(2):
                    pb = 4 + h
                    for k in range(NCH):
                        P.op("pe", lambda t, k=k, h=h, s=s, pb=pb: t.matmul(
                            psb[pb][0:np_, :], lhsT=yg[:, k, s * 128:s * 128 + np_], rhs=W_out[:, k, h * 512:(h + 1) * 512],
                            start=(k == 0), stop=(k == NCH - 1)), r=["yg", "W_out"], w=[pskey(pb)])
                    G = Gs if sample else Gp
                    P.op("dve", lambda v, h=h, pb=pb, si=si, G=G: v.tensor_tensor(
                        out=vt[si][0:np_, h * 512:(h + 1) * 512], in0=psb[pb][0:np_, :], in1=G[0:np_, 0, h * 512:(h + 1) * 512],
                        op=ALU.mult), r=[pskey(pb), "Gs" if sample else "Gp"], w=[vk])
                P.op("dve", lambda v, si=si, s=s: v.scalar_tensor_tensor(
                    out=vt[si][0:np_, :], in0=xt[0:np_, s, :], scalar=ALPHA, in1=vt[si][0:np_, :], op0=ALU.mult, op1=ALU.add),
                    r=[xk, vk], w=[vk])
                layernorm_tm(vt[si], vk, x1t[si], x1k, np_, 0, stat[si], stk)
                if not sample:
                    P.dma("sp", x1scr[t0 + s * 128:t0 + (s + 1) * 128, :], x1t[si][:, :], r=[x1k], w=[("x1scr", ti, s)], semkey=x1k)
                else:
                    P.dma("sp", x1s_scr[:, :], x1t[si][0:np_, :], r=[x1k], w=["x1s_scr"], semkey=x1k)
                P.op("act", lambda a, si=si: a.activation(out=x1b[si][0:np_, :], in_=x1t[si][0:np_, :], func=AF.Identity),
                     r=[x1k], w=[x1bk])

        def tail(self, subs=None, fin=True):
            ti, sample, nsub, np_, t0 = self.ti, self.sample, self.nsub, self.np_, self.t0
            for s in (range(nsub) if subs is None else subs):
                si = self.sis[s]
                x1bk, kvk = "x1b%d" % si, "kvst%d" % si
                pb = 6 + (s % 2)
                for k in range(NCH):
                    P.op("pe", lambda t, k=k, si=si, pb=pb: t.transpose(
                        psb[pb][:, :].bitcast(BF16)[:, k * 128:k * 128 + np_], x1b[si][0:np_, k * 128:(k + 1) * 128],
                        ident_b[0:np_, 0:np_]), r=[x1bk, "ident_b"], w=[pskey(pb)])
                P.op("dve", lambda v, pb=pb, s=s: v.tensor_copy(
                    out=x1T[:, :, s * 128:s * 128 + np_],
                    in_=psb[pb][:, :].bitcast(BF16)[:, 0:1024].rearrange("p (k t) -> p k t", t=128)[:, :, 0:np_]),
                    r=[pskey(pb)], w=[("x1T", s)])
                for c3 in range(3):
                    pb = c3 % 2
                    for k in range(NCH):
                        P.op("pe", lambda t, k=k, c3=c3, s=s, pb=pb: t.matmul(
                            psb[pb][0:np_, :], lhsT=x1T[:, k, s * 128:s * 128 + np_], rhs=W_kv[:, k, c3 * 512:(c3 + 1) * 512],
                            start=(k == 0), stop=(k == NCH - 1)), r=[("x1T", s), "W_kv"], w=[pskey(pb)])
                    P.op("act", lambda a, c3=c3, pb=pb, si=si: a.activation(
                        out=kvst[si][0:np_, c3 * 512:(c3 + 1) * 512], in_=psb[pb][0:np_, :], func=AF.Identity),
                        r=[pskey(pb)], w=[kvk])
                if sample:
                    P.dma("sp", o_cmp_s[:, :], kvst[si][0:np_, 0:512], r=[kvk], w=["o_cmp_s"], semkey=kvk)
                    P.dma("sp", o_sel_s[:, :], kvst[si][0:np_, 512:1024], r=[kvk], w=["o_sel_s"], semkey=kvk)
                    for b in range(DEC_B):
                        P.dma("sp", o_win_s[b * 512 + 504:(b + 1) * 512, :], kvst[si][b * 8:(b + 1) * 8, 1024:1536],
                              r=[kvk], w=[("o_win_s", b, 1)], semkey=kvk)
                else:
                    r0 = t0 + s * 128
                    P.dma("sp", o_cmp_p[r0:r0 + 128, :], kvst[si][:, 0:512], r=[kvk], w=[("o_cmp_p", ti, s)], semkey=kvk)
                    P.dma("sp", o_sel_p[r0:r0 + 128, :], kvst[si][:, 512:1024], r=[kvk], w=[("o_sel_p", ti, s)], semkey=kvk)
                    P.dma("sp", winscr[r0:r0 + 128, :], kvst[si][:, 1024:1536], r=[kvk], w=[("winscr", ti, s)], semkey=kvk)
                    if r0 >= SEQ - 512:
                        P.dma("sp", o_win_p[r0 - (SEQ - 512):r0 - (SEQ - 512) + 128, :], kvst[si][:, 1024:1536],
                              r=[kvk], w=[("o_win_p", ti, s)], semkey=kvk)
            if sample and fin:
                for n in range(NCH):
                    P.dma("sp", o_h_s.rearrange("b (c p) -> c p b", p=128)[n], hlast_s[:, n, :], r=["hlast_s"], w=[("o_h_s", n)],
                          semkey="hlast_s")
                    for b in range(DEC_B):
                        P.dma("sp", o_conv_s[b * 3:(b + 1) * 3, n * 128:(n + 1) * 128].rearrange("k p -> p k"),
                              xbe_s[:, n, b, DEC_S:DEC_S + 3], r=["xbe_s"], w=[("o_conv_s", n, b)], semkey="xbe_s_o")

        def final_state(self):
            P.dma("sp", o_h_p[0:1, :].rearrange("o (c p) -> p (o c)", p=128), hprev[:, :],
                  r=[("hprev", n) for n in range(NCH)], w=["o_h_p"], semkey="hprev_o")
            for n in range(NCH):
                P.dma("sp", o_conv_p.rearrange("k (c p) -> c p k", p=128)[n], xbe[:, n, TT:TT + 3], r=[("xbe", n)],
                      w=[("o_conv_p", n)], semkey="xbe_o")

    import os
    L0PIPE = int(os.environ.get("L0PIPE", "2"))
    n_ptiles = SEQ // TT if stage >= 1 else 2
    if L0PIPE == -1:
        for i in range(n_ptiles + 1):
            tl = L0Tile(0, True) if i == 0 else L0Tile(i - 1, False)
            tl.front()
            tl.chunks(0, NCH)
            for s_ in range(tl.nsub):
                tl.outproj_ln([s_])
                tl.tail([s_], fin=(s_ == tl.nsub - 1))
        tl.final_state()
    elif L0PIPE == 0:
        seq = [L0Tile(0, True)] + [None] * n_ptiles
        for i in range(n_ptiles + 1):
            tl = seq[i] if i == 0 else L0Tile(i - 1, False)
            tl.front()
            tl.chunks(0, NCH)
            tl.outproj_ln()
            tl.tail()
        tl.final_state()
    elif L0PIPE == 1:
        cur = L0Tile(0, True)
        cur.front()
        for i in range(n_ptiles + 1):
            cur.chunks(0, NCH)
            nxt = L0Tile(i, False) if i < n_ptiles else None
            if nxt is not None:
                nxt.front()
            for s_ in range(cur.nsub):
                cur.outproj_ln([s_])
                cur.tail([s_], fin=(s_ == cur.nsub - 1))
            last = cur
            cur = nxt
        last.final_state()
    else:
        tiles = [L0Tile(0, True)]
        tiles[0].front()
        tiles[0].chunks(0, NCH)
        tiles[0].outproj_ln()
        nxt = L0Tile(0, False)
        nxt.front()
        prev = tiles[0]
        for ti in range(n_ptiles):
            cur = nxt
            cur.chunks(0, NCH // 2)
            prev.tail()
            cur.chunks(NCH // 2, NCH)
            if ti + 1 < n_ptiles:
                nxt = L0Tile(ti + 1, False)
                nxt.front()
            cur.outproj_ln()
            prev = cur
        prev.tail()
        prev.final_state()

    scA.__exit__(None, None, None)
    cmpscr_p = P.dram("cmpscr_p", [128, 512], F32)
    cmpscr_s = P.dram("cmpscr_s", [DEC_B * 128, 512], F32)
    do_sample = stage >= 3
    with P.scope():
        W1r = P.sb("W1r", [128, 2, 64, 128], BF16)
        CB2s = [P.sb("CB2_%d" % i, [128, 64, 512], BF16) for i in range(2)]
        cbi = {"i": 0}
        Hh = P.sb("Hh", [128, 2, 2, 256], BF16)
        W2 = P.sb("W2", [128, 2, 64], BF16)
        PEsb = P.sb("PEsb", [64, 128], BF16)
        bias1 = P.sb("bias1", [128, 4], F32)
        b2bc = P.sb("b2bc", [64, 2, 4, 128], F32)
        CS = P.sb("CS", [64, 2, 512], F32)
        IDXf = P.sb("IDXf", [128, DEC_B * 64], F32)
        IDXi = P.sb("IDXi", [128, DEC_B * 64], I32)
        iop = P.sb("iop", [128, 2], I32)
        iopf = P.sb("iopf", [128, 2], F32)
        for c in range(2):
            for half in range(2):
                P.dma("pool", W1r[half * 64:(half + 1) * 64, c, :, :], w_phi1[c], w=["W1r"])
            P.dma("pool", W2[:, c, :], w_phi2[c], w=["W2"])
            P.dma("sp", bias1[:, c:c + 1], b_phi1[c:c + 1, :].rearrange("o p -> p o"), w=["bias1"])
        P.dma("pool", PEsb[:], phi_pe[:, :], w=["PEsb"])
        for nl in range(2):
            for g in range(4):
                P.dma("sp", b2bc[:, nl, g, :], b_phi2.rearrange("c d -> (c d)").rearrange("(o n) -> o n", o=1).partition_broadcast(64),
                      w=["b2bc"])
        for c in range(2):
            for d in range(64):
                P.op("pe", lambda t, c=c, d=d: t.matmul(psb[0][:, c:c + 1], lhsT=W1r[0:64, c, d, :],
                                                        rhs=PEsb[0:64, c * 64 + d:c * 64 + d + 1],
                                                        start=(d == 0), stop=(d == 63)), r=["W1r", "PEsb"], w=[pskey(0)])
        P.op("dve", lambda v: v.tensor_tensor(out=bias1[:, 0:2], in0=psb[0][:, 0:2], in1=bias1[:, 0:2], op=ALU.add),
             r=[pskey(0), "bias1"], w=["bias1"])
        P.op("pool", lambda g_: g_.iota(out=iop[:, 0:1], pattern=[[0, 1]], base=0, channel_multiplier=1), w=["iop"])
        P.op("dve", lambda v: v.tensor_copy(out=iopf[:, 0:1], in_=iop[:, 0:1]), r=["iop"], w=["iopf"])
        P.dma("sp", IDXi[:], ptab.rearrange("b n -> (b n)").rearrange("(o n) -> o n", o=1).partition_broadcast(128), w=["IDXi"])
        P.op("dve", lambda v: v.tensor_copy(out=IDXf[:], in_=IDXi[:]), r=["IDXi"], w=["IDXf"])
        P.op("dve", lambda v: v.tensor_scalar(out=IDXf[:], in0=IDXf[:], scalar1=128.0, scalar2=iopf[:, 0:1],
                                              op0=ALU.mult, op1=ALU.add), r=["IDXf", "iopf"], w=["IDXf"])
        P.op("dve", lambda v: v.tensor_copy(out=IDXi[:], in_=IDXf[:]), r=["IDXf"], w=["IDXi"])
        idxscr = P.dram("idxscr", [128, DEC_B * 64], I32)
        P.dma("sp", idxscr[:, :], IDXi[:], r=["IDXi"], w=["idxscr"], semkey="IDXi_o")

        def compress(load_pages, out_rows, okey):
            CB2 = CB2s[cbi["i"] % 2]
            cbk = "CB2_%d" % (cbi["i"] % 2)
            cbi["i"] += 1
            CB2v = CB2[:].rearrange("p n (g c d) -> p n g c d", g=4, c=2)
            load_pages(CB2, cbk)
            for c in range(2):
                for nl in range(2):
                    pb = c * 2 + nl
                    for d in range(64):
                        P.op("pe", lambda t, c=c, nl=nl, d=d, pb=pb: t.matmul(
                            psb[pb][:, 0:256], lhsT=W1r[nl * 64:(nl + 1) * 64, c, d, :],
                            rhs=CB2v[nl * 64:(nl + 1) * 64, :, :, c, d], start=(d == 0), stop=(d == 63)),
                            r=["W1r", cbk], w=[pskey(pb)])
                    P.op("act", lambda a, c=c, nl=nl, pb=pb: a.activation(out=Hh[:, c, nl, :], in_=psb[pb][:, 0:256], func=AF.Silu,
                                                                          bias=bias1[:, c:c + 1]), r=[pskey(pb), "bias1"], w=["Hh"])
            Hv = Hh[:].rearrange("p c n (pg g) -> p c n pg g", g=4)
            for nl in range(2):
                pb = 4 + nl
                for g in range(4):
                    for c in range(2):
                        col = (g * 2 + c) * 64
                        P.op("pe", lambda t, nl=nl, g=g, c=c, pb=pb, col=col: t.matmul(
                            psb[pb][0:64, col:col + 64], lhsT=Hv[:, c, nl, :, g], rhs=W2[:, c, :], start=True, stop=True),
                            r=["Hh", "W2"], w=[pskey(pb)])
                P.op("dve", lambda v, nl=nl, pb=pb: v.tensor_tensor(
                    out=CS[:, nl, :], in0=psb[pb][0:64, :], in1=b2bc[:, nl, :, :].rearrange("p g f -> p (g f)"), op=ALU.add),
                    r=[pskey(pb), "b2bc"], w=["CS"])
            P.dma("sp", out_rows.rearrange("(pg n) f -> pg n f", n=2), CS[:], r=["CS"], w=[okey], semkey="CS")

        def load_prompt_pages(CB2, cbk):
            for p8 in range(8):
                P.dma("pool", CB2[:, p8 * 8:(p8 + 1) * 8, :],
                      o_cmp_p[p8 * 1024:(p8 + 1) * 1024, :].rearrange("(t p) c -> p t c", p=128),
                      r=[("o_cmp_p", pg // NSUB, pg % NSUB) for pg in range(p8 * 8, p8 * 8 + 8)], w=[cbk])

        if stage >= 2:
            compress(load_prompt_pages, cmpscr_p[:, :], "cmpscr_p")
        if do_sample:
            for b in range(DEC_B):
                def load_sample_pages(CB2, cbk, b=b):
                    for pg in range(64):
                        P.gather(CB2[:, pg, :], ccmp[:, :], IDXi[:, b * 64 + pg:b * 64 + pg + 1], r=["IDXi"], w=[cbk])
                compress(load_sample_pages, cmpscr_s[b * 128:(b + 1) * 128, :], ("cmpscr_s", b))

    NTOK = 2048 + NS_TOK
    QTscr = P.dram("QTscr", [4, 64, 4, NTOK], BF16)
    ZSscr = P.dram("ZSscr", [NTOK, D], BF16)
    GLscr = P.dram("GLscr", [NTOK, 48], F32)
    OGscr = P.dram("OGscr", [NTOK, D], BF16)
    qtiles = [(jl * 128, 128, jl, None) for jl in range(16)] if stage >= 2 else []
    if do_sample:
        qtiles += [(2048 + b * 8, 8, None, b) for b in range(DEC_B)]
    if stage == 2.5:
        qtiles = qtiles[:2]

    with P.scope():
        W_inb = P.sb("W_inb", [128, NCH, 2096], BF16)
        for k in range(NCH):
            P.dma("pool", W_inb[:, k, :], w_in_b[k * 128:(k + 1) * 128, :], w=["W_inb"])
        bgbc = P.sb("bgbc", [128, 48], F32)
        P.dma("sp", bgbc[:], b_gate[0:1, :].partition_broadcast(128), w=["bgbc"])
        idxo = P.sb("idxo", [128, 16], I32)
        P.dma("sp", idxo[:], t_idx_own[:, :], w=["idxo"])
        X1 = [P.sb("X1_%d" % i, [128, D], F32) for i in range(2)]
        X1b = P.sb("X1b", [128, D], BF16)
        m1T = P.sb("m1T", [128, NCH, 128], BF16)
        QTst = [P.sb("QTst%d" % i, [64, 4, 128], BF16) for i in range(2)]
        ZSt = [P.sb("ZSt%d" % i, [128, D], BF16) for i in range(2)]
        GLt = [P.sb("GLt%d" % i, [128, 48], F32) for i in range(2)]
        qi = 0
        for (tok0, nq, jl, sb_) in qtiles:
            i2 = qi % 2
            xk = "X1_%d" % i2
            if jl is not None:
                P.gather(X1[i2][:, :], x1scr[:, :], idxo[:, jl:jl + 1], r=["idxo"], w=[xk])
                mj = 0
            else:
                P.dma("sp", X1[i2][0:nq, :], x1s_scr[sb_ * 8:(sb_ + 1) * 8, :], w=[xk])
                mj = 1 + sb_
            P.op("pool", lambda g_, i2=i2, nq=nq: g_.tensor_copy(out=X1b[0:nq, :], in_=X1[i2][0:nq, :]), r=[xk], w=["X1b"])
            for k in range(NCH):
                P.op("pe", lambda t, k=k, nq=nq: t.transpose(psb[0][:, :].bitcast(BF16)[:, k * 128:k * 128 + nq],
                                                             X1b[0:nq, k * 128:(k + 1) * 128], ident_b[0:nq, 0:nq]),
                     r=["X1b", "ident_b"], w=[pskey(0)])
            for k in range(NCH):
                P.op("act", lambda a, k=k, nq=nq, mj=mj: a.activation(
                    out=m1T[:, k, 0:nq], in_=psb[0][:, :].bitcast(BF16)[:, k * 128:k * 128 + nq], func=AF.Identity,
                    scale=mod_fm[:, 1, 8 + k, mj:mj + 1], bias=mod_fm[:, 1, k, mj:mj + 1]), r=[pskey(0), "mod_fm"], w=["m1T"])
            for g in range(4):
                pb = 2 + (g % 2)
                qk = "QTst%d" % (g % 2)
                for hh in range(4):
                    for k in range(NCH):
                        P.op("pe", lambda t, k=k, hh=hh, g=g, pb=pb, nq=nq: t.matmul(
                            psb[pb][0:64, hh * 128:hh * 128 + nq], lhsT=W_inb[:, k, (4 * g + hh) * 64:(4 * g + hh + 1) * 64],
                            rhs=m1T[:, k, 0:nq], start=(k == 0), stop=(k == NCH - 1)), r=["W_inb", "m1T"], w=[pskey(pb)])
                P.op("dve", lambda v, g=g, pb=pb, nq=nq: v.tensor_scalar_mul(
                    out=QTst[g % 2][:, :, 0:nq], in0=psb[pb][0:64, :].rearrange("p (h q) -> p h q", h=4)[:, :, 0:nq],
                    scalar1=0.125), r=[pskey(pb)], w=[qk])
                P.dma("sp", QTscr[g, :, :, tok0:tok0 + nq], QTst[g % 2][:, :, 0:nq], r=[qk], w=[("QTscr", g, tok0)], semkey=qk)
            zk = "ZSt%d" % i2
            for half in range(2):
                pb = 4 + half
                for k in range(NCH):
                    P.op("pe", lambda t, k=k, half=half, pb=pb, nq=nq: t.matmul(
                        psb[pb][0:nq, :], lhsT=m1T[:, k, 0:nq], rhs=W_inb[:, k, D + half * 512:D + (half + 1) * 512],
                        start=(k == 0), stop=(k == NCH - 1)), r=["W_inb", "m1T"], w=[pskey(pb)])
                P.op("act", lambda a, half=half, pb=pb, nq=nq, i2=i2: a.activation(
                    out=ZSt[i2][0:nq, half * 512:(half + 1) * 512], in_=psb[pb][0:nq, :], func=AF.Silu), r=[pskey(pb)], w=[zk])
            P.dma("sp", ZSscr[tok0:tok0 + nq, :], ZSt[i2][0:nq, :], r=[zk], w=[("ZSscr", tok0)], semkey=zk)
            gk = "GLt%d" % i2
            for k in range(NCH):
                P.op("pe", lambda t, k=k, nq=nq: t.matmul(psb[6][0:nq, 0:48], lhsT=m1T[:, k, 0:nq], rhs=W_inb[:, k, 2048:2096],
                                                          start=(k == 0), stop=(k == NCH - 1)), r=["W_inb", "m1T"], w=[pskey(6)])
            P.op("dve", lambda v, nq=nq, i2=i2: v.tensor_tensor(out=GLt[i2][0:nq, :], in0=psb[6][0:nq, 0:48], in1=bgbc[0:nq, :],
                                                                op=ALU.add), r=[pskey(6), "bgbc"], w=[gk])
            P.op("act", lambda a, nq=nq, i2=i2: a.activation(out=GLt[i2][0:nq, :], in_=GLt[i2][0:nq, :], func=AF.Sigmoid),
                 r=[gk], w=[gk])
            P.dma("sp", GLscr[tok0:tok0 + nq, :], GLt[i2][0:nq, :], r=[gk], w=[("GLscr", tok0)], semkey=gk)
            qi += 1

    with P.scope():
        EE = P.sb("EE", [128, 64, 128], BF16)
        ones_b = P.sb("ones_b", [128, 1024], BF16)
        P.op("pool", lambda g_: g_.memset(ones_b[:], 1.0), w=["ones_b"])
        for T8 in range(8):
            P.op("pool", lambda g_, T8=T8: g_.affine_select(
                out=EE[:, T8 * 8:(T8 + 1) * 8, :].rearrange("p t (a b) -> p t a b", a=2),
                in_=ones_b[:, :].rearrange("p (t a b) -> p t a b", t=8, a=2),
                pattern=[[-2, 8], [-1, 2], [0, 64]], compare_op=ALU.is_equal, fill=0.0, base=-16 * T8, channel_multiplier=1),
                r=["ones_b"], w=["EE"])
        TRIp = P.sb("TRIp", [128, 512], BF16)
        TRI2p = P.sb("TRI2p", [128, 512], BF16)
        TRIs = P.sb("TRIs", [128, 32], BF16)
        TRI2s = P.sb("TRI2s", [128, 32], BF16)
        P.dma("sp", TRIp[:], t_tri_p[:, :], w=["TRIp"])
        P.dma("sp", TRI2p[:], t_tri2_p[:, :], w=["TRI2p"])
        P.dma("sp", TRIs[:], t_tri_s[:, :], w=["TRIs"])
        P.dma("sp", TRI2s[:], t_tri2_s[:, :], w=["TRI2s"])
        RB = [P.sb("RB%d" % i, [128, 4, 512], BF16) for i in range(2)]
        RBF = [P.sb("RBF%d" % i, [128, 512], BF16) for i in range(4)]
        KsT4 = P.sb("KsT4", [72, 4, 65 * 128], BF16)
        Vs4 = P.sb("Vs4", [128, 65, 4, 65], BF16)
        KwT4 = P.sb("KwT4", [72, 4, 21 * 128], BF16)
        Vw4 = P.sb("Vw4", [128, 21, 4, 65], BF16)
        KcT4 = P.sb("KcT4", [72, 4, 128], BF16)
        Vc4 = P.sb("Vc4", [128, 4, 64], BF16)
        P.op("pool", lambda g_: g_.memset(Vs4[:, :, :, 64:65], 1.0), w=["Vs4"])
        P.op("pool", lambda g_: g_.memset(Vw4[:, :, :, 64:65], 1.0), w=["Vw4"])
        idxo2 = P.sb("idxo2", [128, 16], I32)
        idxw = P.sb("idxw", [128, 20], I32)
        idxs = P.sb("idxs", [128, 1], I32)
        idxpg = P.sb("idxpg", [128, DEC_B * 64], I32)
        P.dma("sp", idxo2[:], t_idx_own[:, :], w=["idxo2"])
        P.dma("sp", idxw[:], t_idx_win[:, :], w=["idxw"])
        P.dma("sp", idxs[:], t_idx_slot[:, :], w=["idxs"])
        P.dma("sp", idxpg[:], idxscr[:, :], w=["idxpg"])
        QT = [P.sb("QT%d" % i, [72, 4, 128], BF16) for i in range(2)]
        FBNt = [P.sb("FBNt%d" % i, [128, 128], F32) for i in range(2)]
        CAUt = [P.sb("CAUt%d" % i, [128, 128], F32) for i in range(2)]
        TMt = [P.sb("TMt%d" % i, [128, 128], F32) for i in range(2)]
        GLg = [P.sb("GLg%d" % i, [128, 48], F32) for i in range(2)]
        ZSg = [P.sb("ZSg%d" % i, [128, 256], BF16) for i in range(2)]
        Ssb = P.sb("Ssb", [128, 512], F32)
        Esb = P.sb("Esb", [128, 512], F32)
        Pn = P.sb("Pn", [128, 512], F32)
        Pnb = P.sb("Pnb", [128, 512], BF16)
        imp = P.sb("imp", [128, 128], F32)
        scr = P.sb("scr", [128, 128], F32)
        scr2 = P.sb("scr2", [128, 128], F32)
        m8 = P.sb("m8", [128, 16], F32)
        smh = P.sb("smh", [128, 16], F32)
        smb = P.sb("smb", [128, 16], F32)
        MselT = P.sb("MselT", [128, 128], BF16)
        Msel4s = [P.sb("Msel4_%d" % i, [128, 512], BF16) for i in range(2)]
        PTc = P.sb("PTc", [128, 512], BF16)
        PT = [P.sb("PT%d" % i, [128, 512], BF16) for i in range(4)]
        PTm = [P.sb("PTm%d" % i, [128, 512], BF16) for i in range(4)]
        OaugSB = P.sb("OaugSB", [65, 512], F32)
        accs = [P.sb("acc%d" % i, [128, 256], F32) for i in range(2)]
        OGt = [P.sb("OGt%d" % i, [128, 256], BF16) for i in range(2)]
        cnt2 = {"rb": 0, "rbf": 0, "pt": 0, "ps": 0, "q": 0, "mx": 0}

        def prep_from_rbf(rbf, rk, g, nk, ktdst, kkey, vdst, vkey):
            P.op("pe", lambda t: t.transpose(psb[7][:, :].bitcast(BF16)[0:64, 0:nk], rbf[0:nk, g * 128:g * 128 + 64],
                                             ident_b[0:nk, 0:nk]), r=[rk, "ident_b"], w=[pskey(7)])
            P.op("dve", lambda v: v.tensor_copy(out=ktdst, in_=psb[7][:, :].bitcast(BF16)[0:64, 0:nk]), r=[pskey(7)], w=[kkey])
            P.op("pool", lambda g_: g_.tensor_copy(out=vdst, in_=rbf[0:nk, g * 128 + 64:g * 128 + 128]), r=[rk], w=[vkey])

        def prep4(rbf, rk, nk, ktdst3, kkey, vdst3, vkey):
            for g4 in range(4):
                P.op("pe", lambda t, g4=g4: t.transpose(psb[7][:, :].bitcast(BF16)[0:64, g4 * 128:g4 * 128 + nk],
                                                        rbf[0:nk, g4 * 128:g4 * 128 + 64], ident_b[0:nk, 0:nk]),
                     r=[rk, "ident_b"], w=[pskey(7)])
            P.op("dve", lambda v: v.tensor_copy(
                out=ktdst3, in_=psb[7][:, :].bitcast(BF16)[0:64, 0:512].rearrange("p (g k) -> p g k", g=4)[:, :, 0:nk]),
                r=[pskey(7)], w=[kkey])
            P.op("pool", lambda g_: g_.tensor_copy(
                out=vdst3, in_=rbf[0:nk, :].rearrange("p (g c d) -> p g c d", g=4, c=2)[:, :, 1, :]), r=[rk], w=[vkey])

        def load_rows_gather(src, idx_ap, ikey):
            i = cnt2["rbf"] % 4
            cnt2["rbf"] += 1
            P.gather(RBF[i][:, :], src, idx_ap, r=[ikey], w=["RBF%d" % i])
            return RBF[i], "RBF%d" % i

        def load_rows_plain(src_rows, nk):
            i = cnt2["rbf"] % 4
            cnt2["rbf"] += 1
            P.dma("pool", RBF[i][0:nk, :], src_rows, w=["RBF%d" % i])
            return RBF[i], "RBF%d" % i

        def nsa_tile(g, tok0, nq, jl, sb_, kth):
            ncol = 4 * nq
            sample = jl is None
            i2 = cnt2["q"] % 2
            cnt2["q"] += 1
            qt, qk = QT[i2], "QT%d" % i2
            Msel4, mk4 = Msel4s[i2], "Msel4_%d" % i2
            acc, ak = accs[i2], "acc%d" % i2
            P.dma("sp", qt[0:64, :, 0:nq], QTscr[g, :, :, tok0:tok0 + nq], w=[qk])
            if sample:
                P.dma("sp", qt[64:72, :, 0:nq], t_qaug_s.rearrange("r (h q) -> r h q", h=16)[:, 4 * g:4 * g + 4, :], w=[qk])
                P.dma("sp", FBNt[i2][0:nq, :], t_fbn_s[:, :], w=["FBNt%d" % i2])
                P.dma("sp", CAUt[i2][0:nq, :], t_caus_s[:, :], w=["CAUt%d" % i2])
                P.dma("sp", TMt[i2][0:nq, :], t_tm_s[:, :], w=["TMt%d" % i2])
            else:
                P.dma("sp", qt[64:72, :, 0:nq], t_qaug_p.rearrange("r (h q) -> r h q", h=16)[:, 4 * g:4 * g + 4, tok0:tok0 + nq],
                      w=[qk])
                P.dma("sp", FBNt[i2][:, :], t_fbn[jl], w=["FBNt%d" % i2])
                P.dma("sp", CAUt[i2][:, :], t_caus[jl], w=["CAUt%d" % i2])
                P.dma("sp", TMt[i2][:, :], t_tm[jl], w=["TMt%d" % i2])
            fk, ck, tk, glk, zk = "FBNt%d" % i2, "CAUt%d" % i2, "TMt%d" % i2, "GLg%d" % i2, "ZSg%d" % i2
            P.dma("sp", GLg[i2][0:nq, :], GLscr[tok0:tok0 + nq, :], w=[glk])
            P.dma("sp", ZSg[i2][0:nq, :], ZSscr[tok0:tok0 + nq, g * 256:(g + 1) * 256], w=[zk])
            qrhs = qt[0:72, :, 0:nq]
            kc_ap, kck, vc_ap, vck = KcT4[0:72, g, :], "KcT4", Vc4[:, g, :], "Vc4"
            ksel = lambda T, nk: (KsT4[0:72, g, T * 128:T * 128 + nk], "KsT4", Vs4[0:nk, T, g, :], "Vs4")
            kwin = lambda T, nk: (KwT4[0:72, g, T * 128:T * 128 + nk], "KwT4", Vw4[0:nk, T, g, :], "Vw4")
            gl3 = GLg[i2][0:nq, :].rearrange("p (h b) -> p h b", b=3)
            for hh in range(4):
                P.op("pe", lambda t, hh=hh: t.matmul(psb[7][0:nq, hh * 128:(hh + 1) * 128], lhsT=qt[0:72, hh, 0:nq], rhs=kc_ap,
                                                     start=True, stop=True), r=[qk, kck], w=[pskey(7)])
            P.op("dve", lambda v: v.tensor_tensor(
                out=Ssb[0:nq, :].rearrange("p (h s) -> p h s", h=4), in0=psb[7][0:nq, :].rearrange("p (h s) -> p h s", h=4),
                in1=TMt[i2][0:nq, :].unsqueeze(1).to_broadcast([nq, 4, 128]), op=ALU.add), r=[pskey(7), tk], w=["Ssb"])
            for hh in range(4):
                P.op("act", lambda a, hh=hh: a.activation(out=Esb[0:nq, hh * 128:(hh + 1) * 128], in_=Ssb[0:nq, hh * 128:(hh + 1) * 128],
                                                          func=AF.Exp, accum_out=smh[0:nq, hh:hh + 1]), r=["Ssb"], w=["Esb", "smh"])
            P.op("dve", lambda v: v.tensor_scalar_add(out=smh[0:nq, 4:8], in0=smh[0:nq, 0:4], scalar1=1e-30), r=["smh"], w=["smh"])
            P.op("dve", lambda v: v.reciprocal(out=smh[0:nq, 4:8], in_=smh[0:nq, 4:8]), r=["smh"], w=["smh"])
            for hh in range(4):
                P.op("dve", lambda v, hh=hh: v.tensor_scalar_mul(out=Pn[0:nq, hh * 128:(hh + 1) * 128],
                                                                 in0=Esb[0:nq, hh * 128:(hh + 1) * 128],
                                                                 scalar1=smh[0:nq, 4 + hh:5 + hh]), r=["Esb", "smh"], w=["Pn"])
            P.op("pool", lambda g_: g_.tensor_copy(out=Pnb[0:nq, :], in_=Pn[0:nq, :]), r=["Pn"], w=["Pnb"])
            P.op("dve", lambda v: v.tensor_tensor(out=imp[0:nq, :], in0=Pn[0:nq, 0:128], in1=Pn[0:nq, 128:256], op=ALU.add),
                 r=["Pn"], w=["imp"])
            P.op("dve", lambda v: v.tensor_tensor(out=imp[0:nq, :], in0=imp[0:nq, :], in1=Pn[0:nq, 256:384], op=ALU.add),
                 r=["Pn", "imp"], w=["imp"])
            P.op("dve", lambda v: v.tensor_tensor(out=imp[0:nq, :], in0=imp[0:nq, :], in1=Pn[0:nq, 384:512], op=ALU.add),
                 r=["Pn", "imp"], w=["imp"])
            P.op("dve", lambda v: v.tensor_tensor(out=scr[0:nq, :], in0=imp[0:nq, :], in1=CAUt[i2][0:nq, :], op=ALU.mult),
                 r=["imp", ck], w=["scr"])
            P.op("dve", lambda v: v.tensor_tensor(out=scr[0:nq, :], in0=scr[0:nq, :], in1=FBNt[i2][0:nq, :], op=ALU.add),
                 r=["scr", fk], w=["scr"])
            P.op("dve", lambda v: v.max(out=m8[0:nq, 0:8], in_=scr[0:nq, :]), r=["scr"], w=["m8"])
            P.op("dve", lambda v: v.match_replace(out=scr2[0:nq, :], in_to_replace=m8[0:nq, 0:8], in_values=scr[0:nq, :],
                                                  imm_value=-1.0e30), r=["scr", "m8"], w=["scr2"])
            P.op("dve", lambda v: v.max(out=m8[0:nq, 8:16], in_=scr2[0:nq, :]), r=["scr2"], w=["m8"])
            thr = m8[0:nq, kth - 1:kth]
            P.op("dve", lambda v: v.tensor_scalar(out=scr2[0:nq, :], in0=scr[0:nq, :], scalar1=thr, scalar2=None, op0=ALU.is_ge),
                 r=["scr", "m8"], w=["scr2"])
            P.op("dve", lambda v: v.tensor_tensor(out=scr2[0:nq, :], in0=scr2[0:nq, :], in1=CAUt[i2][0:nq, :], op=ALU.mult),
                 r=["scr2", ck], w=["scr2"])
            P.op("dve", lambda v: v.tensor_copy(out=MselT[0:nq, :], in_=scr2[0:nq, :]), r=["scr2"], w=["MselT"])
            yield "a"
            P.op("pe", lambda t: t.transpose(psb[7][:, :].bitcast(BF16)[:, 0:nq], MselT[0:nq, :], ident_b[0:nq, 0:nq]),
                 r=["MselT", "ident_b"], w=[pskey(7)])
            P.op("dve", lambda v: v.tensor_copy(
                out=Msel4[:, 0:nq], in_=psb[7][:, :].bitcast(BF16)[:, 0:nq]), r=[pskey(7)], w=[mk4])
            for hh in range(4):
                P.op("pe", lambda t, hh=hh: t.transpose(psb[7][:, :].bitcast(BF16)[:, 512 + hh * nq:512 + (hh + 1) * nq],
                                                        Pnb[0:nq, hh * 128:(hh + 1) * 128], ident_b[0:nq, 0:nq]),
                     r=["Pnb", "ident_b"], w=[pskey(7)])
            P.op("dve", lambda v: v.tensor_copy(out=PTc[:, 0:ncol], in_=psb[7][:, :].bitcast(BF16)[:, 512:512 + ncol]),
                 r=[pskey(7)], w=["PTc"])
            for hh in range(4):
                P.op("pe", lambda t, hh=hh: t.matmul(psb[7][0:nq, hh * 64:(hh + 1) * 64], lhsT=PTc[:, hh * nq:(hh + 1) * nq], rhs=vc_ap,
                                                     start=True, stop=True), r=["PTc", vck], w=[pskey(7)])
            for hh in range(4):
                P.op("dve", lambda v, hh=hh: v.tensor_scalar_mul(out=acc[0:nq, hh * 64:(hh + 1) * 64],
                                                                 in0=psb[7][0:nq, hh * 64:(hh + 1) * 64],
                                                                 scalar1=gl3[:, 4 * g + hh, 0:1]), r=[pskey(7), glk], w=[ak])

            yield "b"
            def attend(tiles, br, ob):
                nt = len(tiles)
                slots = {}
                mslots = {}

                SB = (0, 1, 2, 6)

                def s_stage(i):
                    T = tiles[i]
                    sbk = SB[cnt2["ps"] % 4]
                    cnt2["ps"] += 1
                    nk = T["nk"]
                    mm = [(T["kt"], qrhs, T["kkey"], qk)] + T["masks"]
                    for j, (l, r_, lk, rk) in enumerate(mm):
                        P.op("pe", lambda t, l=l, r_=r_, j=j, nk=nk, sbk=sbk, n=len(mm): t.matmul(
                            psb[sbk][0:nk, 0:ncol], lhsT=l, rhs=r_, start=(j == 0), stop=(j == n - 1)),
                            r=[lk, rk], w=[pskey(sbk)])
                    slots[i] = sbk

                def m_stage(i):
                    T = tiles[i]
                    nk = T["nk"]
                    mxb = None
                    if T["msel"] is not None:
                        mxb = 4 + cnt2["mx"] % 2
                        cnt2["mx"] += 1
                        P.op("pe", lambda t, Tm=T["msel"], nk=nk, mxb=mxb: t.matmul(
                            psb[mxb][0:nk, 0:nq], lhsT=EE[:, Tm, 0:nk], rhs=Msel4[:, 0:nq], start=True, stop=True),
                            r=["EE", mk4], w=[pskey(mxb)])
                    mslots[i] = mxb

                def e_stage(i):
                    T = tiles[i]
                    nk = T["nk"]
                    sbk, mxb = slots[i], mslots[i]
                    pi = cnt2["pt"] % 4
                    cnt2["pt"] += 1
                    P.op("act", lambda a, nk=nk, sbk=sbk, pi=pi: a.activation(out=PT[pi][0:nk, 0:ncol], in_=psb[sbk][0:nk, 0:ncol],
                                                                              func=AF.Exp), r=[pskey(sbk)], w=["PT%d" % pi])
                    if mxb is None:
                        return PT[pi], "PT%d" % pi
                    P.op("dve", lambda v, nk=nk, pi=pi, mxb=mxb: v.tensor_tensor(
                        out=PTm[pi][0:nk, 0:ncol].rearrange("p (h q) -> p h q", h=4),
                        in0=PT[pi][0:nk, 0:ncol].rearrange("p (h q) -> p h q", h=4),
                        in1=psb[mxb][0:nk, 0:nq].unsqueeze(1).to_broadcast([nk, 4, nq]), op=ALU.mult),
                        r=["PT%d" % pi, pskey(mxb)], w=["PTm%d" % pi])
                    return PTm[pi], "PTm%d" % pi

                def v_stage(i, pi):
                    T = tiles[i]
                    nk = T["nk"]
                    pbuf, pkey_ = pi
                    P.op("pe", lambda t, T=T, nk=nk, pbuf=pbuf, i=i: t.matmul(psb[ob][0:65, 0:ncol], lhsT=T["v"], rhs=pbuf[0:nk, 0:ncol],
                                                                             start=(i == 0), stop=(i == nt - 1)),
                         r=[T["vkey"], pkey_], w=[pskey(ob)])

                LOOK_S, LOOK_M = 3, 2
                for i in range(min(LOOK_S, nt)):
                    s_stage(i)
                for i in range(min(LOOK_M, nt)):
                    m_stage(i)
                for i in range(nt):
                    pi = e_stage(i)
                    if i + LOOK_S < nt:
                        s_stage(i + LOOK_S)
                    if i + LOOK_M < nt:
                        m_stage(i + LOOK_M)
                    v_stage(i, pi)
                P.op("dve", lambda v: v.tensor_copy(out=OaugSB[0:65, 0:ncol], in_=psb[ob][0:65, 0:ncol]), r=[pskey(ob)], w=["OaugSB"])
                for hh in range(4):
                    P.op("pe", lambda t, hh=hh: t.transpose(psb[7][0:nq, hh * 65:(hh + 1) * 65], OaugSB[0:65, hh * nq:(hh + 1) * nq],
                                                            ident_f[0:65, 0:65]), r=["OaugSB", "ident_f"], w=[pskey(7)])
                o3 = psb[7][0:nq, 0:260].rearrange("p (h e) -> p h e", e=65)
                P.op("dve", lambda v: v.tensor_scalar_add(out=smb[0:nq, 8:12], in0=o3[:, :, 64], scalar1=1e-30), r=[pskey(7)], w=["smb"])
                P.op("dve", lambda v: v.reciprocal(out=smb[0:nq, 8:12], in_=smb[0:nq, 8:12]), r=["smb"], w=["smb"])
                P.op("dve", lambda v: v.tensor_tensor(out=smb[0:nq, 12:16], in0=smb[0:nq, 8:12], in1=gl3[:, 4 * g:4 * g + 4, br],
                                                      op=ALU.mult), r=["smb", glk], w=["smb"])
                for hh in range(4):
                    P.op("dve", lambda v, hh=hh: v.scalar_tensor_tensor(
                        out=acc[0:nq, hh * 64:(hh + 1) * 64], in0=o3[:, hh, 0:64], scalar=smb[0:nq, 12 + hh:13 + hh],
                        in1=acc[0:nq, hh * 64:(hh + 1) * 64], op0=ALU.mult, op1=ALU.add), r=[pskey(7), "smb", ak], w=[ak])

            def ktile(src, T, nk=128, masks=()):
                kt_, kkey_, v_, vkey_ = src(T, nk)
                add = [m for m in masks if m[0] != "MSEL"]
                ms_ = [m[1] for m in masks if m[0] == "MSEL"]
                return {"kt": kt_, "kkey": kkey_, "v": v_, "vkey": vkey_, "nk": nk, "masks": add,
                        "msel": (ms_[0] if ms_ else None)}

            msel = lambda T: ("MSEL", T)
            if sample:
                tri = (ident_b[0:8, 0:8], TRIs[0:8, 0:ncol], "ident_b", "TRIs")
                tri2 = (ident_b[:, :], TRI2s[:, 0:ncol], "ident_b", "TRI2s")
                sel_tiles = [ktile(ksel, T, masks=[msel(T)]) for T in range(64)]
                sel_tiles.append(ktile(ksel, 64, nk=8, masks=[tri]))
                win_tiles = [ktile(kwin, 0, masks=[tri2])] + [ktile(kwin, w) for w in range(1, 4)]
                win_tiles.append(ktile(kwin, 4, nk=8, masks=[tri]))
            else:
                tri = (ident_b[:, :], TRIp[:, 0:ncol], "ident_b", "TRIp")
                tri2 = (ident_b[:, :], TRI2p[:, 0:ncol], "ident_b", "TRI2p")
                sel_tiles = [ktile(ksel, T, masks=[msel(T)]) for T in range(48)]
                for j2 in range(jl + 1):
                    ms = [msel(48 + j2)] + ([tri] if j2 == jl else [])
                    sel_tiles.append(ktile(ksel, 48 + j2, masks=ms))
                win_tiles = []
                for w in range(jl, jl + 5):
                    ms = [tri2] if w == jl else ([tri] if w == jl + 4 else [])
                    win_tiles.append(ktile(kwin, w, masks=ms))
            attend(sel_tiles, 1, 3)
            yield "sel"
            attend(win_tiles, 2, 3)
            ogk = "OGt%d" % i2
            P.op("dve", lambda v: v.tensor_tensor(out=OGt[i2][0:nq, :], in0=acc[0:nq, :], in1=ZSg[i2][0:nq, :], op=ALU.mult),
                 r=[ak, zk], w=[ogk])
            P.dma("sp", OGscr[tok0:tok0 + nq, g * 256:(g + 1) * 256], OGt[i2][0:nq, :], r=[ogk], w=[("OGscr", tok0, g)], semkey=ogk)
            yield "done"

        def run_tiles(specs):
            gens = [nsa_tile(*sp) for sp in specs]
            n = len(gens)
            if n == 0:
                return
            next(gens[0])
            next(gens[0])
            for i in range(n):
                if i + 1 < n:
                    next(gens[i + 1])
                next(gens[i])
                if i + 1 < n:
                    next(gens[i + 1])
                next(gens[i])

        if stage >= 2:
            for g in range(4):
                P.dma("sp", KsT4[64:72, g, 0:8192], t_kaug_sel[:, :], w=["KsT4"])
                P.dma("sp", KwT4[64:72, g, 0:2560], t_kaug_win[:, :], w=["KwT4"])
                P.dma("sp", KcT4[64:72, g, :], t_kaug_cmp[:, :], w=["KcT4"])
            for T4 in range(12):
                i = cnt2["rb"] % 2
                cnt2["rb"] += 1
                rbk = "RB%d" % i
                P.dma("pool", RB[i][:, :, :], o_sel_p[T4 * 512:(T4 + 1) * 512, :].rearrange("(t p) c -> p t c", p=128), w=[rbk])
                for t_ in range(4):
                    T = T4 * 4 + t_
                    prep4(RB[i][:, t_, :], rbk, 128, KsT4[0:64, :, T * 128:(T + 1) * 128], "KsT4", Vs4[:, T, :, 0:64], "Vs4")
            for j2 in range(16):
                rbf, rk = load_rows_gather(o_sel_p[:, :], idxo2[:, j2:j2 + 1], "idxo2")
                prep4(rbf, rk, 128, KsT4[0:64, :, (48 + j2) * 128:(49 + j2) * 128], "KsT4", Vs4[:, 48 + j2, :, 0:64], "Vs4")
            for w in range(20):
                rbf, rk = load_rows_gather(winscr[:, :], idxw[:, w:w + 1], "idxw")
                prep4(rbf, rk, 128, KwT4[0:64, :, w * 128:(w + 1) * 128], "KwT4", Vw4[:, w, :, 0:64], "Vw4")
            rbf, rk = load_rows_gather(cmpscr_p[:, :], idxs[:, 0:1], "idxs")
            prep4(rbf, rk, 128, KcT4[0:64, :, :], "KcT4", Vc4[:, :, :], "Vc4")
            run_tiles([(g, tok0, nq, jl, None, 16) for g in range(4) for (tok0, nq, jl, sb_) in qtiles if jl is not None])
        if do_sample:
            for g in range(4):
                P.dma("sp", KsT4[64:72, g, 0:8192], t_kaug_sel_s[:, :], w=["KsT4"])
                P.dma("sp", KsT4[64:72, g, 8192:8320], t_kaug_new_s[:, :], w=["KsT4"])
                P.dma("sp", KwT4[64:72, g, 0:512], t_kaug_win_s[:, :], w=["KwT4"])
                P.dma("sp", KwT4[64:72, g, 512:640], t_kaug_new_s[:, :], w=["KwT4"])
                P.dma("sp", KcT4[64:72, g, :], t_kaug_cmp_s[:, :], w=["KcT4"])
            for (tok0, nq, jl, sb_) in qtiles:
                if jl is not None:
                    continue
                b = sb_
                for pg in range(64):
                    rbf, rk = load_rows_gather(csel[:, :], idxpg[:, b * 64 + pg:b * 64 + pg + 1], "idxpg")
                    prep4(rbf, rk, 128, KsT4[0:64, :, pg * 128:(pg + 1) * 128], "KsT4", Vs4[:, pg, :, 0:64], "Vs4")
                rbf, rk = load_rows_plain(o_sel_s[b * 8:(b + 1) * 8, :], 8)
                prep4(rbf, rk, 8, KsT4[0:64, :, 8192:8200], "KsT4", Vs4[0:8, 64, :, 0:64], "Vs4")
                for w in range(4):
                    rbf, rk = load_rows_plain(swin[b * 512 + w * 128:b * 512 + (w + 1) * 128, :], 128)
                    prep4(rbf, rk, 128, KwT4[0:64, :, w * 128:(w + 1) * 128], "KwT4", Vw4[:, w, :, 0:64], "Vw4")
                rbf, rk = load_rows_plain(o_win_s[b * 512 + 504:b * 512 + 512, :], 8)
                prep4(rbf, rk, 8, KwT4[0:64, :, 512:520], "KwT4", Vw4[0:8, 4, :, 0:64], "Vw4")
                rbf, rk = load_rows_plain(cmpscr_s[b * 128:(b + 1) * 128, :], 128)
                prep4(rbf, rk, 128, KcT4[0:64, :, :], "KcT4", Vc4[:, :, :], "Vc4")
                run_tiles([(g, tok0, nq, None, b, 15) for g in range(4)])

    with P.scope():
        W_outb = P.sb("W_outb", [128, NCH, D], BF16)
        for k in range(NCH):
            P.dma("pool", W_outb[:, k, :], w_out_b[k * 128:(k + 1) * 128, :], w=["W_outb"])
        lnG1 = P.sb("lnG1", [128, 1, D], F32)
        lnB1 = P.sb("lnB1", [128, 1, D], F32)
        P.dma("sp", lnG1[:, 0, :], ln_g[1:2, :].partition_broadcast(128), w=["lnG1"])
        P.dma("sp", lnB1[:, 0, :], ln_b[1:2, :].partition_broadcast(128), w=["lnB1"])
        Gp1 = P.sb("Gp1", [128, D], F32)
        Gs1 = P.sb("Gs1", [8, DEC_B, D], F32)
        P.dma("sp", Gp1[:, :], modscr[1, 0:1, 2 * D:3 * D].partition_broadcast(128), w=["Gp1"])
        for b in range(DEC_B):
            P.dma("sp", Gs1[0:8, b, :], modscr[1, 1 + b:2 + b, 2 * D:3 * D].partition_broadcast(8), w=["Gs1"])
        P.op("pool", lambda g_: g_.tensor_scalar_add(out=Gp1[:], in0=Gp1[:], scalar1=1.0), r=["Gp1"], w=["Gp1"])
        P.op("pool", lambda g_: g_.tensor_scalar_add(out=Gs1[:], in0=Gs1[:], scalar1=1.0), r=["Gs1"], w=["Gs1"])
        idxo3 = P.sb("idxo3", [128, 16], I32)
        P.dma("sp", idxo3[:], t_idx_own[:, :], w=["idxo3"])
        X1o = [P.sb("X1o%d" % i, [128, D], F32) for i in range(2)]
        OGl = [P.sb("OGl%d" % i, [128, D], BF16) for i in range(2)]
        OGT = P.sb("OGT", [128, NCH, 128], BF16)
        vo = [P.sb("vo%d" % i, [128, D], F32) for i in range(2)]
        yo = [P.sb("yo%d" % i, [128, D], F32) for i in range(2)]
        sto = [P.sb("sto%d" % i, [128, 16], F32) for i in range(2)]
        qi = 0
        for (tok0, nq, jl, sb_) in qtiles:
            i2 = qi % 2
            qi += 1
            xk, ok_, vk, yk, sk = "X1o%d" % i2, "OGl%d" % i2, "vo%d" % i2, "yo%d" % i2, "sto%d" % i2
            if jl is not None:
                P.gather(X1o[i2][:, :], x1scr[:, :], idxo3[:, jl:jl + 1], r=["idxo3"], w=[xk])
                Gt, gkey = Gp1[0:nq, :], "Gp1"
            else:
                P.dma("sp", X1o[i2][0:nq, :], x1s_scr[sb_ * 8:(sb_ + 1) * 8, :], w=[xk])
                Gt, gkey = Gs1[0:8, sb_, :], "Gs1"
            P.dma("sp", OGl[i2][0:nq, :], OGscr[tok0:tok0 + nq, :], w=[ok_])
            for k in range(NCH):
                P.op("pe", lambda t, k=k, nq=nq, i2=i2: t.transpose(psb[0][:, :].bitcast(BF16)[:, k * 128:k * 128 + nq],
                                                                   OGl[i2][0:nq, k * 128:(k + 1) * 128], ident_b[0:nq, 0:nq]),
                     r=[ok_, "ident_b"], w=[pskey(0)])
            P.op("dve", lambda v, nq=nq: v.tensor_copy(
                out=OGT[:, :, 0:nq], in_=psb[0][:, :].bitcast(BF16)[:, 0:1024].rearrange("p (k t) -> p k t", t=128)[:, :, 0:nq]),
                r=[pskey(0)], w=["OGT"])
            for half in range(2):
                pb = 2 + half
                for k in range(NCH):
                    P.op("pe", lambda t, k=k, half=half, pb=pb, nq=nq: t.matmul(
                        psb[pb][0:nq, :], lhsT=OGT[:, k, 0:nq], rhs=W_outb[:, k, half * 512:(half + 1) * 512],
                        start=(k == 0), stop=(k == NCH - 1)), r=["OGT", "W_outb"], w=[pskey(pb)])
                if jl is None and sb_ > 0:
                    pass
                P.op("dve", lambda v, half=half, pb=pb, nq=nq, i2=i2, Gt=Gt: v.tensor_tensor(
                    out=vo[i2][0:nq, half * 512:(half + 1) * 512], in0=psb[pb][0:nq, :], in1=Gt[:, half * 512:(half + 1) * 512],
                    op=ALU.mult), r=[pskey(pb), gkey], w=[vk])
            P.op("dve", lambda v, nq=nq, i2=i2: v.scalar_tensor_tensor(
                out=vo[i2][0:nq, :], in0=X1o[i2][0:nq, :], scalar=ALPHA, in1=vo[i2][0:nq, :], op0=ALU.mult, op1=ALU.add),
                r=[xk, vk], w=[vk])
            layernorm_tm(vo[i2], vk, yo[i2], yk, nq, 0, sto[i2], sk, lnG1, "lnG1", lnB1, "lnB1")
            if jl is not None:
                P.dma("sp", y_p[tok0:tok0 + nq, :], yo[i2][0:nq, :], r=[yk], w=[("y_p", tok0)], semkey=yk)
            else:
                P.dma("sp", y_s[sb_ * 8:(sb_ + 1) * 8, :], yo[i2][0:nq, :], r=[yk], w=[("y_s", sb_)], semkey=yk)

    for b in range(DEC_B):
        P.dma("act", o_win_s[b * 512:b * 512 + 504, :], swin[b * 512 + 8:(b + 1) * 512, :], w=[("o_win_s", b, 0)],
              semkey="winscopy")

    P.finish()
    print("instructions:", P.n_ins, {e: P.cnt[e] for e in P.ENG}, "dma sems:", len(P.dsem))
    return P


def _bf(x):
    return np.asarray(x, np.float32).astype(ml_dtypes.bfloat16)


def _split_pos(pos):
    pos = np.asarray(pos, np.int64)
    a = np.floor_divide(pos, 64)
    b = pos - 64 * a
    return a.astype(np.float32), b.astype(np.float32)


def _kaug(pos, valid):
    a, b = _split_pos(pos)
    n = a.shape[0]
    out = np.zeros((8, n), np.float32)
    out[0] = a; out[1] = a; out[2] = b; out[3] = b; out[4] = 1.0
    out[5] = np.where(valid, 0.0, NEGM)
    return _bf(out)


def _slopes_hi_lo():
    s = (2.0 ** (-8.0 * np.arange(1, 17) / 16.0)).astype(np.float32)
    hi = s.astype(ml_dtypes.bfloat16).astype(np.float32)
    lo = (s - hi).astype(ml_dtypes.bfloat16).astype(np.float32)
    return s, hi, lo


def _qaug(tq):
    s, hi, lo = _slopes_hi_lo()
    tq = np.asarray(tq, np.float32)
    nq = tq.shape[0]
    out = np.zeros((8, 16, nq), np.float32)
    out[0] = (64.0 * hi)[:, None]; out[1] = (64.0 * lo)[:, None]
    out[2] = hi[:, None]; out[3] = lo[:, None]
    out[4] = -(s[:, None] * tq[None, :])
    out[5] = 1.0
    return _bf(out)


def prompt_tables(k):
    cs = 2048 * k
    p = np.arange(128)
    t = {}
    t["idx_own"] = (cs + 128 * np.arange(16)[None, :] + p[:, None]).astype(np.int32)
    pos_pref = (np.arange(48 * 128) - cs)
    valid_pref = np.repeat(128 * np.arange(48) < cs, 128)
    pos_own = np.arange(2048)
    t["kaug_sel"] = np.concatenate([_kaug(pos_pref, valid_pref), _kaug(pos_own, np.ones(2048, bool))], axis=1)
    wtok = cs - 512 + np.arange(20 * 128)
    t["idx_win"] = np.maximum(wtok, 0).reshape(20, 128).T.astype(np.int32).copy()
    t["kaug_win"] = _kaug(wtok - cs, wtok >= 0)
    blk = np.concatenate([np.arange(96), 32 * k + np.arange(32)])
    valid = np.concatenate([np.arange(96) < 32 * k, np.ones(32, bool)])
    t["idx_slot"] = blk.astype(np.int32).reshape(128, 1)
    cend = 64 * blk + 63 - cs
    t["kaug_cmp"] = _kaug(cend, valid)
    tq = np.arange(2048)
    tabs = np.arange(2048) + cs
    cb = tabs // 64
    blk_abs = np.where(valid, blk, 10 ** 6)
    forced = (blk_abs[None, :] == 0) | (blk_abs[None, :] == cb[:, None]) | (blk_abs[None, :] == cb[:, None] - 1)
    caus = blk_abs[None, :] <= cb[:, None]
    fbn = np.where(forced, FORCEDV, 0.0) - np.where(caus, 0.0, 1.0)
    t["fbn"] = fbn.astype(np.float32).reshape(16, 128, 128)
    t["caus"] = caus.astype(np.float32).reshape(16, 128, 128)
    cend_abs = 64 * blk + 63
    tm = np.where(cend_abs[None, :] <= tabs[:, None], 0.0, NEGM)
    t["tm"] = tm.astype(np.float32).reshape(16, 128, 128)
    return t


def static_tables():
    t = {}
    t["qaug_p"] = _qaug(np.arange(2048)).reshape(8, 16 * 2048)
    j = np.arange(128)[:, None]
    i = np.arange(128)[None, :]
    tri = np.where(j > i, NEGM, 0.0)
    tri2 = np.where(j < i, NEGM, 0.0)
    t["tri_p"] = _bf(np.tile(tri, (1, 4)))
    t["tri2_p"] = _bf(np.tile(tri2, (1, 4)))
    i8 = np.arange(8)[None, :]
    t["tri_s"] = _bf(np.tile(np.where(j > i8, NEGM, 0.0), (1, 4)))
    t["tri2_s"] = _bf(np.tile(np.where(j < i8, NEGM, 0.0), (1, 4)))
    t["qaug_s"] = _qaug(np.arange(8)).reshape(8, 16 * 8)
    t["kaug_sel_s"] = _kaug(np.arange(8192) - 8192, np.ones(8192, bool))
    t["kaug_win_s"] = _kaug(np.arange(512) - 512, np.ones(512, bool))
    t["kaug_new_s"] = _kaug(np.arange(128), np.arange(128) < 8)
    cend = 64 * np.arange(128) + 63 - 8192
    t["kaug_cmp_s"] = _kaug(cend, np.ones(128, bool))
    fb = np.zeros((8, 128), np.float32)
    fb[:, 0] = FORCEDV; fb[:, 127] = FORCEDV
    t["fbn_s"] = fb
    t["caus_s"] = np.ones((8, 128), np.float32)
    t["tm_s"] = np.zeros((8, 128), np.float32)
    return t


def core_inputs(inp, c):
    b = c // 4
    sb = slice(4 * c, 4 * c + 4)
    f = np.ascontiguousarray
    d = {
        "xf": f(inp["x_prompt"][b]),
        "xs": f(inp["x_sample"][sb].reshape(NS_TOK, D)),
        "cvec": f(np.concatenate([inp["c_prompt"][b:b + 1], inp["c_sample"][sb]], axis=0)),
        "sh0": f(inp["state_h"][0, sb]),
        "sc0": f(inp["state_conv"][0, sb].reshape(DEC_B * 3, D)),
        "swin": f(inp["state_win"][sb].reshape(DEC_B * 512, 512)),
        "ptab": f(inp["page_table"][sb]).astype(np.int32),
        "ccmp": inp["cache_cmp"].reshape(-1, 512),
        "csel": inp["cache_sel"].reshape(-1, 512),
        "w_ada": inp["w_ada"], "b_ada": inp["b_ada"], "ln_g": inp["ln_g"], "ln_b": inp["ln_b"],
        "w_in_a": inp["w_in_a"][0], "conv_w": inp["conv_w_a"][0], "conv_b": inp["conv_b_a"],
        "w_r": inp["w_r_a"][0], "b_r": inp["b_r_a"], "w_i": inp["w_i_a"][0], "b_i": inp["b_i_a"],
        "lam": inp["lam_a"], "w_out_a": inp["w_out_a"][0], "w_kv": inp["w_kv"],
        "phi_pe": inp["phi_pe"].reshape(64, 128), "w_phi1": inp["w_phi1"], "b_phi1": inp["b_phi1"],
        "w_phi2": inp["w_phi2"], "b_phi2": inp["b_phi2"], "w_in_b": inp["w_in_b"][0],
        "b_gate": inp["b_gate_b"], "w_out_b": inp["w_out_b"][0],
    }
    for k2, v in prompt_tables(c % 4).items():
        d["t_" + k2] = v
    for k2, v in static_tables().items():
        d["t_" + k2] = v
    return {k: np.asarray(v) for k, v in d.items()}


def assemble(results, cores):
    y_prompt = np.zeros((2, SEQ, D), np.float32)
    y_sample = np.zeros((32, DEC_S, D), np.float32)
    new_cmp_p = np.zeros((2, SEQ, 4, 2, 64), np.float32)
    new_sel_p = np.zeros((2, SEQ, 4, 2, 64), np.float32)
    new_win_p = np.zeros((2, 512, 4, 2, 64), np.float32)
    new_h_p = np.zeros((1, 2, D), np.float32)
    new_conv_p = np.zeros((1, 2, 3, D), np.float32)
    new_cmp_s = np.zeros((32, DEC_S, 4, 2, 64), np.float32)
    new_sel_s = np.zeros((32, DEC_S, 4, 2, 64), np.float32)
    new_win_s = np.zeros((32, 512, 4, 2, 64), np.float32)
    new_h_s = np.zeros((1, 32, D), np.float32)
    new_conv_s = np.zeros((1, 32, 3, D), np.float32)
    for r, c in zip(results, cores):
        b, k = c // 4, c % 4
        sb = slice(4 * c, 4 * c + 4)
        y_prompt[b, k * 2048:(k + 1) * 2048] = r["y_p"]
        y_sample[sb] = r["y_s"].reshape(DEC_B, DEC_S, D)
        if k == 0:
            new_cmp_p[b] = r["o_cmp_p"].reshape(SEQ, 4, 2, 64)
            new_sel_p[b] = r["o_sel_p"].reshape(SEQ, 4, 2, 64)
            new_win_p[b] = r["o_win_p"].reshape(512, 4, 2, 64)
            new_h_p[0, b] = r["o_h_p"][0]
            new_conv_p[0, b] = r["o_conv_p"]
        new_cmp_s[sb] = r["o_cmp_s"].reshape(DEC_B, DEC_S, 4, 2, 64)
        new_sel_s[sb] = r["o_sel_s"].reshape(DEC_B, DEC_S, 4, 2, 64)
        new_win_s[sb] = r["o_win_s"].reshape(DEC_B, 512, 4, 2, 64)
        new_h_s[0, sb] = r["o_h_s"]
        new_conv_s[0, sb] = r["o_conv_s"].reshape(DEC_B, 3, D)
    return (y_prompt, y_sample, new_cmp_p, new_sel_p, new_win_p, new_h_p, new_conv_p,
            new_cmp_s, new_sel_s, new_win_s, new_h_s, new_conv_s)


def kernel(**inputs):
    inp = {k: np.asarray(v) for k, v in inputs.items()}
    n_phys = inp["cache_cmp"].shape[0]
    P = build(n_phys)
    cores = list(range(8))
    in_maps = [core_inputs(inp, c) for c in cores]
    res = run_bass_kernel_spmd(P.nc, in_maps, core_ids=cores)
    return assemble(res.results, cores)
```
